# Optimizing a Trainium2 kernel written in Bass

```python
import math
import jax
import jax.numpy as jnp
from jax import lax
import numpy as np

D_MODEL = 1024
BATCH = 8
SEQ = 4096
DEPTH = 4

GRID_W = 64
CTX_LEN = 256
EPS = 1e-6

DA_HEADS = 4
DA_HEAD_DIM = 64
DA_V_DIM = 2 * DA_HEAD_DIM
DA_WIDTH = DA_HEADS * 2 * DA_HEAD_DIM
ROPE_AXIS_DIM = DA_HEAD_DIM // 2
ROPE_THETA = 10000.0
Q_BLOCK = 128

CM_CHUNK = 128
CM_GROUPS = 4
CM_WIDTH = 512
CM_GROUP_W = CM_WIDTH // CM_GROUPS

FN_GROUPS = 4
FN_WIDTH = 512
FN_GROUP_W = FN_WIDTH // FN_GROUPS

N_BRANCH = 3
BRANCH_W = 512
SPLIT_POINTS = [DA_WIDTH, 2 * DA_WIDTH, 3 * DA_WIDTH, 3 * DA_WIDTH + CM_WIDTH,
                3 * DA_WIDTH + 2 * CM_WIDTH, 3 * DA_WIDTH + 2 * CM_WIDTH + FN_WIDTH]
IN_COLS = SPLIT_POINTS[-1] + N_BRANCH * D_MODEL

PEER_HEADS = 8
PEER_KEYS = 128
PEER_EXPERTS = PEER_KEYS * PEER_KEYS
PEER_QDIM = 256
PEER_HALF = PEER_QDIM // 2
PEER_TOPK = 16
PEER_TOKEN_BLOCK = 128

kernel_name = 'hybrid_diffattn_chunkmlp_fourier_peer_dit'

F32 = jnp.float32


def rmsnorm(x, g):
    xf = x.astype(F32)
    y = xf * lax.rsqrt(jnp.mean(xf * xf, axis=-1, keepdims=True) + EPS)
    return (y * g.astype(F32)).astype(x.dtype)


def rms_unit(x):
    xf = x.astype(F32)
    return (xf * lax.rsqrt(jnp.mean(xf * xf, axis=-1, keepdims=True) + EPS)).astype(x.dtype)


def layernorm(x, g):
    xf = x.astype(F32)
    mu = jnp.mean(xf, axis=-1, keepdims=True)
    var = jnp.mean(jnp.square(xf - mu), axis=-1, keepdims=True)
    return ((xf - mu) * lax.rsqrt(var + EPS) * g.astype(F32)).astype(x.dtype)


def modulate(x, shift, scale):
    return x * (1 + scale) + shift


def axial_angles(rows):
    r, col = jnp.meshgrid(jnp.arange(rows, dtype=F32), jnp.arange(GRID_W, dtype=F32), indexing='ij')
    freqs = ROPE_THETA ** (-jnp.arange(0, ROPE_AXIS_DIM, 2, dtype=F32) / ROPE_AXIS_DIM)
    ang_r = r.reshape(-1, 1) * freqs
    ang_c = col.reshape(-1, 1) * freqs
    return ang_r[:, None, None, :], ang_c[:, None, None, :]


def rope_axis(xp, ang):
    x1, x2 = jnp.split(xp, 2, axis=-1)
    cos, sin = jnp.cos(ang), jnp.sin(ang)
    return jnp.concatenate([x1 * cos - x2 * sin, x2 * cos + x1 * sin], axis=-1)


def apply_axial_rope(x, ang_r, ang_c):
    xf = x.astype(F32)
    out = jnp.concatenate([rope_axis(xf[..., :ROPE_AXIS_DIM], ang_r),
                           rope_axis(xf[..., ROPE_AXIS_DIM:], ang_c)], axis=-1)
    return out.astype(x.dtype)


def split_cols(p):
    B, L, _ = p.shape
    q, k, v, zu, zv, zf, gl = jnp.split(p, SPLIT_POINTS, axis=-1)
    return (q.reshape(B, L, DA_HEADS, 2, DA_HEAD_DIM),
            k.reshape(B, L, DA_HEADS, 2, DA_HEAD_DIM),
            v.reshape(B, L, DA_HEADS, DA_V_DIM),
            zu, zv, zf,
            gl.reshape(B, L, N_BRANCH, D_MODEL))


def diff_attend(q, k, v, lam):
    s = jnp.einsum('bqhmd,bkhmd->bhmqk', q, k).astype(F32) * (DA_HEAD_DIM ** -0.5)
    a = jax.nn.softmax(s, axis=-1)
    w = (a[:, :, 0] - lam * a[:, :, 1]).astype(v.dtype)
    return jnp.einsum('bhqk,bkhe->bqhe', w, v)


def latent_diff_attention(q, k_all, v_all, lam):
    B, L = q.shape[0], q.shape[1]
    nb = L // Q_BLOCK
    qb = jnp.moveaxis(q.reshape(B, nb, Q_BLOCK, DA_HEADS, 2, DA_HEAD_DIM), 1, 0)
    ob = lax.map(lambda qq: diff_attend(qq, k_all, v_all, lam), qb)
    return jnp.moveaxis(ob, 0, 1).reshape(B, L, DA_HEADS, DA_V_DIM)


def chunk_mlp(zu, zv, ln_g, w_s, b_s):
    B, L, _ = zu.shape
    u = jax.nn.gelu(zu, approximate=False)
    v = layernorm(jax.nn.gelu(zv, approximate=False), ln_g)
    vc = v.reshape(B, L // CM_CHUNK, CM_CHUNK, CM_GROUPS, CM_GROUP_W)
    s = jnp.einsum('gpq,bnqgc->bnpgc', w_s, vc) + b_s.T[:, :, None]
    return u * s.reshape(B, L, CM_WIDTH)


def fourier_mix(z):
    B, L, _ = z.shape
    zg = z.astype(F32).reshape(B, L, FN_GROUPS, FN_GROUP_W)
    return jnp.fft.fftn(zg, axes=(1, 3), norm='ortho').real.reshape(B, L, FN_WIDTH).astype(z.dtype)


def mixer_merge(att, zu, zv, zf, gl, lam_init, subln_g, cm_ln_g, cm_w_s, cm_b_s, w_branch, b_gate, w_out):
    B, L = att.shape[0], att.shape[1]
    a = (rmsnorm(att, subln_g) * (1.0 - lam_init)).reshape(B, L, DA_WIDTH)
    m = chunk_mlp(zu, zv, cm_ln_g, cm_w_s, cm_b_s)
    f = fourier_mix(zf)
    br = jnp.stack([a, m, f], axis=2)
    y = jnp.einsum('blnw,nwd->blnd', br, w_branch)
    g = jax.nn.sigmoid(gl + b_gate.reshape(N_BRANCH, D_MODEL))
    return jnp.sum(g * y, axis=2) @ w_out


def peer_ffn(h, w_q, sub_keys, u_tab, v_tab):
    B, L, D = h.shape
    hb = h.reshape(-1, PEER_TOKEN_BLOCK, D)

    def block(t):
        T = t.shape[0]
        q = rms_unit((t @ w_q).reshape(T, PEER_HEADS, 2, PEER_HALF))
        s = jnp.einsum('thpc,hpkc->thpk', q, sub_keys).astype(F32)
        s1, i1 = lax.top_k(s[:, :, 0], PEER_TOPK)
        s2, i2 = lax.top_k(s[:, :, 1], PEER_TOPK)
        cand_s = (s1[..., :, None] + s2[..., None, :]).reshape(T, PEER_HEADS, PEER_TOPK * PEER_TOPK)
        cand_i = (i1[..., :, None] * PEER_KEYS + i2[..., None, :]).reshape(T, PEER_HEADS, PEER_TOPK * PEER_TOPK)
        top_s, pos = lax.top_k(cand_s, PEER_TOPK)
        idx = jnp.take_along_axis(cand_i, pos, axis=-1)
        g = jax.nn.softmax(top_s, axis=-1)
        act = jax.nn.gelu(jnp.einsum('td,thkd->thk', t, u_tab[idx]).astype(F32), approximate=False)
        return jnp.einsum('thk,thkd->td', (g * act).astype(t.dtype), v_tab[idx])

    return lax.map(block, hb).reshape(B, L, D)


def setup_inputs(seed: int = 0) -> dict:
    key = jax.random.key(seed)
    ks = jax.random.split(key, 23)

    def nrm(k, shape, scale):
        return jax.random.normal(k, shape, F32) * scale

    D = D_MODEL
    return {
        'x': nrm(ks[0], (BATCH, SEQ, D), 1.0),
        'c': nrm(ks[1], (BATCH, D), 1.0),
        'ctx': nrm(ks[2], (BATCH, CTX_LEN, D), 1.0),
        'c_ctx': nrm(ks[3], (D,), 1.0),
        'w_ada': nrm(ks[4], (DEPTH, D, 6 * D), 0.5 * D ** -0.5),
        'b_ada': nrm(ks[5], (DEPTH, 6 * D), 0.02),
        'norm1_g': 1.0 + nrm(ks[6], (DEPTH, D), 0.02),
        'norm2_g': 1.0 + nrm(ks[7], (DEPTH, D), 0.02),
        'w_in': nrm(ks[8], (DEPTH, D, IN_COLS), D ** -0.5),
        'b_gate': nrm(ks[9], (DEPTH, N_BRANCH * D), 0.02),
        'q_norm_g': 1.0 + nrm(ks[10], (DEPTH, DA_HEAD_DIM), 0.02),
        'k_norm_g': 1.0 + nrm(ks[11], (DEPTH, DA_HEAD_DIM), 0.02),
        'lam_params': nrm(ks[12], (DEPTH, 4, DA_HEAD_DIM), 0.1),
        'subln_g': 1.0 + nrm(ks[13], (DEPTH, DA_V_DIM), 0.02),
        'cm_ln_g': 1.0 + nrm(ks[14], (DEPTH, CM_WIDTH), 0.02),
        'cm_w_s': nrm(ks[15], (DEPTH, CM_GROUPS, CM_CHUNK, CM_CHUNK), CM_CHUNK ** -0.5),
        'cm_b_s': 1.0 + nrm(ks[16], (DEPTH, CM_GROUPS, CM_CHUNK), 0.02),
        'w_branch': nrm(ks[17], (DEPTH, N_BRANCH, BRANCH_W, D), BRANCH_W ** -0.5),
        'w_out': nrm(ks[18], (DEPTH, D, D), D ** -0.5),
        'peer_w_q': nrm(ks[19], (DEPTH, D, PEER_HEADS * PEER_QDIM), D ** -0.5),
        'peer_sub_keys': nrm(ks[20], (DEPTH, PEER_HEADS, 2, PEER_KEYS, PEER_HALF), PEER_HALF ** -0.5),
        'peer_u': nrm(ks[21], (DEPTH, PEER_EXPERTS, D), D ** -0.5),
        'peer_v': nrm(ks[22], (DEPTH, PEER_EXPERTS, D), PEER_HEADS ** -0.5),
    }


def reference(x, c, ctx, c_ctx, w_ada, b_ada, norm1_g, norm2_g, w_in, b_gate, q_norm_g, k_norm_g,
              lam_params, subln_g, cm_ln_g, cm_w_s, cm_b_s, w_branch, w_out,
              peer_w_q, peer_sub_keys, peer_u, peer_v):
    B, L, D = x.shape
    Lc = ctx.shape[1]
    rows = L // GRID_W
    ang_r, ang_c = axial_angles(rows)
    c_act = jax.nn.silu(c)
    cc_act = jax.nn.silu(c_ctx)
    xc = ctx
    for l in range(DEPTH):
        last = l == DEPTH - 1
        lam_init = 0.8 - 0.6 * math.exp(-0.3 * l)
        lp = lam_params[l].astype(F32)
        lam = jnp.exp(jnp.sum(lp[0] * lp[1])) - jnp.exp(jnp.sum(lp[2] * lp[3])) + lam_init
        mod = (c_act @ w_ada[l] + b_ada[l]).reshape(B, 6, 1, D)
        mod_c = (cc_act @ w_ada[l] + b_ada[l]).reshape(6, 1, D)

        h = modulate(rmsnorm(x, norm1_g[l]), mod[:, 0], mod[:, 1])
        hc = modulate(rmsnorm(xc, norm1_g[l]), mod_c[0], mod_c[1])
        q, k, v, zu, zv, zf, gl = split_cols(h @ w_in[l])
        q = apply_axial_rope(rmsnorm(q, q_norm_g[l]), ang_r, ang_c)
        k = apply_axial_rope(rmsnorm(k, k_norm_g[l]), ang_r, ang_c)
        if last:
            kvc = hc @ w_in[l][:, SPLIT_POINTS[0]:SPLIT_POINTS[2]]
            kc = kvc[..., :DA_WIDTH].reshape(B, Lc, DA_HEADS, 2, DA_HEAD_DIM)
            vc = kvc[..., DA_WIDTH:].reshape(B, Lc, DA_HEADS, DA_V_DIM)
        else:
            qc, kc, vc, zuc, zvc, zfc, glc = split_cols(hc @ w_in[l])
        kc = rmsnorm(kc, k_norm_g[l])
        att = latent_diff_attention(q, jnp.concatenate([kc, k], axis=1),
                                    jnp.concatenate([vc, v], axis=1), lam)
        x = x + mod[:, 2] * mixer_merge(att, zu, zv, zf, gl, lam_init, subln_g[l], cm_ln_g[l], cm_w_s[l],
                                        cm_b_s[l], w_branch[l], b_gate[l], w_out[l])
        if not last:
            attc = diff_attend(rmsnorm(qc, q_norm_g[l]), kc, vc, lam)
            xc = xc + mod_c[2] * mixer_merge(attc, zuc, zvc, zfc, glc, lam_init, subln_g[l], cm_ln_g[l],
                                             cm_w_s[l], cm_b_s[l], w_branch[l], b_gate[l], w_out[l])

        h2 = modulate(rmsnorm(x, norm2_g[l]), mod[:, 3], mod[:, 4])
        x = x + mod[:, 5] * peer_ffn(h2, peer_w_q[l], peer_sub_keys[l], peer_u[l], peer_v[l])
        if not last:
            h2c = modulate(rmsnorm(xc, norm2_g[l]), mod_c[3], mod_c[4])
            xc = xc + mod_c[5] * peer_ffn(h2c, peer_w_q[l], peer_sub_keys[l], peer_u[l], peer_v[l])
    return x
```

```python
import math
from contextlib import ExitStack

import numpy as np
import ml_dtypes

import concourse.bass as bass
import concourse.mybir as mybir
from concourse.bass_utils import run_bass_kernel_spmd

F32 = mybir.dt.float32
BF16 = mybir.dt.bfloat16
U32 = mybir.dt.uint32
AF = mybir.ActivationFunctionType
ALU = mybir.AluOpType
AX = mybir.AxisListType

D = 1024
SEQ = 4096
CTX = 256
NTOK = SEQ + CTX
DEPTH = 4
NCORES = 8
EPS = 1e-6
IN_COLS = 6144
NEXP = 16384

ENGS = ["tensor", "vector", "scalar", "gpsimd", "sync"]


class Emitter:
    def __init__(self, nc, n_dma_sems=28):
        self.nc = nc
        self.lists = {e: [] for e in ENGS}
        self.cnt = {e: 0 for e in ENGS}
        self.known = {e: {} for e in ENGS}
        self.last_w = {}
        self.readers = {}
        self.n_dma = n_dma_sems
        self.dma_val = [0] * n_dma_sems
        self.dma_rr = 0
        self.n_inst = 0

    def _deps(self, reads, writes):
        deps = {}

        def add(d):
            if d is None:
                return
            s, v = d
            if deps.get(s, 0) < v:
                deps[s] = v

        for k in reads:
            add(self.last_w.get(k))
        for k in writes:
            add(self.last_w.get(k))
            for r in self.readers.get(k, ()):
                add(r)
        return deps

    def _emit_waits(self, eng, deps):
        kn = self.known[eng]
        for s, v in deps.items():
            if eng == "tensor" and s == "tensor":
                continue
            if kn.get(s, 0) >= v:
                continue
            kn[s] = v
            self.lists[eng].append(("wait", s, v))

    def _commit(self, token, reads, writes):
        for k in reads:
            lst = self.readers.setdefault(k, [])
            lst.append(token)
            if len(lst) > 64:
                mx = {}
                for s, v in lst:
                    if mx.get(s, 0) < v:
                        mx[s] = v
                self.readers[k] = list(mx.items())
        for k in writes:
            self.last_w[k] = token
            self.readers[k] = []

    def op(self, eng, fn, reads=(), writes=()):
        deps = self._deps(reads, writes)
        self._emit_waits(eng, deps)
        self.cnt[eng] += 1
        token = (eng, self.cnt[eng])
        self.lists[eng].append(("op", fn, eng, 1))
        self._commit(token, reads, writes)
        self.n_inst += 1
        return token

    def dma(self, eng, fn, reads=(), writes=()):
        deps = self._deps(reads, writes)
        i = self.dma_rr
        self.dma_rr = (self.dma_rr + 1) % self.n_dma
        s = ("dma", i)
        if self.dma_val[i] > 0:
            deps[s] = max(deps.get(s, 0), self.dma_val[i])
        self._emit_waits(eng, deps)
        self.dma_val[i] += 16
        token = (s, self.dma_val[i])
        self.lists[eng].append(("op", fn, s, 16))
        self._commit(token, reads, writes)
        self.n_inst += 1
        return token

    def _all(self):
        deps = {e: self.cnt[e] for e in ENGS if self.cnt[e] > 0}
        for i in range(self.n_dma):
            if self.dma_val[i] > 0:
                deps[("dma", i)] = self.dma_val[i]
        return deps

    def barrier_all(self):
        deps = self._all()
        for e in ENGS:
            self._emit_waits(e, dict(deps))

    def finish(self, eng="sync"):
        self._emit_waits(eng, self._all())

    def replay(self, sems, engname, engobj):
        for item in self.lists[engname]:
            if item[0] == "wait":
                engobj.wait_ge(sems[item[1]], item[2])
            else:
                _, fn, s, inc = item
                fn(engobj).then_inc(sems[s], inc)


def build(depth=DEPTH, total_depth=DEPTH, dbg=False, stop_after=None):
    nc = bass.Bass("TRN2", target_bir_lowering=False)
    em = Emitter(nc)

    def din(name, shape, dt=F32):
        return nc.dram_tensor(name, list(shape), dt, kind="ExternalInput").ap()

    x_in = din("x", [SEQ, D])
    ctx_in = din("ctx", [CTX, D])
    cT_in = din("cT", [128, 8, 2])
    w_ada = din("w_ada", [depth, D, 6 * D])
    b_ada = din("b_ada", [depth, 6 * D])
    norm1_g = din("norm1_g", [depth, D])
    norm2_g = din("norm2_g", [depth, D])
    w_in = din("w_in", [depth, D, IN_COLS])
    bgate_c = din("bgate_c", [depth, 128, 24])
    q_norm_g = din("q_norm_g", [depth, 64])
    k_norm_g = din("k_norm_g", [depth, 64])
    lam_params = din("lam_params", [depth, 256])
    subln_c = din("subln_c", [depth, 128, 1])
    cm_ln_g = din("cm_ln_g", [depth, 512])
    cmws_T = din("cmws_T", [depth, 128, 4, 128])
    cmbs_c = din("cmbs_c", [depth, 128, 4])
    w_branch = din("w_branch", [depth, 3, 512, D])
    w_out = din("w_out", [depth, D, D])
    peer_w_q = din("peer_w_q", [depth, D, 2048])
    keysT_in = din("keysT", [depth, 128, 16, 128])
    peer_u = din("peer_u", [depth, NEXP, D])
    peer_v = din("peer_v", [depth, NEXP, D])
    peer_u_flat = peer_u.rearrange("l e d -> (l e) d")
    peer_v_flat = peer_v.rearrange("l e d -> (l e) d")
    identf_in = din("identf", [128, 128])
    rope_in = din("rope", [SEQ, 64])
    dftL = din("dftL", [2, SEQ, SEQ], BF16)
    dft256 = din("dft256", [2, CTX, CTX], BF16)
    dftC = din("dftC", [2, 128, 128], BF16)

    out = nc.dram_tensor("out", [SEQ, D], F32, kind="ExternalOutput").ap()
    skind = "ExternalOutput" if dbg else "Internal"
    xs = nc.dram_tensor("xs", [NTOK, D], F32, kind=skind).ap()
    der = nc.dram_tensor("der", [2, 6, D], F32, kind=skind).ap()
    aT_s = nc.dram_tensor("aT_s", [4, 128, NTOK], BF16, kind=skind).ap()
    fT_s = nc.dram_tensor("fT_s", [4, 128, NTOK], BF16, kind=skind).ap()
    UVb = nc.dram_tensor("UVb", [depth * NEXP, 2 * D], BF16, kind="Internal").ap()

    def V(fn, r=(), w=()):
        return em.op("vector", fn, r, w)

    def A(fn, r=(), w=()):
        return em.op("scalar", fn, r, w)

    def PE(fn, r=(), w=()):
        return em.op("tensor", fn, r, w)

    def G(fn, r=(), w=()):
        return em.op("gpsimd", fn, r, w)

    def DS(out_, in_, r=(), w=()):
        return em.dma("sync", lambda e: e.dma_start(out=out_, in_=in_), r, w)

    def DG(out_, in_, r=(), w=()):
        return em.dma("gpsimd", lambda e: e.dma_start(out=out_, in_=in_), r, w)

    def tt(out_, a, b, op, r, w, eng="vector"):
        return em.op(eng, lambda e: e.tensor_tensor(out=out_, in0=a, in1=b, op=op), r, w)

    def stt(out_, a, s, b, op0, op1, r, w, accum=None):
        return V(lambda e: e.scalar_tensor_tensor(out=out_, in0=a, scalar=s, in1=b, op0=op0, op1=op1,
                                                  accum_out=accum), r, w)

    def ts(out_, a, s1, s2, op0, op1, r, w):
        if s2 is None:
            return V(lambda e: e.tensor_scalar(out=out_, in0=a, scalar1=s1, scalar2=None, op0=op0), r, w)
        return V(lambda e: e.tensor_scalar(out=out_, in0=a, scalar1=s1, scalar2=s2, op0=op0, op1=op1), r, w)

    def act(out_, in_, func, r, w, bias=None, scale=None, accum=None):
        kw = {}
        if bias is not None:
            kw["bias"] = bias
        if scale is not None:
            kw["scale"] = scale
        if accum is not None:
            kw["accum_out"] = accum
        return A(lambda e: e.activation(out=out_, in_=in_, func=func, **kw), r, w)

    def vcopy(out_, in_, r, w):
        return V(lambda e: e.tensor_copy(out=out_, in_=in_), r, w)

    def mm(out_, lhsT, rhs, start, stop, r, w):
        return PE(lambda e: e.matmul(out_, lhsT=lhsT, rhs=rhs, start=start, stop=stop), r, w)

    def tr(out_, in_, ident, r, w):
        return PE(lambda e: e.transpose(out_, in_, ident), r, w)

    def recip(out_, in_, r, w):
        return V(lambda e: e.reciprocal(out=out_, in_=in_), r, w)

    def vmax(out_, in_, r, w):
        return V(lambda e: e.max(out=out_, in_=in_), r, w)

    def vmaxidx(out_, inmax, invals, r, w):
        return V(lambda e: e.max_index(out=out_, in_max=inmax, in_values=invals), r, w)

    def vmatchrep(out_, rep, vals, r, w):
        return V(lambda e: e.match_replace(out=out_, in_to_replace=rep, in_values=vals, imm_value=-1e30), r, w)

    def vreduce(out_, in_, r, w):
        return V(lambda e: e.tensor_reduce(out=out_, in_=in_, axis=AX.X, op=ALU.add), r, w)

    def vsingle(out_, in_, scalar, op, r, w):
        return V(lambda e: e.tensor_single_scalar(out=out_, in_=in_, scalar=scalar, op=op), r, w)

    def rstd_op(t_ap, key, scale, mode):
        if mode == "ln":
            act(t_ap, t_ap, AF.Ln, [key], [key], bias=EPS, scale=scale)
            act(t_ap, t_ap, AF.Exp, [key], [key], scale=-0.5)
        else:
            act(t_ap, t_ap, AF.Sqrt, [key], [key], bias=EPS, scale=scale)
            recip(t_ap, t_ap, [key], [key])

    def gather(out_, table, idx_ap, r, w):
        return em.dma("gpsimd", lambda e: e.indirect_dma_start(
            out=out_, out_offset=None, in_=table,
            in_offset=bass.IndirectOffsetOnAxis(ap=idx_ap, axis=0)), r, w)

    with ExitStack() as top:
        _tcnt = [0]

        def T(es, name, shape, dt):
            _tcnt[0] += 1
            return es.enter_context(nc.sbuf_tensor(f"t{_tcnt[0]}_{name}", list(shape), dt))

        pA = top.enter_context(nc.psum_tensor("pA", [128, 1024], F32))
        pB = top.enter_context(nc.psum_tensor("pB", [128, 1024], F32))
        pC = top.enter_context(nc.psum_tensor("pC", [128, 1024], F32))
        pD = top.enter_context(nc.psum_tensor("pD", [128, 1024], F32))
        PA0, PA1 = pA[:, 0:512], pA[:, 512:1024]
        PB0, PB1 = pB[:, 0:512], pB[:, 512:1024]
        PC0, PC1 = pC[:, 0:512], pC[:, 512:1024]
        PD0, PD1 = pD[:, 0:512], pD[:, 512:1024]

        identf = T(top, "identf", [128, 128], F32)
        identb = T(top, "identb", [128, 128], BF16)
        onesf = T(top, "onesf", [128, 128], F32)
        onesb = T(top, "onesb", [128, 128], BF16)
        ropet = T(top, "ropet", [128, 32, 64], F32)
        io16 = T(top, "io16", [128, 16], F32)
        cact = T(top, "cact", [128, 8, 2], F32)
        CCt = T(top, "CCt", [128, 128], BF16)
        SCt = T(top, "SCt", [128, 128], BF16)
        neglam = T(top, "neglam", [128, 1], F32)
        sgcol = T(top, "sgcol", [128, 1], F32)
        lamt = T(top, "lamt", [128, 2], F32)
        lpb = T(top, "lpb", [128, 256], F32)
        lj = T(top, "lj", [128, 64], F32)

        DS(identf[:], identf_in, w=["identf"])
        vcopy(identb[:], identf[:], ["identf"], ["identb"])
        V(lambda e: e.memset(onesf[:], 1.0), w=["onesf"])
        V(lambda e: e.memset(onesb[:], 1.0), w=["onesb"])
        DS(ropet[:], rope_in.rearrange("(t p) c -> p t c", p=128), w=["ropet"])
        G(lambda e: e.iota(io16[:], pattern=[[1, 16]], base=0, channel_multiplier=0,
                           allow_small_or_imprecise_dtypes=True), w=["io16"])
        DS(cact[:], cT_in, w=["cact"])
        act(cact[:], cact[:], AF.Silu, ["cact"], ["cact"])
        DS(CCt[:], dftC[0], w=["CCt"])
        DS(SCt[:], dftC[1], w=["SCt"])
        XS = [("xs", i) for i in range(34)]
        DS(xs[0:CTX, :], ctx_in, w=XS[0:2])
        for q4 in range(4):
            DS(xs[CTX + q4 * 1024:CTX + (q4 + 1) * 1024, :], x_in[q4 * 1024:(q4 + 1) * 1024, :],
               w=XS[2 + q4 * 8:2 + (q4 + 1) * 8])

        def norm_mod(xtile, xkey, gsb, shb, mkeys, hT_out, hT_key, tp, tpkeys, sqj, ssq, h32,
                     h32key="h32", sqkey="sqj", rs="sqrt"):
            act(sqj[:], xtile, AF.Square, [xkey], [sqkey, "ssq"], accum=ssq[:])
            rstd_op(ssq[:], "ssq", 1.0 / D, rs)
            stt(h32[:], xtile, ssq[:, 0:1], gsb, ALU.mult, ALU.mult, [xkey, "ssq", mkeys[0]], [h32key])
            tt(h32[:], h32[:], shb, ALU.add, [h32key, mkeys[1]], [h32key])
            if isinstance(tp, list):
                for hf in range(2):
                    for k4 in range(4):
                        kc = hf * 4 + k4
                        tr(tp[hf][:, k4 * 128:(k4 + 1) * 128], h32[:, kc * 128:(kc + 1) * 128], identf[:],
                           [h32key, "identf"], [tpkeys[hf]])
                    act(hT_out[:, hf * 4:(hf + 1) * 4, :], tp[hf].rearrange("p (k t) -> p k t", t=128), AF.Copy,
                        [tpkeys[hf]], [hT_key])
                return
            for kc in range(8):
                tr(tp[:, kc * 128:(kc + 1) * 128], h32[:, kc * 128:(kc + 1) * 128], identf[:],
                   [h32key, "identf"], tpkeys)
            act(hT_out, tp[:, 0:1024].rearrange("p (k t) -> p k t", t=128), AF.Copy, tpkeys, [hT_key])

        def group_norm_rope(ps, pskey, gb, gbkey, nsq, ss8, kn, ra, rb_, outb, outkey, rope_tile, rs="sqrt"):
            act(nsq[:], ps, AF.Square, [pskey], ["nsq"])
            vreduce(ss8[:], nsq[:].rearrange("p (g d) -> p g d", d=64), ["nsq"], ["ss8"])
            rstd_op(ss8[:], "ss8", 1.0 / 64, rs)
            kn3 = kn[:].rearrange("p (g d) -> p g d", d=64)
            tt(kn3, ps.rearrange("p (g d) -> p g d", d=64), ss8[:].unsqueeze(2).to_broadcast([128, 8, 64]),
               ALU.mult, [pskey, "ss8"], ["kn"])
            if rope_tile is None:
                tt(outb[:].rearrange("p (g d) -> p g d", d=64), kn3, gb[:].unsqueeze(1).to_broadcast([128, 8, 64]),
                   ALU.mult, ["kn", gbkey], [outkey])
                return
            tt(kn3, kn3, gb[:].unsqueeze(1).to_broadcast([128, 8, 64]), ALU.mult, ["kn", gbkey], ["kn"])
            kn5 = kn[:].rearrange("p (g a h f) -> p g a h f", g=8, a=2, h=2, f=16)
            ob5 = outb[:].rearrange("p (g a h f) -> p g a h f", g=8, a=2, h=2, f=16)
            x1, x2 = kn5[:, :, :, 0, :], kn5[:, :, :, 1, :]
            cosb = ropet[:, rope_tile, 0:32].rearrange("p (a f) -> p a f", a=2).unsqueeze(1).to_broadcast([128, 8, 2, 16])
            sinb = ropet[:, rope_tile, 32:64].rearrange("p (a f) -> p a f", a=2).unsqueeze(1).to_broadcast([128, 8, 2, 16])
            ra4 = ra[:].rearrange("p (g a f) -> p g a f", g=8, a=2)
            rb4 = rb_[:].rearrange("p (g a f) -> p g a f", g=8, a=2)
            tt(ra4, x1, cosb, ALU.mult, ["kn", "ropet"], ["ra"])
            tt(rb4, x2, sinb, ALU.mult, ["kn", "ropet"], ["rb"])
            tt(ob5[:, :, :, 0, :], ra4, rb4, ALU.subtract, ["ra", "rb"], [outkey])
            tt(ra4, x2, cosb, ALU.mult, ["kn", "ropet"], ["ra"])
            tt(rb4, x1, sinb, ALU.mult, ["kn", "ropet"], ["rb"])
            tt(ob5[:, :, :, 1, :], ra4, rb4, ALU.add, ["ra", "rb"], [outkey])

        def load_bcast(tile, j, r, key):
            DS(tile[:], der[r, j:j + 1, :].partition_broadcast(128), r=["der"], w=[key])

        for L in range(depth):
            last = (L == total_depth - 1)
            lam_init = 0.8 - 0.6 * math.exp(-0.3 * L)
            groups = []
            if not last:
                groups.append((0, 256, True))
            for g8 in range(8):
                groups.append((CTX + g8 * 512, 512, False))

            em.barrier_all()
            with ExitStack() as s0:
                wada = [T(s0, f"wada{i}", [128, 8, 512], F32) for i in range(2)]
                badat = [T(s0, f"badat{i}", [2, 512], F32) for i in range(2)]
                gch = [T(s0, f"gch{i}", [2, 512], F32) for i in range(2)]
                modrow = [T(s0, f"modrow{i}", [2, 512], F32) for i in range(2)]
                jmap = {0: 1, 1: 0, 2: 2, 3: 4, 4: 3, 5: 5}
                for nt in range(12):
                    b = nt % 2
                    part, half = nt // 2, nt % 2
                    DS(wada[b][:], w_ada[L][:, nt * 512:(nt + 1) * 512].rearrange("(kc p) n -> p kc n", p=128),
                       w=[("wada", b)])
                    DS(badat[b][:], b_ada[L:L + 1, nt * 512:(nt + 1) * 512].partition_broadcast(2), w=[("badat", b)])
                    for kc in range(8):
                        mm(PA0[0:2, :], cact[:, kc, :], wada[b][:, kc, :], kc == 0, kc == 7,
                           [("wada", b), "cact"], ["pA0"])
                    tt(modrow[b][:], PA0[0:2, :], badat[b][:], ALU.add, ["pA0", ("badat", b)], [("modrow", b)])
                    if part in (1, 4):
                        ng = norm1_g if part == 1 else norm2_g
                        DS(gch[b][:], ng[L:L + 1, half * 512:(half + 1) * 512].partition_broadcast(2), w=[("gch", b)])
                        stt(modrow[b][:], modrow[b][:], 1.0, gch[b][:], ALU.add, ALU.mult,
                            [("modrow", b), ("gch", b)], [("modrow", b)])
                    DS(der[:, jmap[part], half * 512:(half + 1) * 512], modrow[b][:], r=[("modrow", b)], w=["der"])
                DS(lpb[:], lam_params[L:L + 1, :].partition_broadcast(128), w=["lpb"])
                stt(lj[:], lpb[:, 0:64], 1.0, lpb[:, 64:128], ALU.mult, ALU.mult, ["lpb"], ["lj", "lamt"], accum=lamt[:, 0:1])
                stt(lj[:], lpb[:, 128:192], 1.0, lpb[:, 192:256], ALU.mult, ALU.mult, ["lpb"], ["lj", "lamt"], accum=lamt[:, 1:2])
                act(lamt[:], lamt[:], AF.Exp, ["lamt"], ["lamt"])
                tt(neglam[:], lamt[:, 1:2], lamt[:, 0:1], ALU.subtract, ["lamt"], ["neglam"])
                ts(neglam[:], neglam[:], -lam_init, None, ALU.add, None, ["neglam"], ["neglam"])
                DS(sgcol[:], subln_c[L], w=["sgcol"])
                ts(sgcol[:], sgcol[:], 1.0 - lam_init, None, ALU.mult, None, ["sgcol"], ["sgcol"])
            em.barrier_all()
            if stop_after == "P0":
                break

            with ExitStack() as sZ:
                Zs = T(sZ, "Zs", [128, 34, 512], BF16)
                gs1b = [T(sZ, f"gs1b{r}", [128, D], F32) for r in range(2)]
                sh1b = [T(sZ, f"sh1b{r}", [128, D], F32) for r in range(2)]
                for r in range(2):
                    load_bcast(gs1b[r], 0, r, ("gs1b", r))
                    load_bcast(sh1b[r], 1, r, ("sh1b", r))
                with ExitStack() as sKV:
                    KT = T(sKV, "KT", [128, 4, NTOK], BF16)
                    Vs = T(sKV, "Vs", [128, 34, 512], BF16)
                    xt = [T(sKV, f"xt{i}", [128, D], F32) for i in range(2)]
                    sqj = T(sKV, "sqj", [128, D], F32)
                    ssq = T(sKV, "ssq", [128, 1], F32)
                    h32 = T(sKV, "h32", [128, D], F32)
                    nsq = T(sKV, "nsq", [128, 512], F32)
                    ss8 = T(sKV, "ss8", [128, 8], F32)
                    kn = T(sKV, "kn", [128, 512], F32)
                    ra = T(sKV, "ra", [128, 256], F32)
                    rb_ = T(sKV, "rb", [128, 256], F32)
                    kb = T(sKV, "kb", [128, 512], BF16)
                    kgb = T(sKV, "kgb", [128, 64], F32)
                    qgb = T(sKV, "qgb", [128, 64], F32)
                    DS(kgb[:], k_norm_g[L:L + 1, :].partition_broadcast(128), w=["kgb"])
                    DS(qgb[:], q_norm_g[L:L + 1, :].partition_broadcast(128), w=["qgb"])
                    ts(qgb[:], qgb[:], 0.125, None, ALU.mult, None, ["qgb"], ["qgb"])
                    with ExitStack() as s1:
                        w1 = T(s1, "w1", [128, 8, 1536], BF16)
                        hT1 = [T(s1, f"hT1_{i}", [128, 8, 128], BF16) for i in range(2)]
                        wsrc = w_in[L].rearrange("(kc p) n -> p kc n", p=128)
                        DG(w1[:, :, 0:512], wsrc[:, :, 512:1024], w=["w1"])
                        DG(w1[:, :, 512:1024], wsrc[:, :, 1024:1536], w=["w1"])
                        DG(w1[:, :, 1024:1536], wsrc[:, :, 2560:3072], w=["w1"])
                        DS(xt[0][:], xs[0:128, :], r=[XS[0]], w=[("xt", 0)])
                        for i in range(34):
                            b = i % 2
                            r = 1 if i < 2 else 0
                            if i + 1 < 34:
                                DS(xt[1 - b][:], xs[(i + 1) * 128:(i + 2) * 128, :], r=[XS[i + 1]], w=[("xt", 1 - b)])
                            tp, tpk = (pA, ["pA0", "pA1"]) if b == 0 else (pD, ["pD0", "pD1"])
                            norm_mod(xt[b][:], ("xt", b), gs1b[r][:], sh1b[r][:], [("gs1b", r), ("sh1b", r)],
                                     hT1[b][:], ("hT1", b), tp, tpk, sqj, ssq, h32)
                            for nt, (pb_, pk) in enumerate([(PB0, "pB0"), (PB1, "pB1"), (PC0, "pC0")]):
                                for kc in range(8):
                                    mm(pb_, hT1[b][:, kc, :], w1[:, kc, nt * 512:(nt + 1) * 512], kc == 0, kc == 7,
                                       [("hT1", b), "w1"], [pk])
                            group_norm_rope(PB0, "pB0", kgb, "kgb", nsq, ss8, kn, ra, rb_, kb, "kb",
                                            None if i < 2 else i - 2)
                            pcv = PC1.bitcast(BF16)
                            for h in range(4):
                                tr(pcv[:, h * 128:(h + 1) * 128], kb[:, h * 128:(h + 1) * 128], identb[:],
                                   ["kb", "identb"], ["pC1"])
                            act(KT[:, :, i * 128:(i + 1) * 128], pcv[:, 0:512].rearrange("p (h t) -> p h t", t=128),
                                AF.Copy, ["pC1"], [("KT", i)])
                            act(Vs[:, i, :], PB1, AF.Copy, ["pB1"], [("Vs", i)])
                            vcopy(Zs[:, i, :], PC0, ["pC0"], [("Zs", i)])
                    em.barrier_all()
                    if stop_after == "P1":
                        break
                    with ExitStack() as s2:
                        wq = T(s2, "wq", [128, 8, 512], BF16)
                        hT = T(s2, "hT", [128, 8, 128], BF16)
                        QT = [T(s2, f"QT{i}", [128, 4, 512], BF16) for i in range(2)]
                        PTp = [T(s2, f"PTp{i}", [128, 1024], BF16) for i in range(2)]
                        zacc = [T(s2, f"zacc{m}", [128, 512], F32) for m in range(2)]
                        rz = T(s2, "rz", [128, 512], F32)
                        O0 = T(s2, "O0", [128, 512], F32)
                        O1 = T(s2, "O1", [128, 512], F32)
                        att = T(s2, "att", [128, 512], F32)
                        asq = T(s2, "asq", [128, 512], F32)
                        rst = T(s2, "rst", [128, 512], F32)
                        aTb = [T(s2, f"aTb{i}", [128, 512], BF16) for i in range(2)]
                        DG(wq[:], w_in[L].rearrange("(kc p) n -> p kc n", p=128)[:, :, 0:512], w=["wq"])
                        cv_list = []
                        for c8 in range(8):
                            e0 = c8 * 2048
                            cv_list.append((UVb[L * NEXP + e0:L * NEXP + e0 + 2048, 0:D], peer_u[L][e0:e0 + 2048, :], ("UVb", L, c8, 0)))
                            cv_list.append((UVb[L * NEXP + e0:L * NEXP + e0 + 2048, D:2 * D], peer_v[L][e0:e0 + 2048, :], ("UVb", L, c8, 1)))
                        cv_state = {"i": 0}

                        def prep(gi):
                            row0, N, is_ctx = groups[gi]
                            r = 1 if is_ctx else 0
                            qt = QT[gi % 2]
                            qk = ("QT", gi % 2)
                            for j in range(N // 128):
                                ti = row0 // 128 + j
                                DS(xt[0][:], xs[ti * 128:(ti + 1) * 128, :], r=[XS[ti]], w=[("xt", 0)])
                                norm_mod(xt[0][:], ("xt", 0), gs1b[r][:], sh1b[r][:], [("gs1b", r), ("sh1b", r)],
                                         hT[:], "hT", [PB1, PC1], ["pB1", "pC1"], sqj, ssq, h32, rs="ln")
                                yield
                                for kc in range(8):
                                    mm(PB1, hT[:, kc, :], wq[:, kc, :], kc == 0, kc == 7, ["hT", "wq"], ["pB1"])
                                yield
                                group_norm_rope(PB1, "pB1", qgb, "qgb", nsq, ss8, kn, ra, rb_, kb, "kb",
                                                None if is_ctx else ti - 2, rs="ln")
                                yield
                                pcv = PC1.bitcast(BF16)
                                for h in range(4):
                                    tr(pcv[:, h * 128:(h + 1) * 128], kb[:, h * 128:(h + 1) * 128], identb[:],
                                       ["kb", "identb"], ["pC1"])
                                act(qt[:, :, j * 128:(j + 1) * 128], pcv[:, 0:512].rearrange("p (h t) -> p h t", t=128),
                                    AF.Copy, ["pC1"], [qk])
                                yield

                        def attend(gi, nxt):
                            row0, N, is_ctx = groups[gi]
                            qt = QT[gi % 2]
                            qk = ("QT", gi % 2)
                            if not is_ctx:
                                for _ in range(2):
                                    o_, i_, k_ = cv_list[cv_state["i"]]
                                    DG(o_, i_, w=[k_, ("cv", cv_state["i"] % 4)])
                                    cv_state["i"] += 1
                            kchunks = [0, 1] if is_ctx else list(range(34))
                            nch = len(kchunks)
                            sb = [(pD, ["pD0", "pD1"]), (pA, ["pA0", "pA1"])]
                            obank = [(PB0, "pB0"), (PC0, "pC0")]
                            step = 0
                            for h in range(4):
                                def s_mm(ci):
                                    kc = kchunks[ci]
                                    pS, pSk = sb[ci % 2]
                                    for m in range(2):
                                        lo, hi = m * 64, (m + 1) * 64
                                        mm(pS[:, m * 512:m * 512 + N], KT[lo:hi, h, kc * 128:(kc + 1) * 128], qt[lo:hi, h, 0:N],
                                           True, True, [("KT", kc), qk], [pSk[m]])

                                s_mm(0)
                                for ci, kc in enumerate(kchunks):
                                    first, lastc = ci == 0, ci == nch - 1
                                    if ci + 1 < nch:
                                        s_mm(ci + 1)
                                    pS, pSk = sb[ci % 2]
                                    pt = PTp[ci % 2]
                                    ptk = ("PT", ci % 2)
                                    act(pt[:].rearrange("p (m n) -> p m n", m=2)[:, :, 0:N],
                                        pS[:, 0:1024].rearrange("p (m n) -> p m n", m=2)[:, :, 0:N], AF.Exp, pSk, [ptk])
                                    for m in range(2):
                                        pO, pOk = obank[m]
                                        mm(pO[:, 0:N], Vs[:, kc, h * 128:(h + 1) * 128], pt[:, m * 512:m * 512 + N], first, lastc,
                                           [("Vs", kc), ptk], [pOk])
                                    for m in range(2):
                                        eng = "vector" if m == 0 else "gpsimd"
                                        if first:
                                            em.op(eng, lambda e, o_=zacc[m][:, 0:N], i_=pt[:, m * 512:m * 512 + N]: e.tensor_copy(out=o_, in_=i_),
                                                  [ptk], [("zacc", m)])
                                        else:
                                            tt(zacc[m][:, 0:N], zacc[m][:, 0:N], pt[:, m * 512:m * 512 + N], ALU.add,
                                               [("zacc", m), ptk], [("zacc", m)], eng=eng)
                                    step += 1
                                    if nxt is not None and step % 4 == 0:
                                        next(nxt, None)
                                mm(PA0[:, 0:N], onesf[:], zacc[0][:, 0:N], True, True, ["onesf", ("zacc", 0)], ["pA0"])
                                mm(PA1[:, 0:N], onesf[:], zacc[1][:, 0:N], True, True, ["onesf", ("zacc", 1)], ["pA1"])
                                recip(rz[:, 0:N], PA0[:, 0:N], ["pA0"], ["rz"])
                                tt(O0[:, 0:N], PB0[:, 0:N], rz[:, 0:N], ALU.mult, ["pB0", "rz"], ["O0"])
                                recip(rz[:, 0:N], PA1[:, 0:N], ["pA1"], ["rz"])
                                tt(O1[:, 0:N], PC0[:, 0:N], rz[:, 0:N], ALU.mult, ["pC0", "rz"], ["O1"])
                                stt(att[:, 0:N], O1[:, 0:N], neglam[:, 0:1], O0[:, 0:N], ALU.mult, ALU.add,
                                    ["O0", "O1", "neglam"], ["att"])
                                act(asq[:, 0:N], att[:, 0:N], AF.Square, ["att"], ["asq"])
                                mm(PA0[:, 0:N], onesf[:], asq[:, 0:N], True, True, ["onesf", "asq"], ["pA0"])
                                act(rst[:, 0:N], PA0[:, 0:N], AF.Ln, ["pA0"], ["rst"], bias=EPS, scale=1.0 / 128)
                                act(rst[:, 0:N], rst[:, 0:N], AF.Exp, ["rst"], ["rst"], scale=-0.5)
                                ab = aTb[h % 2]
                                stt(ab[:, 0:N], att[:, 0:N], sgcol[:, 0:1], rst[:, 0:N], ALU.mult, ALU.mult,
                                    ["att", "sgcol", "rst"], [("aTb", h % 2)])
                                DS(aT_s[h, :, row0:row0 + N], ab[:, 0:N], r=[("aTb", h % 2)], w=[("aT", gi)])
                            if nxt is not None:
                                for _ in nxt:
                                    pass

                        for _ in prep(0):
                            pass
                        for gi in range(len(groups)):
                            attend(gi, prep(gi + 1) if gi + 1 < len(groups) else None)
                    em.barrier_all()
                if stop_after == "P2a":
                    break
                with ExitStack() as s3:
                    tabC = T(s3, "tabC", [128, 32, 512], BF16)
                    tabS = T(s3, "tabS", [128, 32, 512], BF16)
                    Wc = T(s3, "Wc", [128, 512], BF16)
                    Ws = T(s3, "Ws", [128, 512], BF16)
                    fTb = [T(s3, f"fTb{i}", [128, 512], BF16) for i in range(2)]
                    for gi, (row0, N, is_ctx) in enumerate(groups):
                        if is_ctx:
                            ntc, z0 = 2, 0
                            DS(tabC[:, 0:2, 0:256], dft256[0].rearrange("(tc p) n -> p tc n", p=128), w=["tabC"])
                            DS(tabS[:, 0:2, 0:256], dft256[1].rearrange("(tc p) n -> p tc n", p=128), w=["tabS"])
                        else:
                            ntc, z0 = 32, 2
                            t0 = row0 - CTX
                            for q4 in range(4):
                                DS(tabC[:, q4 * 8:(q4 + 1) * 8, :],
                                   dftL[0][q4 * 1024:(q4 + 1) * 1024, t0:t0 + 512].rearrange("(tc p) n -> p tc n", p=128),
                                   w=["tabC"])
                                DS(tabS[:, q4 * 8:(q4 + 1) * 8, :],
                                   dftL[1][q4 * 1024:(q4 + 1) * 1024, t0:t0 + 512].rearrange("(tc p) n -> p tc n", p=128),
                                   w=["tabS"])
                        for g in range(4):
                            for tcx in range(ntc):
                                mm(PA0[:, 0:N], Zs[:, z0 + tcx, g * 128:(g + 1) * 128], tabC[:, tcx, 0:N],
                                   tcx == 0, tcx == ntc - 1, [("Zs", z0 + tcx), "tabC"], ["pA0"])
                            for tcx in range(ntc):
                                mm(PA1[:, 0:N], Zs[:, z0 + tcx, g * 128:(g + 1) * 128], tabS[:, tcx, 0:N],
                                   tcx == 0, tcx == ntc - 1, [("Zs", z0 + tcx), "tabS"], ["pA1"])
                            act(Wc[:, 0:N], PA0[:, 0:N], AF.Copy, ["pA0"], ["Wc"])
                            vcopy(Ws[:, 0:N], PA1[:, 0:N], ["pA1"], ["Ws"])
                            mm(PB0[:, 0:N], CCt[:], Wc[:, 0:N], True, False, ["CCt", "Wc"], ["pB0"])
                            mm(PB0[:, 0:N], SCt[:], Ws[:, 0:N], False, True, ["SCt", "Ws"], ["pB0"])
                            fb = fTb[g % 2]
                            act(fb[:, 0:N], PB0[:, 0:N], AF.Copy, ["pB0"], [("fTb", g % 2)])
                            DS(fT_s[g, :, row0:row0 + N], fb[:, 0:N], r=[("fTb", g % 2)], w=[("fT", gi)])
                em.barrier_all()
            if stop_after == "P2b":
                break
            with ExitStack() as s4:
                gs1b = [T(s4, f"c_gs1b{r}", [128, D], F32) for r in range(2)]
                sh1b = [T(s4, f"c_sh1b{r}", [128, D], F32) for r in range(2)]
                g1b = [T(s4, f"c_g1b{r}", [128, D], F32) for r in range(2)]
                for r in range(2):
                    load_bcast(gs1b[r], 0, r, ("gs1b", r))
                    load_bcast(sh1b[r], 1, r, ("sh1b", r))
                    load_bcast(g1b[r], 2, r, ("g1b", r))
                xg = T(s4, "xg", [128, 4, D], F32)
                sqj = T(s4, "c_sqj", [128, D], F32)
                ssq = T(s4, "c_ssq", [128, 1], F32)
                h32 = T(s4, "c_h32", [128, D], F32)
                hT = T(s4, "c_hT", [128, 8, 512], BF16)
                wzz = T(s4, "wzz", [128, 8, 1024], BF16)
                wbr = T(s4, "wbr", [128, 12, D], BF16)
                wot = T(s4, "wot", [128, 8, D], BF16)
                wgl = [T(s4, f"wgl{i}", [128, 8, 3, 128], BF16) for i in range(2)]
                wsT = T(s4, "wsT", [128, 4, 128], BF16)
                bsc = T(s4, "bsc", [128, 4], F32)
                bgc = T(s4, "bgc", [128, 24], F32)
                lngb = T(s4, "lngb", [128, 512], F32)
                u_t = T(s4, "u_t", [128, 512], F32)
                gv = T(s4, "gv", [128, 512], F32)
                bst = T(s4, "bst", [128, 6], F32)
                bag = T(s4, "bag", [128, 2], F32)
                vb = T(s4, "vb", [128, 512], BF16)
                mb = T(s4, "mb", [128, 512], BF16)
                mT = T(s4, "mT", [128, 4, 512], BF16)
                aT = T(s4, "aT", [128, 4, 512], BF16)
                fT = T(s4, "fT", [128, 4, 512], BF16)
                zT = T(s4, "zT", [128, 8, 512], BF16)
                gsig = [T(s4, f"gsig{i}", [128, 512], F32) for i in range(2)]
                zacc = T(s4, "zacc", [128, 512], F32)
                ztmp = T(s4, "ztmp", [128, 512], F32)
                otmp = T(s4, "otmp", [128, 512], F32)
                xnew = [T(s4, f"xnew{i}", [128, D], F32) for i in range(2)]
                wsrc = w_in[L].rearrange("(kc p) n -> p kc n", p=128)
                DG(wzz[:, :, 0:512], wsrc[:, :, 1536:2048], w=["wzz"])
                DG(wzz[:, :, 512:1024], wsrc[:, :, 2048:2560], w=["wzz"])
                for n3 in range(3):
                    for hf in range(2):
                        DG(wbr[:, n3 * 4:(n3 + 1) * 4, hf * 512:(hf + 1) * 512],
                           w_branch[L, n3].rearrange("(wc p) d -> p wc d", p=128)[:, :, hf * 512:(hf + 1) * 512], w=["wbr"])
                for hf in range(2):
                    DG(wot[:, :, hf * 512:(hf + 1) * 512],
                       w_out[L].rearrange("(kc p) n -> p kc n", p=128)[:, :, hf * 512:(hf + 1) * 512], w=["wot"])
                DG(wsT[:], cmws_T[L], w=["wsT"])
                DS(bsc[:], cmbs_c[L], w=["bsc"])
                DS(bgc[:], bgate_c[L], w=["bgc"])
                DS(lngb[:], cm_ln_g[L:L + 1, :].partition_broadcast(128), w=["lngb"])
                wgl_i = 0
                for gi, (row0, N, is_ctx) in enumerate(groups):
                    r = 1 if is_ctx else 0
                    nj = N // 128
                    DS(aT[:, :, 0:N], aT_s[:, :, row0:row0 + N].rearrange("h p t -> p h t"), r=[("aT", gi)], w=["aTt"])
                    DS(fT[:, :, 0:N], fT_s[:, :, row0:row0 + N].rearrange("h p t -> p h t"), r=[("fT", gi)], w=["fTt"])
                    for j in range(nj):
                        ti = row0 // 128 + j
                        DS(xg[:, j, :], xs[ti * 128:(ti + 1) * 128, :], r=[XS[ti]], w=[("xg", j)])
                        norm_mod(xg[:, j, :], ("xg", j), gs1b[r][:], sh1b[r][:], [("gs1b", r), ("sh1b", r)],
                                 hT[:, :, j * 128:(j + 1) * 128], "hT", pA, ["pA0", "pA1"], sqj, ssq, h32)
                        for nt, (pb_, pk) in enumerate([(PB0, "pB0"), (PB1, "pB1")]):
                            for kc in range(8):
                                mm(pb_, hT[:, kc, j * 128:(j + 1) * 128], wzz[:, kc, nt * 512:(nt + 1) * 512],
                                   kc == 0, kc == 7, ["hT", "wzz"], [pk])
                        act(u_t[:], PB0, AF.Gelu, ["pB0"], ["u_t"])
                        act(gv[:], PB1, AF.Gelu, ["pB1"], ["gv"])
                        V(lambda e, bst=bst, gv=gv: e.bn_stats(out=bst[:], in_=gv[:]), ["gv"], ["bst"])
                        V(lambda e, bst=bst, bag=bag: e.bn_aggr(out=bag[:], in_=bst[:]), ["bst"], ["bag"])
                        act(bag[:, 1:2], bag[:, 1:2], AF.Sqrt, ["bag"], ["bag"], bias=EPS, scale=1.0)
                        recip(bag[:, 1:2], bag[:, 1:2], ["bag"], ["bag"])
                        ts(gv[:], gv[:], bag[:, 0:1], bag[:, 1:2], ALU.subtract, ALU.mult, ["gv", "bag"], ["gv"])
                        tt(vb[:], gv[:], lngb[:], ALU.mult, ["gv", "lngb"], ["vb"])
                        for g in range(4):
                            mm(PC0[:, g * 128:(g + 1) * 128], wsT[:, g, :], vb[:, g * 128:(g + 1) * 128], True, True,
                               ["wsT", "vb"], ["pC0"])
                        for g in range(4):
                            stt(mb[:, g * 128:(g + 1) * 128], PC0[:, g * 128:(g + 1) * 128], bsc[:, g:g + 1],
                                u_t[:, g * 128:(g + 1) * 128], ALU.add, ALU.mult, ["pC0", "bsc", "u_t"], ["mb"])
                        pcv = PC1.bitcast(BF16)
                        for g in range(4):
                            tr(pcv[:, g * 128:(g + 1) * 128], mb[:, g * 128:(g + 1) * 128], identb[:],
                               ["mb", "identb"], ["pC1"])
                        act(mT[:, :, j * 128:(j + 1) * 128], pcv[:, 0:512].rearrange("p (h t) -> p h t", t=128),
                            AF.Copy, ["pC1"], ["mT"])
                    brs = [(aT, "aTt"), (mT, "mT"), (fT, "fTt")]
                    for dc in range(8):
                        wb_ = wgl_i % 2
                        wgl_i += 1
                        for n3 in range(3):
                            c0 = 3072 + n3 * 1024 + dc * 128
                            DG(wgl[wb_][:, :, n3, :], wsrc[:, :, c0:c0 + 128], w=[("wgl", wb_)])
                        for n3 in range(3):
                            brT, brk = brs[n3]
                            par = (dc * 3 + n3) % 2
                            pY, pYk = (PD0, "pD0") if par == 0 else (PC0, "pC0")
                            pG, pGk = (PD1, "pD1") if par == 0 else (PC1, "pC1")
                            for wc in range(4):
                                mm(pY[:, 0:N], wbr[:, n3 * 4 + wc, dc * 128:(dc + 1) * 128], brT[:, wc, 0:N],
                                   wc == 0, wc == 3, ["wbr", brk], [pYk])
                            for kc in range(8):
                                mm(pG[:, 0:N], wgl[wb_][:, kc, n3, :], hT[:, kc, 0:N], kc == 0, kc == 7,
                                   [("wgl", wb_), "hT"], [pGk])
                            gs_ = gsig[par]
                            act(gs_[:, 0:N], pG[:, 0:N], AF.Sigmoid, [pGk, "bgc"], [("gsig", par)],
                                bias=bgc[:, n3 * 8 + dc:n3 * 8 + dc + 1])
                            if n3 == 0:
                                tt(zacc[:, 0:N], pY[:, 0:N], gs_[:, 0:N], ALU.mult, [pYk, ("gsig", par)], ["zacc"])
                            elif n3 == 1:
                                tt(ztmp[:, 0:N], pY[:, 0:N], gs_[:, 0:N], ALU.mult, [pYk, ("gsig", par)], ["ztmp"])
                                tt(zacc[:, 0:N], zacc[:, 0:N], ztmp[:, 0:N], ALU.add, ["zacc", "ztmp"], ["zacc"])
                            else:
                                tt(ztmp[:, 0:N], pY[:, 0:N], gs_[:, 0:N], ALU.mult, [pYk, ("gsig", par)], ["ztmp"])
                                tt(zT[:, dc, 0:N], zacc[:, 0:N], ztmp[:, 0:N], ALU.add, ["zacc", "ztmp"], ["zT"])
                    for j in range(nj):
                        ti = row0 // 128 + j
                        xn_ = xnew[j % 2]
                        for hf, (pb_, pk) in enumerate([(PB0, "pB0"), (PB1, "pB1")]):
                            for dc in range(8):
                                mm(pb_, zT[:, dc, j * 128:(j + 1) * 128], wot[:, dc, hf * 512:(hf + 1) * 512],
                                   dc == 0, dc == 7, ["zT", "wot"], [pk])
                            tt(otmp[:], pb_, g1b[r][:, hf * 512:(hf + 1) * 512], ALU.mult, [pk, ("g1b", r)], ["otmp"])
                            tt(xn_[:, hf * 512:(hf + 1) * 512], otmp[:], xg[:, j, hf * 512:(hf + 1) * 512], ALU.add,
                               ["otmp", ("xg", j)], [("xnew", j % 2)])
                        DG(xs[ti * 128:(ti + 1) * 128, :], xn_[:], r=[("xnew", j % 2)], w=[XS[ti]])
            em.barrier_all()
            if stop_after == "P2c":
                break
            with ExitStack() as s5:
                gs2b = T(s5, "gs2b", [128, D], F32)
                sh2b = T(s5, "sh2b", [128, D], F32)
                g2b = [T(s5, f"g2b{r}", [128, D], F32) for r in range(2)]
                nr = 1 if last else 2
                for r in range(nr):
                    load_bcast(g2b[r], 5, r, ("g2b", r))
                xt = [T(s5, f"p_xt{i}", [128, D], F32) for i in range(2)]
                ssq = T(s5, "p_ssq", [128, 1], F32)
                h32 = [T(s5, f"p_h32_{i}", [128, D], F32) for i in range(2)]
                hT2 = T(s5, "hT2", [128, 8, 128], BF16)
                wpq = T(s5, "wpq", [128, 8, 2048], BF16)
                keysT = T(s5, "keysT", [128, 16, 128], BF16)
                ss16 = T(s5, "ss16", [128, 16], F32)
                qn = T(s5, "qn", [128, 2048], BF16)
                qnT = T(s5, "qnT", [128, 16, 128], BF16)
                s_sb = T(s5, "s_sb", [128, 2048], F32)
                s2 = T(s5, "s2", [128, 128], F32)
                ta = T(s5, "ta", [128, 8, 16], F32)
                tb = T(s5, "tb", [128, 8, 16], F32)
                tcv = T(s5, "tcv", [128, 8, 16], F32)
                ia = T(s5, "ia", [128, 8, 16], U32)
                ib = T(s5, "ib", [128, 8, 16], U32)
                pos = T(s5, "pos", [128, 8, 16], U32)
                k1 = T(s5, "k1", [128, 8, 16], U32)
                k2 = T(s5, "k2", [128, 8, 16], U32)
                k1f = T(s5, "k1f", [128, 8, 16], F32)
                k2f = T(s5, "k2f", [128, 8, 16], F32)
                iaf = T(s5, "iaf", [128, 8, 16], F32)
                ibf = T(s5, "ibf", [128, 8, 16], F32)
                isel = T(s5, "isel", [128, 8, 16], F32)
                jsel = T(s5, "jsel", [128, 8, 16], F32)
                idxf = T(s5, "idxf", [128, 128], F32)
                idxu = [T(s5, f"idxu{i}", [128, 128], U32) for i in range(2)]
                cand = T(s5, "cand", [128, 16, 16], F32)
                cand2 = T(s5, "cand2", [128, 256], F32)
                eq4 = T(s5, "eq4", [128, 8, 16, 16], F32)
                ee = T(s5, "ee", [128, 8, 16], F32)
                zz = T(s5, "zz", [128, 8], F32)
                gw = [T(s5, f"gw{i}", [128, 128], F32) for i in range(2)]
                actv = T(s5, "actv", [128, 128], F32)
                gact = T(s5, "gact", [128, 128], F32)
                xo = T(s5, "xo", [128, D], F32)
                junk = T(s5, "junk", [128, D], F32)
                gw2 = T(s5, "gw2", [128, 128], F32)
                dgt = [T(s5, f"dgt{i}", [128, 128], BF16) for i in range(4)]
                rem = int(nc.sbuf_bytes_remaining)
                NS = max(8, min(24, (rem - 3072) // 4096))
                gbuf = [T(s5, f"gbuf{i}", [128, 2 * D], BF16) for i in range(NS)]
                uvkeys = [("UVb", L, c8, uv) for c8 in range(8) for uv in range(2)]
                for q4 in range(4):
                    DG(wpq[:, :, q4 * 512:(q4 + 1) * 512],
                       peer_w_q[L].rearrange("(kc p) n -> p kc n", p=128)[:, :, q4 * 512:(q4 + 1) * 512], w=["wpq"])
                DG(keysT[:], keysT_in[L], w=["keysT"])
                tiles = list(range(2, 34)) if last else list(range(34))
                qbanks = [(PC0, "pC0"), (PC1, "pC1"), (PD0, "pD0"), (PD1, "pD1")]
                state = {"dcnt": 0, "gcnt": 0, "mod_r": None}

                def front(tix):
                    ti = tiles[tix]
                    b = tix % 2
                    r = 1 if ti < 2 else 0
                    if state["mod_r"] != r:
                        load_bcast(gs2b, 3, r, "gs2b")
                        load_bcast(sh2b, 4, r, "sh2b")
                        state["mod_r"] = r
                    norm_mod(xt[b][:], ("xt", b), gs2b[:], sh2b[:], ["gs2b", "sh2b"],
                             hT2[:], "hT2", pA, ["pA0", "pA1"], junk, ssq, h32[b], h32key=("h32", b), sqkey="junk")
                    yield
                    for nt, (pb_, pk) in enumerate(qbanks):
                        for kc in range(8):
                            mm(pb_, hT2[:, kc, :], wpq[:, kc, nt * 512:(nt + 1) * 512], kc == 0, kc == 7,
                               ["hT2", "wpq"], [pk])
                    yield
                    for nt, (pb_, pk) in enumerate(qbanks):
                        act(s_sb[:, nt * 512:(nt + 1) * 512], pb_, AF.Square, [pk], ["s_sb"])
                    vreduce(ss16[:], s_sb[:].rearrange("p (g d) -> p g d", d=128), ["s_sb"], ["ss16"])
                    act(ss16[:], ss16[:], AF.Sqrt, ["ss16"], ["ss16"], bias=EPS, scale=1.0 / 128)
                    recip(ss16[:], ss16[:], ["ss16"], ["ss16"])
                    for hp in range(16):
                        pb_, pk = qbanks[hp // 4]
                        act(qn[:, hp * 128:(hp + 1) * 128], pb_[:, (hp % 4) * 128:(hp % 4 + 1) * 128], AF.Copy,
                            [pk, "ss16"], ["qn"], scale=ss16[:, hp:hp + 1])
                    yield
                    pav = pA[:, 0:1024].bitcast(BF16)
                    for hp in range(16):
                        tr(pav[:, hp * 128:(hp + 1) * 128], qn[:, hp * 128:(hp + 1) * 128], identb[:],
                           ["qn", "identb"], ["pA0", "pA1"])
                    act(qnT[:].rearrange("p h t -> p (h t)"), pav[:, 0:2048], AF.Copy, ["pA0", "pA1"], ["qnT"])
                    yield
                    for hp in range(16):
                        pb_, pk = qbanks[hp // 4]
                        mm(pb_[:, (hp % 4) * 128:(hp % 4 + 1) * 128], qnT[:, hp, :], keysT[:, hp, :], True, True,
                           ["qnT", "keysT"], [pk])
                    for nt, (pb_, pk) in enumerate(qbanks):
                        act(s_sb[:, nt * 512:(nt + 1) * 512], pb_, AF.Copy, [pk], ["s_sb"])
                    yield
                    for h in range(8):
                        for side, (tv, iv) in enumerate([(ta, ia), (tb, ib)]):
                            sv = s_sb[:, (2 * h + side) * 128:(2 * h + side + 1) * 128]
                            tk, ik = ("ta", "ia") if side == 0 else ("tb", "ib")
                            vmax(tv[:, h, 0:8], sv, ["s_sb"], [tk])
                            vmaxidx(iv[:, h, 0:8], tv[:, h, 0:8], sv, ["s_sb", tk], [ik])
                            vmatchrep(s2[:], tv[:, h, 0:8], sv, ["s_sb", tk], ["s2"])
                            vmax(tv[:, h, 8:16], s2[:], ["s2"], [tk])
                            vmaxidx(iv[:, h, 8:16], tv[:, h, 8:16], s2[:], ["s2", tk], [ik])
                        tt(cand[:], ta[:, h, :].unsqueeze(2).to_broadcast([128, 16, 16]),
                           tb[:, h, :].unsqueeze(1).to_broadcast([128, 16, 16]), ALU.add, ["ta", "tb"], ["cand"])
                        cf = cand[:].rearrange("p a b -> p (a b)")
                        vmax(tcv[:, h, 0:8], cf, ["cand"], ["tcv"])
                        vmaxidx(pos[:, h, 0:8], tcv[:, h, 0:8], cf, ["cand", "tcv"], ["pos"])
                        vmatchrep(cand2[:], tcv[:, h, 0:8], cf, ["cand", "tcv"], ["cand2"])
                        vmax(tcv[:, h, 8:16], cand2[:], ["cand2"], ["tcv"])
                        vmaxidx(pos[:, h, 8:16], tcv[:, h, 8:16], cand2[:], ["cand2", "tcv"], ["pos"])
                        yield
                    vsingle(k1[:], pos[:], 4, ALU.arith_shift_right, ["pos"], ["k1"])
                    vsingle(k2[:], pos[:], 15, ALU.bitwise_and, ["pos"], ["k2"])
                    vcopy(k1f[:], k1[:], ["k1"], ["k1f"])
                    vcopy(k2f[:], k2[:], ["k2"], ["k2f"])
                    vcopy(iaf[:], ia[:], ["ia"], ["iaf"])
                    vcopy(ibf[:], ib[:], ["ib"], ["ibf"])
                    iob = io16[:].unsqueeze(1).unsqueeze(1).to_broadcast([128, 8, 16, 16])
                    for kf, kfk, ixf, ixk, osel, osk in [(k1f, "k1f", iaf, "iaf", isel, "isel"),
                                                         (k2f, "k2f", ibf, "ibf", jsel, "jsel")]:
                        tt(eq4[:], kf[:].unsqueeze(3).to_broadcast([128, 8, 16, 16]), iob, ALU.is_equal,
                           [kfk, "io16"], ["eq4"])
                        tt(eq4[:], eq4[:], ixf[:].unsqueeze(2).to_broadcast([128, 8, 16, 16]), ALU.mult,
                           ["eq4", ixk], ["eq4"])
                        vreduce(osel[:], eq4[:], ["eq4"], [osk])
                    yield
                    stt(idxf[:], isel[:].rearrange("p h k -> p (h k)"), 128.0, jsel[:].rearrange("p h k -> p (h k)"),
                        ALU.mult, ALU.add, ["isel", "jsel"], ["idxf"])
                    if L > 0:
                        ts(idxf[:], idxf[:], float(L * NEXP), None, ALU.add, None, ["idxf"], ["idxf"])
                    vcopy(idxu[b][:], idxf[:], ["idxf"], [("idxu", b)])
                    tt(ee[:], tcv[:], tcv[:, :, 0:1].to_broadcast([128, 8, 16]), ALU.subtract, ["tcv"], ["ee"])
                    act(ee[:], ee[:], AF.Exp, ["ee"], ["ee"])
                    vreduce(zz[:], ee[:], ["ee"], ["zz"])
                    recip(zz[:], zz[:], ["zz"], ["zz"])
                    tt(gw[b][:].rearrange("p (h k) -> p h k", k=16), ee[:], zz[:].unsqueeze(2).to_broadcast([128, 8, 16]),
                       ALU.mult, ["ee", "zz"], [("gw", b)])

                def back(tix, nxt):
                    ti = tiles[tix]
                    b = tix % 2
                    r = 1 if ti < 2 else 0

                    def stage1(bi):
                        for q8 in range(8):
                            hk = bi * 8 + q8
                            sl = state["gcnt"] % NS
                            state["gcnt"] += 1
                            slots[hk] = sl
                            gather(gbuf[sl][:], UVb, idxu[b][:, hk:hk + 1], [("idxu", b)] + uvkeys, [("gb", sl)])
                            stt(junk[:], gbuf[sl][:, 0:D], 1.0, h32[b][:], ALU.mult, ALU.mult, [("gb", sl), ("h32", b)],
                                ["junk", ("actv", bi)], accum=actv[:, hk:hk + 1])
                        act(gact[:, bi * 8:(bi + 1) * 8], actv[:, bi * 8:(bi + 1) * 8], AF.Gelu, [("actv", bi)], [("gact", bi)])

                    def stage2(bi):
                        tt(gw2[:, bi * 8:(bi + 1) * 8], gw[b][:, bi * 8:(bi + 1) * 8], gact[:, bi * 8:(bi + 1) * 8], ALU.mult,
                           [("gw", b), ("gact", bi)], [("gw2", bi)])
                        for q8 in range(8):
                            hk = bi * 8 + q8
                            sl = slots[hk]
                            dd = state["dcnt"] % 4
                            state["dcnt"] += 1
                            act(dgt[dd][:], identb[:], AF.Copy, ["identb", ("gw2", bi)], [("dg", dd)], scale=gw2[:, hk:hk + 1])
                            mm(PB0, dgt[dd][:], gbuf[sl][:, D:D + 512], hk == 0, hk == 127, [("dg", dd), ("gb", sl)], ["pB0"])
                            mm(PB1, dgt[dd][:], gbuf[sl][:, D + 512:2 * D], hk == 0, hk == 127, [("dg", dd), ("gb", sl)], ["pB1"])

                    slots = {}
                    stage1(0)
                    for bi in range(1, 16):
                        stage1(bi)
                        stage2(bi - 1)
                        if nxt is not None:
                            next(nxt, None)
                    stage2(15)
                    if nxt is not None:
                        for _ in nxt:
                            pass
                    tt(xo[:, 0:512], PB0, g2b[r][:, 0:512], ALU.mult, ["pB0", ("g2b", r)], ["xo"])
                    tt(xo[:, 512:D], PB1, g2b[r][:, 512:D], ALU.mult, ["pB1", ("g2b", r)], ["xo"])
                    tt(xo[:], xo[:], xt[b][:], ALU.add, ["xo", ("xt", b)], ["xo"])
                    if last:
                        DS(out[(ti - 2) * 128:(ti - 1) * 128, :], xo[:], r=["xo"], w=[("out", ti)])
                    else:
                        DS(xs[ti * 128:(ti + 1) * 128, :], xo[:], r=["xo"], w=[XS[ti]])
                    if tix + 2 < len(tiles):
                        tn = tiles[tix + 2]
                        DS(xt[b][:], xs[tn * 128:(tn + 1) * 128, :], r=[XS[tn]], w=[("xt", b)])

                DS(xt[0][:], xs[tiles[0] * 128:(tiles[0] + 1) * 128, :], r=[XS[tiles[0]]], w=[("xt", 0)])
                if len(tiles) > 1:
                    DS(xt[1][:], xs[tiles[1] * 128:(tiles[1] + 1) * 128, :], r=[XS[tiles[1]]], w=[("xt", 1)])
                for _ in front(0):
                    pass
                for tix in range(len(tiles)):
                    nxt = front(tix + 1) if tix + 1 < len(tiles) else None
                    back(tix, nxt)
            em.barrier_all()

        em.finish("sync")
        semkeys = list(ENGS) + [("dma", i) for i in range(em.n_dma)]
        sems = {k: top.enter_context(nc.semaphore(f"sem{j}")) for j, k in enumerate(semkeys)}
        with nc.Block() as block:
            @block.sync
            def _(e):
                em.replay(sems, "sync", e)

            @block.scalar
            def _(e):
                em.replay(sems, "scalar", e)

            @block.vector
            def _(e):
                em.replay(sems, "vector", e)

            @block.gpsimd
            def _(e):
                em.replay(sems, "gpsimd", e)

            @block.tensor
            def _(e):
                em.replay(sems, "tensor", e)
    return nc, em


_CONST = {}


def _constants():
    if _CONST:
        return _CONST
    bf = ml_dtypes.bfloat16
    t = np.arange(SEQ, dtype=np.int64)
    m = (t[:, None] * t[None, :]) % SEQ
    ang = (2.0 * np.pi / SEQ) * m.astype(np.float64)
    dftL = np.empty((2, SEQ, SEQ), dtype=bf)
    dftL[0] = (np.cos(ang) / 64.0).astype(np.float32).astype(bf)
    dftL[1] = (-np.sin(ang) / 64.0).astype(np.float32).astype(bf)
    del ang, m
    t2 = np.arange(CTX, dtype=np.int64)
    a2 = (2.0 * np.pi / CTX) * ((t2[:, None] * t2[None, :]) % CTX).astype(np.float64)
    dft256 = np.stack([np.cos(a2) / 16.0, -np.sin(a2) / 16.0]).astype(np.float32).astype(bf)
    c = np.arange(128, dtype=np.int64)
    a3 = (2.0 * np.pi / 128) * ((c[:, None] * c[None, :]) % 128).astype(np.float64)
    s128 = 1.0 / math.sqrt(128.0)
    dftC = np.stack([np.cos(a3) * s128, np.sin(a3) * s128]).astype(np.float32).astype(bf)
    freqs = (10000.0 ** (-np.arange(0, 32, 2, dtype=np.float32) / 32.0)).astype(np.float32)
    rr = (t // 64).astype(np.float32)
    cc = (t % 64).astype(np.float32)
    ang_r = rr[:, None] * freqs[None, :]
    ang_c = cc[:, None] * freqs[None, :]
    rope = np.concatenate([np.cos(ang_r), np.cos(ang_c), np.sin(ang_r), np.sin(ang_c)], axis=1).astype(np.float32)
    _CONST.update(dftL=dftL, dft256=dft256, dftC=dftC, rope=rope, identf=np.eye(128, dtype=np.float32))
    return _CONST


def make_in_maps(inputs, depth=DEPTH, cores=NCORES):
    f = lambda a: np.ascontiguousarray(np.asarray(a, dtype=np.float32))
    cst = _constants()
    x = f(inputs["x"]); c = f(inputs["c"]); ctx = f(inputs["ctx"]); c_ctx = f(inputs["c_ctx"])
    sl = slice(0, depth)
    shared = {
        "w_ada": f(inputs["w_ada"])[sl], "b_ada": f(inputs["b_ada"])[sl],
        "norm1_g": f(inputs["norm1_g"])[sl], "norm2_g": f(inputs["norm2_g"])[sl],
        "w_in": f(inputs["w_in"])[sl],
        "bgate_c": np.ascontiguousarray(f(inputs["b_gate"])[sl].reshape(depth, 24, 128).transpose(0, 2, 1)),
        "q_norm_g": f(inputs["q_norm_g"])[sl], "k_norm_g": f(inputs["k_norm_g"])[sl],
        "lam_params": f(inputs["lam_params"])[sl].reshape(depth, 256),
        "subln_c": f(inputs["subln_g"])[sl].reshape(depth, 128, 1),
        "cm_ln_g": f(inputs["cm_ln_g"])[sl],
        "cmws_T": np.ascontiguousarray(f(inputs["cm_w_s"])[sl].transpose(0, 3, 1, 2)),
        "cmbs_c": np.ascontiguousarray(f(inputs["cm_b_s"])[sl].transpose(0, 2, 1)),
        "w_branch": f(inputs["w_branch"])[sl], "w_out": f(inputs["w_out"])[sl],
        "peer_w_q": f(inputs["peer_w_q"])[sl],
        "keysT": np.ascontiguousarray(f(inputs["peer_sub_keys"])[sl].reshape(depth, 16, 128, 128).transpose(0, 3, 1, 2)),
        "peer_u": f(inputs["peer_u"])[sl], "peer_v": f(inputs["peer_v"])[sl],
        "identf": cst["identf"], "rope": cst["rope"], "dftL": cst["dftL"], "dft256": cst["dft256"], "dftC": cst["dftC"],
    }
    maps = []
    for b in range(cores):
        cv = np.stack([c[b], c_ctx], axis=0)
        cT = np.ascontiguousarray(cv.reshape(2, 8, 128).transpose(2, 1, 0))
        mp = dict(shared)
        mp.update({"x": x[b], "ctx": ctx[b], "cT": cT})
        maps.append(mp)
    return maps


_NC = {}


def kernel(**inputs):
    if "nc" not in _NC:
        _NC["nc"] = build()[0]
    nc = _NC["nc"]
    maps = make_in_maps(inputs)
    res = run_bass_kernel_spmd(nc, maps, core_ids=list(range(NCORES)))
    outs = [np.asarray(r["out"], dtype=np.float32) for r in res.results]
    return np.stack(outs, axis=0)
```

```python
import math
from contextlib import ExitStack

import numpy as np
import ml_dtypes

import concourse.bass as bass
import concourse.mybir as mybir
from concourse.bass_utils import run_bass_kernel_spmd

F32 = mybir.dt.float32
BF16 = mybir.dt.bfloat16
U32 = mybir.dt.uint32
AF = mybir.ActivationFunctionType
ALU = mybir.AluOpType
AX = mybir.AxisListType

D = 1024
SEQ = 4096
CTX = 256
NTOK = SEQ + CTX
DEPTH = 4
NCORES = 8
EPS = 1e-6
IN_COLS = 6144
NEXP = 16384

ENGS = ["tensor", "vector", "scalar", "gpsimd", "sync"]


class Emitter:
    def __init__(self, nc, n_dma_sems=28):
        self.nc = nc
        self.lists = {e: [] for e in ENGS}
        self.cnt = {e: 0 for e in ENGS}
        self.known = {e: {} for e in ENGS}
        self.last_w = {}
        self.readers = {}
        self.n_dma = n_dma_sems
        self.dma_val = [0] * n_dma_sems
        self.dma_rr = 0
        self.n_inst = 0

    def _deps(self, reads, writes, eng=None):
        deps = {}

        def add(d, same_ok):
            if d is None:
                return
            s, v = d
            if not same_ok and s == eng:
                return
            if deps.get(s, 0) < v:
                deps[s] = v

        for k in reads:
            add(self.last_w.get(k), True)
        for k in writes:
            add(self.last_w.get(k), False)
            for r in self.readers.get(k, ()):
                add(r, False)
        return deps

    def _emit_waits(self, eng, deps):
        kn = self.known[eng]
        for s, v in deps.items():
            if eng == "tensor" and s == "tensor":
                continue
            if kn.get(s, 0) >= v:
                continue
            kn[s] = v
            self.lists[eng].append(("wait", s, v))

    def _commit(self, token, reads, writes):
        for k in reads:
            lst = self.readers.setdefault(k, [])
            lst.append(token)
            if len(lst) > 64:
                mx = {}
                for s, v in lst:
                    if mx.get(s, 0) < v:
                        mx[s] = v
                self.readers[k] = list(mx.items())
        for k in writes:
            self.last_w[k] = token
            self.readers[k] = []

    def op(self, eng, fn, reads=(), writes=()):
        deps = self._deps(reads, writes, eng)
        self._emit_waits(eng, deps)
        self.cnt[eng] += 1
        token = (eng, self.cnt[eng])
        self.lists[eng].append(("op", fn, eng, 1))
        self._commit(token, reads, writes)
        self.n_inst += 1
        return token

    def dma(self, eng, fn, reads=(), writes=()):
        deps = self._deps(reads, writes)
        i = self.dma_rr
        self.dma_rr = (self.dma_rr + 1) % self.n_dma
        s = ("dma", i)
        if self.dma_val[i] > 0:
            deps[s] = max(deps.get(s, 0), self.dma_val[i])
        self._emit_waits(eng, deps)
        self.dma_val[i] += 16
        token = (s, self.dma_val[i])
        self.lists[eng].append(("op", fn, s, 16))
        self._commit(token, reads, writes)
        self.n_inst += 1
        return token

    def _all(self):
        deps = {e: self.cnt[e] for e in ENGS if self.cnt[e] > 0}
        for i in range(self.n_dma):
            if self.dma_val[i] > 0:
                deps[("dma", i)] = self.dma_val[i]
        return deps

    def barrier_all(self):
        deps = self._all()
        for e in ENGS:
            self._emit_waits(e, dict(deps))

    def finish(self, eng="sync"):
        self._emit_waits(eng, self._all())

    def replay(self, sems, engname, engobj):
        for item in self.lists[engname]:
            if item[0] == "wait":
                engobj.wait_ge(sems[item[1]], item[2])
            else:
                _, fn, s, inc = item
                fn(engobj).then_inc(sems[s], inc)


def build(depth=DEPTH, total_depth=DEPTH, dbg=False, stop_after=None):
    nc = bass.Bass("TRN2", target_bir_lowering=False)
    em = Emitter(nc)

    def din(name, shape, dt=F32):
        return nc.dram_tensor(name, list(shape), dt, kind="ExternalInput").ap()

    x_in = din("x", [SEQ, D])
    ctx_in = din("ctx", [CTX, D])
    cT_in = din("cT", [128, 8, 2])
    w_ada = din("w_ada", [depth, D, 6 * D])
    b_ada = din("b_ada", [depth, 6 * D])
    norm1_g = din("norm1_g", [depth, D])
    norm2_g = din("norm2_g", [depth, D])
    w_in = din("w_in", [depth, D, IN_COLS])
    bgate_c = din("bgate_c", [depth, 128, 24])
    q_norm_g = din("q_norm_g", [depth, 64])
    k_norm_g = din("k_norm_g", [depth, 64])
    lam_params = din("lam_params", [depth, 256])
    subln_c = din("subln_c", [depth, 128, 1])
    cm_ln_g = din("cm_ln_g", [depth, 512])
    cmws_T = din("cmws_T", [depth, 128, 4, 128])
    cmbs_c = din("cmbs_c", [depth, 128, 4])
    w_branch = din("w_branch", [depth, 3, 512, D])
    w_out = din("w_out", [depth, D, D])
    peer_w_q = din("peer_w_q", [depth, D, 2048])
    keysT_in = din("keysT", [depth, 128, 16, 128])
    peer_u = din("peer_u", [depth, NEXP, D])
    peer_v = din("peer_v", [depth, NEXP, D])
    peer_u_flat = peer_u.rearrange("l e d -> (l e) d")
    peer_v_flat = peer_v.rearrange("l e d -> (l e) d")
    identf_in = din("identf", [128, 128])
    rope_in = din("rope", [SEQ, 64])
    dftL = din("dftL", [2, SEQ, SEQ], BF16)
    dft256 = din("dft256", [2, CTX, CTX], BF16)
    dftC = din("dftC", [2, 128, 128], BF16)

    out = nc.dram_tensor("out", [SEQ, D], F32, kind="ExternalOutput").ap()
    skind = "ExternalOutput" if dbg else "Internal"
    xs = nc.dram_tensor("xs", [NTOK, D], F32, kind=skind).ap()
    der = nc.dram_tensor("der", [2, 6, D], F32, kind=skind).ap()
    aT_s = nc.dram_tensor("aT_s", [4, 128, NTOK], BF16, kind=skind).ap()
    fT_s = nc.dram_tensor("fT_s", [4, 128, NTOK], BF16, kind=skind).ap()
    UVb = nc.dram_tensor("UVb", [depth * NEXP, 2 * D], BF16, kind="Internal").ap()

    def V(fn, r=(), w=()):
        return em.op("vector", fn, r, w)

    def A(fn, r=(), w=()):
        return em.op("scalar", fn, r, w)

    def PE(fn, r=(), w=()):
        return em.op("tensor", fn, r, w)

    def G(fn, r=(), w=()):
        return em.op("gpsimd", fn, r, w)

    def DS(out_, in_, r=(), w=()):
        return em.dma("sync", lambda e: e.dma_start(out=out_, in_=in_), r, w)

    def DG(out_, in_, r=(), w=()):
        return em.dma("gpsimd", lambda e: e.dma_start(out=out_, in_=in_), r, w)

    def tt(out_, a, b, op, r, w, eng="vector"):
        return em.op(eng, lambda e: e.tensor_tensor(out=out_, in0=a, in1=b, op=op), r, w)

    def stt(out_, a, s, b, op0, op1, r, w, accum=None):
        return V(lambda e: e.scalar_tensor_tensor(out=out_, in0=a, scalar=s, in1=b, op0=op0, op1=op1,
                                                  accum_out=accum), r, w)

    def ts(out_, a, s1, s2, op0, op1, r, w):
        if s2 is None:
            return V(lambda e: e.tensor_scalar(out=out_, in0=a, scalar1=s1, scalar2=None, op0=op0), r, w)
        return V(lambda e: e.tensor_scalar(out=out_, in0=a, scalar1=s1, scalar2=s2, op0=op0, op1=op1), r, w)

    def act(out_, in_, func, r, w, bias=None, scale=None, accum=None):
        kw = {}
        if bias is not None:
            kw["bias"] = bias
        if scale is not None:
            kw["scale"] = scale
        if accum is not None:
            kw["accum_out"] = accum
        return A(lambda e: e.activation(out=out_, in_=in_, func=func, **kw), r, w)

    def vcopy(out_, in_, r, w):
        return V(lambda e: e.tensor_copy(out=out_, in_=in_), r, w)

    def mm(out_, lhsT, rhs, start, stop, r, w):
        return PE(lambda e: e.matmul(out_, lhsT=lhsT, rhs=rhs, start=start, stop=stop), r, w)

    def tr(out_, in_, ident, r, w):
        return PE(lambda e: e.transpose(out_, in_, ident), r, w)

    def recip(out_, in_, r, w):
        return V(lambda e: e.reciprocal(out=out_, in_=in_), r, w)

    def vmax(out_, in_, r, w):
        return V(lambda e: e.max(out=out_, in_=in_), r, w)

    def vmaxidx(out_, inmax, invals, r, w):
        return V(lambda e: e.max_index(out=out_, in_max=inmax, in_values=invals), r, w)

    def vmatchrep(out_, rep, vals, r, w):
        return V(lambda e: e.match_replace(out=out_, in_to_replace=rep, in_values=vals, imm_value=-1e30), r, w)

    def vreduce(out_, in_, r, w):
        return V(lambda e: e.tensor_reduce(out=out_, in_=in_, axis=AX.X, op=ALU.add), r, w)

    def vsingle(out_, in_, scalar, op, r, w):
        return V(lambda e: e.tensor_single_scalar(out=out_, in_=in_, scalar=scalar, op=op), r, w)

    def rstd_op(t_ap, key, scale, mode):
        if mode == "ln":
            act(t_ap, t_ap, AF.Ln, [key], [key], bias=EPS, scale=scale)
            act(t_ap, t_ap, AF.Exp, [key], [key], scale=-0.5)
        else:
            act(t_ap, t_ap, AF.Sqrt, [key], [key], bias=EPS, scale=scale)
            recip(t_ap, t_ap, [key], [key])

    def gather(out_, table, idx_ap, r, w):
        return em.dma("gpsimd", lambda e: e.indirect_dma_start(
            out=out_, out_offset=None, in_=table,
            in_offset=bass.IndirectOffsetOnAxis(ap=idx_ap, axis=0)), r, w)

    with ExitStack() as top:
        _tcnt = [0]

        def T(es, name, shape, dt):
            _tcnt[0] += 1
            return es.enter_context(nc.sbuf_tensor(f"t{_tcnt[0]}_{name}", list(shape), dt))

        pA = top.enter_context(nc.psum_tensor("pA", [128, 1024], F32))
        pB = top.enter_context(nc.psum_tensor("pB", [128, 1024], F32))
        pC = top.enter_context(nc.psum_tensor("pC", [128, 1024], F32))
        pD = top.enter_context(nc.psum_tensor("pD", [128, 1024], F32))
        PA0, PA1 = pA[:, 0:512], pA[:, 512:1024]
        PB0, PB1 = pB[:, 0:512], pB[:, 512:1024]
        PC0, PC1 = pC[:, 0:512], pC[:, 512:1024]
        PD0, PD1 = pD[:, 0:512], pD[:, 512:1024]

        identf = T(top, "identf", [128, 128], F32)
        identb = T(top, "identb", [128, 128], BF16)
        onesf = T(top, "onesf", [128, 128], F32)
        onesb = T(top, "onesb", [128, 128], BF16)
        ropet = T(top, "ropet", [128, 32, 64], F32)
        io16 = T(top, "io16", [128, 16], F32)
        cact = T(top, "cact", [128, 8, 2], F32)
        CCt = T(top, "CCt", [128, 128], BF16)
        SCt = T(top, "SCt", [128, 128], BF16)
        neglam = T(top, "neglam", [128, 1], F32)
        sgcol = T(top, "sgcol", [128, 1], F32)
        lamt = T(top, "lamt", [128, 2], F32)
        lpb = T(top, "lpb", [128, 256], F32)
        lj = T(top, "lj", [128, 64], F32)

        DS(identf[:], identf_in, w=["identf"])
        vcopy(identb[:], identf[:], ["identf"], ["identb"])
        V(lambda e: e.memset(onesf[:], 1.0), w=["onesf"])
        V(lambda e: e.memset(onesb[:], 1.0), w=["onesb"])
        DS(ropet[:], rope_in.rearrange("(t p) c -> p t c", p=128), w=["ropet"])
        G(lambda e: e.iota(io16[:], pattern=[[1, 16]], base=0, channel_multiplier=0,
                           allow_small_or_imprecise_dtypes=True), w=["io16"])
        DS(cact[:], cT_in, w=["cact"])
        act(cact[:], cact[:], AF.Silu, ["cact"], ["cact"])
        DS(CCt[:], dftC[0], w=["CCt"])
        DS(SCt[:], dftC[1], w=["SCt"])
        XS = [("xs", i) for i in range(34)]
        DS(xs[0:CTX, :], ctx_in, w=XS[0:2])
        for q4 in range(4):
            DS(xs[CTX + q4 * 1024:CTX + (q4 + 1) * 1024, :], x_in[q4 * 1024:(q4 + 1) * 1024, :],
               w=XS[2 + q4 * 8:2 + (q4 + 1) * 8])

        def norm_mod(xtile, xkey, gsb, shb, mkeys, hT_out, hT_key, tp, tpkeys, sqj, ssq, h32,
                     h32key="h32", sqkey="sqj", rs="sqrt"):
            act(sqj[:], xtile, AF.Square, [xkey], [sqkey, "ssq"], accum=ssq[:])
            rstd_op(ssq[:], "ssq", 1.0 / D, rs)
            stt(h32[:], xtile, ssq[:, 0:1], gsb, ALU.mult, ALU.mult, [xkey, "ssq", mkeys[0]], [h32key])
            tt(h32[:], h32[:], shb, ALU.add, [h32key, mkeys[1]], [h32key])
            if isinstance(tp, list):
                for hf in range(2):
                    for k4 in range(4):
                        kc = hf * 4 + k4
                        tr(tp[hf][:, k4 * 128:(k4 + 1) * 128], h32[:, kc * 128:(kc + 1) * 128], identf[:],
                           [h32key, "identf"], [tpkeys[hf]])
                    act(hT_out[:, hf * 4:(hf + 1) * 4, :], tp[hf].rearrange("p (k t) -> p k t", t=128), AF.Copy,
                        [tpkeys[hf]], [hT_key])
                return
            for kc in range(8):
                tr(tp[:, kc * 128:(kc + 1) * 128], h32[:, kc * 128:(kc + 1) * 128], identf[:],
                   [h32key, "identf"], tpkeys)
            act(hT_out, tp[:, 0:1024].rearrange("p (k t) -> p k t", t=128), AF.Copy, tpkeys, [hT_key])

        def group_norm_rope(ps, pskey, gb, gbkey, nsq, ss8, kn, ra, rb_, outb, outkey, rope_tile, rs="sqrt"):
            act(nsq[:], ps, AF.Square, [pskey], ["nsq"])
            vreduce(ss8[:], nsq[:].rearrange("p (g d) -> p g d", d=64), ["nsq"], ["ss8"])
            rstd_op(ss8[:], "ss8", 1.0 / 64, rs)
            kn3 = kn[:].rearrange("p (g d) -> p g d", d=64)
            tt(kn3, ps.rearrange("p (g d) -> p g d", d=64), ss8[:].unsqueeze(2).to_broadcast([128, 8, 64]),
               ALU.mult, [pskey, "ss8"], ["kn"])
            if rope_tile is None:
                tt(outb[:].rearrange("p (g d) -> p g d", d=64), kn3, gb[:].unsqueeze(1).to_broadcast([128, 8, 64]),
                   ALU.mult, ["kn", gbkey], [outkey])
                return
            tt(kn3, kn3, gb[:].unsqueeze(1).to_broadcast([128, 8, 64]), ALU.mult, ["kn", gbkey], ["kn"])
            kn5 = kn[:].rearrange("p (g a h f) -> p g a h f", g=8, a=2, h=2, f=16)
            ob5 = outb[:].rearrange("p (g a h f) -> p g a h f", g=8, a=2, h=2, f=16)
            x1, x2 = kn5[:, :, :, 0, :], kn5[:, :, :, 1, :]
            cosb = ropet[:, rope_tile, 0:32].rearrange("p (a f) -> p a f", a=2).unsqueeze(1).to_broadcast([128, 8, 2, 16])
            sinb = ropet[:, rope_tile, 32:64].rearrange("p (a f) -> p a f", a=2).unsqueeze(1).to_broadcast([128, 8, 2, 16])
            ra4 = ra[:].rearrange("p (g a f) -> p g a f", g=8, a=2)
            rb4 = rb_[:].rearrange("p (g a f) -> p g a f", g=8, a=2)
            tt(ra4, x1, cosb, ALU.mult, ["kn", "ropet"], ["ra"])
            tt(rb4, x2, sinb, ALU.mult, ["kn", "ropet"], ["rb"])
            tt(ob5[:, :, :, 0, :], ra4, rb4, ALU.subtract, ["ra", "rb"], [outkey])
            tt(ra4, x2, cosb, ALU.mult, ["kn", "ropet"], ["ra"])
            tt(rb4, x1, sinb, ALU.mult, ["kn", "ropet"], ["rb"])
            tt(ob5[:, :, :, 1, :], ra4, rb4, ALU.add, ["ra", "rb"], [outkey])

        def load_bcast(tile, j, r, key):
            DS(tile[:], der[r, j:j + 1, :].partition_broadcast(128), r=["der"], w=[key])

        for L in range(depth):
            last = (L == total_depth - 1)
            lam_init = 0.8 - 0.6 * math.exp(-0.3 * L)
            groups = []
            if not last:
                groups.append((0, 256, True))
            for g8 in range(8):
                groups.append((CTX + g8 * 512, 512, False))

            em.barrier_all()
            with ExitStack() as s0:
                wada = [T(s0, f"wada{i}", [128, 8, 512], F32) for i in range(2)]
                badat = [T(s0, f"badat{i}", [2, 512], F32) for i in range(2)]
                gch = [T(s0, f"gch{i}", [2, 512], F32) for i in range(2)]
                modrow = [T(s0, f"modrow{i}", [2, 512], F32) for i in range(2)]
                jmap = {0: 1, 1: 0, 2: 2, 3: 4, 4: 3, 5: 5}
                for nt in range(12):
                    b = nt % 2
                    part, half = nt // 2, nt % 2
                    DS(wada[b][:], w_ada[L][:, nt * 512:(nt + 1) * 512].rearrange("(kc p) n -> p kc n", p=128),
                       w=[("wada", b)])
                    DS(badat[b][:], b_ada[L:L + 1, nt * 512:(nt + 1) * 512].partition_broadcast(2), w=[("badat", b)])
                    for kc in range(8):
                        mm(PA0[0:2, :], cact[:, kc, :], wada[b][:, kc, :], kc == 0, kc == 7,
                           [("wada", b), "cact"], ["pA0"])
                    tt(modrow[b][:], PA0[0:2, :], badat[b][:], ALU.add, ["pA0", ("badat", b)], [("modrow", b)])
                    if part in (1, 4):
                        ng = norm1_g if part == 1 else norm2_g
                        DS(gch[b][:], ng[L:L + 1, half * 512:(half + 1) * 512].partition_broadcast(2), w=[("gch", b)])
                        stt(modrow[b][:], modrow[b][:], 1.0, gch[b][:], ALU.add, ALU.mult,
                            [("modrow", b), ("gch", b)], [("modrow", b)])
                    DS(der[:, jmap[part], half * 512:(half + 1) * 512], modrow[b][:], r=[("modrow", b)], w=["der"])
                DS(lpb[:], lam_params[L:L + 1, :].partition_broadcast(128), w=["lpb"])
                stt(lj[:], lpb[:, 0:64], 1.0, lpb[:, 64:128], ALU.mult, ALU.mult, ["lpb"], ["lj", "lamt"], accum=lamt[:, 0:1])
                stt(lj[:], lpb[:, 128:192], 1.0, lpb[:, 192:256], ALU.mult, ALU.mult, ["lpb"], ["lj", "lamt"], accum=lamt[:, 1:2])
                act(lamt[:], lamt[:], AF.Exp, ["lamt"], ["lamt"])
                tt(neglam[:], lamt[:, 1:2], lamt[:, 0:1], ALU.subtract, ["lamt"], ["neglam"])
                ts(neglam[:], neglam[:], -lam_init, None, ALU.add, None, ["neglam"], ["neglam"])
                DS(sgcol[:], subln_c[L], w=["sgcol"])
                ts(sgcol[:], sgcol[:], 1.0 - lam_init, None, ALU.mult, None, ["sgcol"], ["sgcol"])
            em.barrier_all()
            if stop_after == "P0":
                break

            with ExitStack() as sZ:
                Zs = T(sZ, "Zs", [128, 34, 512], BF16)
                gs1b = [T(sZ, f"gs1b{r}", [128, D], F32) for r in range(2)]
                sh1b = [T(sZ, f"sh1b{r}", [128, D], F32) for r in range(2)]
                for r in range(2):
                    load_bcast(gs1b[r], 0, r, ("gs1b", r))
                    load_bcast(sh1b[r], 1, r, ("sh1b", r))
                with ExitStack() as sKV:
                    KT = T(sKV, "KT", [128, 4, NTOK], BF16)
                    Vs = T(sKV, "Vs", [128, 34, 512], BF16)
                    xt = [T(sKV, f"xt{i}", [128, D], F32) for i in range(2)]
                    sqj = T(sKV, "sqj", [128, D], F32)
                    ssq = T(sKV, "ssq", [128, 1], F32)
                    h32 = T(sKV, "h32", [128, D], F32)
                    nsq = T(sKV, "nsq", [128, 512], F32)
                    ss8 = T(sKV, "ss8", [128, 8], F32)
                    kn = T(sKV, "kn", [128, 512], F32)
                    ra = T(sKV, "ra", [128, 256], F32)
                    rb_ = T(sKV, "rb", [128, 256], F32)
                    kb = T(sKV, "kb", [128, 512], BF16)
                    kgb = T(sKV, "kgb", [128, 64], F32)
                    qgb = T(sKV, "qgb", [128, 64], F32)
                    DS(kgb[:], k_norm_g[L:L + 1, :].partition_broadcast(128), w=["kgb"])
                    DS(qgb[:], q_norm_g[L:L + 1, :].partition_broadcast(128), w=["qgb"])
                    ts(qgb[:], qgb[:], 0.125, None, ALU.mult, None, ["qgb"], ["qgb"])
                    with ExitStack() as s1:
                        w1 = T(s1, "w1", [128, 8, 1536], BF16)
                        hT1 = [T(s1, f"hT1_{i}", [128, 8, 128], BF16) for i in range(2)]
                        wsrc = w_in[L].rearrange("(kc p) n -> p kc n", p=128)
                        DG(w1[:, :, 0:512], wsrc[:, :, 512:1024], w=["w1"])
                        DG(w1[:, :, 512:1024], wsrc[:, :, 1024:1536], w=["w1"])
                        DG(w1[:, :, 1024:1536], wsrc[:, :, 2560:3072], w=["w1"])
                        DS(xt[0][:], xs[0:128, :], r=[XS[0]], w=[("xt", 0)])
                        for i in range(34):
                            b = i % 2
                            r = 1 if i < 2 else 0
                            if i + 1 < 34:
                                DS(xt[1 - b][:], xs[(i + 1) * 128:(i + 2) * 128, :], r=[XS[i + 1]], w=[("xt", 1 - b)])
                            tp, tpk = (pA, ["pA0", "pA1"]) if b == 0 else (pD, ["pD0", "pD1"])
                            norm_mod(xt[b][:], ("xt", b), gs1b[r][:], sh1b[r][:], [("gs1b", r), ("sh1b", r)],
                                     hT1[b][:], ("hT1", b), tp, tpk, sqj, ssq, h32)
                            for nt, (pb_, pk) in enumerate([(PB0, "pB0"), (PB1, "pB1"), (PC0, "pC0")]):
                                for kc in range(8):
                                    mm(pb_, hT1[b][:, kc, :], w1[:, kc, nt * 512:(nt + 1) * 512], kc == 0, kc == 7,
                                       [("hT1", b), "w1"], [pk])
                            group_norm_rope(PB0, "pB0", kgb, "kgb", nsq, ss8, kn, ra, rb_, kb, "kb",
                                            None if i < 2 else i - 2)
                            pcv = PC1.bitcast(BF16)
                            for h in range(4):
                                tr(pcv[:, h * 128:(h + 1) * 128], kb[:, h * 128:(h + 1) * 128], identb[:],
                                   ["kb", "identb"], ["pC1"])
                            act(KT[:, :, i * 128:(i + 1) * 128], pcv[:, 0:512].rearrange("p (h t) -> p h t", t=128),
                                AF.Copy, ["pC1"], [("KT", i)])
                            act(Vs[:, i, :], PB1, AF.Copy, ["pB1"], [("Vs", i)])
                            vcopy(Zs[:, i, :], PC0, ["pC0"], [("Zs", i)])
                    em.barrier_all()
                    if stop_after == "P1":
                        break
                    with ExitStack() as s2:
                        wq = T(s2, "wq", [128, 8, 512], BF16)
                        hT = T(s2, "hT", [128, 8, 128], BF16)
                        QT = [T(s2, f"QT{i}", [128, 4, 512], BF16) for i in range(2)]
                        PTp = [T(s2, f"PTp{i}", [128, 1024], BF16) for i in range(2)]
                        zacc = [T(s2, f"zacc{m}", [128, 512], F32) for m in range(2)]
                        rz = T(s2, "rz", [128, 512], F32)
                        O0 = T(s2, "O0", [128, 512], F32)
                        O1 = T(s2, "O1", [128, 512], F32)
                        att = T(s2, "att", [128, 512], F32)
                        asq = T(s2, "asq", [128, 512], F32)
                        rst = T(s2, "rst", [128, 512], F32)
                        aTb = [T(s2, f"aTb{i}", [128, 512], BF16) for i in range(2)]
                        DG(wq[:], w_in[L].rearrange("(kc p) n -> p kc n", p=128)[:, :, 0:512], w=["wq"])
                        cv_list = []
                        for c8 in range(8):
                            e0 = c8 * 2048
                            cv_list.append((UVb[L * NEXP + e0:L * NEXP + e0 + 2048, 0:D], peer_u[L][e0:e0 + 2048, :], ("UVb", L, c8, 0)))
                            cv_list.append((UVb[L * NEXP + e0:L * NEXP + e0 + 2048, D:2 * D], peer_v[L][e0:e0 + 2048, :], ("UVb", L, c8, 1)))
                        cv_state = {"i": 0}

                        def prep(gi):
                            row0, N, is_ctx = groups[gi]
                            r = 1 if is_ctx else 0
                            qt = QT[gi % 2]
                            qk = ("QT", gi % 2)
                            for j in range(N // 128):
                                ti = row0 // 128 + j
                                DS(xt[0][:], xs[ti * 128:(ti + 1) * 128, :], r=[XS[ti]], w=[("xt", 0)])
                                norm_mod(xt[0][:], ("xt", 0), gs1b[r][:], sh1b[r][:], [("gs1b", r), ("sh1b", r)],
                                         hT[:], "hT", [PB1, PC1], ["pB1", "pC1"], sqj, ssq, h32, rs="ln")
                                yield
                                for kc in range(8):
                                    mm(PB1, hT[:, kc, :], wq[:, kc, :], kc == 0, kc == 7, ["hT", "wq"], ["pB1"])
                                yield
                                group_norm_rope(PB1, "pB1", qgb, "qgb", nsq, ss8, kn, ra, rb_, kb, "kb",
                                                None if is_ctx else ti - 2, rs="ln")
                                yield
                                pcv = PC1.bitcast(BF16)
                                for h in range(4):
                                    tr(pcv[:, h * 128:(h + 1) * 128], kb[:, h * 128:(h + 1) * 128], identb[:],
                                       ["kb", "identb"], ["pC1"])
                                act(qt[:, :, j * 128:(j + 1) * 128], pcv[:, 0:512].rearrange("p (h t) -> p h t", t=128),
                                    AF.Copy, ["pC1"], [qk])
                                yield

                        def attend(gi, nxt):
                            row0, N, is_ctx = groups[gi]
                            qt = QT[gi % 2]
                            qk = ("QT", gi % 2)
                            if not is_ctx:
                                for _ in range(2):
                                    o_, i_, k_ = cv_list[cv_state["i"]]
                                    DG(o_, i_, w=[k_, ("cv", cv_state["i"] % 4)])
                                    cv_state["i"] += 1
                            kchunks = [0, 1] if is_ctx else list(range(34))
                            nch = len(kchunks)
                            sb = [(pD, ["pD0", "pD1"]), (pA, ["pA0", "pA1"])]
                            obank = [(PB0, "pB0"), (PC0, "pC0")]
                            step = 0
                            for h in range(4):
                                def s_mm(ci):
                                    kc = kchunks[ci]
                                    pS, pSk = sb[ci % 2]
                                    for m in range(2):
                                        lo, hi = m * 64, (m + 1) * 64
                                        mm(pS[:, m * 512:m * 512 + N], KT[lo:hi, h, kc * 128:(kc + 1) * 128], qt[lo:hi, h, 0:N],
                                           True, True, [("KT", kc), qk], [pSk[m]])

                                s_mm(0)
                                for ci, kc in enumerate(kchunks):
                                    first, lastc = ci == 0, ci == nch - 1
                                    if ci + 1 < nch:
                                        s_mm(ci + 1)
                                    pS, pSk = sb[ci % 2]
                                    pt = PTp[ci % 2]
                                    ptk = ("PT", ci % 2)
                                    act(pt[:].rearrange("p (m n) -> p m n", m=2)[:, :, 0:N],
                                        pS[:, 0:1024].rearrange("p (m n) -> p m n", m=2)[:, :, 0:N], AF.Exp, pSk, [ptk])
                                    for m in range(2):
                                        pO, pOk = obank[m]
                                        mm(pO[:, 0:N], Vs[:, kc, h * 128:(h + 1) * 128], pt[:, m * 512:m * 512 + N], first, lastc,
                                           [("Vs", kc), ptk], [pOk])
                                    for m in range(2):
                                        eng = "vector" if m == 0 else "gpsimd"
                                        if first:
                                            em.op(eng, lambda e, o_=zacc[m][:, 0:N], i_=pt[:, m * 512:m * 512 + N]: e.tensor_copy(out=o_, in_=i_),
                                                  [ptk], [("zacc", m)])
                                        else:
                                            tt(zacc[m][:, 0:N], zacc[m][:, 0:N], pt[:, m * 512:m * 512 + N], ALU.add,
                                               [("zacc", m), ptk], [("zacc", m)], eng=eng)
                                    step += 1
                                    if nxt is not None and step % 4 == 0:
                                        next(nxt, None)
                                mm(PA0[:, 0:N], onesf[:], zacc[0][:, 0:N], True, True, ["onesf", ("zacc", 0)], ["pA0"])
                                mm(PA1[:, 0:N], onesf[:], zacc[1][:, 0:N], True, True, ["onesf", ("zacc", 1)], ["pA1"])
                                recip(rz[:, 0:N], PA0[:, 0:N], ["pA0"], ["rz"])
                                tt(O0[:, 0:N], PB0[:, 0:N], rz[:, 0:N], ALU.mult, ["pB0", "rz"], ["O0"])
                                recip(rz[:, 0:N], PA1[:, 0:N], ["pA1"], ["rz"])
                                tt(O1[:, 0:N], PC0[:, 0:N], rz[:, 0:N], ALU.mult, ["pC0", "rz"], ["O1"])
                                stt(att[:, 0:N], O1[:, 0:N], neglam[:, 0:1], O0[:, 0:N], ALU.mult, ALU.add,
                                    ["O0", "O1", "neglam"], ["att"])
                                act(asq[:, 0:N], att[:, 0:N], AF.Square, ["att"], ["asq"])
                                mm(PA0[:, 0:N], onesf[:], asq[:, 0:N], True, True, ["onesf", "asq"], ["pA0"])
                                act(rst[:, 0:N], PA0[:, 0:N], AF.Ln, ["pA0"], ["rst"], bias=EPS, scale=1.0 / 128)
                                act(rst[:, 0:N], rst[:, 0:N], AF.Exp, ["rst"], ["rst"], scale=-0.5)
                                ab = aTb[h % 2]
                                stt(ab[:, 0:N], att[:, 0:N], sgcol[:, 0:1], rst[:, 0:N], ALU.mult, ALU.mult,
                                    ["att", "sgcol", "rst"], [("aTb", h % 2)])
                                DS(aT_s[h, :, row0:row0 + N], ab[:, 0:N], r=[("aTb", h % 2)], w=[("aT", gi)])
                            if nxt is not None:
                                for _ in nxt:
                                    pass

                        for _ in prep(0):
                            pass
                        for gi in range(len(groups)):
                            attend(gi, prep(gi + 1) if gi + 1 < len(groups) else None)
                    em.barrier_all()
                if stop_after == "P2a":
                    break
                with ExitStack() as s3:
                    tabC = [T(s3, f"tabC{i}", [128, 32, 512], BF16) for i in range(2)]
                    tabS = [T(s3, f"tabS{i}", [128, 32, 512], BF16) for i in range(2)]
                    Wc = T(s3, "Wc", [128, 512], BF16)
                    Ws = T(s3, "Ws", [128, 512], BF16)
                    fTb = [T(s3, f"fTb{i}", [128, 512], BF16) for i in range(2)]

                    def load_tabs(gi):
                        row0, N, is_ctx = groups[gi]
                        tb_ = gi % 2
                        if is_ctx:
                            DS(tabC[tb_][:, 0:2, 0:256], dft256[0].rearrange("(tc p) n -> p tc n", p=128), w=[("tabC", tb_)])
                            DS(tabS[tb_][:, 0:2, 0:256], dft256[1].rearrange("(tc p) n -> p tc n", p=128), w=[("tabS", tb_)])
                        else:
                            t0 = row0 - CTX
                            for q4 in range(4):
                                DS(tabC[tb_][:, q4 * 8:(q4 + 1) * 8, :],
                                   dftL[0][q4 * 1024:(q4 + 1) * 1024, t0:t0 + 512].rearrange("(tc p) n -> p tc n", p=128),
                                   w=[("tabC", tb_)])
                                DS(tabS[tb_][:, q4 * 8:(q4 + 1) * 8, :],
                                   dftL[1][q4 * 1024:(q4 + 1) * 1024, t0:t0 + 512].rearrange("(tc p) n -> p tc n", p=128),
                                   w=[("tabS", tb_)])

                    load_tabs(0)
                    for gi, (row0, N, is_ctx) in enumerate(groups):
                        tb_ = gi % 2
                        if gi + 1 < len(groups):
                            load_tabs(gi + 1)
                        ntc, z0 = (2, 0) if is_ctx else (32, 2)
                        for g in range(4):
                            for tcx in range(ntc):
                                mm(PA0[:, 0:N], Zs[:, z0 + tcx, g * 128:(g + 1) * 128], tabC[tb_][:, tcx, 0:N],
                                   tcx == 0, tcx == ntc - 1, [("Zs", z0 + tcx), ("tabC", tb_)], ["pA0"])
                            for tcx in range(ntc):
                                mm(PA1[:, 0:N], Zs[:, z0 + tcx, g * 128:(g + 1) * 128], tabS[tb_][:, tcx, 0:N],
                                   tcx == 0, tcx == ntc - 1, [("Zs", z0 + tcx), ("tabS", tb_)], ["pA1"])
                            act(Wc[:, 0:N], PA0[:, 0:N], AF.Copy, ["pA0"], ["Wc"])
                            vcopy(Ws[:, 0:N], PA1[:, 0:N], ["pA1"], ["Ws"])
                            pF, pFk = (PB0, "pB0") if g % 2 == 0 else (PB1, "pB1")
                            mm(pF[:, 0:N], CCt[:], Wc[:, 0:N], True, False, ["CCt", "Wc"], [pFk])
                            mm(pF[:, 0:N], SCt[:], Ws[:, 0:N], False, True, ["SCt", "Ws"], [pFk])
                            fb = fTb[g % 2]
                            act(fb[:, 0:N], pF[:, 0:N], AF.Copy, [pFk], [("fTb", g % 2)])
                            DS(fT_s[g, :, row0:row0 + N], fb[:, 0:N], r=[("fTb", g % 2)], w=[("fT", gi)])
                em.barrier_all()
            if stop_after == "P2b":
                break
            with ExitStack() as s4:
                gs1b = [T(s4, f"c_gs1b{r}", [128, D], F32) for r in range(2)]
                sh1b = [T(s4, f"c_sh1b{r}", [128, D], F32) for r in range(2)]
                g1b = [T(s4, f"c_g1b{r}", [128, D], F32) for r in range(2)]
                for r in range(2):
                    load_bcast(gs1b[r], 0, r, ("gs1b", r))
                    load_bcast(sh1b[r], 1, r, ("sh1b", r))
                    load_bcast(g1b[r], 2, r, ("g1b", r))
                xg = T(s4, "xg", [128, 4, D], F32)
                sqj = T(s4, "c_sqj", [128, D], F32)
                ssq = T(s4, "c_ssq", [128, 1], F32)
                h32 = T(s4, "c_h32", [128, D], F32)
                hT = T(s4, "c_hT", [128, 8, 512], BF16)
                wzz = T(s4, "wzz", [128, 8, 1024], BF16)
                wbr = T(s4, "wbr", [128, 12, D], BF16)
                wot = T(s4, "wot", [128, 8, D], BF16)
                wgl = [T(s4, f"wgl{i}", [128, 8, 3, 128], BF16) for i in range(2)]
                wsT = T(s4, "wsT", [128, 4, 128], BF16)
                bsc = T(s4, "bsc", [128, 4], F32)
                bgc = T(s4, "bgc", [128, 24], F32)
                lngb = T(s4, "lngb", [128, 512], F32)
                u_t = T(s4, "u_t", [128, 512], F32)
                gv = T(s4, "gv", [128, 512], F32)
                bst = T(s4, "bst", [128, 6], F32)
                bag = T(s4, "bag", [128, 2], F32)
                vb = T(s4, "vb", [128, 512], BF16)
                mb = T(s4, "mb", [128, 512], BF16)
                mT = T(s4, "mT", [128, 4, 512], BF16)
                aT = T(s4, "aT", [128, 4, 512], BF16)
                fT = T(s4, "fT", [128, 4, 512], BF16)
                zT = T(s4, "zT", [128, 8, 512], BF16)
                gsig = [T(s4, f"gsig{i}", [128, 512], F32) for i in range(2)]
                zacc = T(s4, "zacc", [128, 512], F32)
                ztmp = T(s4, "ztmp", [128, 512], F32)
                otmp = T(s4, "otmp", [128, 512], F32)
                xnew = [T(s4, f"xnew{i}", [128, D], F32) for i in range(2)]
                wsrc = w_in[L].rearrange("(kc p) n -> p kc n", p=128)
                DG(wzz[:, :, 0:512], wsrc[:, :, 1536:2048], w=["wzz"])
                DG(wzz[:, :, 512:1024], wsrc[:, :, 2048:2560], w=["wzz"])
                for n3 in range(3):
                    for hf in range(2):
                        DG(wbr[:, n3 * 4:(n3 + 1) * 4, hf * 512:(hf + 1) * 512],
                           w_branch[L, n3].rearrange("(wc p) d -> p wc d", p=128)[:, :, hf * 512:(hf + 1) * 512], w=["wbr"])
                for hf in range(2):
                    DG(wot[:, :, hf * 512:(hf + 1) * 512],
                       w_out[L].rearrange("(kc p) n -> p kc n", p=128)[:, :, hf * 512:(hf + 1) * 512], w=["wot"])
                DG(wsT[:], cmws_T[L], w=["wsT"])
                DS(bsc[:], cmbs_c[L], w=["bsc"])
                DS(bgc[:], bgate_c[L], w=["bgc"])
                DS(lngb[:], cm_ln_g[L:L + 1, :].partition_broadcast(128), w=["lngb"])
                wgl_i = 0
                for gi, (row0, N, is_ctx) in enumerate(groups):
                    r = 1 if is_ctx else 0
                    nj = N // 128
                    DS(aT[:, :, 0:N], aT_s[:, :, row0:row0 + N].rearrange("h p t -> p h t"), r=[("aT", gi)], w=["aTt"])
                    DS(fT[:, :, 0:N], fT_s[:, :, row0:row0 + N].rearrange("h p t -> p h t"), r=[("fT", gi)], w=["fTt"])
                    for j in range(nj):
                        ti = row0 // 128 + j
                        DS(xg[:, j, :], xs[ti * 128:(ti + 1) * 128, :], r=[XS[ti]], w=[("xg", j)])
                        norm_mod(xg[:, j, :], ("xg", j), gs1b[r][:], sh1b[r][:], [("gs1b", r), ("sh1b", r)],
                                 hT[:, :, j * 128:(j + 1) * 128], "hT", pA, ["pA0", "pA1"], sqj, ssq, h32)
                        for nt, (pb_, pk) in enumerate([(PB0, "pB0"), (PB1, "pB1")]):
                            for kc in range(8):
                                mm(pb_, hT[:, kc, j * 128:(j + 1) * 128], wzz[:, kc, nt * 512:(nt + 1) * 512],
                                   kc == 0, kc == 7, ["hT", "wzz"], [pk])
                        act(u_t[:], PB0, AF.Gelu, ["pB0"], ["u_t"])
                        act(gv[:], PB1, AF.Gelu, ["pB1"], ["gv"])
                        V(lambda e, bst=bst, gv=gv: e.bn_stats(out=bst[:], in_=gv[:]), ["gv"], ["bst"])
                        V(lambda e, bst=bst, bag=bag: e.bn_aggr(out=bag[:], in_=bst[:]), ["bst"], ["bag"])
                        act(bag[:, 1:2], bag[:, 1:2], AF.Sqrt, ["bag"], ["bag"], bias=EPS, scale=1.0)
                        recip(bag[:, 1:2], bag[:, 1:2], ["bag"], ["bag"])
                        ts(gv[:], gv[:], bag[:, 0:1], bag[:, 1:2], ALU.subtract, ALU.mult, ["gv", "bag"], ["gv"])
                        tt(vb[:], gv[:], lngb[:], ALU.mult, ["gv", "lngb"], ["vb"])
                        for g in range(4):
                            mm(PC0[:, g * 128:(g + 1) * 128], wsT[:, g, :], vb[:, g * 128:(g + 1) * 128], True, True,
                               ["wsT", "vb"], ["pC0"])
                        for g in range(4):
                            stt(mb[:, g * 128:(g + 1) * 128], PC0[:, g * 128:(g + 1) * 128], bsc[:, g:g + 1],
                                u_t[:, g * 128:(g + 1) * 128], ALU.add, ALU.mult, ["pC0", "bsc", "u_t"], ["mb"])
                        pcv = PC1.bitcast(BF16)
                        for g in range(4):
                            tr(pcv[:, g * 128:(g + 1) * 128], mb[:, g * 128:(g + 1) * 128], identb[:],
                               ["mb", "identb"], ["pC1"])
                        act(mT[:, :, j * 128:(j + 1) * 128], pcv[:, 0:512].rearrange("p (h t) -> p h t", t=128),
                            AF.Copy, ["pC1"], ["mT"])
                    brs = [(aT, "aTt"), (mT, "mT"), (fT, "fTt")]
                    for dc in range(8):
                        wb_ = wgl_i % 2
                        wgl_i += 1
                        for n3 in range(3):
                            c0 = 3072 + n3 * 1024 + dc * 128
                            DG(wgl[wb_][:, :, n3, :], wsrc[:, :, c0:c0 + 128], w=[("wgl", wb_)])
                        for n3 in range(3):
                            brT, brk = brs[n3]
                            par = (dc * 3 + n3) % 2
                            pY, pYk = (PD0, "pD0") if par == 0 else (PC0, "pC0")
                            pG, pGk = (PD1, "pD1") if par == 0 else (PC1, "pC1")
                            for wc in range(4):
                                mm(pY[:, 0:N], wbr[:, n3 * 4 + wc, dc * 128:(dc + 1) * 128], brT[:, wc, 0:N],
                                   wc == 0, wc == 3, ["wbr", brk], [pYk])
                            for kc in range(8):
                                mm(pG[:, 0:N], wgl[wb_][:, kc, n3, :], hT[:, kc, 0:N], kc == 0, kc == 7,
                                   [("wgl", wb_), "hT"], [pGk])
                            gs_ = gsig[par]
                            act(gs_[:, 0:N], pG[:, 0:N], AF.Sigmoid, [pGk, "bgc"], [("gsig", par)],
                                bias=bgc[:, n3 * 8 + dc:n3 * 8 + dc + 1])
                            if n3 == 0:
                                tt(zacc[:, 0:N], pY[:, 0:N], gs_[:, 0:N], ALU.mult, [pYk, ("gsig", par)], ["zacc"])
                            elif n3 == 1:
                                tt(ztmp[:, 0:N], pY[:, 0:N], gs_[:, 0:N], ALU.mult, [pYk, ("gsig", par)], ["ztmp"])
                                tt(zacc[:, 0:N], zacc[:, 0:N], ztmp[:, 0:N], ALU.add, ["zacc", "ztmp"], ["zacc"])
                            else:
                                tt(ztmp[:, 0:N], pY[:, 0:N], gs_[:, 0:N], ALU.mult, [pYk, ("gsig", par)], ["ztmp"])
                                tt(zT[:, dc, 0:N], zacc[:, 0:N], ztmp[:, 0:N], ALU.add, ["zacc", "ztmp"], ["zT"])
                    for j in range(nj):
                        ti = row0 // 128 + j
                        xn_ = xnew[j % 2]
                        for hf, (pb_, pk) in enumerate([(PB0, "pB0"), (PB1, "pB1")]):
                            for dc in range(8):
                                mm(pb_, zT[:, dc, j * 128:(j + 1) * 128], wot[:, dc, hf * 512:(hf + 1) * 512],
                                   dc == 0, dc == 7, ["zT", "wot"], [pk])
                            tt(otmp[:], pb_, g1b[r][:, hf * 512:(hf + 1) * 512], ALU.mult, [pk, ("g1b", r)], ["otmp"])
                            tt(xn_[:, hf * 512:(hf + 1) * 512], otmp[:], xg[:, j, hf * 512:(hf + 1) * 512], ALU.add,
                               ["otmp", ("xg", j)], [("xnew", j % 2)])
                        DG(xs[ti * 128:(ti + 1) * 128, :], xn_[:], r=[("xnew", j % 2)], w=[XS[ti]])
            em.barrier_all()
            if stop_after == "P2c":
                break
            with ExitStack() as s5:
                gs2b = T(s5, "gs2b", [128, D], F32)
                sh2b = T(s5, "sh2b", [128, D], F32)
                g2b = [T(s5, f"g2b{r}", [128, D], F32) for r in range(2)]
                nr = 1 if last else 2
                for r in range(nr):
                    load_bcast(g2b[r], 5, r, ("g2b", r))
                xt = [T(s5, f"p_xt{i}", [128, D], F32) for i in range(2)]
                ssq = T(s5, "p_ssq", [128, 1], F32)
                h32 = [T(s5, f"p_h32_{i}", [128, D], F32) for i in range(2)]
                hT2 = T(s5, "hT2", [128, 8, 128], BF16)
                wpq = T(s5, "wpq", [128, 8, 2048], BF16)
                keysT = T(s5, "keysT", [128, 16, 128], BF16)
                ss16 = T(s5, "ss16", [128, 16], F32)
                qn = T(s5, "qn", [128, 2048], BF16)
                qnT = T(s5, "qnT", [128, 16, 128], BF16)
                s_sb = T(s5, "s_sb", [128, 2048], F32)
                s2x = [T(s5, f"s2_{i}", [128, 128], F32) for i in range(2)]
                ta = T(s5, "ta", [128, 8, 16], F32)
                tb = T(s5, "tb", [128, 8, 16], F32)
                tcv = T(s5, "tcv", [128, 8, 16], F32)
                ia = T(s5, "ia", [128, 8, 16], U32)
                ib = T(s5, "ib", [128, 8, 16], U32)
                pos = T(s5, "pos", [128, 8, 16], U32)
                k1 = T(s5, "k1", [128, 8, 16], U32)
                k2 = T(s5, "k2", [128, 8, 16], U32)
                k1f = T(s5, "k1f", [128, 8, 16], F32)
                k2f = T(s5, "k2f", [128, 8, 16], F32)
                iaf = T(s5, "iaf", [128, 8, 16], F32)
                ibf = T(s5, "ibf", [128, 8, 16], F32)
                isel = T(s5, "isel", [128, 8, 16], F32)
                jsel = T(s5, "jsel", [128, 8, 16], F32)
                idxf = T(s5, "idxf", [128, 128], F32)
                idxu = [T(s5, f"idxu{i}", [128, 128], U32) for i in range(2)]
                cand = T(s5, "cand", [128, 16, 16], F32)
                cand2 = T(s5, "cand2", [128, 256], F32)
                eq4 = T(s5, "eq4", [128, 8, 16, 16], F32)
                ee = T(s5, "ee", [128, 8, 16], F32)
                zz = T(s5, "zz", [128, 8], F32)
                gw = [T(s5, f"gw{i}", [128, 128], F32) for i in range(2)]
                actv = T(s5, "actv", [128, 128], F32)
                gact = T(s5, "gact", [128, 128], F32)
                xo = T(s5, "xo", [128, D], F32)
                junk = T(s5, "junk", [128, D], F32)
                gw2 = T(s5, "gw2", [128, 128], F32)
                dgt = [T(s5, f"dgt{i}", [128, 128], BF16) for i in range(4)]
                rem = int(nc.sbuf_bytes_remaining)
                NS = max(8, min(24, (rem - 3072) // 4096))
                gbuf = [T(s5, f"gbuf{i}", [128, 2 * D], BF16) for i in range(NS)]
                uvkeys = [("UVb", L, c8, uv) for c8 in range(8) for uv in range(2)]
                for q4 in range(4):
                    DG(wpq[:, :, q4 * 512:(q4 + 1) * 512],
                       peer_w_q[L].rearrange("(kc p) n -> p kc n", p=128)[:, :, q4 * 512:(q4 + 1) * 512], w=["wpq"])
                DG(keysT[:], keysT_in[L], w=["keysT"])
                tiles = list(range(2, 34)) if last else list(range(34))
                qbanks = [(PC0, "pC0"), (PC1, "pC1"), (PD0, "pD0"), (PD1, "pD1")]
                state = {"dcnt": 0, "gcnt": 0, "mod_r": None}

                def front(tix):
                    ti = tiles[tix]
                    b = tix % 2
                    r = 1 if ti < 2 else 0
                    if state["mod_r"] != r:
                        load_bcast(gs2b, 3, r, "gs2b")
                        load_bcast(sh2b, 4, r, "sh2b")
                        state["mod_r"] = r
                    xk = ("xt", b)
                    hk32 = ("h32", b)
                    h32b = h32[b]
                    act(junk[:], xt[b][:], AF.Square, [xk], ["junk", "ssq"], accum=ssq[:])
                    act(ssq[:], ssq[:], AF.Sqrt, ["ssq"], ["ssq"], bias=EPS, scale=1.0 / D)
                    yield
                    recip(ssq[:], ssq[:], ["ssq"], ["ssq"])
                    stt(h32b[:], xt[b][:], ssq[:, 0:1], gs2b[:], ALU.mult, ALU.mult, [xk, "ssq", "gs2b"], [hk32])
                    tt(h32b[:], h32b[:], sh2b[:], ALU.add, [hk32, "sh2b"], [hk32])
                    yield
                    for kc in range(8):
                        tr(pA[:, kc * 128:(kc + 1) * 128], h32b[:, kc * 128:(kc + 1) * 128], identf[:],
                           [hk32, "identf"], ["pA0", "pA1"])
                    yield
                    act(hT2[:], pA[:, 0:1024].rearrange("p (k t) -> p k t", t=128), AF.Copy, ["pA0", "pA1"], ["hT2"])
                    yield
                    for nt, (pb_, pk) in enumerate(qbanks):
                        for kc in range(8):
                            mm(pb_, hT2[:, kc, :], wpq[:, kc, nt * 512:(nt + 1) * 512], kc == 0, kc == 7,
                               ["hT2", "wpq"], [pk])
                    yield
                    for nt, (pb_, pk) in enumerate(qbanks):
                        act(s_sb[:, nt * 512:(nt + 1) * 512], pb_, AF.Square, [pk], ["s_sb"])
                    yield
                    vreduce(ss16[:], s_sb[:].rearrange("p (g d) -> p g d", d=128), ["s_sb"], ["ss16"])
                    yield
                    act(ss16[:], ss16[:], AF.Sqrt, ["ss16"], ["ss16"], bias=EPS, scale=1.0 / 128)
                    yield
                    recip(ss16[:], ss16[:], ["ss16"], ["ss16"])
                    yield
                    for hp in range(16):
                        pb_, pk = qbanks[hp // 4]
                        act(qn[:, hp * 128:(hp + 1) * 128], pb_[:, (hp % 4) * 128:(hp % 4 + 1) * 128], AF.Copy,
                            [pk, "ss16"], ["qn"], scale=ss16[:, hp:hp + 1])
                    yield
                    pav = pA[:, 0:1024].bitcast(BF16)
                    for hp in range(16):
                        tr(pav[:, hp * 128:(hp + 1) * 128], qn[:, hp * 128:(hp + 1) * 128], identb[:],
                           ["qn", "identb"], ["pA0", "pA1"])
                    yield
                    act(qnT[:].rearrange("p h t -> p (h t)"), pav[:, 0:2048], AF.Copy, ["pA0", "pA1"], ["qnT"])
                    yield
                    for hp in range(16):
                        pb_, pk = qbanks[hp // 4]
                        mm(pb_[:, (hp % 4) * 128:(hp % 4 + 1) * 128], qnT[:, hp, :], keysT[:, hp, :], True, True,
                           ["qnT", "keysT"], [pk])
                    yield
                    for nt, (pb_, pk) in enumerate(qbanks):
                        act(s_sb[:, nt * 512:(nt + 1) * 512], pb_, AF.Copy, [pk], ["s_sb"])
                    yield
                    for h in range(8):
                        sides = []
                        for side, (tv, iv) in enumerate([(ta, ia), (tb, ib)]):
                            sv = s_sb[:, (2 * h + side) * 128:(2 * h + side + 1) * 128]
                            tk, ik = ("ta", "ia") if side == 0 else ("tb", "ib")
                            sides.append((tv, iv, sv, tk, ik, s2x[side], ("s2", side)))
                        for tv, iv, sv, tk, ik, s2_, s2k in sides:
                            vmax(tv[:, h, 0:8], sv, ["s_sb"], [tk])
                        for tv, iv, sv, tk, ik, s2_, s2k in sides:
                            vmatchrep(s2_[:], tv[:, h, 0:8], sv, ["s_sb", tk], [s2k])
                        for tv, iv, sv, tk, ik, s2_, s2k in sides:
                            vmaxidx(iv[:, h, 0:8], tv[:, h, 0:8], sv, ["s_sb", tk], [ik])
                        for tv, iv, sv, tk, ik, s2_, s2k in sides:
                            vmax(tv[:, h, 8:16], s2_[:], [s2k], [tk])
                        for tv, iv, sv, tk, ik, s2_, s2k in sides:
                            vmaxidx(iv[:, h, 8:16], tv[:, h, 8:16], s2_[:], [s2k, tk], [ik])
                        tt(cand[:], ta[:, h, :].unsqueeze(2).to_broadcast([128, 16, 16]),
                           tb[:, h, :].unsqueeze(1).to_broadcast([128, 16, 16]), ALU.add, ["ta", "tb"], ["cand"])
                        cf = cand[:].rearrange("p a b -> p (a b)")
                        vmax(tcv[:, h, 0:8], cf, ["cand"], ["tcv"])
                        vmaxidx(pos[:, h, 0:8], tcv[:, h, 0:8], cf, ["cand", "tcv"], ["pos"])
                        vmatchrep(cand2[:], tcv[:, h, 0:8], cf, ["cand", "tcv"], ["cand2"])
                        vmax(tcv[:, h, 8:16], cand2[:], ["cand2"], ["tcv"])
                        vmaxidx(pos[:, h, 8:16], tcv[:, h, 8:16], cand2[:], ["cand2", "tcv"], ["pos"])
                        yield
                    vsingle(k1[:], pos[:], 4, ALU.arith_shift_right, ["pos"], ["k1"])
                    vsingle(k2[:], pos[:], 15, ALU.bitwise_and, ["pos"], ["k2"])
                    vcopy(k1f[:], k1[:], ["k1"], ["k1f"])
                    vcopy(k2f[:], k2[:], ["k2"], ["k2f"])
                    vcopy(iaf[:], ia[:], ["ia"], ["iaf"])
                    vcopy(ibf[:], ib[:], ["ib"], ["ibf"])
                    iob = io16[:].unsqueeze(1).unsqueeze(1).to_broadcast([128, 8, 16, 16])
                    for kf, kfk, ixf, ixk, osel, osk in [(k1f, "k1f", iaf, "iaf", isel, "isel"),
                                                         (k2f, "k2f", ibf, "ibf", jsel, "jsel")]:
                        tt(eq4[:], kf[:].unsqueeze(3).to_broadcast([128, 8, 16, 16]), iob, ALU.is_equal,
                           [kfk, "io16"], ["eq4"])
                        tt(eq4[:], eq4[:], ixf[:].unsqueeze(2).to_broadcast([128, 8, 16, 16]), ALU.mult,
                           ["eq4", ixk], ["eq4"])
                        vreduce(osel[:], eq4[:], ["eq4"], [osk])
                    yield
                    stt(idxf[:], isel[:].rearrange("p h k -> p (h k)"), 128.0, jsel[:].rearrange("p h k -> p (h k)"),
                        ALU.mult, ALU.add, ["isel", "jsel"], ["idxf"])
                    if L > 0:
                        ts(idxf[:], idxf[:], float(L * NEXP), None, ALU.add, None, ["idxf"], ["idxf"])
                    vcopy(idxu[b][:], idxf[:], ["idxf"], [("idxu", b)])
                    tt(ee[:], tcv[:], tcv[:, :, 0:1].to_broadcast([128, 8, 16]), ALU.subtract, ["tcv"], ["ee"])
                    yield
                    act(ee[:], ee[:], AF.Exp, ["ee"], ["ee"])
                    yield
                    vreduce(zz[:], ee[:], ["ee"], ["zz"])
                    recip(zz[:], zz[:], ["zz"], ["zz"])
                    tt(gw[b][:].rearrange("p (h k) -> p h k", k=16), ee[:], zz[:].unsqueeze(2).to_broadcast([128, 8, 16]),
                       ALU.mult, ["ee", "zz"], [("gw", b)])

                def back(tix, nxt):
                    ti = tiles[tix]
                    b = tix % 2
                    r = 1 if ti < 2 else 0

                    def stage1(bi, mid=None):
                        for q8 in range(8):
                            hk = bi * 8 + q8
                            sl = state["gcnt"] % NS
                            state["gcnt"] += 1
                            slots[hk] = sl
                            gather(gbuf[sl][:], UVb, idxu[b][:, hk:hk + 1], [("idxu", b)] + uvkeys, [("gb", sl)])
                        for q8 in range(8):
                            hk = bi * 8 + q8
                            sl = slots[hk]
                            stt(junk[:], gbuf[sl][:, 0:D], 1.0, h32[b][:], ALU.mult, ALU.mult, [("gb", sl), ("h32", b)],
                                ["junk", ("actv", bi)], accum=actv[:, hk:hk + 1])
                            if q8 == 1 and mid is not None:
                                mid()
                        act(gact[:, bi * 8:(bi + 1) * 8], actv[:, bi * 8:(bi + 1) * 8], AF.Gelu, [("actv", bi)], [("gact", bi)])

                    def stage2(bi):
                        tt(gw2[:, bi * 8:(bi + 1) * 8], gw[b][:, bi * 8:(bi + 1) * 8], gact[:, bi * 8:(bi + 1) * 8], ALU.mult,
                           [("gw", b), ("gact", bi)], [("gw2", bi)])
                        for q8 in range(8):
                            hk = bi * 8 + q8
                            sl = slots[hk]
                            dd = state["dcnt"] % 4
                            state["dcnt"] += 1
                            act(dgt[dd][:], identb[:], AF.Copy, ["identb", ("gw2", bi)], [("dg", dd)], scale=gw2[:, hk:hk + 1])
                            mm(PB0, dgt[dd][:], gbuf[sl][:, D:D + 512], hk == 0, hk == 127, [("dg", dd), ("gb", sl)], ["pB0"])
                            mm(PB1, dgt[dd][:], gbuf[sl][:, D + 512:2 * D], hk == 0, hk == 127, [("dg", dd), ("gb", sl)], ["pB1"])

                    slots = {}
                    stage1(0)
                    for bi in range(1, 16):
                        stage1(bi, mid=lambda bi=bi: stage2(bi - 1))
                        if nxt is not None:
                            next(nxt, None)
                            next(nxt, None)
                    stage2(15)
                    if nxt is not None:
                        for _ in nxt:
                            pass
                    tt(xo[:, 0:512], PB0, g2b[r][:, 0:512], ALU.mult, ["pB0", ("g2b", r)], ["xo"])
                    tt(xo[:, 512:D], PB1, g2b[r][:, 512:D], ALU.mult, ["pB1", ("g2b", r)], ["xo"])
                    tt(xo[:], xo[:], xt[b][:], ALU.add, ["xo", ("xt", b)], ["xo"])
                    if last:
                        DS(out[(ti - 2) * 128:(ti - 1) * 128, :], xo[:], r=["xo"], w=[("out", ti)])
                    else:
                        DS(xs[ti * 128:(ti + 1) * 128, :], xo[:], r=["xo"], w=[XS[ti]])
                    if tix + 2 < len(tiles):
                        tn = tiles[tix + 2]
                        DS(xt[b][:], xs[tn * 128:(tn + 1) * 128, :], r=[XS[tn]], w=[("xt", b)])

                DS(xt[0][:], xs[tiles[0] * 128:(tiles[0] + 1) * 128, :], r=[XS[tiles[0]]], w=[("xt", 0)])
                if len(tiles) > 1:
                    DS(xt[1][:], xs[tiles[1] * 128:(tiles[1] + 1) * 128, :], r=[XS[tiles[1]]], w=[("xt", 1)])
                for _ in front(0):
                    pass
                for tix in range(len(tiles)):
                    nxt = front(tix + 1) if tix + 1 < len(tiles) else None
                    back(tix, nxt)
            em.barrier_all()

        em.finish("sync")
        semkeys = list(ENGS) + [("dma", i) for i in range(em.n_dma)]
        sems = {k: top.enter_context(nc.semaphore(f"sem{j}")) for j, k in enumerate(semkeys)}
        with nc.Block() as block:
            @block.sync
            def _(e):
                em.replay(sems, "sync", e)

            @block.scalar
            def _(e):
                em.replay(sems, "scalar", e)

            @block.vector
            def _(e):
                em.replay(sems, "vector", e)

            @block.gpsimd
            def _(e):
                em.replay(sems, "gpsimd", e)

            @block.tensor
            def _(e):
                em.replay(sems, "tensor", e)
    return nc, em


_CONST = {}


def _constants():
    if _CONST:
        return _CONST
    bf = ml_dtypes.bfloat16
    t = np.arange(SEQ, dtype=np.int64)
    m = (t[:, None] * t[None, :]) % SEQ
    ang = (2.0 * np.pi / SEQ) * m.astype(np.float64)
    dftL = np.empty((2, SEQ, SEQ), dtype=bf)
    dftL[0] = (np.cos(ang) / 64.0).astype(np.float32).astype(bf)
    dftL[1] = (-np.sin(ang) / 64.0).astype(np.float32).astype(bf)
    del ang, m
    t2 = np.arange(CTX, dtype=np.int64)
    a2 = (2.0 * np.pi / CTX) * ((t2[:, None] * t2[None, :]) % CTX).astype(np.float64)
    dft256 = np.stack([np.cos(a2) / 16.0, -np.sin(a2) / 16.0]).astype(np.float32).astype(bf)
    c = np.arange(128, dtype=np.int64)
    a3 = (2.0 * np.pi / 128) * ((c[:, None] * c[None, :]) % 128).astype(np.float64)
    s128 = 1.0 / math.sqrt(128.0)
    dftC = np.stack([np.cos(a3) * s128, np.sin(a3) * s128]).astype(np.float32).astype(bf)
    freqs = (10000.0 ** (-np.arange(0, 32, 2, dtype=np.float32) / 32.0)).astype(np.float32)
    rr = (t // 64).astype(np.float32)
    cc = (t % 64).astype(np.float32)
    ang_r = rr[:, None] * freqs[None, :]
    ang_c = cc[:, None] * freqs[None, :]
    rope = np.concatenate([np.cos(ang_r), np.cos(ang_c), np.sin(ang_r), np.sin(ang_c)], axis=1).astype(np.float32)
    _CONST.update(dftL=dftL, dft256=dft256, dftC=dftC, rope=rope, identf=np.eye(128, dtype=np.float32))
    return _CONST


def make_in_maps(inputs, depth=DEPTH, cores=NCORES):
    f = lambda a: np.ascontiguousarray(np.asarray(a, dtype=np.float32))
    cst = _constants()
    x = f(inputs["x"]); c = f(inputs["c"]); ctx = f(inputs["ctx"]); c_ctx = f(inputs["c_ctx"])
    sl = slice(0, depth)
    shared = {
        "w_ada": f(inputs["w_ada"])[sl], "b_ada": f(inputs["b_ada"])[sl],
        "norm1_g": f(inputs["norm1_g"])[sl], "norm2_g": f(inputs["norm2_g"])[sl],
        "w_in": f(inputs["w_in"])[sl],
        "bgate_c": np.ascontiguousarray(f(inputs["b_gate"])[sl].reshape(depth, 24, 128).transpose(0, 2, 1)),
        "q_norm_g": f(inputs["q_norm_g"])[sl], "k_norm_g": f(inputs["k_norm_g"])[sl],
        "lam_params": f(inputs["lam_params"])[sl].reshape(depth, 256),
        "subln_c": f(inputs["subln_g"])[sl].reshape(depth, 128, 1),
        "cm_ln_g": f(inputs["cm_ln_g"])[sl],
        "cmws_T": np.ascontiguousarray(f(inputs["cm_w_s"])[sl].transpose(0, 3, 1, 2)),
        "cmbs_c": np.ascontiguousarray(f(inputs["cm_b_s"])[sl].transpose(0, 2, 1)),
        "w_branch": f(inputs["w_branch"])[sl], "w_out": f(inputs["w_out"])[sl],
        "peer_w_q": f(inputs["peer_w_q"])[sl],
        "keysT": np.ascontiguousarray(f(inputs["peer_sub_keys"])[sl].reshape(depth, 16, 128, 128).transpose(0, 3, 1, 2)),
        "peer_u": f(inputs["peer_u"])[sl], "peer_v": f(inputs["peer_v"])[sl],
        "identf": cst["identf"], "rope": cst["rope"], "dftL": cst["dftL"], "dft256": cst["dft256"], "dftC": cst["dftC"],
    }
    maps = []
    for b in range(cores):
        cv = np.stack([c[b], c_ctx], axis=0)
        cT = np.ascontiguousarray(cv.reshape(2, 8, 128).transpose(2, 1, 0))
        mp = dict(shared)
        mp.update({"x": x[b], "ctx": ctx[b], "cT": cT})
        maps.append(mp)
    return maps


_NC = {}


def kernel(**inputs):
    if "nc" not in _NC:
        _NC["nc"] = build()[0]
    nc = _NC["nc"]
    maps = make_in_maps(inputs)
    res = run_bass_kernel_spmd(nc, maps, core_ids=list(range(NCORES)))
    outs = [np.asarray(r["out"], dtype=np.float32) for r in res.results]
    return np.stack(outs, axis=0)
```

```python
import math
from contextlib import ExitStack

import numpy as np
import ml_dtypes

import concourse.bass as bass
import concourse.mybir as mybir
from concourse.bass_utils import run_bass_kernel_spmd

F32 = mybir.dt.float32
BF16 = mybir.dt.bfloat16
U32 = mybir.dt.uint32
AF = mybir.ActivationFunctionType
ALU = mybir.AluOpType
AX = mybir.AxisListType

D = 1024
SEQ = 4096
CTX = 256
NTOK = SEQ + CTX
DEPTH = 4
NCORES = 8
EPS = 1e-6
IN_COLS = 6144
NEXP = 16384

ENGS = ["tensor", "vector", "scalar", "gpsimd", "sync"]


class Emitter:
    def __init__(self, nc, n_dma_sems=28):
        self.nc = nc
        self.lists = {e: [] for e in ENGS}
        self.cnt = {e: 0 for e in ENGS}
        self.known = {e: {} for e in ENGS}
        self.last_w = {}
        self.readers = {}
        self.n_dma = n_dma_sems
        self.dma_val = [0] * n_dma_sems
        self.dma_rr = 0
        self.n_inst = 0

    def _deps(self, reads, writes, eng=None):
        deps = {}

        def add(d, same_ok):
            if d is None:
                return
            s, v = d
            if not same_ok and s == eng:
                return
            if deps.get(s, 0) < v:
                deps[s] = v

        for k in reads:
            add(self.last_w.get(k), True)
        for k in writes:
            add(self.last_w.get(k), False)
            for r in self.readers.get(k, ()):
                add(r, False)
        return deps

    def _emit_waits(self, eng, deps):
        kn = self.known[eng]
        for s, v in deps.items():
            if eng == "tensor" and s == "tensor":
                continue
            if kn.get(s, 0) >= v:
                continue
            kn[s] = v
            self.lists[eng].append(("wait", s, v))

    def _commit(self, token, reads, writes):
        for k in reads:
            lst = self.readers.setdefault(k, [])
            lst.append(token)
            if len(lst) > 64:
                mx = {}
                for s, v in lst:
                    if mx.get(s, 0) < v:
                        mx[s] = v
                self.readers[k] = list(mx.items())
        for k in writes:
            self.last_w[k] = token
            self.readers[k] = []

    def op(self, eng, fn, reads=(), writes=()):
        deps = self._deps(reads, writes, eng)
        self._emit_waits(eng, deps)
        self.cnt[eng] += 1
        token = (eng, self.cnt[eng])
        self.lists[eng].append(("op", fn, eng, 1))
        self._commit(token, reads, writes)
        self.n_inst += 1
        return token

    def dma(self, eng, fn, reads=(), writes=()):
        deps = self._deps(reads, writes)
        i = self.dma_rr
        self.dma_rr = (self.dma_rr + 1) % self.n_dma
        s = ("dma", i)
        if self.dma_val[i] > 0:
            deps[s] = max(deps.get(s, 0), self.dma_val[i])
        self._emit_waits(eng, deps)
        self.dma_val[i] += 16
        token = (s, self.dma_val[i])
        self.lists[eng].append(("op", fn, s, 16))
        self._commit(token, reads, writes)
        self.n_inst += 1
        return token

    def _all(self):
        deps = {e: self.cnt[e] for e in ENGS if self.cnt[e] > 0}
        for i in range(self.n_dma):
            if self.dma_val[i] > 0:
                deps[("dma", i)] = self.dma_val[i]
        return deps

    def barrier_all(self):
        deps = self._all()
        for e in ENGS:
            self._emit_waits(e, dict(deps))

    def finish(self, eng="sync"):
        self._emit_waits(eng, self._all())

    def replay(self, sems, engname, engobj):
        for item in self.lists[engname]:
            if item[0] == "wait":
                engobj.wait_ge(sems[item[1]], item[2])
            else:
                _, fn, s, inc = item
                fn(engobj).then_inc(sems[s], inc)


def build(depth=DEPTH, total_depth=DEPTH, dbg=False, stop_after=None):
    nc = bass.Bass("TRN2", target_bir_lowering=False)
    em = Emitter(nc)

    def din(name, shape, dt=F32):
        return nc.dram_tensor(name, list(shape), dt, kind="ExternalInput").ap()

    x_in = din("x", [SEQ, D])
    ctx_in = din("ctx", [CTX, D])
    cT_in = din("cT", [128, 8, 2])
    w_ada = din("w_ada", [depth, D, 6 * D])
    b_ada = din("b_ada", [depth, 6 * D])
    norm1_g = din("norm1_g", [depth, D])
    norm2_g = din("norm2_g", [depth, D])
    w_in = din("w_in", [depth, D, IN_COLS])
    bgate_c = din("bgate_c", [depth, 128, 24])
    q_norm_g = din("q_norm_g", [depth, 64])
    k_norm_g = din("k_norm_g", [depth, 64])
    lam_params = din("lam_params", [depth, 256])
    subln_c = din("subln_c", [depth, 128, 1])
    cm_ln_g = din("cm_ln_g", [depth, 512])
    cmws_T = din("cmws_T", [depth, 128, 4, 128])
    cmbs_c = din("cmbs_c", [depth, 128, 4])
    w_branch = din("w_branch", [depth, 3, 512, D])
    w_out = din("w_out", [depth, D, D])
    peer_w_q = din("peer_w_q", [depth, D, 2048])
    keysT_in = din("keysT", [depth, 128, 16, 128])
    peer_u = din("peer_u", [depth, NEXP, D])
    peer_v = din("peer_v", [depth, NEXP, D])
    peer_u_flat = peer_u.rearrange("l e d -> (l e) d")
    peer_v_flat = peer_v.rearrange("l e d -> (l e) d")
    identf_in = din("identf", [128, 128])
    rope_in = din("rope", [SEQ, 64])
    dftL = din("dftL", [2, SEQ, SEQ], BF16)
    dft256 = din("dft256", [2, CTX, CTX], BF16)
    dftC = din("dftC", [2, 128, 128], BF16)

    out = nc.dram_tensor("out", [SEQ, D], F32, kind="ExternalOutput").ap()
    skind = "ExternalOutput" if dbg else "Internal"
    xs = nc.dram_tensor("xs", [NTOK, D], F32, kind=skind).ap()
    der = nc.dram_tensor("der", [2, 6, D], F32, kind=skind).ap()
    aT_s = nc.dram_tensor("aT_s", [4, 128, NTOK], BF16, kind=skind).ap()
    fT_s = nc.dram_tensor("fT_s", [4, 128, NTOK], BF16, kind=skind).ap()
    UVb = nc.dram_tensor("UVb", [depth * NEXP, 2 * D], BF16, kind="Internal").ap()

    def V(fn, r=(), w=()):
        return em.op("vector", fn, r, w)

    def A(fn, r=(), w=()):
        return em.op("scalar", fn, r, w)

    def PE(fn, r=(), w=()):
        return em.op("tensor", fn, r, w)

    def G(fn, r=(), w=()):
        return em.op("gpsimd", fn, r, w)

    def DS(out_, in_, r=(), w=()):
        return em.dma("sync", lambda e: e.dma_start(out=out_, in_=in_), r, w)

    def DG(out_, in_, r=(), w=()):
        return em.dma("gpsimd", lambda e: e.dma_start(out=out_, in_=in_), r, w)

    def tt(out_, a, b, op, r, w, eng="vector"):
        return em.op(eng, lambda e: e.tensor_tensor(out=out_, in0=a, in1=b, op=op), r, w)

    def stt(out_, a, s, b, op0, op1, r, w, accum=None):
        return V(lambda e: e.scalar_tensor_tensor(out=out_, in0=a, scalar=s, in1=b, op0=op0, op1=op1,
                                                  accum_out=accum), r, w)

    def ts(out_, a, s1, s2, op0, op1, r, w):
        if s2 is None:
            return V(lambda e: e.tensor_scalar(out=out_, in0=a, scalar1=s1, scalar2=None, op0=op0), r, w)
        return V(lambda e: e.tensor_scalar(out=out_, in0=a, scalar1=s1, scalar2=s2, op0=op0, op1=op1), r, w)

    def act(out_, in_, func, r, w, bias=None, scale=None, accum=None):
        kw = {}
        if bias is not None:
            kw["bias"] = bias
        if scale is not None:
            kw["scale"] = scale
        if accum is not None:
            kw["accum_out"] = accum
        return A(lambda e: e.activation(out=out_, in_=in_, func=func, **kw), r, w)

    def vcopy(out_, in_, r, w):
        return V(lambda e: e.tensor_copy(out=out_, in_=in_), r, w)

    def mm(out_, lhsT, rhs, start, stop, r, w):
        return PE(lambda e: e.matmul(out_, lhsT=lhsT, rhs=rhs, start=start, stop=stop), r, w)

    def tr(out_, in_, ident, r, w):
        return PE(lambda e: e.transpose(out_, in_, ident), r, w)

    def recip(out_, in_, r, w):
        return V(lambda e: e.reciprocal(out=out_, in_=in_), r, w)

    def vmax(out_, in_, r, w):
        return V(lambda e: e.max(out=out_, in_=in_), r, w)

    def vmaxidx(out_, inmax, invals, r, w):
        return V(lambda e: e.max_index(out=out_, in_max=inmax, in_values=invals), r, w)

    def vmatchrep(out_, rep, vals, r, w):
        return V(lambda e: e.match_replace(out=out_, in_to_replace=rep, in_values=vals, imm_value=-1e30), r, w)

    def vreduce(out_, in_, r, w):
        return V(lambda e: e.tensor_reduce(out=out_, in_=in_, axis=AX.X, op=ALU.add), r, w)

    def vsingle(out_, in_, scalar, op, r, w):
        return V(lambda e: e.tensor_single_scalar(out=out_, in_=in_, scalar=scalar, op=op), r, w)

    def rstd_op(t_ap, key, scale, mode):
        if mode == "ln":
            act(t_ap, t_ap, AF.Ln, [key], [key], bias=EPS, scale=scale)
            act(t_ap, t_ap, AF.Exp, [key], [key], scale=-0.5)
        else:
            act(t_ap, t_ap, AF.Sqrt, [key], [key], bias=EPS, scale=scale)
            recip(t_ap, t_ap, [key], [key])

    def gather(out_, table, idx_ap, r, w):
        return em.dma("gpsimd", lambda e: e.indirect_dma_start(
            out=out_, out_offset=None, in_=table,
            in_offset=bass.IndirectOffsetOnAxis(ap=idx_ap, axis=0)), r, w)

    with ExitStack() as top:
        _tcnt = [0]

        def T(es, name, shape, dt):
            _tcnt[0] += 1
            return es.enter_context(nc.sbuf_tensor(f"t{_tcnt[0]}_{name}", list(shape), dt))

        pA = top.enter_context(nc.psum_tensor("pA", [128, 1024], F32))
        pB = top.enter_context(nc.psum_tensor("pB", [128, 1024], F32))
        pC = top.enter_context(nc.psum_tensor("pC", [128, 1024], F32))
        pD = top.enter_context(nc.psum_tensor("pD", [128, 1024], F32))
        PA0, PA1 = pA[:, 0:512], pA[:, 512:1024]
        PB0, PB1 = pB[:, 0:512], pB[:, 512:1024]
        PC0, PC1 = pC[:, 0:512], pC[:, 512:1024]
        PD0, PD1 = pD[:, 0:512], pD[:, 512:1024]

        identf = T(top, "identf", [128, 128], F32)
        identb = T(top, "identb", [128, 128], BF16)
        onesf = T(top, "onesf", [128, 128], F32)
        onesb = T(top, "onesb", [128, 128], BF16)
        ropet = T(top, "ropet", [128, 32, 64], F32)
        io16 = T(top, "io16", [128, 16], F32)
        cact = T(top, "cact", [128, 8, 2], F32)
        CCt = T(top, "CCt", [128, 128], BF16)
        SCt = T(top, "SCt", [128, 128], BF16)
        neglam = T(top, "neglam", [128, 1], F32)
        sgcol = T(top, "sgcol", [128, 1], F32)
        lamt = T(top, "lamt", [128, 2], F32)
        lpb = T(top, "lpb", [128, 256], F32)
        lj = T(top, "lj", [128, 64], F32)

        DS(identf[:], identf_in, w=["identf"])
        vcopy(identb[:], identf[:], ["identf"], ["identb"])
        V(lambda e: e.memset(onesf[:], 1.0), w=["onesf"])
        V(lambda e: e.memset(onesb[:], 1.0), w=["onesb"])
        DS(ropet[:], rope_in.rearrange("(t p) c -> p t c", p=128), w=["ropet"])
        G(lambda e: e.iota(io16[:], pattern=[[1, 16]], base=0, channel_multiplier=0,
                           allow_small_or_imprecise_dtypes=True), w=["io16"])
        DS(cact[:], cT_in, w=["cact"])
        act(cact[:], cact[:], AF.Silu, ["cact"], ["cact"])
        DS(CCt[:], dftC[0], w=["CCt"])
        DS(SCt[:], dftC[1], w=["SCt"])
        XS = [("xs", i) for i in range(34)]
        DS(xs[0:CTX, :], ctx_in, w=XS[0:2])
        for q4 in range(4):
            DS(xs[CTX + q4 * 1024:CTX + (q4 + 1) * 1024, :], x_in[q4 * 1024:(q4 + 1) * 1024, :],
               w=XS[2 + q4 * 8:2 + (q4 + 1) * 8])

        def norm_mod(xtile, xkey, gsb, shb, mkeys, hT_out, hT_key, tp, tpkeys, sqj, ssq, h32,
                     h32key="h32", sqkey="sqj", rs="sqrt"):
            act(sqj[:], xtile, AF.Square, [xkey], [sqkey, "ssq"], accum=ssq[:])
            rstd_op(ssq[:], "ssq", 1.0 / D, rs)
            stt(h32[:], xtile, ssq[:, 0:1], gsb, ALU.mult, ALU.mult, [xkey, "ssq", mkeys[0]], [h32key])
            tt(h32[:], h32[:], shb, ALU.add, [h32key, mkeys[1]], [h32key])
            if isinstance(tp, list):
                for hf in range(2):
                    for k4 in range(4):
                        kc = hf * 4 + k4
                        tr(tp[hf][:, k4 * 128:(k4 + 1) * 128], h32[:, kc * 128:(kc + 1) * 128], identf[:],
                           [h32key, "identf"], [tpkeys[hf]])
                    act(hT_out[:, hf * 4:(hf + 1) * 4, :], tp[hf].rearrange("p (k t) -> p k t", t=128), AF.Copy,
                        [tpkeys[hf]], [hT_key])
                return
            for kc in range(8):
                tr(tp[:, kc * 128:(kc + 1) * 128], h32[:, kc * 128:(kc + 1) * 128], identf[:],
                   [h32key, "identf"], tpkeys)
            act(hT_out, tp[:, 0:1024].rearrange("p (k t) -> p k t", t=128), AF.Copy, tpkeys, [hT_key])

        def group_norm_rope(ps, pskey, gb, gbkey, nsq, ss8, kn, ra, rb_, outb, outkey, rope_tile, rs="sqrt"):
            act(nsq[:], ps, AF.Square, [pskey], ["nsq"])
            vreduce(ss8[:], nsq[:].rearrange("p (g d) -> p g d", d=64), ["nsq"], ["ss8"])
            rstd_op(ss8[:], "ss8", 1.0 / 64, rs)
            kn3 = kn[:].rearrange("p (g d) -> p g d", d=64)
            tt(kn3, ps.rearrange("p (g d) -> p g d", d=64), ss8[:].unsqueeze(2).to_broadcast([128, 8, 64]),
               ALU.mult, [pskey, "ss8"], ["kn"])
            if rope_tile is None:
                tt(outb[:].rearrange("p (g d) -> p g d", d=64), kn3, gb[:].unsqueeze(1).to_broadcast([128, 8, 64]),
                   ALU.mult, ["kn", gbkey], [outkey])
                return
            tt(kn3, kn3, gb[:].unsqueeze(1).to_broadcast([128, 8, 64]), ALU.mult, ["kn", gbkey], ["kn"])
            kn5 = kn[:].rearrange("p (g a h f) -> p g a h f", g=8, a=2, h=2, f=16)
            ob5 = outb[:].rearrange("p (g a h f) -> p g a h f", g=8, a=2, h=2, f=16)
            x1, x2 = kn5[:, :, :, 0, :], kn5[:, :, :, 1, :]
            cosb = ropet[:, rope_tile, 0:32].rearrange("p (a f) -> p a f", a=2).unsqueeze(1).to_broadcast([128, 8, 2, 16])
            sinb = ropet[:, rope_tile, 32:64].rearrange("p (a f) -> p a f", a=2).unsqueeze(1).to_broadcast([128, 8, 2, 16])
            ra4 = ra[:].rearrange("p (g a f) -> p g a f", g=8, a=2)
            rb4 = rb_[:].rearrange("p (g a f) -> p g a f", g=8, a=2)
            tt(ra4, x1, cosb, ALU.mult, ["kn", "ropet"], ["ra"])
            tt(rb4, x2, sinb, ALU.mult, ["kn", "ropet"], ["rb"])
            tt(ob5[:, :, :, 0, :], ra4, rb4, ALU.subtract, ["ra", "rb"], [outkey])
            tt(ra4, x2, cosb, ALU.mult, ["kn", "ropet"], ["ra"])
            tt(rb4, x1, sinb, ALU.mult, ["kn", "ropet"], ["rb"])
            tt(ob5[:, :, :, 1, :], ra4, rb4, ALU.add, ["ra", "rb"], [outkey])

        def load_bcast(tile, j, r, key):
            DS(tile[:], der[r, j:j + 1, :].partition_broadcast(128), r=["der"], w=[key])

        for L in range(depth):
            last = (L == total_depth - 1)
            lam_init = 0.8 - 0.6 * math.exp(-0.3 * L)
            groups = []
            if not last:
                groups.append((0, 256, True))
            for g8 in range(8):
                groups.append((CTX + g8 * 512, 512, False))

            em.barrier_all()
            with ExitStack() as s0:
                wada = [T(s0, f"wada{i}", [128, 8, 512], F32) for i in range(2)]
                badat = [T(s0, f"badat{i}", [2, 512], F32) for i in range(2)]
                gch = [T(s0, f"gch{i}", [2, 512], F32) for i in range(2)]
                modrow = [T(s0, f"modrow{i}", [2, 512], F32) for i in range(2)]
                jmap = {0: 1, 1: 0, 2: 2, 3: 4, 4: 3, 5: 5}
                for nt in range(12):
                    b = nt % 2
                    part, half = nt // 2, nt % 2
                    DS(wada[b][:], w_ada[L][:, nt * 512:(nt + 1) * 512].rearrange("(kc p) n -> p kc n", p=128),
                       w=[("wada", b)])
                    DS(badat[b][:], b_ada[L:L + 1, nt * 512:(nt + 1) * 512].partition_broadcast(2), w=[("badat", b)])
                    for kc in range(8):
                        mm(PA0[0:2, :], cact[:, kc, :], wada[b][:, kc, :], kc == 0, kc == 7,
                           [("wada", b), "cact"], ["pA0"])
                    tt(modrow[b][:], PA0[0:2, :], badat[b][:], ALU.add, ["pA0", ("badat", b)], [("modrow", b)])
                    if part in (1, 4):
                        ng = norm1_g if part == 1 else norm2_g
                        DS(gch[b][:], ng[L:L + 1, half * 512:(half + 1) * 512].partition_broadcast(2), w=[("gch", b)])
                        stt(modrow[b][:], modrow[b][:], 1.0, gch[b][:], ALU.add, ALU.mult,
                            [("modrow", b), ("gch", b)], [("modrow", b)])
                    DS(der[:, jmap[part], half * 512:(half + 1) * 512], modrow[b][:], r=[("modrow", b)], w=["der"])
                DS(lpb[:], lam_params[L:L + 1, :].partition_broadcast(128), w=["lpb"])
                stt(lj[:], lpb[:, 0:64], 1.0, lpb[:, 64:128], ALU.mult, ALU.mult, ["lpb"], ["lj", "lamt"], accum=lamt[:, 0:1])
                stt(lj[:], lpb[:, 128:192], 1.0, lpb[:, 192:256], ALU.mult, ALU.mult, ["lpb"], ["lj", "lamt"], accum=lamt[:, 1:2])
                act(lamt[:], lamt[:], AF.Exp, ["lamt"], ["lamt"])
                tt(neglam[:], lamt[:, 1:2], lamt[:, 0:1], ALU.subtract, ["lamt"], ["neglam"])
                ts(neglam[:], neglam[:], -lam_init, None, ALU.add, None, ["neglam"], ["neglam"])
                DS(sgcol[:], subln_c[L], w=["sgcol"])
                ts(sgcol[:], sgcol[:], 1.0 - lam_init, None, ALU.mult, None, ["sgcol"], ["sgcol"])
            em.barrier_all()
            if stop_after == "P0":
                break

            with ExitStack() as sZ:
                Zs = T(sZ, "Zs", [128, 34, 512], BF16)
                gs1b = [T(sZ, f"gs1b{r}", [128, D], F32) for r in range(2)]
                sh1b = [T(sZ, f"sh1b{r}", [128, D], F32) for r in range(2)]
                for r in range(2):
                    load_bcast(gs1b[r], 0, r, ("gs1b", r))
                    load_bcast(sh1b[r], 1, r, ("sh1b", r))
                with ExitStack() as sKV:
                    KT = T(sKV, "KT", [128, 4, NTOK], BF16)
                    Vs = T(sKV, "Vs", [128, 34, 512], BF16)
                    xt = [T(sKV, f"xt{i}", [128, D], F32) for i in range(2)]
                    sqj = T(sKV, "sqj", [128, D], F32)
                    ssq = T(sKV, "ssq", [128, 1], F32)
                    h32 = T(sKV, "h32", [128, D], F32)
                    nsq = T(sKV, "nsq", [128, 512], F32)
                    ss8 = T(sKV, "ss8", [128, 8], F32)
                    kn = T(sKV, "kn", [128, 512], F32)
                    ra = T(sKV, "ra", [128, 256], F32)
                    rb_ = T(sKV, "rb", [128, 256], F32)
                    kb = T(sKV, "kb", [128, 512], BF16)
                    kgb = T(sKV, "kgb", [128, 64], F32)
                    qgb = T(sKV, "qgb", [128, 64], F32)
                    DS(kgb[:], k_norm_g[L:L + 1, :].partition_broadcast(128), w=["kgb"])
                    DS(qgb[:], q_norm_g[L:L + 1, :].partition_broadcast(128), w=["qgb"])
                    ts(qgb[:], qgb[:], 0.125, None, ALU.mult, None, ["qgb"], ["qgb"])
                    with ExitStack() as s1:
                        w1 = T(s1, "w1", [128, 8, 1536], BF16)
                        hT1 = [T(s1, f"hT1_{i}", [128, 8, 128], BF16) for i in range(2)]
                        wsrc = w_in[L].rearrange("(kc p) n -> p kc n", p=128)
                        DG(w1[:, :, 0:512], wsrc[:, :, 512:1024], w=["w1"])
                        DG(w1[:, :, 512:1024], wsrc[:, :, 1024:1536], w=["w1"])
                        DG(w1[:, :, 1024:1536], wsrc[:, :, 2560:3072], w=["w1"])
                        DS(xt[0][:], xs[0:128, :], r=[XS[0]], w=[("xt", 0)])
                        for i in range(34):
                            b = i % 2
                            r = 1 if i < 2 else 0
                            if i + 1 < 34:
                                DS(xt[1 - b][:], xs[(i + 1) * 128:(i + 2) * 128, :], r=[XS[i + 1]], w=[("xt", 1 - b)])
                            tp, tpk = (pA, ["pA0", "pA1"]) if b == 0 else (pD, ["pD0", "pD1"])
                            norm_mod(xt[b][:], ("xt", b), gs1b[r][:], sh1b[r][:], [("gs1b", r), ("sh1b", r)],
                                     hT1[b][:], ("hT1", b), tp, tpk, sqj, ssq, h32)
                            for nt, (pb_, pk) in enumerate([(PB0, "pB0"), (PB1, "pB1"), (PC0, "pC0")]):
                                for kc in range(8):
                                    mm(pb_, hT1[b][:, kc, :], w1[:, kc, nt * 512:(nt + 1) * 512], kc == 0, kc == 7,
                                       [("hT1", b), "w1"], [pk])
                            group_norm_rope(PB0, "pB0", kgb, "kgb", nsq, ss8, kn, ra, rb_, kb, "kb",
                                            None if i < 2 else i - 2)
                            pcv = PC1.bitcast(BF16)
                            for h in range(4):
                                tr(pcv[:, h * 128:(h + 1) * 128], kb[:, h * 128:(h + 1) * 128], identb[:],
                                   ["kb", "identb"], ["pC1"])
                            act(KT[:, :, i * 128:(i + 1) * 128], pcv[:, 0:512].rearrange("p (h t) -> p h t", t=128),
                                AF.Copy, ["pC1"], [("KT", i)])
                            act(Vs[:, i, :], PB1, AF.Copy, ["pB1"], [("Vs", i)])
                            vcopy(Zs[:, i, :], PC0, ["pC0"], [("Zs", i)])
                    em.barrier_all()
                    if stop_after == "P1":
                        break
                    with ExitStack() as s2:
                        wq = T(s2, "wq", [128, 8, 512], BF16)
                        hT = T(s2, "hT", [128, 8, 128], BF16)
                        QT = [T(s2, f"QT{i}", [128, 4, 512], BF16) for i in range(2)]
                        PTp = [T(s2, f"PTp{i}", [128, 1024], BF16) for i in range(2)]
                        zacc = [T(s2, f"zacc{m}", [128, 512], F32) for m in range(2)]
                        rz = T(s2, "rz", [128, 512], F32)
                        O0 = T(s2, "O0", [128, 512], F32)
                        O1 = T(s2, "O1", [128, 512], F32)
                        att = T(s2, "att", [128, 512], F32)
                        asq = T(s2, "asq", [128, 512], F32)
                        rst = T(s2, "rst", [128, 512], F32)
                        aTb = [T(s2, f"aTb{i}", [128, 512], BF16) for i in range(2)]
                        DG(wq[:], w_in[L].rearrange("(kc p) n -> p kc n", p=128)[:, :, 0:512], w=["wq"])
                        cv_list = []
                        for c8 in range(8):
                            e0 = c8 * 2048
                            cv_list.append((UVb[L * NEXP + e0:L * NEXP + e0 + 2048, 0:D], peer_u[L][e0:e0 + 2048, :], ("UVb", L, c8, 0)))
                            cv_list.append((UVb[L * NEXP + e0:L * NEXP + e0 + 2048, D:2 * D], peer_v[L][e0:e0 + 2048, :], ("UVb", L, c8, 1)))
                        cv_state = {"i": 0}

                        def prep(gi):
                            row0, N, is_ctx = groups[gi]
                            r = 1 if is_ctx else 0
                            qt = QT[gi % 2]
                            qk = ("QT", gi % 2)
                            for j in range(N // 128):
                                ti = row0 // 128 + j
                                DS(xt[0][:], xs[ti * 128:(ti + 1) * 128, :], r=[XS[ti]], w=[("xt", 0)])
                                norm_mod(xt[0][:], ("xt", 0), gs1b[r][:], sh1b[r][:], [("gs1b", r), ("sh1b", r)],
                                         hT[:], "hT", [PB1, PC1], ["pB1", "pC1"], sqj, ssq, h32, rs="ln")
                                yield
                                for kc in range(8):
                                    mm(PB1, hT[:, kc, :], wq[:, kc, :], kc == 0, kc == 7, ["hT", "wq"], ["pB1"])
                                yield
                                group_norm_rope(PB1, "pB1", qgb, "qgb", nsq, ss8, kn, ra, rb_, kb, "kb",
                                                None if is_ctx else ti - 2, rs="ln")
                                yield
                                pcv = PC1.bitcast(BF16)
                                for h in range(4):
                                    tr(pcv[:, h * 128:(h + 1) * 128], kb[:, h * 128:(h + 1) * 128], identb[:],
                                       ["kb", "identb"], ["pC1"])
                                act(qt[:, :, j * 128:(j + 1) * 128], pcv[:, 0:512].rearrange("p (h t) -> p h t", t=128),
                                    AF.Copy, ["pC1"], [qk])
                                yield

                        def attend(gi, nxt):
                            row0, N, is_ctx = groups[gi]
                            qt = QT[gi % 2]
                            qk = ("QT", gi % 2)
                            if not is_ctx:
                                for _ in range(2):
                                    o_, i_, k_ = cv_list[cv_state["i"]]
                                    DG(o_, i_, w=[k_, ("cv", cv_state["i"] % 4)])
                                    cv_state["i"] += 1
                            kchunks = [0, 1] if is_ctx else list(range(34))
                            nch = len(kchunks)
                            sb = [(pD, ["pD0", "pD1"]), (pA, ["pA0", "pA1"])]
                            obank = [(PB0, "pB0"), (PC0, "pC0")]
                            step = 0
                            for h in range(4):
                                def s_mm(ci):
                                    kc = kchunks[ci]
                                    pS, pSk = sb[ci % 2]
                                    for m in range(2):
                                        lo, hi = m * 64, (m + 1) * 64
                                        mm(pS[:, m * 512:m * 512 + N], KT[lo:hi, h, kc * 128:(kc + 1) * 128], qt[lo:hi, h, 0:N],
                                           True, True, [("KT", kc), qk], [pSk[m]])

                                s_mm(0)
                                for ci, kc in enumerate(kchunks):
                                    first, lastc = ci == 0, ci == nch - 1
                                    if ci + 1 < nch:
                                        s_mm(ci + 1)
                                    pS, pSk = sb[ci % 2]
                                    pt = PTp[ci % 2]
                                    ptk = ("PT", ci % 2)
                                    act(pt[:].rearrange("p (m n) -> p m n", m=2)[:, :, 0:N],
                                        pS[:, 0:1024].rearrange("p (m n) -> p m n", m=2)[:, :, 0:N], AF.Exp, pSk, [ptk])
                                    for m in range(2):
                                        pO, pOk = obank[m]
                                        mm(pO[:, 0:N], Vs[:, kc, h * 128:(h + 1) * 128], pt[:, m * 512:m * 512 + N], first, lastc,
                                           [("Vs", kc), ptk], [pOk])
                                    for m in range(2):
                                        eng = "vector" if m == 0 else "gpsimd"
                                        if first:
                                            em.op(eng, lambda e, o_=zacc[m][:, 0:N], i_=pt[:, m * 512:m * 512 + N]: e.tensor_copy(out=o_, in_=i_),
                                                  [ptk], [("zacc", m)])
                                        else:
                                            tt(zacc[m][:, 0:N], zacc[m][:, 0:N], pt[:, m * 512:m * 512 + N], ALU.add,
                                               [("zacc", m), ptk], [("zacc", m)], eng=eng)
                                    step += 1
                                    if nxt is not None and step % 4 == 0:
                                        next(nxt, None)
                                mm(PA0[:, 0:N], onesf[:], zacc[0][:, 0:N], True, True, ["onesf", ("zacc", 0)], ["pA0"])
                                mm(PA1[:, 0:N], onesf[:], zacc[1][:, 0:N], True, True, ["onesf", ("zacc", 1)], ["pA1"])
                                recip(rz[:, 0:N], PA0[:, 0:N], ["pA0"], ["rz"])
                                tt(O0[:, 0:N], PB0[:, 0:N], rz[:, 0:N], ALU.mult, ["pB0", "rz"], ["O0"])
                                recip(rz[:, 0:N], PA1[:, 0:N], ["pA1"], ["rz"])
                                tt(O1[:, 0:N], PC0[:, 0:N], rz[:, 0:N], ALU.mult, ["pC0", "rz"], ["O1"])
                                stt(att[:, 0:N], O1[:, 0:N], neglam[:, 0:1], O0[:, 0:N], ALU.mult, ALU.add,
                                    ["O0", "O1", "neglam"], ["att"])
                                act(asq[:, 0:N], att[:, 0:N], AF.Square, ["att"], ["asq"])
                                mm(PA0[:, 0:N], onesf[:], asq[:, 0:N], True, True, ["onesf", "asq"], ["pA0"])
                                act(rst[:, 0:N], PA0[:, 0:N], AF.Ln, ["pA0"], ["rst"], bias=EPS, scale=1.0 / 128)
                                act(rst[:, 0:N], rst[:, 0:N], AF.Exp, ["rst"], ["rst"], scale=-0.5)
                                ab = aTb[h % 2]
                                stt(ab[:, 0:N], att[:, 0:N], sgcol[:, 0:1], rst[:, 0:N], ALU.mult, ALU.mult,
                                    ["att", "sgcol", "rst"], [("aTb", h % 2)])
                                DS(aT_s[h, :, row0:row0 + N], ab[:, 0:N], r=[("aTb", h % 2)], w=[("aT", gi)])
                            if nxt is not None:
                                for _ in nxt:
                                    pass

                        for _ in prep(0):
                            pass
                        for gi in range(len(groups)):
                            attend(gi, prep(gi + 1) if gi + 1 < len(groups) else None)
                    em.barrier_all()
                if stop_after == "P2a":
                    break
                with ExitStack() as s3:
                    tabC = [T(s3, f"tabC{i}", [128, 32, 512], BF16) for i in range(2)]
                    tabS = [T(s3, f"tabS{i}", [128, 32, 512], BF16) for i in range(2)]
                    Wc = T(s3, "Wc", [128, 512], BF16)
                    Ws = T(s3, "Ws", [128, 512], BF16)
                    fTb = [T(s3, f"fTb{i}", [128, 512], BF16) for i in range(2)]

                    def load_tabs(gi):
                        row0, N, is_ctx = groups[gi]
                        tb_ = gi % 2
                        if is_ctx:
                            DS(tabC[tb_][:, 0:2, 0:256], dft256[0].rearrange("(tc p) n -> p tc n", p=128), w=[("tabC", tb_)])
                            DS(tabS[tb_][:, 0:2, 0:256], dft256[1].rearrange("(tc p) n -> p tc n", p=128), w=[("tabS", tb_)])
                        else:
                            t0 = row0 - CTX
                            for q4 in range(4):
                                DS(tabC[tb_][:, q4 * 8:(q4 + 1) * 8, :],
                                   dftL[0][q4 * 1024:(q4 + 1) * 1024, t0:t0 + 512].rearrange("(tc p) n -> p tc n", p=128),
                                   w=[("tabC", tb_)])
                                DS(tabS[tb_][:, q4 * 8:(q4 + 1) * 8, :],
                                   dftL[1][q4 * 1024:(q4 + 1) * 1024, t0:t0 + 512].rearrange("(tc p) n -> p tc n", p=128),
                                   w=[("tabS", tb_)])

                    load_tabs(0)
                    for gi, (row0, N, is_ctx) in enumerate(groups):
                        tb_ = gi % 2
                        if gi + 1 < len(groups):
                            load_tabs(gi + 1)
                        ntc, z0 = (2, 0) if is_ctx else (32, 2)
                        for g in range(4):
                            for tcx in range(ntc):
                                mm(PA0[:, 0:N], Zs[:, z0 + tcx, g * 128:(g + 1) * 128], tabC[tb_][:, tcx, 0:N],
                                   tcx == 0, tcx == ntc - 1, [("Zs", z0 + tcx), ("tabC", tb_)], ["pA0"])
                            for tcx in range(ntc):
                                mm(PA1[:, 0:N], Zs[:, z0 + tcx, g * 128:(g + 1) * 128], tabS[tb_][:, tcx, 0:N],
                                   tcx == 0, tcx == ntc - 1, [("Zs", z0 + tcx), ("tabS", tb_)], ["pA1"])
                            act(Wc[:, 0:N], PA0[:, 0:N], AF.Copy, ["pA0"], ["Wc"])
                            vcopy(Ws[:, 0:N], PA1[:, 0:N], ["pA1"], ["Ws"])
                            pF, pFk = (PB0, "pB0") if g % 2 == 0 else (PB1, "pB1")
                            mm(pF[:, 0:N], CCt[:], Wc[:, 0:N], True, False, ["CCt", "Wc"], [pFk])
                            mm(pF[:, 0:N], SCt[:], Ws[:, 0:N], False, True, ["SCt", "Ws"], [pFk])
                            fb = fTb[g % 2]
                            act(fb[:, 0:N], pF[:, 0:N], AF.Copy, [pFk], [("fTb", g % 2)])
                            DS(fT_s[g, :, row0:row0 + N], fb[:, 0:N], r=[("fTb", g % 2)], w=[("fT", gi)])
                em.barrier_all()
            if stop_after == "P2b":
                break
            with ExitStack() as s4:
                gs1b = [T(s4, f"c_gs1b{r}", [128, D], F32) for r in range(2)]
                sh1b = [T(s4, f"c_sh1b{r}", [128, D], F32) for r in range(2)]
                g1b = [T(s4, f"c_g1b{r}", [128, D], F32) for r in range(2)]
                for r in range(2):
                    load_bcast(gs1b[r], 0, r, ("gs1b", r))
                    load_bcast(sh1b[r], 1, r, ("sh1b", r))
                    load_bcast(g1b[r], 2, r, ("g1b", r))
                xg = T(s4, "xg", [128, 4, D], F32)
                sqj = T(s4, "c_sqj", [128, D], F32)
                ssq = T(s4, "c_ssq", [128, 1], F32)
                h32 = T(s4, "c_h32", [128, D], F32)
                hT = T(s4, "c_hT", [128, 8, 512], BF16)
                wzz = T(s4, "wzz", [128, 8, 1024], BF16)
                wbr = T(s4, "wbr", [128, 12, D], BF16)
                wot = T(s4, "wot", [128, 8, D], BF16)
                wgl = [T(s4, f"wgl{i}", [128, 8, 3, 128], BF16) for i in range(2)]
                wsT = T(s4, "wsT", [128, 4, 128], BF16)
                bsc = T(s4, "bsc", [128, 4], F32)
                bgc = T(s4, "bgc", [128, 24], F32)
                lngb = T(s4, "lngb", [128, 512], F32)
                u_t = T(s4, "u_t", [128, 512], F32)
                gv = T(s4, "gv", [128, 512], F32)
                bst = T(s4, "bst", [128, 6], F32)
                bag = T(s4, "bag", [128, 2], F32)
                vb = T(s4, "vb", [128, 512], BF16)
                mb = T(s4, "mb", [128, 512], BF16)
                mT = T(s4, "mT", [128, 4, 512], BF16)
                aT = T(s4, "aT", [128, 4, 512], BF16)
                fT = T(s4, "fT", [128, 4, 512], BF16)
                zT = T(s4, "zT", [128, 8, 512], BF16)
                gsig = [T(s4, f"gsig{i}", [128, 512], F32) for i in range(2)]
                zacc = T(s4, "zacc", [128, 512], F32)
                ztmp = T(s4, "ztmp", [128, 512], F32)
                otmp = T(s4, "otmp", [128, 512], F32)
                xnew = [T(s4, f"xnew{i}", [128, D], F32) for i in range(2)]
                wsrc = w_in[L].rearrange("(kc p) n -> p kc n", p=128)
                DG(wzz[:, :, 0:512], wsrc[:, :, 1536:2048], w=["wzz"])
                DG(wzz[:, :, 512:1024], wsrc[:, :, 2048:2560], w=["wzz"])
                for n3 in range(3):
                    for hf in range(2):
                        DG(wbr[:, n3 * 4:(n3 + 1) * 4, hf * 512:(hf + 1) * 512],
                           w_branch[L, n3].rearrange("(wc p) d -> p wc d", p=128)[:, :, hf * 512:(hf + 1) * 512], w=["wbr"])
                for hf in range(2):
                    DG(wot[:, :, hf * 512:(hf + 1) * 512],
                       w_out[L].rearrange("(kc p) n -> p kc n", p=128)[:, :, hf * 512:(hf + 1) * 512], w=["wot"])
                DG(wsT[:], cmws_T[L], w=["wsT"])
                DS(bsc[:], cmbs_c[L], w=["bsc"])
                DS(bgc[:], bgate_c[L], w=["bgc"])
                DS(lngb[:], cm_ln_g[L:L + 1, :].partition_broadcast(128), w=["lngb"])
                wgl_i = 0
                for gi, (row0, N, is_ctx) in enumerate(groups):
                    r = 1 if is_ctx else 0
                    nj = N // 128
                    DS(aT[:, :, 0:N], aT_s[:, :, row0:row0 + N].rearrange("h p t -> p h t"), r=[("aT", gi)], w=["aTt"])
                    DS(fT[:, :, 0:N], fT_s[:, :, row0:row0 + N].rearrange("h p t -> p h t"), r=[("fT", gi)], w=["fTt"])
                    for j in range(nj):
                        ti = row0 // 128 + j
                        DS(xg[:, j, :], xs[ti * 128:(ti + 1) * 128, :], r=[XS[ti]], w=[("xg", j)])
                        norm_mod(xg[:, j, :], ("xg", j), gs1b[r][:], sh1b[r][:], [("gs1b", r), ("sh1b", r)],
                                 hT[:, :, j * 128:(j + 1) * 128], "hT", pA, ["pA0", "pA1"], sqj, ssq, h32)
                        for nt, (pb_, pk) in enumerate([(PB0, "pB0"), (PB1, "pB1")]):
                            for kc in range(8):
                                mm(pb_, hT[:, kc, j * 128:(j + 1) * 128], wzz[:, kc, nt * 512:(nt + 1) * 512],
                                   kc == 0, kc == 7, ["hT", "wzz"], [pk])
                        act(u_t[:], PB0, AF.Gelu, ["pB0"], ["u_t"])
                        act(gv[:], PB1, AF.Gelu, ["pB1"], ["gv"])
                        V(lambda e, bst=bst, gv=gv: e.bn_stats(out=bst[:], in_=gv[:]), ["gv"], ["bst"])
                        V(lambda e, bst=bst, bag=bag: e.bn_aggr(out=bag[:], in_=bst[:]), ["bst"], ["bag"])
                        act(bag[:, 1:2], bag[:, 1:2], AF.Sqrt, ["bag"], ["bag"], bias=EPS, scale=1.0)
                        recip(bag[:, 1:2], bag[:, 1:2], ["bag"], ["bag"])
                        ts(gv[:], gv[:], bag[:, 0:1], bag[:, 1:2], ALU.subtract, ALU.mult, ["gv", "bag"], ["gv"])
                        tt(vb[:], gv[:], lngb[:], ALU.mult, ["gv", "lngb"], ["vb"])
                        for g in range(4):
                            mm(PC0[:, g * 128:(g + 1) * 128], wsT[:, g, :], vb[:, g * 128:(g + 1) * 128], True, True,
                               ["wsT", "vb"], ["pC0"])
                        for g in range(4):
                            stt(mb[:, g * 128:(g + 1) * 128], PC0[:, g * 128:(g + 1) * 128], bsc[:, g:g + 1],
                                u_t[:, g * 128:(g + 1) * 128], ALU.add, ALU.mult, ["pC0", "bsc", "u_t"], ["mb"])
                        pcv = PC1.bitcast(BF16)
                        for g in range(4):
                            tr(pcv[:, g * 128:(g + 1) * 128], mb[:, g * 128:(g + 1) * 128], identb[:],
                               ["mb", "identb"], ["pC1"])
                        act(mT[:, :, j * 128:(j + 1) * 128], pcv[:, 0:512].rearrange("p (h t) -> p h t", t=128),
                            AF.Copy, ["pC1"], ["mT"])
                    brs = [(aT, "aTt"), (mT, "mT"), (fT, "fTt")]
                    for dc in range(8):
                        wb_ = wgl_i % 2
                        wgl_i += 1
                        for n3 in range(3):
                            c0 = 3072 + n3 * 1024 + dc * 128
                            DG(wgl[wb_][:, :, n3, :], wsrc[:, :, c0:c0 + 128], w=[("wgl", wb_)])
                        for n3 in range(3):
                            brT, brk = brs[n3]
                            par = (dc * 3 + n3) % 2
                            pY, pYk = (PD0, "pD0") if par == 0 else (PC0, "pC0")
                            pG, pGk = (PD1, "pD1") if par == 0 else (PC1, "pC1")
                            for wc in range(4):
                                mm(pY[:, 0:N], wbr[:, n3 * 4 + wc, dc * 128:(dc + 1) * 128], brT[:, wc, 0:N],
                                   wc == 0, wc == 3, ["wbr", brk], [pYk])
                            for kc in range(8):
                                mm(pG[:, 0:N], wgl[wb_][:, kc, n3, :], hT[:, kc, 0:N], kc == 0, kc == 7,
                                   [("wgl", wb_), "hT"], [pGk])
                            gs_ = gsig[par]
                            act(gs_[:, 0:N], pG[:, 0:N], AF.Sigmoid, [pGk, "bgc"], [("gsig", par)],
                                bias=bgc[:, n3 * 8 + dc:n3 * 8 + dc + 1])
                            if n3 == 0:
                                tt(zacc[:, 0:N], pY[:, 0:N], gs_[:, 0:N], ALU.mult, [pYk, ("gsig", par)], ["zacc"])
                            elif n3 == 1:
                                tt(ztmp[:, 0:N], pY[:, 0:N], gs_[:, 0:N], ALU.mult, [pYk, ("gsig", par)], ["ztmp"])
                                tt(zacc[:, 0:N], zacc[:, 0:N], ztmp[:, 0:N], ALU.add, ["zacc", "ztmp"], ["zacc"])
                            else:
                                tt(ztmp[:, 0:N], pY[:, 0:N], gs_[:, 0:N], ALU.mult, [pYk, ("gsig", par)], ["ztmp"])
                                tt(zT[:, dc, 0:N], zacc[:, 0:N], ztmp[:, 0:N], ALU.add, ["zacc", "ztmp"], ["zT"])
                    for j in range(nj):
                        ti = row0 // 128 + j
                        xn_ = xnew[j % 2]
                        for hf, (pb_, pk) in enumerate([(PB0, "pB0"), (PB1, "pB1")]):
                            for dc in range(8):
                                mm(pb_, zT[:, dc, j * 128:(j + 1) * 128], wot[:, dc, hf * 512:(hf + 1) * 512],
                                   dc == 0, dc == 7, ["zT", "wot"], [pk])
                            tt(otmp[:], pb_, g1b[r][:, hf * 512:(hf + 1) * 512], ALU.mult, [pk, ("g1b", r)], ["otmp"])
                            tt(xn_[:, hf * 512:(hf + 1) * 512], otmp[:], xg[:, j, hf * 512:(hf + 1) * 512], ALU.add,
                               ["otmp", ("xg", j)], [("xnew", j % 2)])
                        DG(xs[ti * 128:(ti + 1) * 128, :], xn_[:], r=[("xnew", j % 2)], w=[XS[ti]])
            em.barrier_all()
            if stop_after == "P2c":
                break
            with ExitStack() as s5:
                gs2b = T(s5, "gs2b", [128, D], F32)
                sh2b = T(s5, "sh2b", [128, D], F32)
                g2b = [T(s5, f"g2b{r}", [128, D], F32) for r in range(2)]
                nr = 1 if last else 2
                for r in range(nr):
                    load_bcast(g2b[r], 5, r, ("g2b", r))
                xt = [T(s5, f"p_xt{i}", [128, D], F32) for i in range(2)]
                ssq = T(s5, "p_ssq", [128, 1], F32)
                h32 = [T(s5, f"p_h32_{i}", [128, D], F32) for i in range(2)]
                hT2 = T(s5, "hT2", [128, 8, 128], BF16)
                wpq = T(s5, "wpq", [128, 8, 2048], BF16)
                keysT = T(s5, "keysT", [128, 16, 128], BF16)
                ss16 = T(s5, "ss16", [128, 16], F32)
                qn = T(s5, "qn", [128, 2048], BF16)
                qnT = T(s5, "qnT", [128, 16, 128], BF16)
                s_sb = T(s5, "s_sb", [128, 2048], F32)
                s2x = [T(s5, f"s2_{i}", [128, 128], F32) for i in range(2)]
                ta = T(s5, "ta", [128, 8, 16], F32)
                tb = T(s5, "tb", [128, 8, 16], F32)
                tcv = T(s5, "tcv", [128, 8, 16], F32)
                ia = T(s5, "ia", [128, 8, 16], U32)
                ib = T(s5, "ib", [128, 8, 16], U32)
                pos = T(s5, "pos", [128, 8, 16], U32)
                k1 = T(s5, "k1", [128, 8, 16], U32)
                k2 = T(s5, "k2", [128, 8, 16], U32)
                k1f = T(s5, "k1f", [128, 8, 16], F32)
                k2f = T(s5, "k2f", [128, 8, 16], F32)
                iaf = T(s5, "iaf", [128, 8, 16], F32)
                ibf = T(s5, "ibf", [128, 8, 16], F32)
                isel = T(s5, "isel", [128, 8, 16], F32)
                jsel = T(s5, "jsel", [128, 8, 16], F32)
                idxf = T(s5, "idxf", [128, 128], F32)
                idxu = [T(s5, f"idxu{i}", [128, 128], U32) for i in range(2)]
                cand = T(s5, "cand", [128, 16, 16], F32)
                cand2 = T(s5, "cand2", [128, 256], F32)
                ee = T(s5, "ee", [128, 8, 16], F32)
                zz = T(s5, "zz", [128, 8], F32)
                gw = [T(s5, f"gw{i}", [128, 128], F32) for i in range(2)]
                actv = T(s5, "actv", [128, 128], F32)
                gact = T(s5, "gact", [128, 128], F32)
                xo = T(s5, "xo", [128, D], F32)
                junk = T(s5, "junk", [128, D], F32)
                gw2 = T(s5, "gw2", [128, 128], F32)
                dgt = [T(s5, f"dgt{i}", [128, 128], BF16) for i in range(4)]
                rem = int(nc.sbuf_bytes_remaining)
                NS = min(24, (rem - 3072) // 4096)
                assert NS >= 16, f"PEER gather pipeline needs >= 16 slots, got {NS} (sbuf remaining {rem})"
                if L == 0:
                    print("PEER gather slots:", NS)
                gbuf = [T(s5, f"gbuf{i}", [128, 2 * D], BF16) for i in range(NS)]
                uvkeys = [("UVb", L, c8, uv) for c8 in range(8) for uv in range(2)]
                for q4 in range(4):
                    DG(wpq[:, :, q4 * 512:(q4 + 1) * 512],
                       peer_w_q[L].rearrange("(kc p) n -> p kc n", p=128)[:, :, q4 * 512:(q4 + 1) * 512], w=["wpq"])
                DG(keysT[:], keysT_in[L], w=["keysT"])
                tiles = list(range(2, 34)) if last else list(range(34))
                qbanks = [(PC0, "pC0"), (PC1, "pC1"), (PD0, "pD0"), (PD1, "pD1")]
                state = {"dcnt": 0, "gcnt": 0, "mod_r": None}

                def front(tix):
                    ti = tiles[tix]
                    b = tix % 2
                    r = 1 if ti < 2 else 0
                    if state["mod_r"] != r:
                        load_bcast(gs2b, 3, r, "gs2b")
                        load_bcast(sh2b, 4, r, "sh2b")
                        state["mod_r"] = r
                    xk = ("xt", b)
                    hk32 = ("h32", b)
                    h32b = h32[b]
                    act(junk[:], xt[b][:], AF.Square, [xk], ["junk", "ssq"], accum=ssq[:])
                    act(ssq[:], ssq[:], AF.Sqrt, ["ssq"], ["ssq"], bias=EPS, scale=1.0 / D)
                    yield
                    recip(ssq[:], ssq[:], ["ssq"], ["ssq"])
                    stt(h32b[:], xt[b][:], ssq[:, 0:1], gs2b[:], ALU.mult, ALU.mult, [xk, "ssq", "gs2b"], [hk32])
                    tt(h32b[:], h32b[:], sh2b[:], ALU.add, [hk32, "sh2b"], [hk32])
                    yield
                    for kc in range(8):
                        tr(pA[:, kc * 128:(kc + 1) * 128], h32b[:, kc * 128:(kc + 1) * 128], identf[:],
                           [hk32, "identf"], ["pA0", "pA1"])
                    yield
                    act(hT2[:], pA[:, 0:1024].rearrange("p (k t) -> p k t", t=128), AF.Copy, ["pA0", "pA1"], ["hT2"])
                    yield
                    for nt, (pb_, pk) in enumerate(qbanks):
                        for kc in range(8):
                            mm(pb_, hT2[:, kc, :], wpq[:, kc, nt * 512:(nt + 1) * 512], kc == 0, kc == 7,
                               ["hT2", "wpq"], [pk])
                    yield
                    for nt, (pb_, pk) in enumerate(qbanks):
                        act(s_sb[:, nt * 512:(nt + 1) * 512], pb_, AF.Square, [pk], ["s_sb"])
                    yield
                    vreduce(ss16[:], s_sb[:].rearrange("p (g d) -> p g d", d=128), ["s_sb"], ["ss16"])
                    yield
                    act(ss16[:], ss16[:], AF.Sqrt, ["ss16"], ["ss16"], bias=EPS, scale=1.0 / 128)
                    yield
                    recip(ss16[:], ss16[:], ["ss16"], ["ss16"])
                    yield
                    for hp in range(16):
                        pb_, pk = qbanks[hp // 4]
                        act(qn[:, hp * 128:(hp + 1) * 128], pb_[:, (hp % 4) * 128:(hp % 4 + 1) * 128], AF.Copy,
                            [pk, "ss16"], ["qn"], scale=ss16[:, hp:hp + 1])
                    yield
                    pav = pA[:, 0:1024].bitcast(BF16)
                    for hp in range(16):
                        tr(pav[:, hp * 128:(hp + 1) * 128], qn[:, hp * 128:(hp + 1) * 128], identb[:],
                           ["qn", "identb"], ["pA0", "pA1"])
                    yield
                    act(qnT[:].rearrange("p h t -> p (h t)"), pav[:, 0:2048], AF.Copy, ["pA0", "pA1"], ["qnT"])
                    yield
                    for hp in range(16):
                        pb_, pk = qbanks[hp // 4]
                        mm(pb_[:, (hp % 4) * 128:(hp % 4 + 1) * 128], qnT[:, hp, :], keysT[:, hp, :], True, True,
                           ["qnT", "keysT"], [pk])
                    yield
                    for nt, (pb_, pk) in enumerate(qbanks):
                        act(s_sb[:, nt * 512:(nt + 1) * 512], pb_, AF.Copy, [pk], ["s_sb"])
                    yield
                    for h in range(8):
                        sides = []
                        for side, (tv, iv) in enumerate([(ta, ia), (tb, ib)]):
                            sv = s_sb[:, (2 * h + side) * 128:(2 * h + side + 1) * 128]
                            tk, ik = ("ta", "ia") if side == 0 else ("tb", "ib")
                            sides.append((tv, iv, sv, tk, ik, s2x[side], ("s2", side)))
                        for tv, iv, sv, tk, ik, s2_, s2k in sides:
                            vmax(tv[:, h, 0:8], sv, ["s_sb"], [tk])
                        for tv, iv, sv, tk, ik, s2_, s2k in sides:
                            vmatchrep(s2_[:], tv[:, h, 0:8], sv, ["s_sb", tk], [s2k])
                        for tv, iv, sv, tk, ik, s2_, s2k in sides:
                            vmaxidx(iv[:, h, 0:8], tv[:, h, 0:8], sv, ["s_sb", tk], [ik])
                        for tv, iv, sv, tk, ik, s2_, s2k in sides:
                            vmax(tv[:, h, 8:16], s2_[:], [s2k], [tk])
                        for tv, iv, sv, tk, ik, s2_, s2k in sides:
                            vmaxidx(iv[:, h, 8:16], tv[:, h, 8:16], s2_[:], [s2k, tk], [ik])
                        tt(cand[:], ta[:, h, :].unsqueeze(2).to_broadcast([128, 16, 16]),
                           tb[:, h, :].unsqueeze(1).to_broadcast([128, 16, 16]), ALU.add, ["ta", "tb"], ["cand"])
                        cf = cand[:].rearrange("p a b -> p (a b)")
                        vmax(tcv[:, h, 0:8], cf, ["cand"], ["tcv"])
                        vmaxidx(pos[:, h, 0:8], tcv[:, h, 0:8], cf, ["cand", "tcv"], ["pos"])
                        vmatchrep(cand2[:], tcv[:, h, 0:8], cf, ["cand", "tcv"], ["cand2"])
                        vmax(tcv[:, h, 8:16], cand2[:], ["cand2"], ["tcv"])
                        vmaxidx(pos[:, h, 8:16], tcv[:, h, 8:16], cand2[:], ["cand2", "tcv"], ["pos"])
                        yield
                    vsingle(k1[:], pos[:], 4, ALU.arith_shift_right, ["pos"], ["k1"])
                    vsingle(k2[:], pos[:], 15, ALU.bitwise_and, ["pos"], ["k2"])
                    vcopy(k1f[:], k1[:], ["k1"], ["k1f"])
                    vcopy(k2f[:], k2[:], ["k2"], ["k2f"])
                    vcopy(iaf[:], ia[:], ["ia"], ["iaf"])
                    vcopy(ibf[:], ib[:], ["ib"], ["ibf"])
                    iob = io16[:].unsqueeze(1).unsqueeze(1).to_broadcast([128, 8, 16, 16])
                    eq4v = s_sb[:].rearrange("p (h a b) -> p h a b", h=8, a=16)
                    for kf, kfk, ixf, ixk, osel, osk in [(k1f, "k1f", iaf, "iaf", isel, "isel"),
                                                         (k2f, "k2f", ibf, "ibf", jsel, "jsel")]:
                        tt(eq4v, kf[:].unsqueeze(3).to_broadcast([128, 8, 16, 16]), iob, ALU.is_equal,
                           [kfk, "io16"], ["s_sb"])
                        tt(eq4v, eq4v, ixf[:].unsqueeze(2).to_broadcast([128, 8, 16, 16]), ALU.mult,
                           ["s_sb", ixk], ["s_sb"])
                        vreduce(osel[:], eq4v, ["s_sb"], [osk])
                    yield
                    stt(idxf[:], isel[:].rearrange("p h k -> p (h k)"), 128.0, jsel[:].rearrange("p h k -> p (h k)"),
                        ALU.mult, ALU.add, ["isel", "jsel"], ["idxf"])
                    if L > 0:
                        ts(idxf[:], idxf[:], float(L * NEXP), None, ALU.add, None, ["idxf"], ["idxf"])
                    vcopy(idxu[b][:], idxf[:], ["idxf"], [("idxu", b)])
                    tt(ee[:], tcv[:], tcv[:, :, 0:1].to_broadcast([128, 8, 16]), ALU.subtract, ["tcv"], ["ee"])
                    yield
                    act(ee[:], ee[:], AF.Exp, ["ee"], ["ee"])
                    yield
                    vreduce(zz[:], ee[:], ["ee"], ["zz"])
                    recip(zz[:], zz[:], ["zz"], ["zz"])
                    tt(gw[b][:].rearrange("p (h k) -> p h k", k=16), ee[:], zz[:].unsqueeze(2).to_broadcast([128, 8, 16]),
                       ALU.mult, ["ee", "zz"], [("gw", b)])

                def back(tix, nxt):
                    ti = tiles[tix]
                    b = tix % 2
                    r = 1 if ti < 2 else 0

                    def stage1(bi, mid=None):
                        for q8 in range(8):
                            hk = bi * 8 + q8
                            sl = state["gcnt"] % NS
                            state["gcnt"] += 1
                            slots[hk] = sl
                            gather(gbuf[sl][:], UVb, idxu[b][:, hk:hk + 1], [("idxu", b)] + uvkeys, [("gb", sl)])
                        for q8 in range(8):
                            hk = bi * 8 + q8
                            sl = slots[hk]
                            stt(junk[:], gbuf[sl][:, 0:D], 1.0, h32[b][:], ALU.mult, ALU.mult, [("gb", sl), ("h32", b)],
                                ["junk", ("actv", bi)], accum=actv[:, hk:hk + 1])
                            if q8 == 1 and mid is not None:
                                mid()
                        act(gact[:, bi * 8:(bi + 1) * 8], actv[:, bi * 8:(bi + 1) * 8], AF.Gelu, [("actv", bi)], [("gact", bi)])

                    def stage2(bi):
                        tt(gw2[:, bi * 8:(bi + 1) * 8], gw[b][:, bi * 8:(bi + 1) * 8], gact[:, bi * 8:(bi + 1) * 8], ALU.mult,
                           [("gw", b), ("gact", bi)], [("gw2", bi)])
                        for q8 in range(8):
                            hk = bi * 8 + q8
                            sl = slots[hk]
                            dd = state["dcnt"] % 4
                            state["dcnt"] += 1
                            act(dgt[dd][:], identb[:], AF.Copy, ["identb", ("gw2", bi)], [("dg", dd)], scale=gw2[:, hk:hk + 1])
                            mm(PB0, dgt[dd][:], gbuf[sl][:, D:D + 512], hk == 0, hk == 127, [("dg", dd), ("gb", sl)], ["pB0"])
                            mm(PB1, dgt[dd][:], gbuf[sl][:, D + 512:2 * D], hk == 0, hk == 127, [("dg", dd), ("gb", sl)], ["pB1"])

                    slots = {}
                    stage1(0)
                    for bi in range(1, 16):
                        stage1(bi, mid=lambda bi=bi: stage2(bi - 1))
                        if nxt is not None:
                            next(nxt, None)
                            next(nxt, None)
                    stage2(15)
                    if nxt is not None:
                        for _ in nxt:
                            pass
                    tt(xo[:, 0:512], PB0, g2b[r][:, 0:512], ALU.mult, ["pB0", ("g2b", r)], ["xo"])
                    tt(xo[:, 512:D], PB1, g2b[r][:, 512:D], ALU.mult, ["pB1", ("g2b", r)], ["xo"])
                    tt(xo[:], xo[:], xt[b][:], ALU.add, ["xo", ("xt", b)], ["xo"])
                    if last:
                        DS(out[(ti - 2) * 128:(ti - 1) * 128, :], xo[:], r=["xo"], w=[("out", ti)])
                    else:
                        DS(xs[ti * 128:(ti + 1) * 128, :], xo[:], r=["xo"], w=[XS[ti]])
                    if tix + 2 < len(tiles):
                        tn = tiles[tix + 2]
                        DS(xt[b][:], xs[tn * 128:(tn + 1) * 128, :], r=[XS[tn]], w=[("xt", b)])

                DS(xt[0][:], xs[tiles[0] * 128:(tiles[0] + 1) * 128, :], r=[XS[tiles[0]]], w=[("xt", 0)])
                if len(tiles) > 1:
                    DS(xt[1][:], xs[tiles[1] * 128:(tiles[1] + 1) * 128, :], r=[XS[tiles[1]]], w=[("xt", 1)])
                for _ in front(0):
                    pass
                for tix in range(len(tiles)):
                    nxt = front(tix + 1) if tix + 1 < len(tiles) else None
                    back(tix, nxt)
            em.barrier_all()

        em.finish("sync")
        semkeys = list(ENGS) + [("dma", i) for i in range(em.n_dma)]
        sems = {k: top.enter_context(nc.semaphore(f"sem{j}")) for j, k in enumerate(semkeys)}
        with nc.Block() as block:
            @block.sync
            def _(e):
                em.replay(sems, "sync", e)

            @block.scalar
            def _(e):
                em.replay(sems, "scalar", e)

            @block.vector
            def _(e):
                em.replay(sems, "vector", e)

            @block.gpsimd
            def _(e):
                em.replay(sems, "gpsimd", e)

            @block.tensor
            def _(e):
                em.replay(sems, "tensor", e)
    return nc, em


_CONST = {}


def _constants():
    if _CONST:
        return _CONST
    bf = ml_dtypes.bfloat16
    t = np.arange(SEQ, dtype=np.int64)
    m = (t[:, None] * t[None, :]) % SEQ
    ang = (2.0 * np.pi / SEQ) * m.astype(np.float64)
    dftL = np.empty((2, SEQ, SEQ), dtype=bf)
    dftL[0] = (np.cos(ang) / 64.0).astype(np.float32).astype(bf)
    dftL[1] = (-np.sin(ang) / 64.0).astype(np.float32).astype(bf)
    del ang, m
    t2 = np.arange(CTX, dtype=np.int64)
    a2 = (2.0 * np.pi / CTX) * ((t2[:, None] * t2[None, :]) % CTX).astype(np.float64)
    dft256 = np.stack([np.cos(a2) / 16.0, -np.sin(a2) / 16.0]).astype(np.float32).astype(bf)
    c = np.arange(128, dtype=np.int64)
    a3 = (2.0 * np.pi / 128) * ((c[:, None] * c[None, :]) % 128).astype(np.float64)
    s128 = 1.0 / math.sqrt(128.0)
    dftC = np.stack([np.cos(a3) * s128, np.sin(a3) * s128]).astype(np.float32).astype(bf)
    freqs = (10000.0 ** (-np.arange(0, 32, 2, dtype=np.float32) / 32.0)).astype(np.float32)
    rr = (t // 64).astype(np.float32)
    cc = (t % 64).astype(np.float32)
    ang_r = rr[:, None] * freqs[None, :]
    ang_c = cc[:, None] * freqs[None, :]
    rope = np.concatenate([np.cos(ang_r), np.cos(ang_c), np.sin(ang_r), np.sin(ang_c)], axis=1).astype(np.float32)
    _CONST.update(dftL=dftL, dft256=dft256, dftC=dftC, rope=rope, identf=np.eye(128, dtype=np.float32))
    return _CONST


def make_in_maps(inputs, depth=DEPTH, cores=NCORES):
    f = lambda a: np.ascontiguousarray(np.asarray(a, dtype=np.float32))
    cst = _constants()
    x = f(inputs["x"]); c = f(inputs["c"]); ctx = f(inputs["ctx"]); c_ctx = f(inputs["c_ctx"])
    sl = slice(0, depth)
    shared = {
        "w_ada": f(inputs["w_ada"])[sl], "b_ada": f(inputs["b_ada"])[sl],
        "norm1_g": f(inputs["norm1_g"])[sl], "norm2_g": f(inputs["norm2_g"])[sl],
        "w_in": f(inputs["w_in"])[sl],
        "bgate_c": np.ascontiguousarray(f(inputs["b_gate"])[sl].reshape(depth, 24, 128).transpose(0, 2, 1)),
        "q_norm_g": f(inputs["q_norm_g"])[sl], "k_norm_g": f(inputs["k_norm_g"])[sl],
        "lam_params": f(inputs["lam_params"])[sl].reshape(depth, 256),
        "subln_c": f(inputs["subln_g"])[sl].reshape(depth, 128, 1),
        "cm_ln_g": f(inputs["cm_ln_g"])[sl],
        "cmws_T": np.ascontiguousarray(f(inputs["cm_w_s"])[sl].transpose(0, 3, 1, 2)),
        "cmbs_c": np.ascontiguousarray(f(inputs["cm_b_s"])[sl].transpose(0, 2, 1)),
        "w_branch": f(inputs["w_branch"])[sl], "w_out": f(inputs["w_out"])[sl],
        "peer_w_q": f(inputs["peer_w_q"])[sl],
        "keysT": np.ascontiguousarray(f(inputs["peer_sub_keys"])[sl].reshape(depth, 16, 128, 128).transpose(0, 3, 1, 2)),
        "peer_u": f(inputs["peer_u"])[sl], "peer_v": f(inputs["peer_v"])[sl],
        "identf": cst["identf"], "rope": cst["rope"], "dftL": cst["dftL"], "dft256": cst["dft256"], "dftC": cst["dftC"],
    }
    maps = []
    for b in range(cores):
        cv = np.stack([c[b], c_ctx], axis=0)
        cT = np.ascontiguousarray(cv.reshape(2, 8, 128).transpose(2, 1, 0))
        mp = dict(shared)
        mp.update({"x": x[b], "ctx": ctx[b], "cT": cT})
        maps.append(mp)
    return maps


_NC = {}


def kernel(**inputs):
    if "nc" not in _NC:
        _NC["nc"] = build()[0]
    nc = _NC["nc"]
    maps = make_in_maps(inputs)
    res = run_bass_kernel_spmd(nc, maps, core_ids=list(range(NCORES)))
    outs = [np.asarray(r["out"], dtype=np.float32) for r in res.results]
    return np.stack(outs, axis=0)
```

```python
import math
from contextlib import ExitStack

import numpy as np
import ml_dtypes

import concourse.bass as bass
import concourse.mybir as mybir
from concourse.bass_utils import run_bass_kernel_spmd

F32 = mybir.dt.float32
BF16 = mybir.dt.bfloat16
U32 = mybir.dt.uint32
AF = mybir.ActivationFunctionType
ALU = mybir.AluOpType
AX = mybir.AxisListType

D = 1024
SEQ = 4096
CTX = 256
NTOK = SEQ + CTX
DEPTH = 4
NCORES = 8
EPS = 1e-6
IN_COLS = 6144
NEXP = 16384

ENGS = ["tensor", "vector", "scalar", "gpsimd", "sync"]


class Emitter:
    def __init__(self, nc, n_dma_sems=28):
        self.nc = nc
        self.lists = {e: [] for e in ENGS}
        self.cnt = {e: 0 for e in ENGS}
        self.known = {e: {} for e in ENGS}
        self.last_w = {}
        self.readers = {}
        self.n_dma = n_dma_sems
        self.dma_val = [0] * n_dma_sems
        self.dma_rr = 0
        self.n_inst = 0

    def _deps(self, reads, writes, eng=None):
        deps = {}

        def add(d, same_ok):
            if d is None:
                return
            s, v = d
            if not same_ok and s == eng:
                return
            if deps.get(s, 0) < v:
                deps[s] = v

        for k in reads:
            add(self.last_w.get(k), True)
        for k in writes:
            add(self.last_w.get(k), False)
            for r in self.readers.get(k, ()):
                add(r, False)
        return deps

    def _emit_waits(self, eng, deps):
        kn = self.known[eng]
        for s, v in deps.items():
            if eng == "tensor" and s == "tensor":
                continue
            if kn.get(s, 0) >= v:
                continue
            kn[s] = v
            self.lists[eng].append(("wait", s, v))

    def _commit(self, token, reads, writes):
        for k in reads:
            lst = self.readers.setdefault(k, [])
            lst.append(token)
            if len(lst) > 64:
                mx = {}
                for s, v in lst:
                    if mx.get(s, 0) < v:
                        mx[s] = v
                self.readers[k] = list(mx.items())
        for k in writes:
            self.last_w[k] = token
            self.readers[k] = []

    def op(self, eng, fn, reads=(), writes=()):
        deps = self._deps(reads, writes, eng)
        self._emit_waits(eng, deps)
        self.cnt[eng] += 1
        token = (eng, self.cnt[eng])
        self.lists[eng].append(("op", fn, eng, 1))
        self._commit(token, reads, writes)
        self.n_inst += 1
        return token

    def dma(self, eng, fn, reads=(), writes=()):
        deps = self._deps(reads, writes)
        i = self.dma_rr
        self.dma_rr = (self.dma_rr + 1) % self.n_dma
        s = ("dma", i)
        if self.dma_val[i] > 0:
            deps[s] = max(deps.get(s, 0), self.dma_val[i])
        self._emit_waits(eng, deps)
        self.dma_val[i] += 16
        token = (s, self.dma_val[i])
        self.lists[eng].append(("op", fn, s, 16))
        self._commit(token, reads, writes)
        self.n_inst += 1
        return token

    def _all(self):
        deps = {e: self.cnt[e] for e in ENGS if self.cnt[e] > 0}
        for i in range(self.n_dma):
            if self.dma_val[i] > 0:
                deps[("dma", i)] = self.dma_val[i]
        return deps

    def barrier_all(self):
        deps = self._all()
        for e in ENGS:
            self._emit_waits(e, dict(deps))

    def finish(self, eng="sync"):
        self._emit_waits(eng, self._all())

    def replay(self, sems, engname, engobj):
        for item in self.lists[engname]:
            if item[0] == "wait":
                engobj.wait_ge(sems[item[1]], item[2])
            else:
                _, fn, s, inc = item
                fn(engobj).then_inc(sems[s], inc)


def build(depth=DEPTH, total_depth=DEPTH, dbg=False, stop_after=None):
    nc = bass.Bass("TRN2", target_bir_lowering=False)
    em = Emitter(nc)

    def din(name, shape, dt=F32):
        return nc.dram_tensor(name, list(shape), dt, kind="ExternalInput").ap()

    x_in = din("x", [SEQ, D])
    ctx_in = din("ctx", [CTX, D])
    cT_in = din("cT", [128, 8, 2])
    w_ada = din("w_ada", [depth, D, 6 * D])
    b_ada = din("b_ada", [depth, 6 * D])
    norm1_g = din("norm1_g", [depth, D])
    norm2_g = din("norm2_g", [depth, D])
    w_in = din("w_in", [depth, D, IN_COLS])
    bgate_c = din("bgate_c", [depth, 128, 24])
    q_norm_g = din("q_norm_g", [depth, 64])
    k_norm_g = din("k_norm_g", [depth, 64])
    lam_params = din("lam_params", [depth, 256])
    subln_c = din("subln_c", [depth, 128, 1])
    cm_ln_g = din("cm_ln_g", [depth, 512])
    cmws_T = din("cmws_T", [depth, 128, 4, 128])
    cmbs_c = din("cmbs_c", [depth, 128, 4])
    w_branch = din("w_branch", [depth, 3, 512, D])
    w_out = din("w_out", [depth, D, D])
    peer_w_q = din("peer_w_q", [depth, D, 2048])
    keysT_in = din("keysT", [depth, 128, 16, 128])
    peer_u = din("peer_u", [depth, NEXP, D])
    peer_v = din("peer_v", [depth, NEXP, D])
    peer_u_flat = peer_u.rearrange("l e d -> (l e) d")
    peer_v_flat = peer_v.rearrange("l e d -> (l e) d")
    identf_in = din("identf", [128, 128])
    rope_in = din("rope", [SEQ, 64])
    dftL = din("dftL", [2, SEQ, SEQ], BF16)
    dft256 = din("dft256", [2, CTX, CTX], BF16)
    dftC = din("dftC", [2, 128, 128], BF16)

    out = nc.dram_tensor("out", [SEQ, D], F32, kind="ExternalOutput").ap()
    skind = "ExternalOutput" if dbg else "Internal"
    xs = nc.dram_tensor("xs", [NTOK, D], F32, kind=skind).ap()
    der = nc.dram_tensor("der", [2, 6, D], F32, kind=skind).ap()
    aT_s = nc.dram_tensor("aT_s", [4, 128, NTOK], BF16, kind=skind).ap()
    fT_s = nc.dram_tensor("fT_s", [4, 128, NTOK], BF16, kind=skind).ap()
    UVb = nc.dram_tensor("UVb", [depth * NEXP, 2 * D], BF16, kind="Internal").ap()

    def V(fn, r=(), w=()):
        return em.op("vector", fn, r, w)

    def A(fn, r=(), w=()):
        return em.op("scalar", fn, r, w)

    def PE(fn, r=(), w=()):
        return em.op("tensor", fn, r, w)

    def G(fn, r=(), w=()):
        return em.op("gpsimd", fn, r, w)

    def DS(out_, in_, r=(), w=()):
        return em.dma("sync", lambda e: e.dma_start(out=out_, in_=in_), r, w)

    def DG(out_, in_, r=(), w=()):
        return em.dma("gpsimd", lambda e: e.dma_start(out=out_, in_=in_), r, w)

    def tt(out_, a, b, op, r, w, eng="vector"):
        return em.op(eng, lambda e: e.tensor_tensor(out=out_, in0=a, in1=b, op=op), r, w)

    def stt(out_, a, s, b, op0, op1, r, w, accum=None):
        return V(lambda e: e.scalar_tensor_tensor(out=out_, in0=a, scalar=s, in1=b, op0=op0, op1=op1,
                                                  accum_out=accum), r, w)

    def ts(out_, a, s1, s2, op0, op1, r, w):
        if s2 is None:
            return V(lambda e: e.tensor_scalar(out=out_, in0=a, scalar1=s1, scalar2=None, op0=op0), r, w)
        return V(lambda e: e.tensor_scalar(out=out_, in0=a, scalar1=s1, scalar2=s2, op0=op0, op1=op1), r, w)

    def act(out_, in_, func, r, w, bias=None, scale=None, accum=None):
        kw = {}
        if bias is not None:
            kw["bias"] = bias
        if scale is not None:
            kw["scale"] = scale
        if accum is not None:
            kw["accum_out"] = accum
        return A(lambda e: e.activation(out=out_, in_=in_, func=func, **kw), r, w)

    def vcopy(out_, in_, r, w):
        return V(lambda e: e.tensor_copy(out=out_, in_=in_), r, w)

    def mm(out_, lhsT, rhs, start, stop, r, w):
        return PE(lambda e: e.matmul(out_, lhsT=lhsT, rhs=rhs, start=start, stop=stop), r, w)

    def tr(out_, in_, ident, r, w):
        return PE(lambda e: e.transpose(out_, in_, ident), r, w)

    def recip(out_, in_, r, w):
        return V(lambda e: e.reciprocal(out=out_, in_=in_), r, w)

    def vmax(out_, in_, r, w):
        return V(lambda e: e.max(out=out_, in_=in_), r, w)

    def vmaxidx(out_, inmax, invals, r, w):
        return V(lambda e: e.max_index(out=out_, in_max=inmax, in_values=invals), r, w)

    def vmatchrep(out_, rep, vals, r, w):
        return V(lambda e: e.match_replace(out=out_, in_to_replace=rep, in_values=vals, imm_value=-1e30), r, w)

    def vreduce(out_, in_, r, w):
        return V(lambda e: e.tensor_reduce(out=out_, in_=in_, axis=AX.X, op=ALU.add), r, w)

    def vsingle(out_, in_, scalar, op, r, w):
        return V(lambda e: e.tensor_single_scalar(out=out_, in_=in_, scalar=scalar, op=op), r, w)

    def rstd_op(t_ap, key, scale, mode):
        if mode == "ln":
            act(t_ap, t_ap, AF.Ln, [key], [key], bias=EPS, scale=scale)
            act(t_ap, t_ap, AF.Exp, [key], [key], scale=-0.5)
        else:
            act(t_ap, t_ap, AF.Sqrt, [key], [key], bias=EPS, scale=scale)
            recip(t_ap, t_ap, [key], [key])

    def gather(out_, table, idx_ap, r, w):
        return em.dma("gpsimd", lambda e: e.indirect_dma_start(
            out=out_, out_offset=None, in_=table,
            in_offset=bass.IndirectOffsetOnAxis(ap=idx_ap, axis=0)), r, w)

    with ExitStack() as top:
        _tcnt = [0]

        def T(es, name, shape, dt):
            _tcnt[0] += 1
            return es.enter_context(nc.sbuf_tensor(f"t{_tcnt[0]}_{name}", list(shape), dt))

        pA = top.enter_context(nc.psum_tensor("pA", [128, 1024], F32))
        pB = top.enter_context(nc.psum_tensor("pB", [128, 1024], F32))
        pC = top.enter_context(nc.psum_tensor("pC", [128, 1024], F32))
        pD = top.enter_context(nc.psum_tensor("pD", [128, 1024], F32))
        PA0, PA1 = pA[:, 0:512], pA[:, 512:1024]
        PB0, PB1 = pB[:, 0:512], pB[:, 512:1024]
        PC0, PC1 = pC[:, 0:512], pC[:, 512:1024]
        PD0, PD1 = pD[:, 0:512], pD[:, 512:1024]

        identf = T(top, "identf", [128, 128], F32)
        identb = T(top, "identb", [128, 128], BF16)
        onesf = T(top, "onesf", [128, 128], F32)
        onesb = T(top, "onesb", [128, 128], BF16)
        ropet = T(top, "ropet", [128, 32, 64], F32)
        io16 = T(top, "io16", [128, 16], F32)
        cact = T(top, "cact", [128, 8, 2], F32)
        CCt = T(top, "CCt", [128, 128], BF16)
        SCt = T(top, "SCt", [128, 128], BF16)
        neglam = T(top, "neglam", [128, 1], F32)
        sgcol = T(top, "sgcol", [128, 1], F32)
        lamt = T(top, "lamt", [128, 2], F32)
        lpb = T(top, "lpb", [128, 256], F32)
        lj = T(top, "lj", [128, 64], F32)

        DS(identf[:], identf_in, w=["identf"])
        vcopy(identb[:], identf[:], ["identf"], ["identb"])
        V(lambda e: e.memset(onesf[:], 1.0), w=["onesf"])
        V(lambda e: e.memset(onesb[:], 1.0), w=["onesb"])
        DS(ropet[:], rope_in.rearrange("(t p) c -> p t c", p=128), w=["ropet"])
        G(lambda e: e.iota(io16[:], pattern=[[1, 16]], base=0, channel_multiplier=0,
                           allow_small_or_imprecise_dtypes=True), w=["io16"])
        DS(cact[:], cT_in, w=["cact"])
        act(cact[:], cact[:], AF.Silu, ["cact"], ["cact"])
        DS(CCt[:], dftC[0], w=["CCt"])
        DS(SCt[:], dftC[1], w=["SCt"])
        XS = [("xs", i) for i in range(34)]
        DS(xs[0:CTX, :], ctx_in, w=XS[0:2])
        for q4 in range(4):
            DS(xs[CTX + q4 * 1024:CTX + (q4 + 1) * 1024, :], x_in[q4 * 1024:(q4 + 1) * 1024, :],
               w=XS[2 + q4 * 8:2 + (q4 + 1) * 8])

        def norm_mod(xtile, xkey, gsb, shb, mkeys, hT_out, hT_key, tp, tpkeys, sqj, ssq, h32,
                     h32key="h32", sqkey="sqj", rs="sqrt"):
            act(sqj[:], xtile, AF.Square, [xkey], [sqkey, "ssq"], accum=ssq[:])
            rstd_op(ssq[:], "ssq", 1.0 / D, rs)
            stt(h32[:], xtile, ssq[:, 0:1], gsb, ALU.mult, ALU.mult, [xkey, "ssq", mkeys[0]], [h32key])
            tt(h32[:], h32[:], shb, ALU.add, [h32key, mkeys[1]], [h32key])
            if isinstance(tp, list):
                for hf in range(2):
                    for k4 in range(4):
                        kc = hf * 4 + k4
                        tr(tp[hf][:, k4 * 128:(k4 + 1) * 128], h32[:, kc * 128:(kc + 1) * 128], identf[:],
                           [h32key, "identf"], [tpkeys[hf]])
                    act(hT_out[:, hf * 4:(hf + 1) * 4, :], tp[hf].rearrange("p (k t) -> p k t", t=128), AF.Copy,
                        [tpkeys[hf]], [hT_key])
                return
            for kc in range(8):
                tr(tp[:, kc * 128:(kc + 1) * 128], h32[:, kc * 128:(kc + 1) * 128], identf[:],
                   [h32key, "identf"], tpkeys)
            act(hT_out, tp[:, 0:1024].rearrange("p (k t) -> p k t", t=128), AF.Copy, tpkeys, [hT_key])

        def group_norm_rope(ps, pskey, gb, gbkey, nsq, ss8, kn, ra, rb_, outb, outkey, rope_tile, rs="sqrt"):
            act(nsq[:], ps, AF.Square, [pskey], ["nsq"])
            vreduce(ss8[:], nsq[:].rearrange("p (g d) -> p g d", d=64), ["nsq"], ["ss8"])
            rstd_op(ss8[:], "ss8", 1.0 / 64, rs)
            kn3 = kn[:].rearrange("p (g d) -> p g d", d=64)
            tt(kn3, ps.rearrange("p (g d) -> p g d", d=64), ss8[:].unsqueeze(2).to_broadcast([128, 8, 64]),
               ALU.mult, [pskey, "ss8"], ["kn"])
            if rope_tile is None:
                tt(outb[:].rearrange("p (g d) -> p g d", d=64), kn3, gb[:].unsqueeze(1).to_broadcast([128, 8, 64]),
                   ALU.mult, ["kn", gbkey], [outkey])
                return
            tt(kn3, kn3, gb[:].unsqueeze(1).to_broadcast([128, 8, 64]), ALU.mult, ["kn", gbkey], ["kn"])
            kn5 = kn[:].rearrange("p (g a h f) -> p g a h f", g=8, a=2, h=2, f=16)
            ob5 = outb[:].rearrange("p (g a h f) -> p g a h f", g=8, a=2, h=2, f=16)
            x1, x2 = kn5[:, :, :, 0, :], kn5[:, :, :, 1, :]
            cosb = ropet[:, rope_tile, 0:32].rearrange("p (a f) -> p a f", a=2).unsqueeze(1).to_broadcast([128, 8, 2, 16])
            sinb = ropet[:, rope_tile, 32:64].rearrange("p (a f) -> p a f", a=2).unsqueeze(1).to_broadcast([128, 8, 2, 16])
            ra4 = ra[:].rearrange("p (g a f) -> p g a f", g=8, a=2)
            rb4 = rb_[:].rearrange("p (g a f) -> p g a f", g=8, a=2)
            tt(ra4, x1, cosb, ALU.mult, ["kn", "ropet"], ["ra"])
            tt(rb4, x2, sinb, ALU.mult, ["kn", "ropet"], ["rb"])
            tt(ob5[:, :, :, 0, :], ra4, rb4, ALU.subtract, ["ra", "rb"], [outkey])
            tt(ra4, x2, cosb, ALU.mult, ["kn", "ropet"], ["ra"])
            tt(rb4, x1, sinb, ALU.mult, ["kn", "ropet"], ["rb"])
            tt(ob5[:, :, :, 1, :], ra4, rb4, ALU.add, ["ra", "rb"], [outkey])

        def load_bcast(tile, j, r, key):
            DS(tile[:], der[r, j:j + 1, :].partition_broadcast(128), r=["der"], w=[key])

        for L in range(depth):
            last = (L == total_depth - 1)
            lam_init = 0.8 - 0.6 * math.exp(-0.3 * L)
            groups = []
            if not last:
                groups.append((0, 256, True))
            for g8 in range(8):
                groups.append((CTX + g8 * 512, 512, False))

            em.barrier_all()
            with ExitStack() as s0:
                wada = [T(s0, f"wada{i}", [128, 8, 512], F32) for i in range(2)]
                badat = [T(s0, f"badat{i}", [2, 512], F32) for i in range(2)]
                gch = [T(s0, f"gch{i}", [2, 512], F32) for i in range(2)]
                modrow = [T(s0, f"modrow{i}", [2, 512], F32) for i in range(2)]
                jmap = {0: 1, 1: 0, 2: 2, 3: 4, 4: 3, 5: 5}
                for nt in range(12):
                    b = nt % 2
                    part, half = nt // 2, nt % 2
                    DS(wada[b][:], w_ada[L][:, nt * 512:(nt + 1) * 512].rearrange("(kc p) n -> p kc n", p=128),
                       w=[("wada", b)])
                    DS(badat[b][:], b_ada[L:L + 1, nt * 512:(nt + 1) * 512].partition_broadcast(2), w=[("badat", b)])
                    for kc in range(8):
                        mm(PA0[0:2, :], cact[:, kc, :], wada[b][:, kc, :], kc == 0, kc == 7,
                           [("wada", b), "cact"], ["pA0"])
                    tt(modrow[b][:], PA0[0:2, :], badat[b][:], ALU.add, ["pA0", ("badat", b)], [("modrow", b)])
                    if part in (1, 4):
                        ng = norm1_g if part == 1 else norm2_g
                        DS(gch[b][:], ng[L:L + 1, half * 512:(half + 1) * 512].partition_broadcast(2), w=[("gch", b)])
                        stt(modrow[b][:], modrow[b][:], 1.0, gch[b][:], ALU.add, ALU.mult,
                            [("modrow", b), ("gch", b)], [("modrow", b)])
                    DS(der[:, jmap[part], half * 512:(half + 1) * 512], modrow[b][:], r=[("modrow", b)], w=["der"])
                DS(lpb[:], lam_params[L:L + 1, :].partition_broadcast(128), w=["lpb"])
                stt(lj[:], lpb[:, 0:64], 1.0, lpb[:, 64:128], ALU.mult, ALU.mult, ["lpb"], ["lj", "lamt"], accum=lamt[:, 0:1])
                stt(lj[:], lpb[:, 128:192], 1.0, lpb[:, 192:256], ALU.mult, ALU.mult, ["lpb"], ["lj", "lamt"], accum=lamt[:, 1:2])
                act(lamt[:], lamt[:], AF.Exp, ["lamt"], ["lamt"])
                tt(neglam[:], lamt[:, 1:2], lamt[:, 0:1], ALU.subtract, ["lamt"], ["neglam"])
                ts(neglam[:], neglam[:], -lam_init, None, ALU.add, None, ["neglam"], ["neglam"])
                DS(sgcol[:], subln_c[L], w=["sgcol"])
                ts(sgcol[:], sgcol[:], 1.0 - lam_init, None, ALU.mult, None, ["sgcol"], ["sgcol"])
            em.barrier_all()
            if stop_after == "P0":
                break

            with ExitStack() as sZ:
                Zs = T(sZ, "Zs", [128, 34, 512], BF16)
                gs1b = [T(sZ, f"gs1b{r}", [128, D], F32) for r in range(2)]
                sh1b = [T(sZ, f"sh1b{r}", [128, D], F32) for r in range(2)]
                for r in range(2):
                    load_bcast(gs1b[r], 0, r, ("gs1b", r))
                    load_bcast(sh1b[r], 1, r, ("sh1b", r))
                with ExitStack() as sKV:
                    KT = T(sKV, "KT", [128, 4, NTOK], BF16)
                    Vs = T(sKV, "Vs", [128, 34, 512], BF16)
                    xt = [T(sKV, f"xt{i}", [128, D], F32) for i in range(2)]
                    sqj = T(sKV, "sqj", [128, D], F32)
                    ssq = T(sKV, "ssq", [128, 1], F32)
                    h32 = T(sKV, "h32", [128, D], F32)
                    nsq = T(sKV, "nsq", [128, 512], F32)
                    ss8 = T(sKV, "ss8", [128, 8], F32)
                    kn = T(sKV, "kn", [128, 512], F32)
                    ra = T(sKV, "ra", [128, 256], F32)
                    rb_ = T(sKV, "rb", [128, 256], F32)
                    kb = T(sKV, "kb", [128, 512], BF16)
                    kgb = T(sKV, "kgb", [128, 64], F32)
                    qgb = T(sKV, "qgb", [128, 64], F32)
                    DS(kgb[:], k_norm_g[L:L + 1, :].partition_broadcast(128), w=["kgb"])
                    DS(qgb[:], q_norm_g[L:L + 1, :].partition_broadcast(128), w=["qgb"])
                    ts(qgb[:], qgb[:], 0.125, None, ALU.mult, None, ["qgb"], ["qgb"])
                    with ExitStack() as s1:
                        w1 = T(s1, "w1", [128, 8, 1536], BF16)
                        hT1 = [T(s1, f"hT1_{i}", [128, 8, 128], BF16) for i in range(2)]
                        wsrc = w_in[L].rearrange("(kc p) n -> p kc n", p=128)
                        DG(w1[:, :, 0:512], wsrc[:, :, 512:1024], w=["w1"])
                        DG(w1[:, :, 512:1024], wsrc[:, :, 1024:1536], w=["w1"])
                        DG(w1[:, :, 1024:1536], wsrc[:, :, 2560:3072], w=["w1"])
                        DS(xt[0][:], xs[0:128, :], r=[XS[0]], w=[("xt", 0)])
                        for i in range(34):
                            b = i % 2
                            r = 1 if i < 2 else 0
                            if i + 1 < 34:
                                DS(xt[1 - b][:], xs[(i + 1) * 128:(i + 2) * 128, :], r=[XS[i + 1]], w=[("xt", 1 - b)])
                            tp, tpk = (pA, ["pA0", "pA1"]) if b == 0 else (pD, ["pD0", "pD1"])
                            norm_mod(xt[b][:], ("xt", b), gs1b[r][:], sh1b[r][:], [("gs1b", r), ("sh1b", r)],
                                     hT1[b][:], ("hT1", b), tp, tpk, sqj, ssq, h32)
                            for nt, (pb_, pk) in enumerate([(PB0, "pB0"), (PB1, "pB1"), (PC0, "pC0")]):
                                for kc in range(8):
                                    mm(pb_, hT1[b][:, kc, :], w1[:, kc, nt * 512:(nt + 1) * 512], kc == 0, kc == 7,
                                       [("hT1", b), "w1"], [pk])
                            group_norm_rope(PB0, "pB0", kgb, "kgb", nsq, ss8, kn, ra, rb_, kb, "kb",
                                            None if i < 2 else i - 2)
                            pcv = PC1.bitcast(BF16)
                            for h in range(4):
                                tr(pcv[:, h * 128:(h + 1) * 128], kb[:, h * 128:(h + 1) * 128], identb[:],
                                   ["kb", "identb"], ["pC1"])
                            act(KT[:, :, i * 128:(i + 1) * 128], pcv[:, 0:512].rearrange("p (h t) -> p h t", t=128),
                                AF.Copy, ["pC1"], [("KT", i)])
                            act(Vs[:, i, :], PB1, AF.Copy, ["pB1"], [("Vs", i)])
                            vcopy(Zs[:, i, :], PC0, ["pC0"], [("Zs", i)])
                    em.barrier_all()
                    if stop_after == "P1":
                        break
                    with ExitStack() as s2:
                        wq = T(s2, "wq", [128, 8, 512], BF16)
                        hT = T(s2, "hT", [128, 8, 128], BF16)
                        QT = [T(s2, f"QT{i}", [128, 4, 512], BF16) for i in range(2)]
                        PTp = [T(s2, f"PTp{i}", [128, 1024], BF16) for i in range(3)]
                        zacc = [T(s2, f"zacc{m}", [128, 512], BF16) for m in range(2)]
                        rz = T(s2, "rz", [128, 512], F32)
                        O0 = T(s2, "O0", [128, 512], F32)
                        O1 = T(s2, "O1", [128, 512], F32)
                        att = T(s2, "att", [128, 512], F32)
                        asq = T(s2, "asq", [128, 512], F32)
                        rst = T(s2, "rst", [128, 512], F32)
                        aTb = [T(s2, f"aTb{i}", [128, 512], BF16) for i in range(2)]
                        DG(wq[:], w_in[L].rearrange("(kc p) n -> p kc n", p=128)[:, :, 0:512], w=["wq"])
                        cv_list = []
                        for c8 in range(8):
                            e0 = c8 * 2048
                            cv_list.append((UVb[L * NEXP + e0:L * NEXP + e0 + 2048, 0:D], peer_u[L][e0:e0 + 2048, :], ("UVb", L, c8, 0)))
                            cv_list.append((UVb[L * NEXP + e0:L * NEXP + e0 + 2048, D:2 * D], peer_v[L][e0:e0 + 2048, :], ("UVb", L, c8, 1)))
                        cv_state = {"i": 0}

                        def prep(gi):
                            row0, N, is_ctx = groups[gi]
                            r = 1 if is_ctx else 0
                            qt = QT[gi % 2]
                            qk = ("QT", gi % 2)
                            for j in range(N // 128):
                                ti = row0 // 128 + j
                                DS(xt[0][:], xs[ti * 128:(ti + 1) * 128, :], r=[XS[ti]], w=[("xt", 0)])
                                norm_mod(xt[0][:], ("xt", 0), gs1b[r][:], sh1b[r][:], [("gs1b", r), ("sh1b", r)],
                                         hT[:], "hT", [PB1, PC1], ["pB1", "pC1"], sqj, ssq, h32, rs="ln")
                                yield
                                for kc in range(8):
                                    mm(PB1, hT[:, kc, :], wq[:, kc, :], kc == 0, kc == 7, ["hT", "wq"], ["pB1"])
                                yield
                                group_norm_rope(PB1, "pB1", qgb, "qgb", nsq, ss8, kn, ra, rb_, kb, "kb",
                                                None if is_ctx else ti - 2, rs="ln")
                                yield
                                pcv = PC1.bitcast(BF16)
                                for h in range(4):
                                    tr(pcv[:, h * 128:(h + 1) * 128], kb[:, h * 128:(h + 1) * 128], identb[:],
                                       ["kb", "identb"], ["pC1"])
                                act(qt[:, :, j * 128:(j + 1) * 128], pcv[:, 0:512].rearrange("p (h t) -> p h t", t=128),
                                    AF.Copy, ["pC1"], [qk])
                                yield

                        def attend(gi, nxt):
                            row0, N, is_ctx = groups[gi]
                            qt = QT[gi % 2]
                            qk = ("QT", gi % 2)
                            if not is_ctx:
                                for _ in range(2):
                                    o_, i_, k_ = cv_list[cv_state["i"]]
                                    DG(o_, i_, w=[k_, ("cv", cv_state["i"] % 4)])
                                    cv_state["i"] += 1
                            kchunks = [0, 1] if is_ctx else list(range(34))
                            nch = len(kchunks)
                            sb = [(pD, ["pD0", "pD1"]), (pA, ["pA0", "pA1"])]
                            obank = [(PB0, "pB0"), (PC0, "pC0")]
                            step = 0
                            for h in range(4):
                                def s_mm(ci):
                                    kc = kchunks[ci]
                                    pS, pSk = sb[ci % 2]
                                    for m in range(2):
                                        lo, hi = m * 64, (m + 1) * 64
                                        mm(pS[:, m * 512:m * 512 + N], KT[lo:hi, h, kc * 128:(kc + 1) * 128], qt[lo:hi, h, 0:N],
                                           True, True, [("KT", kc), qk], [pSk[m]])

                                s_mm(0)
                                for ci, kc in enumerate(kchunks):
                                    first, lastc = ci == 0, ci == nch - 1
                                    if ci + 1 < nch:
                                        s_mm(ci + 1)
                                    pS, pSk = sb[ci % 2]
                                    pt = PTp[ci % 3]
                                    ptk = ("PT", ci % 3)
                                    act(pt[:].rearrange("p (m n) -> p m n", m=2)[:, :, 0:N],
                                        pS[:, 0:1024].rearrange("p (m n) -> p m n", m=2)[:, :, 0:N], AF.Exp, pSk, [ptk])
                                    for m in range(2):
                                        pO, pOk = obank[m]
                                        mm(pO[:, 0:N], Vs[:, kc, h * 128:(h + 1) * 128], pt[:, m * 512:m * 512 + N], first, lastc,
                                           [("Vs", kc), ptk], [pOk])
                                    for m in range(2):
                                        eng = "vector"
                                        if first:
                                            em.op(eng, lambda e, o_=zacc[m][:, 0:N], i_=pt[:, m * 512:m * 512 + N]: e.tensor_copy(out=o_, in_=i_),
                                                  [ptk], [("zacc", m)])
                                        else:
                                            tt(zacc[m][:, 0:N], zacc[m][:, 0:N], pt[:, m * 512:m * 512 + N], ALU.add,
                                               [("zacc", m), ptk], [("zacc", m)], eng=eng)
                                    step += 1
                                    if nxt is not None and step % 4 == 0:
                                        next(nxt, None)
                                mm(PA0[:, 0:N], onesb[:], zacc[0][:, 0:N], True, True, ["onesb", ("zacc", 0)], ["pA0"])
                                mm(PA1[:, 0:N], onesb[:], zacc[1][:, 0:N], True, True, ["onesb", ("zacc", 1)], ["pA1"])
                                recip(rz[:, 0:N], PA0[:, 0:N], ["pA0"], ["rz"])
                                tt(O0[:, 0:N], PB0[:, 0:N], rz[:, 0:N], ALU.mult, ["pB0", "rz"], ["O0"])
                                recip(rz[:, 0:N], PA1[:, 0:N], ["pA1"], ["rz"])
                                tt(O1[:, 0:N], PC0[:, 0:N], rz[:, 0:N], ALU.mult, ["pC0", "rz"], ["O1"])
                                stt(att[:, 0:N], O1[:, 0:N], neglam[:, 0:1], O0[:, 0:N], ALU.mult, ALU.add,
                                    ["O0", "O1", "neglam"], ["att"])
                                act(asq[:, 0:N], att[:, 0:N], AF.Square, ["att"], ["asq"])
                                mm(PA0[:, 0:N], onesf[:], asq[:, 0:N], True, True, ["onesf", "asq"], ["pA0"])
                                act(rst[:, 0:N], PA0[:, 0:N], AF.Ln, ["pA0"], ["rst"], bias=EPS, scale=1.0 / 128)
                                act(rst[:, 0:N], rst[:, 0:N], AF.Exp, ["rst"], ["rst"], scale=-0.5)
                                ab = aTb[h % 2]
                                stt(ab[:, 0:N], att[:, 0:N], sgcol[:, 0:1], rst[:, 0:N], ALU.mult, ALU.mult,
                                    ["att", "sgcol", "rst"], [("aTb", h % 2)])
                                DS(aT_s[h, :, row0:row0 + N], ab[:, 0:N], r=[("aTb", h % 2)], w=[("aT", gi)])
                            if nxt is not None:
                                for _ in nxt:
                                    pass

                        for _ in prep(0):
                            pass
                        for gi in range(len(groups)):
                            attend(gi, prep(gi + 1) if gi + 1 < len(groups) else None)
                    em.barrier_all()
                if stop_after == "P2a":
                    break
                with ExitStack() as s3:
                    tabC = [T(s3, f"tabC{i}", [128, 32, 512], BF16) for i in range(2)]
                    tabS = [T(s3, f"tabS{i}", [128, 32, 512], BF16) for i in range(2)]
                    Wc = T(s3, "Wc", [128, 512], BF16)
                    Ws = T(s3, "Ws", [128, 512], BF16)
                    fTb = [T(s3, f"fTb{i}", [128, 512], BF16) for i in range(2)]

                    def load_tabs(gi):
                        row0, N, is_ctx = groups[gi]
                        tb_ = gi % 2
                        if is_ctx:
                            DS(tabC[tb_][:, 0:2, 0:256], dft256[0].rearrange("(tc p) n -> p tc n", p=128), w=[("tabC", tb_)])
                            DS(tabS[tb_][:, 0:2, 0:256], dft256[1].rearrange("(tc p) n -> p tc n", p=128), w=[("tabS", tb_)])
                        else:
                            t0 = row0 - CTX
                            for q4 in range(4):
                                DS(tabC[tb_][:, q4 * 8:(q4 + 1) * 8, :],
                                   dftL[0][q4 * 1024:(q4 + 1) * 1024, t0:t0 + 512].rearrange("(tc p) n -> p tc n", p=128),
                                   w=[("tabC", tb_)])
                                DS(tabS[tb_][:, q4 * 8:(q4 + 1) * 8, :],
                                   dftL[1][q4 * 1024:(q4 + 1) * 1024, t0:t0 + 512].rearrange("(tc p) n -> p tc n", p=128),
                                   w=[("tabS", tb_)])

                    load_tabs(0)
                    for gi, (row0, N, is_ctx) in enumerate(groups):
                        tb_ = gi % 2
                        if gi + 1 < len(groups):
                            load_tabs(gi + 1)
                        ntc, z0 = (2, 0) if is_ctx else (32, 2)
                        for g in range(4):
                            for tcx in range(ntc):
                                mm(PA0[:, 0:N], Zs[:, z0 + tcx, g * 128:(g + 1) * 128], tabC[tb_][:, tcx, 0:N],
                                   tcx == 0, tcx == ntc - 1, [("Zs", z0 + tcx), ("tabC", tb_)], ["pA0"])
                            for tcx in range(ntc):
                                mm(PA1[:, 0:N], Zs[:, z0 + tcx, g * 128:(g + 1) * 128], tabS[tb_][:, tcx, 0:N],
                                   tcx == 0, tcx == ntc - 1, [("Zs", z0 + tcx), ("tabS", tb_)], ["pA1"])
                            act(Wc[:, 0:N], PA0[:, 0:N], AF.Copy, ["pA0"], ["Wc"])
                            vcopy(Ws[:, 0:N], PA1[:, 0:N], ["pA1"], ["Ws"])
                            pF, pFk = (PB0, "pB0") if g % 2 == 0 else (PB1, "pB1")
                            mm(pF[:, 0:N], CCt[:], Wc[:, 0:N], True, False, ["CCt", "Wc"], [pFk])
                            mm(pF[:, 0:N], SCt[:], Ws[:, 0:N], False, True, ["SCt", "Ws"], [pFk])
                            fb = fTb[g % 2]
                            act(fb[:, 0:N], pF[:, 0:N], AF.Copy, [pFk], [("fTb", g % 2)])
                            DS(fT_s[g, :, row0:row0 + N], fb[:, 0:N], r=[("fTb", g % 2)], w=[("fT", gi)])
                em.barrier_all()
            if stop_after == "P2b":
                break
            with ExitStack() as s4:
                gs1b = [T(s4, f"c_gs1b{r}", [128, D], F32) for r in range(2)]
                sh1b = [T(s4, f"c_sh1b{r}", [128, D], F32) for r in range(2)]
                g1b = [T(s4, f"c_g1b{r}", [128, D], F32) for r in range(2)]
                for r in range(2):
                    load_bcast(gs1b[r], 0, r, ("gs1b", r))
                    load_bcast(sh1b[r], 1, r, ("sh1b", r))
                    load_bcast(g1b[r], 2, r, ("g1b", r))
                xg = T(s4, "xg", [128, 4, D], F32)
                sqj = T(s4, "c_sqj", [128, D], F32)
                ssq = T(s4, "c_ssq", [128, 1], F32)
                h32 = T(s4, "c_h32", [128, D], F32)
                hT = T(s4, "c_hT", [128, 8, 512], BF16)
                wzz = T(s4, "wzz", [128, 8, 1024], BF16)
                wbr = T(s4, "wbr", [128, 12, D], BF16)
                wot = T(s4, "wot", [128, 8, D], BF16)
                wgl = [T(s4, f"wgl{i}", [128, 8, 3, 128], BF16) for i in range(2)]
                wsT = T(s4, "wsT", [128, 4, 128], BF16)
                bsc = T(s4, "bsc", [128, 4], F32)
                bgc = T(s4, "bgc", [128, 24], F32)
                lngb = T(s4, "lngb", [128, 512], F32)
                u_t = T(s4, "u_t", [128, 512], F32)
                gv = T(s4, "gv", [128, 512], F32)
                bst = T(s4, "bst", [128, 6], F32)
                bag = T(s4, "bag", [128, 2], F32)
                vb = T(s4, "vb", [128, 512], BF16)
                mb = T(s4, "mb", [128, 512], BF16)
                mT = T(s4, "mT", [128, 4, 512], BF16)
                aT = T(s4, "aT", [128, 4, 512], BF16)
                fT = T(s4, "fT", [128, 4, 512], BF16)
                zT = T(s4, "zT", [128, 8, 512], BF16)
                gsig = [T(s4, f"gsig{i}", [128, 512], F32) for i in range(2)]
                zacc = T(s4, "zacc", [128, 512], F32)
                ztmp = T(s4, "ztmp", [128, 512], F32)
                otmp = T(s4, "otmp", [128, 512], F32)
                xnew = [T(s4, f"xnew{i}", [128, D], F32) for i in range(2)]
                wsrc = w_in[L].rearrange("(kc p) n -> p kc n", p=128)
                DG(wzz[:, :, 0:512], wsrc[:, :, 1536:2048], w=["wzz"])
                DG(wzz[:, :, 512:1024], wsrc[:, :, 2048:2560], w=["wzz"])
                for n3 in range(3):
                    for hf in range(2):
                        DG(wbr[:, n3 * 4:(n3 + 1) * 4, hf * 512:(hf + 1) * 512],
                           w_branch[L, n3].rearrange("(wc p) d -> p wc d", p=128)[:, :, hf * 512:(hf + 1) * 512], w=["wbr"])
                for hf in range(2):
                    DG(wot[:, :, hf * 512:(hf + 1) * 512],
                       w_out[L].rearrange("(kc p) n -> p kc n", p=128)[:, :, hf * 512:(hf + 1) * 512], w=["wot"])
                DG(wsT[:], cmws_T[L], w=["wsT"])
                DS(bsc[:], cmbs_c[L], w=["bsc"])
                DS(bgc[:], bgate_c[L], w=["bgc"])
                DS(lngb[:], cm_ln_g[L:L + 1, :].partition_broadcast(128), w=["lngb"])
                wgl_i = 0
                for gi, (row0, N, is_ctx) in enumerate(groups):
                    r = 1 if is_ctx else 0
                    nj = N // 128
                    DS(aT[:, :, 0:N], aT_s[:, :, row0:row0 + N].rearrange("h p t -> p h t"), r=[("aT", gi)], w=["aTt"])
                    DS(fT[:, :, 0:N], fT_s[:, :, row0:row0 + N].rearrange("h p t -> p h t"), r=[("fT", gi)], w=["fTt"])
                    for j in range(nj):
                        ti = row0 // 128 + j
                        DS(xg[:, j, :], xs[ti * 128:(ti + 1) * 128, :], r=[XS[ti]], w=[("xg", j)])
                        norm_mod(xg[:, j, :], ("xg", j), gs1b[r][:], sh1b[r][:], [("gs1b", r), ("sh1b", r)],
                                 hT[:, :, j * 128:(j + 1) * 128], "hT", pA, ["pA0", "pA1"], sqj, ssq, h32)
                        for nt, (pb_, pk) in enumerate([(PB0, "pB0"), (PB1, "pB1")]):
                            for kc in range(8):
                                mm(pb_, hT[:, kc, j * 128:(j + 1) * 128], wzz[:, kc, nt * 512:(nt + 1) * 512],
                                   kc == 0, kc == 7, ["hT", "wzz"], [pk])
                        act(u_t[:], PB0, AF.Gelu, ["pB0"], ["u_t"])
                        act(gv[:], PB1, AF.Gelu, ["pB1"], ["gv"])
                        V(lambda e, bst=bst, gv=gv: e.bn_stats(out=bst[:], in_=gv[:]), ["gv"], ["bst"])
                        V(lambda e, bst=bst, bag=bag: e.bn_aggr(out=bag[:], in_=bst[:]), ["bst"], ["bag"])
                        act(bag[:, 1:2], bag[:, 1:2], AF.Sqrt, ["bag"], ["bag"], bias=EPS, scale=1.0)
                        recip(bag[:, 1:2], bag[:, 1:2], ["bag"], ["bag"])
                        ts(gv[:], gv[:], bag[:, 0:1], bag[:, 1:2], ALU.subtract, ALU.mult, ["gv", "bag"], ["gv"])
                        tt(vb[:], gv[:], lngb[:], ALU.mult, ["gv", "lngb"], ["vb"])
                        for g in range(4):
                            mm(PC0[:, g * 128:(g + 1) * 128], wsT[:, g, :], vb[:, g * 128:(g + 1) * 128], True, True,
                               ["wsT", "vb"], ["pC0"])
                        for g in range(4):
                            stt(mb[:, g * 128:(g + 1) * 128], PC0[:, g * 128:(g + 1) * 128], bsc[:, g:g + 1],
                                u_t[:, g * 128:(g + 1) * 128], ALU.add, ALU.mult, ["pC0", "bsc", "u_t"], ["mb"])
                        pcv = PC1.bitcast(BF16)
                        for g in range(4):
                            tr(pcv[:, g * 128:(g + 1) * 128], mb[:, g * 128:(g + 1) * 128], identb[:],
                               ["mb", "identb"], ["pC1"])
                        act(mT[:, :, j * 128:(j + 1) * 128], pcv[:, 0:512].rearrange("p (h t) -> p h t", t=128),
                            AF.Copy, ["pC1"], ["mT"])
                    brs = [(aT, "aTt"), (mT, "mT"), (fT, "fTt")]
                    for dc in range(8):
                        wb_ = wgl_i % 2
                        wgl_i += 1
                        for n3 in range(3):
                            c0 = 3072 + n3 * 1024 + dc * 128
                            DG(wgl[wb_][:, :, n3, :], wsrc[:, :, c0:c0 + 128], w=[("wgl", wb_)])
                        for n3 in range(3):
                            brT, brk = brs[n3]
                            par = (dc * 3 + n3) % 2
                            pY, pYk = (PD0, "pD0") if par == 0 else (PC0, "pC0")
                            pG, pGk = (PD1, "pD1") if par == 0 else (PC1, "pC1")
                            for wc in range(4):
                                mm(pY[:, 0:N], wbr[:, n3 * 4 + wc, dc * 128:(dc + 1) * 128], brT[:, wc, 0:N],
                                   wc == 0, wc == 3, ["wbr", brk], [pYk])
                            for kc in range(8):
                                mm(pG[:, 0:N], wgl[wb_][:, kc, n3, :], hT[:, kc, 0:N], kc == 0, kc == 7,
                                   [("wgl", wb_), "hT"], [pGk])
                            gs_ = gsig[par]
                            act(gs_[:, 0:N], pG[:, 0:N], AF.Sigmoid, [pGk, "bgc"], [("gsig", par)],
                                bias=bgc[:, n3 * 8 + dc:n3 * 8 + dc + 1])
                            if n3 == 0:
                                tt(zacc[:, 0:N], pY[:, 0:N], gs_[:, 0:N], ALU.mult, [pYk, ("gsig", par)], ["zacc"])
                            elif n3 == 1:
                                tt(ztmp[:, 0:N], pY[:, 0:N], gs_[:, 0:N], ALU.mult, [pYk, ("gsig", par)], ["ztmp"])
                                tt(zacc[:, 0:N], zacc[:, 0:N], ztmp[:, 0:N], ALU.add, ["zacc", "ztmp"], ["zacc"])
                            else:
                                tt(ztmp[:, 0:N], pY[:, 0:N], gs_[:, 0:N], ALU.mult, [pYk, ("gsig", par)], ["ztmp"])
                                tt(zT[:, dc, 0:N], zacc[:, 0:N], ztmp[:, 0:N], ALU.add, ["zacc", "ztmp"], ["zT"])
                    for j in range(nj):
                        ti = row0 // 128 + j
                        xn_ = xnew[j % 2]
                        for hf, (pb_, pk) in enumerate([(PB0, "pB0"), (PB1, "pB1")]):
                            for dc in range(8):
                                mm(pb_, zT[:, dc, j * 128:(j + 1) * 128], wot[:, dc, hf * 512:(hf + 1) * 512],
                                   dc == 0, dc == 7, ["zT", "wot"], [pk])
                            tt(otmp[:], pb_, g1b[r][:, hf * 512:(hf + 1) * 512], ALU.mult, [pk, ("g1b", r)], ["otmp"])
                            tt(xn_[:, hf * 512:(hf + 1) * 512], otmp[:], xg[:, j, hf * 512:(hf + 1) * 512], ALU.add,
                               ["otmp", ("xg", j)], [("xnew", j % 2)])
                        DG(xs[ti * 128:(ti + 1) * 128, :], xn_[:], r=[("xnew", j % 2)], w=[XS[ti]])
            em.barrier_all()
            if stop_after == "P2c":
                break
            with ExitStack() as s5:
                gs2b = T(s5, "gs2b", [128, D], F32)
                sh2b = T(s5, "sh2b", [128, D], F32)
                g2b = [T(s5, f"g2b{r}", [128, D], F32) for r in range(2)]
                nr = 1 if last else 2
                for r in range(nr):
                    load_bcast(g2b[r], 5, r, ("g2b", r))
                xt = [T(s5, f"p_xt{i}", [128, D], F32) for i in range(2)]
                ssq = T(s5, "p_ssq", [128, 1], F32)
                h32 = [T(s5, f"p_h32_{i}", [128, D], F32) for i in range(2)]
                hT2 = T(s5, "hT2", [128, 8, 128], BF16)
                wpq = T(s5, "wpq", [128, 8, 2048], BF16)
                keysT = T(s5, "keysT", [128, 16, 128], BF16)
                ss16 = T(s5, "ss16", [128, 16], F32)
                qn = T(s5, "qn", [128, 2048], BF16)
                qnT = T(s5, "qnT", [128, 16, 128], BF16)
                s_sb = T(s5, "s_sb", [128, 2048], F32)
                s2x = [T(s5, f"s2_{i}", [128, 128], F32) for i in range(2)]
                ta = T(s5, "ta", [128, 8, 16], F32)
                tb = T(s5, "tb", [128, 8, 16], F32)
                tcv = T(s5, "tcv", [128, 8, 16], F32)
                ia = T(s5, "ia", [128, 8, 16], U32)
                ib = T(s5, "ib", [128, 8, 16], U32)
                pos = T(s5, "pos", [128, 8, 16], U32)
                k1 = T(s5, "k1", [128, 8, 16], U32)
                k2 = T(s5, "k2", [128, 8, 16], U32)
                k1f = T(s5, "k1f", [128, 8, 16], F32)
                k2f = T(s5, "k2f", [128, 8, 16], F32)
                iaf = T(s5, "iaf", [128, 8, 16], F32)
                ibf = T(s5, "ibf", [128, 8, 16], F32)
                isel = T(s5, "isel", [128, 8, 16], F32)
                jsel = T(s5, "jsel", [128, 8, 16], F32)
                idxf = T(s5, "idxf", [128, 128], F32)
                idxu = [T(s5, f"idxu{i}", [128, 128], U32) for i in range(2)]
                cand = T(s5, "cand", [128, 16, 16], F32)
                cand2 = T(s5, "cand2", [128, 256], F32)
                ee = T(s5, "ee", [128, 8, 16], F32)
                zz = T(s5, "zz", [128, 8], F32)
                gw = [T(s5, f"gw{i}", [128, 128], F32) for i in range(2)]
                actv = T(s5, "actv", [128, 128], F32)
                gact = T(s5, "gact", [128, 128], F32)
                xo = T(s5, "xo", [128, D], F32)
                junk = T(s5, "junk", [128, D], F32)
                gw2 = T(s5, "gw2", [128, 128], F32)
                dgt = [T(s5, f"dgt{i}", [128, 128], BF16) for i in range(4)]
                rem = int(nc.sbuf_bytes_remaining)
                NS = min(24, (rem - 3072) // 4096)
                assert NS >= 16, f"PEER gather pipeline needs >= 16 slots, got {NS} (sbuf remaining {rem})"
                if L == 0:
                    print("PEER gather slots:", NS)
                gbuf = [T(s5, f"gbuf{i}", [128, 2 * D], BF16) for i in range(NS)]
                uvkeys = [("UVb", L, c8, uv) for c8 in range(8) for uv in range(2)]
                for q4 in range(4):
                    DG(wpq[:, :, q4 * 512:(q4 + 1) * 512],
                       peer_w_q[L].rearrange("(kc p) n -> p kc n", p=128)[:, :, q4 * 512:(q4 + 1) * 512], w=["wpq"])
                DG(keysT[:], keysT_in[L], w=["keysT"])
                tiles = list(range(2, 34)) if last else list(range(34))
                qbanks = [(PC0, "pC0"), (PC1, "pC1"), (PD0, "pD0"), (PD1, "pD1")]
                state = {"dcnt": 0, "gcnt": 0, "mod_r": None}

                def front(tix):
                    ti = tiles[tix]
                    b = tix % 2
                    r = 1 if ti < 2 else 0
                    if state["mod_r"] != r:
                        load_bcast(gs2b, 3, r, "gs2b")
                        load_bcast(sh2b, 4, r, "sh2b")
                        state["mod_r"] = r
                    xk = ("xt", b)
                    hk32 = ("h32", b)
                    h32b = h32[b]
                    act(junk[:], xt[b][:], AF.Square, [xk], ["junk", "ssq"], accum=ssq[:])
                    act(ssq[:], ssq[:], AF.Sqrt, ["ssq"], ["ssq"], bias=EPS, scale=1.0 / D)
                    yield
                    recip(ssq[:], ssq[:], ["ssq"], ["ssq"])
                    stt(h32b[:], xt[b][:], ssq[:, 0:1], gs2b[:], ALU.mult, ALU.mult, [xk, "ssq", "gs2b"], [hk32])
                    tt(h32b[:], h32b[:], sh2b[:], ALU.add, [hk32, "sh2b"], [hk32])
                    yield
                    for kc in range(8):
                        tr(pA[:, kc * 128:(kc + 1) * 128], h32b[:, kc * 128:(kc + 1) * 128], identf[:],
                           [hk32, "identf"], ["pA0", "pA1"])
                    yield
                    act(hT2[:], pA[:, 0:1024].rearrange("p (k t) -> p k t", t=128), AF.Copy, ["pA0", "pA1"], ["hT2"])
                    yield
                    for nt, (pb_, pk) in enumerate(qbanks):
                        for kc in range(8):
                            mm(pb_, hT2[:, kc, :], wpq[:, kc, nt * 512:(nt + 1) * 512], kc == 0, kc == 7,
                               ["hT2", "wpq"], [pk])
                    yield
                    for nt, (pb_, pk) in enumerate(qbanks):
                        act(s_sb[:, nt * 512:(nt + 1) * 512], pb_, AF.Square, [pk], ["s_sb"])
                    yield
                    vreduce(ss16[:], s_sb[:].rearrange("p (g d) -> p g d", d=128), ["s_sb"], ["ss16"])
                    yield
                    act(ss16[:], ss16[:], AF.Sqrt, ["ss16"], ["ss16"], bias=EPS, scale=1.0 / 128)
                    yield
                    recip(ss16[:], ss16[:], ["ss16"], ["ss16"])
                    yield
                    for hp in range(16):
                        pb_, pk = qbanks[hp // 4]
                        act(qn[:, hp * 128:(hp + 1) * 128], pb_[:, (hp % 4) * 128:(hp % 4 + 1) * 128], AF.Copy,
                            [pk, "ss16"], ["qn"], scale=ss16[:, hp:hp + 1])
                    yield
                    pav = pA[:, 0:1024].bitcast(BF16)
                    for hp in range(16):
                        tr(pav[:, hp * 128:(hp + 1) * 128], qn[:, hp * 128:(hp + 1) * 128], identb[:],
                           ["qn", "identb"], ["pA0", "pA1"])
                    yield
                    act(qnT[:].rearrange("p h t -> p (h t)"), pav[:, 0:2048], AF.Copy, ["pA0", "pA1"], ["qnT"])
                    yield
                    for hp in range(16):
                        pb_, pk = qbanks[hp // 4]
                        mm(pb_[:, (hp % 4) * 128:(hp % 4 + 1) * 128], qnT[:, hp, :], keysT[:, hp, :], True, True,
                           ["qnT", "keysT"], [pk])
                    yield
                    for nt, (pb_, pk) in enumerate(qbanks):
                        act(s_sb[:, nt * 512:(nt + 1) * 512], pb_, AF.Copy, [pk], ["s_sb"])
                    yield
                    for h in range(8):
                        sides = []
                        for side, (tv, iv) in enumerate([(ta, ia), (tb, ib)]):
                            sv = s_sb[:, (2 * h + side) * 128:(2 * h + side + 1) * 128]
                            tk, ik = ("ta", "ia") if side == 0 else ("tb", "ib")
                            sides.append((tv, iv, sv, tk, ik, s2x[side], ("s2", side)))
                        for tv, iv, sv, tk, ik, s2_, s2k in sides:
                            vmax(tv[:, h, 0:8], sv, ["s_sb"], [tk])
                        for tv, iv, sv, tk, ik, s2_, s2k in sides:
                            vmatchrep(s2_[:], tv[:, h, 0:8], sv, ["s_sb", tk], [s2k])
                        for tv, iv, sv, tk, ik, s2_, s2k in sides:
                            vmaxidx(iv[:, h, 0:8], tv[:, h, 0:8], sv, ["s_sb", tk], [ik])
                        for tv, iv, sv, tk, ik, s2_, s2k in sides:
                            vmax(tv[:, h, 8:16], s2_[:], [s2k], [tk])
                        for tv, iv, sv, tk, ik, s2_, s2k in sides:
                            vmaxidx(iv[:, h, 8:16], tv[:, h, 8:16], s2_[:], [s2k, tk], [ik])
                        tt(cand[:], ta[:, h, :].unsqueeze(2).to_broadcast([128, 16, 16]),
                           tb[:, h, :].unsqueeze(1).to_broadcast([128, 16, 16]), ALU.add, ["ta", "tb"], ["cand"])
                        cf = cand[:].rearrange("p a b -> p (a b)")
                        vmax(tcv[:, h, 0:8], cf, ["cand"], ["tcv"])
                        vmaxidx(pos[:, h, 0:8], tcv[:, h, 0:8], cf, ["cand", "tcv"], ["pos"])
                        vmatchrep(cand2[:], tcv[:, h, 0:8], cf, ["cand", "tcv"], ["cand2"])
                        vmax(tcv[:, h, 8:16], cand2[:], ["cand2"], ["tcv"])
                        vmaxidx(pos[:, h, 8:16], tcv[:, h, 8:16], cand2[:], ["cand2", "tcv"], ["pos"])
                        yield
                    vsingle(k1[:], pos[:], 4, ALU.arith_shift_right, ["pos"], ["k1"])
                    vsingle(k2[:], pos[:], 15, ALU.bitwise_and, ["pos"], ["k2"])
                    vcopy(k1f[:], k1[:], ["k1"], ["k1f"])
                    vcopy(k2f[:], k2[:], ["k2"], ["k2f"])
                    vcopy(iaf[:], ia[:], ["ia"], ["iaf"])
                    vcopy(ibf[:], ib[:], ["ib"], ["ibf"])
                    iob = io16[:].unsqueeze(1).unsqueeze(1).to_broadcast([128, 8, 16, 16])
                    eq4v = s_sb[:].rearrange("p (h a b) -> p h a b", h=8, a=16)
                    for kf, kfk, ixf, ixk, osel, osk in [(k1f, "k1f", iaf, "iaf", isel, "isel"),
                                                         (k2f, "k2f", ibf, "ibf", jsel, "jsel")]:
                        tt(eq4v, kf[:].unsqueeze(3).to_broadcast([128, 8, 16, 16]), iob, ALU.is_equal,
                           [kfk, "io16"], ["s_sb"])
                        tt(eq4v, eq4v, ixf[:].unsqueeze(2).to_broadcast([128, 8, 16, 16]), ALU.mult,
                           ["s_sb", ixk], ["s_sb"])
                        vreduce(osel[:], eq4v, ["s_sb"], [osk])
                    yield
                    stt(idxf[:], isel[:].rearrange("p h k -> p (h k)"), 128.0, jsel[:].rearrange("p h k -> p (h k)"),
                        ALU.mult, ALU.add, ["isel", "jsel"], ["idxf"])
                    if L > 0:
                        ts(idxf[:], idxf[:], float(L * NEXP), None, ALU.add, None, ["idxf"], ["idxf"])
                    vcopy(idxu[b][:], idxf[:], ["idxf"], [("idxu", b)])
                    tt(ee[:], tcv[:], tcv[:, :, 0:1].to_broadcast([128, 8, 16]), ALU.subtract, ["tcv"], ["ee"])
                    yield
                    act(ee[:], ee[:], AF.Exp, ["ee"], ["ee"])
                    yield
                    vreduce(zz[:], ee[:], ["ee"], ["zz"])
                    recip(zz[:], zz[:], ["zz"], ["zz"])
                    tt(gw[b][:].rearrange("p (h k) -> p h k", k=16), ee[:], zz[:].unsqueeze(2).to_broadcast([128, 8, 16]),
                       ALU.mult, ["ee", "zz"], [("gw", b)])

                def back(tix, nxt):
                    ti = tiles[tix]
                    b = tix % 2
                    r = 1 if ti < 2 else 0

                    def stage1(bi, mid=None):
                        for q8 in range(8):
                            hk = bi * 8 + q8
                            sl = state["gcnt"] % NS
                            state["gcnt"] += 1
                            slots[hk] = sl
                            gather(gbuf[sl][:], UVb, idxu[b][:, hk:hk + 1], [("idxu", b)] + uvkeys, [("gb", sl)])
                        for q8 in range(8):
                            hk = bi * 8 + q8
                            sl = slots[hk]
                            stt(junk[:], gbuf[sl][:, 0:D], 1.0, h32[b][:], ALU.mult, ALU.mult, [("gb", sl), ("h32", b)],
                                ["junk", ("actv", bi)], accum=actv[:, hk:hk + 1])
                            if q8 == 1 and mid is not None:
                                mid()
                        act(gact[:, bi * 8:(bi + 1) * 8], actv[:, bi * 8:(bi + 1) * 8], AF.Gelu, [("actv", bi)], [("gact", bi)])

                    def stage2(bi):
                        tt(gw2[:, bi * 8:(bi + 1) * 8], gw[b][:, bi * 8:(bi + 1) * 8], gact[:, bi * 8:(bi + 1) * 8], ALU.mult,
                           [("gw", b), ("gact", bi)], [("gw2", bi)])
                        for q8 in range(8):
                            hk = bi * 8 + q8
                            sl = slots[hk]
                            dd = state["dcnt"] % 4
                            state["dcnt"] += 1
                            act(dgt[dd][:], identb[:], AF.Copy, ["identb", ("gw2", bi)], [("dg", dd)], scale=gw2[:, hk:hk + 1])
                            mm(PB0, dgt[dd][:], gbuf[sl][:, D:D + 512], hk == 0, hk == 127, [("dg", dd), ("gb", sl)], ["pB0"])
                            mm(PB1, dgt[dd][:], gbuf[sl][:, D + 512:2 * D], hk == 0, hk == 127, [("dg", dd), ("gb", sl)], ["pB1"])

                    slots = {}
                    stage1(0)
                    for bi in range(1, 16):
                        stage1(bi, mid=lambda bi=bi: stage2(bi - 1))
                        if nxt is not None:
                            next(nxt, None)
                            next(nxt, None)
                    stage2(15)
                    if nxt is not None:
                        for _ in nxt:
                            pass
                    tt(xo[:, 0:512], PB0, g2b[r][:, 0:512], ALU.mult, ["pB0", ("g2b", r)], ["xo"])
                    tt(xo[:, 512:D], PB1, g2b[r][:, 512:D], ALU.mult, ["pB1", ("g2b", r)], ["xo"])
                    tt(xo[:], xo[:], xt[b][:], ALU.add, ["xo", ("xt", b)], ["xo"])
                    if last:
                        DS(out[(ti - 2) * 128:(ti - 1) * 128, :], xo[:], r=["xo"], w=[("out", ti)])
                    else:
                        DS(xs[ti * 128:(ti + 1) * 128, :], xo[:], r=["xo"], w=[XS[ti]])
                    if tix + 2 < len(tiles):
                        tn = tiles[tix + 2]
                        DS(xt[b][:], xs[tn * 128:(tn + 1) * 128, :], r=[XS[tn]], w=[("xt", b)])

                DS(xt[0][:], xs[tiles[0] * 128:(tiles[0] + 1) * 128, :], r=[XS[tiles[0]]], w=[("xt", 0)])
                if len(tiles) > 1:
                    DS(xt[1][:], xs[tiles[1] * 128:(tiles[1] + 1) * 128, :], r=[XS[tiles[1]]], w=[("xt", 1)])
                for _ in front(0):
                    pass
                for tix in range(len(tiles)):
                    nxt = front(tix + 1) if tix + 1 < len(tiles) else None
                    back(tix, nxt)
            em.barrier_all()

        em.finish("sync")
        semkeys = list(ENGS) + [("dma", i) for i in range(em.n_dma)]
        sems = {k: top.enter_context(nc.semaphore(f"sem{j}")) for j, k in enumerate(semkeys)}
        with nc.Block() as block:
            @block.sync
            def _(e):
                em.replay(sems, "sync", e)

            @block.scalar
            def _(e):
                em.replay(sems, "scalar", e)

            @block.vector
            def _(e):
                em.replay(sems, "vector", e)

            @block.gpsimd
            def _(e):
                em.replay(sems, "gpsimd", e)

            @block.tensor
            def _(e):
                em.replay(sems, "tensor", e)
    return nc, em


_CONST = {}


def _constants():
    if _CONST:
        return _CONST
    bf = ml_dtypes.bfloat16
    t = np.arange(SEQ, dtype=np.int64)
    m = (t[:, None] * t[None, :]) % SEQ
    ang = (2.0 * np.pi / SEQ) * m.astype(np.float64)
    dftL = np.empty((2, SEQ, SEQ), dtype=bf)
    dftL[0] = (np.cos(ang) / 64.0).astype(np.float32).astype(bf)
    dftL[1] = (-np.sin(ang) / 64.0).astype(np.float32).astype(bf)
    del ang, m
    t2 = np.arange(CTX, dtype=np.int64)
    a2 = (2.0 * np.pi / CTX) * ((t2[:, None] * t2[None, :]) % CTX).astype(np.float64)
    dft256 = np.stack([np.cos(a2) / 16.0, -np.sin(a2) / 16.0]).astype(np.float32).astype(bf)
    c = np.arange(128, dtype=np.int64)
    a3 = (2.0 * np.pi / 128) * ((c[:, None] * c[None, :]) % 128).astype(np.float64)
    s128 = 1.0 / math.sqrt(128.0)
    dftC = np.stack([np.cos(a3) * s128, np.sin(a3) * s128]).astype(np.float32).astype(bf)
    freqs = (10000.0 ** (-np.arange(0, 32, 2, dtype=np.float32) / 32.0)).astype(np.float32)
    rr = (t // 64).astype(np.float32)
    cc = (t % 64).astype(np.float32)
    ang_r = rr[:, None] * freqs[None, :]
    ang_c = cc[:, None] * freqs[None, :]
    rope = np.concatenate([np.cos(ang_r), np.cos(ang_c), np.sin(ang_r), np.sin(ang_c)], axis=1).astype(np.float32)
    _CONST.update(dftL=dftL, dft256=dft256, dftC=dftC, rope=rope, identf=np.eye(128, dtype=np.float32))
    return _CONST


def make_in_maps(inputs, depth=DEPTH, cores=NCORES):
    f = lambda a: np.ascontiguousarray(np.asarray(a, dtype=np.float32))
    cst = _constants()
    x = f(inputs["x"]); c = f(inputs["c"]); ctx = f(inputs["ctx"]); c_ctx = f(inputs["c_ctx"])
    sl = slice(0, depth)
    shared = {
        "w_ada": f(inputs["w_ada"])[sl], "b_ada": f(inputs["b_ada"])[sl],
        "norm1_g": f(inputs["norm1_g"])[sl], "norm2_g": f(inputs["norm2_g"])[sl],
        "w_in": f(inputs["w_in"])[sl],
        "bgate_c": np.ascontiguousarray(f(inputs["b_gate"])[sl].reshape(depth, 24, 128).transpose(0, 2, 1)),
        "q_norm_g": f(inputs["q_norm_g"])[sl], "k_norm_g": f(inputs["k_norm_g"])[sl],
        "lam_params": f(inputs["lam_params"])[sl].reshape(depth, 256),
        "subln_c": f(inputs["subln_g"])[sl].reshape(depth, 128, 1),
        "cm_ln_g": f(inputs["cm_ln_g"])[sl],
        "cmws_T": np.ascontiguousarray(f(inputs["cm_w_s"])[sl].transpose(0, 3, 1, 2)),
        "cmbs_c": np.ascontiguousarray(f(inputs["cm_b_s"])[sl].transpose(0, 2, 1)),
        "w_branch": f(inputs["w_branch"])[sl], "w_out": f(inputs["w_out"])[sl],
        "peer_w_q": f(inputs["peer_w_q"])[sl],
        "keysT": np.ascontiguousarray(f(inputs["peer_sub_keys"])[sl].reshape(depth, 16, 128, 128).transpose(0, 3, 1, 2)),
        "peer_u": f(inputs["peer_u"])[sl], "peer_v": f(inputs["peer_v"])[sl],
        "identf": cst["identf"], "rope": cst["rope"], "dftL": cst["dftL"], "dft256": cst["dft256"], "dftC": cst["dftC"],
    }
    maps = []
    for b in range(cores):
        cv = np.stack([c[b], c_ctx], axis=0)
        cT = np.ascontiguousarray(cv.reshape(2, 8, 128).transpose(2, 1, 0))
        mp = dict(shared)
        mp.update({"x": x[b], "ctx": ctx[b], "cT": cT})
        maps.append(mp)
    return maps


_NC = {}


def kernel(**inputs):
    if "nc" not in _NC:
        _NC["nc"] = build()[0]
    nc = _NC["nc"]
    maps = make_in_maps(inputs)
    res = run_bass_kernel_spmd(nc, maps, core_ids=list(range(NCORES)))
    outs = [np.asarray(r["out"], dtype=np.float32) for r in res.results]
    return np.stack(outs, axis=0)
```

```python
import math
from contextlib import ExitStack

import numpy as np
import ml_dtypes

import concourse.bass as bass
import concourse.mybir as mybir
from concourse.bass_utils import run_bass_kernel_spmd

F32 = mybir.dt.float32
BF16 = mybir.dt.bfloat16
U32 = mybir.dt.uint32
AF = mybir.ActivationFunctionType
ALU = mybir.AluOpType
AX = mybir.AxisListType

D = 1024
SEQ = 4096
CTX = 256
NTOK = SEQ + CTX
DEPTH = 4
NCORES = 8
EPS = 1e-6
IN_COLS = 6144
NEXP = 16384

ENGS = ["tensor", "vector", "scalar", "gpsimd", "sync"]


class Emitter:
    def __init__(self, nc, n_dma_sems=28):
        self.nc = nc
        self.lists = {e: [] for e in ENGS}
        self.cnt = {e: 0 for e in ENGS}
        self.known = {e: {} for e in ENGS}
        self.last_w = {}
        self.readers = {}
        self.n_dma = n_dma_sems
        self.dma_val = [0] * n_dma_sems
        self.dma_rr = 0
        self.n_inst = 0

    def _deps(self, reads, writes, eng=None):
        deps = {}

        def add(d, same_ok):
            if d is None:
                return
            s, v = d
            if not same_ok and s == eng:
                return
            if deps.get(s, 0) < v:
                deps[s] = v

        for k in reads:
            add(self.last_w.get(k), True)
        for k in writes:
            add(self.last_w.get(k), False)
            for r in self.readers.get(k, ()):
                add(r, False)
        return deps

    def _emit_waits(self, eng, deps):
        kn = self.known[eng]
        for s, v in deps.items():
            if eng == "tensor" and s == "tensor":
                continue
            if kn.get(s, 0) >= v:
                continue
            kn[s] = v
            self.lists[eng].append(("wait", s, v))

    def _commit(self, token, reads, writes):
        for k in reads:
            lst = self.readers.setdefault(k, [])
            lst.append(token)
            if len(lst) > 64:
                mx = {}
                for s, v in lst:
                    if mx.get(s, 0) < v:
                        mx[s] = v
                self.readers[k] = list(mx.items())
        for k in writes:
            self.last_w[k] = token
            self.readers[k] = []

    def op(self, eng, fn, reads=(), writes=()):
        deps = self._deps(reads, writes, eng)
        self._emit_waits(eng, deps)
        self.cnt[eng] += 1
        token = (eng, self.cnt[eng])
        self.lists[eng].append(("op", fn, eng, 1))
        self._commit(token, reads, writes)
        self.n_inst += 1
        return token

    def dma(self, eng, fn, reads=(), writes=()):
        deps = self._deps(reads, writes)
        i = self.dma_rr
        self.dma_rr = (self.dma_rr + 1) % self.n_dma
        s = ("dma", i)
        if self.dma_val[i] > 0:
            deps[s] = max(deps.get(s, 0), self.dma_val[i])
        self._emit_waits(eng, deps)
        self.dma_val[i] += 16
        token = (s, self.dma_val[i])
        self.lists[eng].append(("op", fn, s, 16))
        self._commit(token, reads, writes)
        self.n_inst += 1
        return token

    def _all(self):
        deps = {e: self.cnt[e] for e in ENGS if self.cnt[e] > 0}
        for i in range(self.n_dma):
            if self.dma_val[i] > 0:
                deps[("dma", i)] = self.dma_val[i]
        return deps

    def barrier_all(self):
        deps = self._all()
        for e in ENGS:
            self._emit_waits(e, dict(deps))

    def finish(self, eng="sync"):
        self._emit_waits(eng, self._all())

    def replay(self, sems, engname, engobj):
        for item in self.lists[engname]:
            if item[0] == "wait":
                engobj.wait_ge(sems[item[1]], item[2])
            else:
                _, fn, s, inc = item
                fn(engobj).then_inc(sems[s], inc)


def build(depth=DEPTH, total_depth=DEPTH, dbg=False, stop_after=None):
    nc = bass.Bass("TRN2", target_bir_lowering=False)
    em = Emitter(nc)

    def din(name, shape, dt=F32):
        return nc.dram_tensor(name, list(shape), dt, kind="ExternalInput").ap()

    x_in = din("x", [SEQ, D])
    ctx_in = din("ctx", [CTX, D])
    cT_in = din("cT", [128, 8, 2])
    w_ada = din("w_ada", [depth, D, 6 * D])
    b_ada = din("b_ada", [depth, 6 * D])
    norm1_g = din("norm1_g", [depth, D])
    norm2_g = din("norm2_g", [depth, D])
    w_in = din("w_in", [depth, D, IN_COLS])
    bgate_c = din("bgate_c", [depth, 128, 24])
    q_norm_g = din("q_norm_g", [depth, 64])
    k_norm_g = din("k_norm_g", [depth, 64])
    lam_params = din("lam_params", [depth, 256])
    subln_c = din("subln_c", [depth, 128, 1])
    cm_ln_g = din("cm_ln_g", [depth, 512])
    cmws_T = din("cmws_T", [depth, 128, 4, 128])
    cmbs_c = din("cmbs_c", [depth, 128, 4])
    w_branch = din("w_branch", [depth, 3, 512, D])
    w_out = din("w_out", [depth, D, D])
    peer_w_q = din("peer_w_q", [depth, D, 2048])
    keysT_in = din("keysT", [depth, 128, 16, 128])
    peer_u = din("peer_u", [depth, NEXP, D])
    peer_v = din("peer_v", [depth, NEXP, D])
    peer_u_flat = peer_u.rearrange("l e d -> (l e) d")
    peer_v_flat = peer_v.rearrange("l e d -> (l e) d")
    identf_in = din("identf", [128, 128])
    rope_in = din("rope", [SEQ, 64])
    dftL = din("dftL", [2, SEQ, SEQ], BF16)
    dft256 = din("dft256", [2, CTX, CTX], BF16)
    dftC = din("dftC", [2, 128, 128], BF16)

    out = nc.dram_tensor("out", [SEQ, D], F32, kind="ExternalOutput").ap()
    skind = "ExternalOutput" if dbg else "Internal"
    xs = nc.dram_tensor("xs", [NTOK, D], F32, kind=skind).ap()
    der = nc.dram_tensor("der", [2, 6, D], F32, kind=skind).ap()
    aT_s = nc.dram_tensor("aT_s", [4, 128, NTOK], BF16, kind=skind).ap()
    fT_s = nc.dram_tensor("fT_s", [4, 128, NTOK], BF16, kind=skind).ap()
    UVb = nc.dram_tensor("UVb", [depth * NEXP, 2 * D], BF16, kind="Internal").ap()

    def V(fn, r=(), w=()):
        return em.op("vector", fn, r, w)

    def A(fn, r=(), w=()):
        return em.op("scalar", fn, r, w)

    def PE(fn, r=(), w=()):
        return em.op("tensor", fn, r, w)

    def G(fn, r=(), w=()):
        return em.op("gpsimd", fn, r, w)

    def DS(out_, in_, r=(), w=()):
        return em.dma("sync", lambda e: e.dma_start(out=out_, in_=in_), r, w)

    def DG(out_, in_, r=(), w=()):
        return em.dma("gpsimd", lambda e: e.dma_start(out=out_, in_=in_), r, w)

    def tt(out_, a, b, op, r, w, eng="vector"):
        return em.op(eng, lambda e: e.tensor_tensor(out=out_, in0=a, in1=b, op=op), r, w)

    def stt(out_, a, s, b, op0, op1, r, w, accum=None):
        return V(lambda e: e.scalar_tensor_tensor(out=out_, in0=a, scalar=s, in1=b, op0=op0, op1=op1,
                                                  accum_out=accum), r, w)

    def ts(out_, a, s1, s2, op0, op1, r, w):
        if s2 is None:
            return V(lambda e: e.tensor_scalar(out=out_, in0=a, scalar1=s1, scalar2=None, op0=op0), r, w)
        return V(lambda e: e.tensor_scalar(out=out_, in0=a, scalar1=s1, scalar2=s2, op0=op0, op1=op1), r, w)

    def act(out_, in_, func, r, w, bias=None, scale=None, accum=None):
        kw = {}
        if bias is not None:
            kw["bias"] = bias
        if scale is not None:
            kw["scale"] = scale
        if accum is not None:
            kw["accum_out"] = accum
        return A(lambda e: e.activation(out=out_, in_=in_, func=func, **kw), r, w)

    def vcopy(out_, in_, r, w):
        return V(lambda e: e.tensor_copy(out=out_, in_=in_), r, w)

    def mm(out_, lhsT, rhs, start, stop, r, w):
        return PE(lambda e: e.matmul(out_, lhsT=lhsT, rhs=rhs, start=start, stop=stop), r, w)

    def tr(out_, in_, ident, r, w):
        return PE(lambda e: e.transpose(out_, in_, ident), r, w)

    def recip(out_, in_, r, w):
        return V(lambda e: e.reciprocal(out=out_, in_=in_), r, w)

    def vmax(out_, in_, r, w):
        return V(lambda e: e.max(out=out_, in_=in_), r, w)

    def vmaxidx(out_, inmax, invals, r, w):
        return V(lambda e: e.max_index(out=out_, in_max=inmax, in_values=invals), r, w)

    def vmatchrep(out_, rep, vals, r, w):
        return V(lambda e: e.match_replace(out=out_, in_to_replace=rep, in_values=vals, imm_value=-1e30), r, w)

    def vreduce(out_, in_, r, w):
        return V(lambda e: e.tensor_reduce(out=out_, in_=in_, axis=AX.X, op=ALU.add), r, w)

    def vsingle(out_, in_, scalar, op, r, w):
        return V(lambda e: e.tensor_single_scalar(out=out_, in_=in_, scalar=scalar, op=op), r, w)

    def rstd_op(t_ap, key, scale, mode):
        if mode == "ln":
            act(t_ap, t_ap, AF.Ln, [key], [key], bias=EPS, scale=scale)
            act(t_ap, t_ap, AF.Exp, [key], [key], scale=-0.5)
        else:
            act(t_ap, t_ap, AF.Sqrt, [key], [key], bias=EPS, scale=scale)
            recip(t_ap, t_ap, [key], [key])

    def gather(out_, table, idx_ap, r, w):
        return em.dma("gpsimd", lambda e: e.indirect_dma_start(
            out=out_, out_offset=None, in_=table,
            in_offset=bass.IndirectOffsetOnAxis(ap=idx_ap, axis=0)), r, w)

    with ExitStack() as top:
        _tcnt = [0]

        def T(es, name, shape, dt):
            _tcnt[0] += 1
            return es.enter_context(nc.sbuf_tensor(f"t{_tcnt[0]}_{name}", list(shape), dt))

        pA = top.enter_context(nc.psum_tensor("pA", [128, 1024], F32))
        pB = top.enter_context(nc.psum_tensor("pB", [128, 1024], F32))
        pC = top.enter_context(nc.psum_tensor("pC", [128, 1024], F32))
        pD = top.enter_context(nc.psum_tensor("pD", [128, 1024], F32))
        PA0, PA1 = pA[:, 0:512], pA[:, 512:1024]
        PB0, PB1 = pB[:, 0:512], pB[:, 512:1024]
        PC0, PC1 = pC[:, 0:512], pC[:, 512:1024]
        PD0, PD1 = pD[:, 0:512], pD[:, 512:1024]

        identf = T(top, "identf", [128, 128], F32)
        identb = T(top, "identb", [128, 128], BF16)
        onesf = T(top, "onesf", [128, 128], F32)
        onesb = T(top, "onesb", [128, 128], BF16)
        ropet = T(top, "ropet", [128, 32, 64], F32)
        io16 = T(top, "io16", [128, 16], F32)
        cact = T(top, "cact", [128, 8, 2], F32)
        CCt = T(top, "CCt", [128, 128], BF16)
        SCt = T(top, "SCt", [128, 128], BF16)
        neglam = T(top, "neglam", [128, 1], F32)
        sgcol = T(top, "sgcol", [128, 1], F32)
        lamt = T(top, "lamt", [128, 2], F32)
        lpb = T(top, "lpb", [128, 256], F32)
        lj = T(top, "lj", [128, 64], F32)

        DS(identf[:], identf_in, w=["identf"])
        vcopy(identb[:], identf[:], ["identf"], ["identb"])
        V(lambda e: e.memset(onesf[:], 1.0), w=["onesf"])
        V(lambda e: e.memset(onesb[:], 1.0), w=["onesb"])
        DS(ropet[:], rope_in.rearrange("(t p) c -> p t c", p=128), w=["ropet"])
        G(lambda e: e.iota(io16[:], pattern=[[1, 16]], base=0, channel_multiplier=0,
                           allow_small_or_imprecise_dtypes=True), w=["io16"])
        DS(cact[:], cT_in, w=["cact"])
        act(cact[:], cact[:], AF.Silu, ["cact"], ["cact"])
        DS(CCt[:], dftC[0], w=["CCt"])
        DS(SCt[:], dftC[1], w=["SCt"])
        XS = [("xs", i) for i in range(34)]
        DS(xs[0:CTX, :], ctx_in, w=XS[0:2])
        for q4 in range(4):
            DS(xs[CTX + q4 * 1024:CTX + (q4 + 1) * 1024, :], x_in[q4 * 1024:(q4 + 1) * 1024, :],
               w=XS[2 + q4 * 8:2 + (q4 + 1) * 8])

        def norm_mod(xtile, xkey, gsb, shb, mkeys, hT_out, hT_key, tp, tpkeys, sqj, ssq, h32,
                     h32key="h32", sqkey="sqj", rs="sqrt"):
            act(sqj[:], xtile, AF.Square, [xkey], [sqkey, "ssq"], accum=ssq[:])
            rstd_op(ssq[:], "ssq", 1.0 / D, rs)
            stt(h32[:], xtile, ssq[:, 0:1], gsb, ALU.mult, ALU.mult, [xkey, "ssq", mkeys[0]], [h32key])
            tt(h32[:], h32[:], shb, ALU.add, [h32key, mkeys[1]], [h32key])
            if isinstance(tp, list):
                for hf in range(2):
                    for k4 in range(4):
                        kc = hf * 4 + k4
                        tr(tp[hf][:, k4 * 128:(k4 + 1) * 128], h32[:, kc * 128:(kc + 1) * 128], identf[:],
                           [h32key, "identf"], [tpkeys[hf]])
                    act(hT_out[:, hf * 4:(hf + 1) * 4, :], tp[hf].rearrange("p (k t) -> p k t", t=128), AF.Copy,
                        [tpkeys[hf]], [hT_key])
                return
            for kc in range(8):
                tr(tp[:, kc * 128:(kc + 1) * 128], h32[:, kc * 128:(kc + 1) * 128], identf[:],
                   [h32key, "identf"], tpkeys)
            act(hT_out, tp[:, 0:1024].rearrange("p (k t) -> p k t", t=128), AF.Copy, tpkeys, [hT_key])

        def group_norm_rope(ps, pskey, gb, gbkey, nsq, ss8, kn, ra, rb_, outb, outkey, rope_tile, rs="sqrt"):
            act(nsq[:], ps, AF.Square, [pskey], ["nsq"])
            vreduce(ss8[:], nsq[:].rearrange("p (g d) -> p g d", d=64), ["nsq"], ["ss8"])
            rstd_op(ss8[:], "ss8", 1.0 / 64, rs)
            kn3 = kn[:].rearrange("p (g d) -> p g d", d=64)
            tt(kn3, ps.rearrange("p (g d) -> p g d", d=64), ss8[:].unsqueeze(2).to_broadcast([128, 8, 64]),
               ALU.mult, [pskey, "ss8"], ["kn"])
            if rope_tile is None:
                tt(outb[:].rearrange("p (g d) -> p g d", d=64), kn3, gb[:].unsqueeze(1).to_broadcast([128, 8, 64]),
                   ALU.mult, ["kn", gbkey], [outkey])
                return
            tt(kn3, kn3, gb[:].unsqueeze(1).to_broadcast([128, 8, 64]), ALU.mult, ["kn", gbkey], ["kn"])
            kn5 = kn[:].rearrange("p (g a h f) -> p g a h f", g=8, a=2, h=2, f=16)
            ob5 = outb[:].rearrange("p (g a h f) -> p g a h f", g=8, a=2, h=2, f=16)
            x1, x2 = kn5[:, :, :, 0, :], kn5[:, :, :, 1, :]
            cosb = ropet[:, rope_tile, 0:32].rearrange("p (a f) -> p a f", a=2).unsqueeze(1).to_broadcast([128, 8, 2, 16])
            sinb = ropet[:, rope_tile, 32:64].rearrange("p (a f) -> p a f", a=2).unsqueeze(1).to_broadcast([128, 8, 2, 16])
            ra4 = ra[:].rearrange("p (g a f) -> p g a f", g=8, a=2)
            rb4 = rb_[:].rearrange("p (g a f) -> p g a f", g=8, a=2)
            tt(ra4, x1, cosb, ALU.mult, ["kn", "ropet"], ["ra"])
            tt(rb4, x2, sinb, ALU.mult, ["kn", "ropet"], ["rb"])
            tt(ob5[:, :, :, 0, :], ra4, rb4, ALU.subtract, ["ra", "rb"], [outkey])
            tt(ra4, x2, cosb, ALU.mult, ["kn", "ropet"], ["ra"])
            tt(rb4, x1, sinb, ALU.mult, ["kn", "ropet"], ["rb"])
            tt(ob5[:, :, :, 1, :], ra4, rb4, ALU.add, ["ra", "rb"], [outkey])

        def load_bcast(tile, j, r, key):
            DS(tile[:], der[r, j:j + 1, :].partition_broadcast(128), r=["der"], w=[key])

        for L in range(depth):
            last = (L == total_depth - 1)
            lam_init = 0.8 - 0.6 * math.exp(-0.3 * L)
            groups = []
            if not last:
                groups.append((0, 256, True))
            for g8 in range(8):
                groups.append((CTX + g8 * 512, 512, False))

            em.barrier_all()
            with ExitStack() as s0:
                wada = [T(s0, f"wada{i}", [128, 8, 512], F32) for i in range(2)]
                badat = [T(s0, f"badat{i}", [2, 512], F32) for i in range(2)]
                gch = [T(s0, f"gch{i}", [2, 512], F32) for i in range(2)]
                modrow = [T(s0, f"modrow{i}", [2, 512], F32) for i in range(2)]
                jmap = {0: 1, 1: 0, 2: 2, 3: 4, 4: 3, 5: 5}
                for nt in range(12):
                    b = nt % 2
                    part, half = nt // 2, nt % 2
                    DS(wada[b][:], w_ada[L][:, nt * 512:(nt + 1) * 512].rearrange("(kc p) n -> p kc n", p=128),
                       w=[("wada", b)])
                    DS(badat[b][:], b_ada[L:L + 1, nt * 512:(nt + 1) * 512].partition_broadcast(2), w=[("badat", b)])
                    for kc in range(8):
                        mm(PA0[0:2, :], cact[:, kc, :], wada[b][:, kc, :], kc == 0, kc == 7,
                           [("wada", b), "cact"], ["pA0"])
                    tt(modrow[b][:], PA0[0:2, :], badat[b][:], ALU.add, ["pA0", ("badat", b)], [("modrow", b)])
                    if part in (1, 4):
                        ng = norm1_g if part == 1 else norm2_g
                        DS(gch[b][:], ng[L:L + 1, half * 512:(half + 1) * 512].partition_broadcast(2), w=[("gch", b)])
                        stt(modrow[b][:], modrow[b][:], 1.0, gch[b][:], ALU.add, ALU.mult,
                            [("modrow", b), ("gch", b)], [("modrow", b)])
                    DG(der[:, jmap[part], half * 512:(half + 1) * 512], modrow[b][:], r=[("modrow", b)], w=["der"])
                DS(lpb[:], lam_params[L:L + 1, :].partition_broadcast(128), w=["lpb"])
                stt(lj[:], lpb[:, 0:64], 1.0, lpb[:, 64:128], ALU.mult, ALU.mult, ["lpb"], ["lj", "lamt"], accum=lamt[:, 0:1])
                stt(lj[:], lpb[:, 128:192], 1.0, lpb[:, 192:256], ALU.mult, ALU.mult, ["lpb"], ["lj", "lamt"], accum=lamt[:, 1:2])
                act(lamt[:], lamt[:], AF.Exp, ["lamt"], ["lamt"])
                tt(neglam[:], lamt[:, 1:2], lamt[:, 0:1], ALU.subtract, ["lamt"], ["neglam"])
                ts(neglam[:], neglam[:], -lam_init, None, ALU.add, None, ["neglam"], ["neglam"])
                DS(sgcol[:], subln_c[L], w=["sgcol"])
                ts(sgcol[:], sgcol[:], 1.0 - lam_init, None, ALU.mult, None, ["sgcol"], ["sgcol"])
            em.barrier_all()
            if stop_after == "P0":
                break

            with ExitStack() as sZ:
                Zs = T(sZ, "Zs", [128, 34, 512], BF16)
                gs1b = [T(sZ, f"gs1b{r}", [128, D], F32) for r in range(2)]
                sh1b = [T(sZ, f"sh1b{r}", [128, D], F32) for r in range(2)]
                for r in range(2):
                    load_bcast(gs1b[r], 0, r, ("gs1b", r))
                    load_bcast(sh1b[r], 1, r, ("sh1b", r))
                with ExitStack() as sKV:
                    KT = T(sKV, "KT", [128, 4, NTOK], BF16)
                    Vs = T(sKV, "Vs", [128, 34, 512], BF16)
                    xt = [T(sKV, f"xt{i}", [128, D], F32) for i in range(2)]
                    sqj = T(sKV, "sqj", [128, D], F32)
                    ssq = T(sKV, "ssq", [128, 1], F32)
                    h32 = T(sKV, "h32", [128, D], F32)
                    nsq = T(sKV, "nsq", [128, 512], F32)
                    ss8 = T(sKV, "ss8", [128, 8], F32)
                    kn = T(sKV, "kn", [128, 512], F32)
                    ra = T(sKV, "ra", [128, 256], F32)
                    rb_ = T(sKV, "rb", [128, 256], F32)
                    kb = T(sKV, "kb", [128, 512], BF16)
                    kgb = T(sKV, "kgb", [128, 64], F32)
                    qgb = T(sKV, "qgb", [128, 64], F32)
                    DS(kgb[:], k_norm_g[L:L + 1, :].partition_broadcast(128), w=["kgb"])
                    DS(qgb[:], q_norm_g[L:L + 1, :].partition_broadcast(128), w=["qgb"])
                    ts(qgb[:], qgb[:], 0.125, None, ALU.mult, None, ["qgb"], ["qgb"])
                    with ExitStack() as s1:
                        w1 = T(s1, "w1", [128, 8, 1536], BF16)
                        hT1 = [T(s1, f"hT1_{i}", [128, 8, 128], BF16) for i in range(2)]
                        wsrc = w_in[L].rearrange("(kc p) n -> p kc n", p=128)
                        DG(w1[:, :, 0:512], wsrc[:, :, 512:1024], w=["w1"])
                        DG(w1[:, :, 512:1024], wsrc[:, :, 1024:1536], w=["w1"])
                        DG(w1[:, :, 1024:1536], wsrc[:, :, 2560:3072], w=["w1"])
                        DS(xt[0][:], xs[0:128, :], r=[XS[0]], w=[("xt", 0)])
                        for i in range(34):
                            b = i % 2
                            r = 1 if i < 2 else 0
                            if i + 1 < 34:
                                DS(xt[1 - b][:], xs[(i + 1) * 128:(i + 2) * 128, :], r=[XS[i + 1]], w=[("xt", 1 - b)])
                            tp, tpk = (pA, ["pA0", "pA1"]) if b == 0 else (pD, ["pD0", "pD1"])
                            norm_mod(xt[b][:], ("xt", b), gs1b[r][:], sh1b[r][:], [("gs1b", r), ("sh1b", r)],
                                     hT1[b][:], ("hT1", b), tp, tpk, sqj, ssq, h32)
                            for nt, (pb_, pk) in enumerate([(PB0, "pB0"), (PB1, "pB1"), (PC0, "pC0")]):
                                for kc in range(8):
                                    mm(pb_, hT1[b][:, kc, :], w1[:, kc, nt * 512:(nt + 1) * 512], kc == 0, kc == 7,
                                       [("hT1", b), "w1"], [pk])
                            group_norm_rope(PB0, "pB0", kgb, "kgb", nsq, ss8, kn, ra, rb_, kb, "kb",
                                            None if i < 2 else i - 2)
                            pcv = PC1.bitcast(BF16)
                            for h in range(4):
                                tr(pcv[:, h * 128:(h + 1) * 128], kb[:, h * 128:(h + 1) * 128], identb[:],
                                   ["kb", "identb"], ["pC1"])
                            act(KT[:, :, i * 128:(i + 1) * 128], pcv[:, 0:512].rearrange("p (h t) -> p h t", t=128),
                                AF.Copy, ["pC1"], [("KT", i)])
                            act(Vs[:, i, :], PB1, AF.Copy, ["pB1"], [("Vs", i)])
                            vcopy(Zs[:, i, :], PC0, ["pC0"], [("Zs", i)])
                    em.barrier_all()
                    if stop_after == "P1":
                        break
                    with ExitStack() as s2:
                        wq = T(s2, "wq", [128, 8, 512], BF16)
                        hT = T(s2, "hT", [128, 8, 128], BF16)
                        QT = [T(s2, f"QT{i}", [128, 4, 512], BF16) for i in range(2)]
                        PTp = [T(s2, f"PTp{i}", [128, 1024], BF16) for i in range(3)]
                        zacc = [T(s2, f"zacc{m}", [128, 512], BF16) for m in range(2)]
                        rz = T(s2, "rz", [128, 512], F32)
                        O0 = T(s2, "O0", [128, 512], F32)
                        O1 = T(s2, "O1", [128, 512], F32)
                        att = T(s2, "att", [128, 512], F32)
                        asq = T(s2, "asq", [128, 512], F32)
                        rst = T(s2, "rst", [128, 512], F32)
                        aTb = [T(s2, f"aTb{i}", [128, 512], BF16) for i in range(2)]
                        DG(wq[:], w_in[L].rearrange("(kc p) n -> p kc n", p=128)[:, :, 0:512], w=["wq"])
                        cv_list = []
                        for c8 in range(8):
                            e0 = c8 * 2048
                            cv_list.append((UVb[L * NEXP + e0:L * NEXP + e0 + 2048, 0:D], peer_u[L][e0:e0 + 2048, :], ("UVb", L, c8, 0)))
                            cv_list.append((UVb[L * NEXP + e0:L * NEXP + e0 + 2048, D:2 * D], peer_v[L][e0:e0 + 2048, :], ("UVb", L, c8, 1)))
                        cv_state = {"i": 0}

                        def prep(gi):
                            row0, N, is_ctx = groups[gi]
                            r = 1 if is_ctx else 0
                            qt = QT[gi % 2]
                            qk = ("QT", gi % 2)
                            for j in range(N // 128):
                                ti = row0 // 128 + j
                                DS(xt[0][:], xs[ti * 128:(ti + 1) * 128, :], r=[XS[ti]], w=[("xt", 0)])
                                norm_mod(xt[0][:], ("xt", 0), gs1b[r][:], sh1b[r][:], [("gs1b", r), ("sh1b", r)],
                                         hT[:], "hT", [PB1, PC1], ["pB1", "pC1"], sqj, ssq, h32, rs="ln")
                                yield
                                for kc in range(8):
                                    mm(PB1, hT[:, kc, :], wq[:, kc, :], kc == 0, kc == 7, ["hT", "wq"], ["pB1"])
                                yield
                                group_norm_rope(PB1, "pB1", qgb, "qgb", nsq, ss8, kn, ra, rb_, kb, "kb",
                                                None if is_ctx else ti - 2, rs="ln")
                                yield
                                pcv = PC1.bitcast(BF16)
                                for h in range(4):
                                    tr(pcv[:, h * 128:(h + 1) * 128], kb[:, h * 128:(h + 1) * 128], identb[:],
                                       ["kb", "identb"], ["pC1"])
                                act(qt[:, :, j * 128:(j + 1) * 128], pcv[:, 0:512].rearrange("p (h t) -> p h t", t=128),
                                    AF.Copy, ["pC1"], [qk])
                                yield

                        def attend(gi, nxt):
                            row0, N, is_ctx = groups[gi]
                            qt = QT[gi % 2]
                            qk = ("QT", gi % 2)
                            if not is_ctx:
                                for _ in range(2):
                                    o_, i_, k_ = cv_list[cv_state["i"]]
                                    DG(o_, i_, w=[k_, ("cv", cv_state["i"] % 4)])
                                    cv_state["i"] += 1
                            kchunks = [0, 1] if is_ctx else list(range(34))
                            nch = len(kchunks)
                            sb = [(pD, ["pD0", "pD1"]), (pA, ["pA0", "pA1"])]
                            obank = [(PB0, "pB0"), (PC0, "pC0")]
                            step = 0
                            for h in range(4):
                                def s_mm(ci):
                                    kc = kchunks[ci]
                                    pS, pSk = sb[ci % 2]
                                    for m in range(2):
                                        lo, hi = m * 64, (m + 1) * 64
                                        mm(pS[:, m * 512:m * 512 + N], KT[lo:hi, h, kc * 128:(kc + 1) * 128], qt[lo:hi, h, 0:N],
                                           True, True, [("KT", kc), qk], [pSk[m]])

                                s_mm(0)
                                for ci, kc in enumerate(kchunks):
                                    first, lastc = ci == 0, ci == nch - 1
                                    if ci + 1 < nch:
                                        s_mm(ci + 1)
                                    pS, pSk = sb[ci % 2]
                                    pt = PTp[ci % 3]
                                    ptk = ("PT", ci % 3)
                                    act(pt[:].rearrange("p (m n) -> p m n", m=2)[:, :, 0:N],
                                        pS[:, 0:1024].rearrange("p (m n) -> p m n", m=2)[:, :, 0:N], AF.Exp, pSk, [ptk])
                                    for m in range(2):
                                        pO, pOk = obank[m]
                                        mm(pO[:, 0:N], Vs[:, kc, h * 128:(h + 1) * 128], pt[:, m * 512:m * 512 + N], first, lastc,
                                           [("Vs", kc), ptk], [pOk])
                                    for m in range(2):
                                        eng = "vector"
                                        if first:
                                            em.op(eng, lambda e, o_=zacc[m][:, 0:N], i_=pt[:, m * 512:m * 512 + N]: e.tensor_copy(out=o_, in_=i_),
                                                  [ptk], [("zacc", m)])
                                        else:
                                            tt(zacc[m][:, 0:N], zacc[m][:, 0:N], pt[:, m * 512:m * 512 + N], ALU.add,
                                               [("zacc", m), ptk], [("zacc", m)], eng=eng)
                                    step += 1
                                    if nxt is not None and step % 4 == 0:
                                        next(nxt, None)
                                mm(PA0[:, 0:N], onesb[:], zacc[0][:, 0:N], True, True, ["onesb", ("zacc", 0)], ["pA0"])
                                mm(PA1[:, 0:N], onesb[:], zacc[1][:, 0:N], True, True, ["onesb", ("zacc", 1)], ["pA1"])
                                recip(rz[:, 0:N], PA0[:, 0:N], ["pA0"], ["rz"])
                                tt(O0[:, 0:N], PB0[:, 0:N], rz[:, 0:N], ALU.mult, ["pB0", "rz"], ["O0"])
                                recip(rz[:, 0:N], PA1[:, 0:N], ["pA1"], ["rz"])
                                tt(O1[:, 0:N], PC0[:, 0:N], rz[:, 0:N], ALU.mult, ["pC0", "rz"], ["O1"])
                                stt(att[:, 0:N], O1[:, 0:N], neglam[:, 0:1], O0[:, 0:N], ALU.mult, ALU.add,
                                    ["O0", "O1", "neglam"], ["att"])
                                act(asq[:, 0:N], att[:, 0:N], AF.Square, ["att"], ["asq"])
                                mm(PA0[:, 0:N], onesf[:], asq[:, 0:N], True, True, ["onesf", "asq"], ["pA0"])
                                act(rst[:, 0:N], PA0[:, 0:N], AF.Ln, ["pA0"], ["rst"], bias=EPS, scale=1.0 / 128)
                                act(rst[:, 0:N], rst[:, 0:N], AF.Exp, ["rst"], ["rst"], scale=-0.5)
                                ab = aTb[h % 2]
                                stt(ab[:, 0:N], att[:, 0:N], sgcol[:, 0:1], rst[:, 0:N], ALU.mult, ALU.mult,
                                    ["att", "sgcol", "rst"], [("aTb", h % 2)])
                                DG(aT_s[h, :, row0:row0 + N], ab[:, 0:N], r=[("aTb", h % 2)], w=[("aT", gi)])
                            if nxt is not None:
                                for _ in nxt:
                                    pass

                        for _ in prep(0):
                            pass
                        for gi in range(len(groups)):
                            attend(gi, prep(gi + 1) if gi + 1 < len(groups) else None)
                    em.barrier_all()
                if stop_after == "P2a":
                    break
                with ExitStack() as s3:
                    tabC = [T(s3, f"tabC{i}", [128, 32, 512], BF16) for i in range(2)]
                    tabS = [T(s3, f"tabS{i}", [128, 32, 512], BF16) for i in range(2)]
                    Wc = T(s3, "Wc", [128, 512], BF16)
                    Ws = T(s3, "Ws", [128, 512], BF16)
                    fTb = [T(s3, f"fTb{i}", [128, 512], BF16) for i in range(2)]

                    def load_tabs(gi):
                        row0, N, is_ctx = groups[gi]
                        tb_ = gi % 2
                        if is_ctx:
                            DS(tabC[tb_][:, 0:2, 0:256], dft256[0].rearrange("(tc p) n -> p tc n", p=128), w=[("tabC", tb_)])
                            DS(tabS[tb_][:, 0:2, 0:256], dft256[1].rearrange("(tc p) n -> p tc n", p=128), w=[("tabS", tb_)])
                        else:
                            t0 = row0 - CTX
                            for q4 in range(4):
                                DS(tabC[tb_][:, q4 * 8:(q4 + 1) * 8, :],
                                   dftL[0][q4 * 1024:(q4 + 1) * 1024, t0:t0 + 512].rearrange("(tc p) n -> p tc n", p=128),
                                   w=[("tabC", tb_)])
                                DS(tabS[tb_][:, q4 * 8:(q4 + 1) * 8, :],
                                   dftL[1][q4 * 1024:(q4 + 1) * 1024, t0:t0 + 512].rearrange("(tc p) n -> p tc n", p=128),
                                   w=[("tabS", tb_)])

                    load_tabs(0)
                    for gi, (row0, N, is_ctx) in enumerate(groups):
                        tb_ = gi % 2
                        if gi + 1 < len(groups):
                            load_tabs(gi + 1)
                        ntc, z0 = (2, 0) if is_ctx else (32, 2)
                        for g in range(4):
                            for tcx in range(ntc):
                                mm(PA0[:, 0:N], Zs[:, z0 + tcx, g * 128:(g + 1) * 128], tabC[tb_][:, tcx, 0:N],
                                   tcx == 0, tcx == ntc - 1, [("Zs", z0 + tcx), ("tabC", tb_)], ["pA0"])
                            for tcx in range(ntc):
                                mm(PA1[:, 0:N], Zs[:, z0 + tcx, g * 128:(g + 1) * 128], tabS[tb_][:, tcx, 0:N],
                                   tcx == 0, tcx == ntc - 1, [("Zs", z0 + tcx), ("tabS", tb_)], ["pA1"])
                            act(Wc[:, 0:N], PA0[:, 0:N], AF.Copy, ["pA0"], ["Wc"])
                            vcopy(Ws[:, 0:N], PA1[:, 0:N], ["pA1"], ["Ws"])
                            pF, pFk = (PB0, "pB0") if g % 2 == 0 else (PB1, "pB1")
                            mm(pF[:, 0:N], CCt[:], Wc[:, 0:N], True, False, ["CCt", "Wc"], [pFk])
                            mm(pF[:, 0:N], SCt[:], Ws[:, 0:N], False, True, ["SCt", "Ws"], [pFk])
                            fb = fTb[g % 2]
                            act(fb[:, 0:N], pF[:, 0:N], AF.Copy, [pFk], [("fTb", g % 2)])
                            DS(fT_s[g, :, row0:row0 + N], fb[:, 0:N], r=[("fTb", g % 2)], w=[("fT", gi)])
                em.barrier_all()
            if stop_after == "P2b":
                break
            with ExitStack() as s4:
                gs1b = [T(s4, f"c_gs1b{r}", [128, D], F32) for r in range(2)]
                sh1b = [T(s4, f"c_sh1b{r}", [128, D], F32) for r in range(2)]
                g1b = [T(s4, f"c_g1b{r}", [128, D], F32) for r in range(2)]
                for r in range(2):
                    load_bcast(gs1b[r], 0, r, ("gs1b", r))
                    load_bcast(sh1b[r], 1, r, ("sh1b", r))
                    load_bcast(g1b[r], 2, r, ("g1b", r))
                xg = T(s4, "xg", [128, 4, D], F32)
                sqj = T(s4, "c_sqj", [128, D], F32)
                ssq = T(s4, "c_ssq", [128, 1], F32)
                h32 = T(s4, "c_h32", [128, D], F32)
                hT = T(s4, "c_hT", [128, 8, 512], BF16)
                wzz = T(s4, "wzz", [128, 8, 1024], BF16)
                wbr = T(s4, "wbr", [128, 12, D], BF16)
                wot = T(s4, "wot", [128, 8, D], BF16)
                wgl = [T(s4, f"wgl{i}", [128, 8, 3, 128], BF16) for i in range(2)]
                wsT = T(s4, "wsT", [128, 4, 128], BF16)
                bsc = T(s4, "bsc", [128, 4], F32)
                bgc = T(s4, "bgc", [128, 24], F32)
                lngb = T(s4, "lngb", [128, 512], F32)
                u_t = T(s4, "u_t", [128, 512], F32)
                gv = T(s4, "gv", [128, 512], F32)
                bst = T(s4, "bst", [128, 6], F32)
                bag = T(s4, "bag", [128, 2], F32)
                vb = T(s4, "vb", [128, 512], BF16)
                mb = T(s4, "mb", [128, 512], BF16)
                mT = T(s4, "mT", [128, 4, 512], BF16)
                aT = T(s4, "aT", [128, 4, 512], BF16)
                fT = T(s4, "fT", [128, 4, 512], BF16)
                zT = T(s4, "zT", [128, 8, 512], BF16)
                gsig = [T(s4, f"gsig{i}", [128, 512], F32) for i in range(2)]
                zacc = T(s4, "zacc", [128, 512], F32)
                ztmp = T(s4, "ztmp", [128, 512], F32)
                otmp = T(s4, "otmp", [128, 512], F32)
                xnew = [T(s4, f"xnew{i}", [128, D], F32) for i in range(2)]
                wsrc = w_in[L].rearrange("(kc p) n -> p kc n", p=128)
                DG(wzz[:, :, 0:512], wsrc[:, :, 1536:2048], w=["wzz"])
                DG(wzz[:, :, 512:1024], wsrc[:, :, 2048:2560], w=["wzz"])
                for n3 in range(3):
                    for hf in range(2):
                        DG(wbr[:, n3 * 4:(n3 + 1) * 4, hf * 512:(hf + 1) * 512],
                           w_branch[L, n3].rearrange("(wc p) d -> p wc d", p=128)[:, :, hf * 512:(hf + 1) * 512], w=["wbr"])
                for hf in range(2):
                    DG(wot[:, :, hf * 512:(hf + 1) * 512],
                       w_out[L].rearrange("(kc p) n -> p kc n", p=128)[:, :, hf * 512:(hf + 1) * 512], w=["wot"])
                DG(wsT[:], cmws_T[L], w=["wsT"])
                DS(bsc[:], cmbs_c[L], w=["bsc"])
                DS(bgc[:], bgate_c[L], w=["bgc"])
                DS(lngb[:], cm_ln_g[L:L + 1, :].partition_broadcast(128), w=["lngb"])
                wgl_i = 0
                for gi, (row0, N, is_ctx) in enumerate(groups):
                    r = 1 if is_ctx else 0
                    nj = N // 128
                    DS(aT[:, :, 0:N], aT_s[:, :, row0:row0 + N].rearrange("h p t -> p h t"), r=[("aT", gi)], w=["aTt"])
                    DS(fT[:, :, 0:N], fT_s[:, :, row0:row0 + N].rearrange("h p t -> p h t"), r=[("fT", gi)], w=["fTt"])
                    for j in range(nj):
                        ti = row0 // 128 + j
                        DS(xg[:, j, :], xs[ti * 128:(ti + 1) * 128, :], r=[XS[ti]], w=[("xg", j)])
                        norm_mod(xg[:, j, :], ("xg", j), gs1b[r][:], sh1b[r][:], [("gs1b", r), ("sh1b", r)],
                                 hT[:, :, j * 128:(j + 1) * 128], "hT", pA, ["pA0", "pA1"], sqj, ssq, h32)
                        for nt, (pb_, pk) in enumerate([(PB0, "pB0"), (PB1, "pB1")]):
                            for kc in range(8):
                                mm(pb_, hT[:, kc, j * 128:(j + 1) * 128], wzz[:, kc, nt * 512:(nt + 1) * 512],
                                   kc == 0, kc == 7, ["hT", "wzz"], [pk])
                        act(u_t[:], PB0, AF.Gelu, ["pB0"], ["u_t"])
                        act(gv[:], PB1, AF.Gelu, ["pB1"], ["gv"])
                        V(lambda e, bst=bst, gv=gv: e.bn_stats(out=bst[:], in_=gv[:]), ["gv"], ["bst"])
                        V(lambda e, bst=bst, bag=bag: e.bn_aggr(out=bag[:], in_=bst[:]), ["bst"], ["bag"])
                        act(bag[:, 1:2], bag[:, 1:2], AF.Sqrt, ["bag"], ["bag"], bias=EPS, scale=1.0)
                        recip(bag[:, 1:2], bag[:, 1:2], ["bag"], ["bag"])
                        ts(gv[:], gv[:], bag[:, 0:1], bag[:, 1:2], ALU.subtract, ALU.mult, ["gv", "bag"], ["gv"])
                        tt(vb[:], gv[:], lngb[:], ALU.mult, ["gv", "lngb"], ["vb"])
                        for g in range(4):
                            mm(PC0[:, g * 128:(g + 1) * 128], wsT[:, g, :], vb[:, g * 128:(g + 1) * 128], True, True,
                               ["wsT", "vb"], ["pC0"])
                        for g in range(4):
                            stt(mb[:, g * 128:(g + 1) * 128], PC0[:, g * 128:(g + 1) * 128], bsc[:, g:g + 1],
                                u_t[:, g * 128:(g + 1) * 128], ALU.add, ALU.mult, ["pC0", "bsc", "u_t"], ["mb"])
                        pcv = PC1.bitcast(BF16)
                        for g in range(4):
                            tr(pcv[:, g * 128:(g + 1) * 128], mb[:, g * 128:(g + 1) * 128], identb[:],
                               ["mb", "identb"], ["pC1"])
                        act(mT[:, :, j * 128:(j + 1) * 128], pcv[:, 0:512].rearrange("p (h t) -> p h t", t=128),
                            AF.Copy, ["pC1"], ["mT"])
                    brs = [(aT, "aTt"), (mT, "mT"), (fT, "fTt")]
                    for dc in range(8):
                        wb_ = wgl_i % 2
                        wgl_i += 1
                        for n3 in range(3):
                            c0 = 3072 + n3 * 1024 + dc * 128
                            DG(wgl[wb_][:, :, n3, :], wsrc[:, :, c0:c0 + 128], w=[("wgl", wb_)])
                        for n3 in range(3):
                            brT, brk = brs[n3]
                            par = (dc * 3 + n3) % 2
                            pY, pYk = (PD0, "pD0") if par == 0 else (PC0, "pC0")
                            pG, pGk = (PD1, "pD1") if par == 0 else (PC1, "pC1")
                            for wc in range(4):
                                mm(pY[:, 0:N], wbr[:, n3 * 4 + wc, dc * 128:(dc + 1) * 128], brT[:, wc, 0:N],
                                   wc == 0, wc == 3, ["wbr", brk], [pYk])
                            for kc in range(8):
                                mm(pG[:, 0:N], wgl[wb_][:, kc, n3, :], hT[:, kc, 0:N], kc == 0, kc == 7,
                                   [("wgl", wb_), "hT"], [pGk])
                            gs_ = gsig[par]
                            act(gs_[:, 0:N], pG[:, 0:N], AF.Sigmoid, [pGk, "bgc"], [("gsig", par)],
                                bias=bgc[:, n3 * 8 + dc:n3 * 8 + dc + 1])
                            if n3 == 0:
                                tt(zacc[:, 0:N], pY[:, 0:N], gs_[:, 0:N], ALU.mult, [pYk, ("gsig", par)], ["zacc"])
                            elif n3 == 1:
                                tt(ztmp[:, 0:N], pY[:, 0:N], gs_[:, 0:N], ALU.mult, [pYk, ("gsig", par)], ["ztmp"])
                                tt(zacc[:, 0:N], zacc[:, 0:N], ztmp[:, 0:N], ALU.add, ["zacc", "ztmp"], ["zacc"])
                            else:
                                tt(ztmp[:, 0:N], pY[:, 0:N], gs_[:, 0:N], ALU.mult, [pYk, ("gsig", par)], ["ztmp"])
                                tt(zT[:, dc, 0:N], zacc[:, 0:N], ztmp[:, 0:N], ALU.add, ["zacc", "ztmp"], ["zT"])
                    for j in range(nj):
                        ti = row0 // 128 + j
                        xn_ = xnew[j % 2]
                        for hf, (pb_, pk) in enumerate([(PB0, "pB0"), (PB1, "pB1")]):
                            for dc in range(8):
                                mm(pb_, zT[:, dc, j * 128:(j + 1) * 128], wot[:, dc, hf * 512:(hf + 1) * 512],
                                   dc == 0, dc == 7, ["zT", "wot"], [pk])
                            tt(otmp[:], pb_, g1b[r][:, hf * 512:(hf + 1) * 512], ALU.mult, [pk, ("g1b", r)], ["otmp"])
                            tt(xn_[:, hf * 512:(hf + 1) * 512], otmp[:], xg[:, j, hf * 512:(hf + 1) * 512], ALU.add,
                               ["otmp", ("xg", j)], [("xnew", j % 2)])
                        DG(xs[ti * 128:(ti + 1) * 128, :], xn_[:], r=[("xnew", j % 2)], w=[XS[ti]])
            em.barrier_all()
            if stop_after == "P2c":
                break
            with ExitStack() as s5:
                gs2b = T(s5, "gs2b", [128, D], F32)
                sh2b = T(s5, "sh2b", [128, D], F32)
                g2b = [T(s5, f"g2b{r}", [128, D], F32) for r in range(2)]
                nr = 1 if last else 2
                for r in range(nr):
                    load_bcast(g2b[r], 5, r, ("g2b", r))
                xt = [T(s5, f"p_xt{i}", [128, D], F32) for i in range(2)]
                ssq = T(s5, "p_ssq", [128, 1], F32)
                h32 = [T(s5, f"p_h32_{i}", [128, D], F32) for i in range(2)]
                hT2 = T(s5, "hT2", [128, 8, 128], BF16)
                wpq = T(s5, "wpq", [128, 8, 2048], BF16)
                keysT = T(s5, "keysT", [128, 16, 128], BF16)
                ss16 = T(s5, "ss16", [128, 16], F32)
                qn = T(s5, "qn", [128, 2048], BF16)
                qnT = T(s5, "qnT", [128, 16, 128], BF16)
                s_sb = T(s5, "s_sb", [128, 2048], F32)
                s2x = [T(s5, f"s2_{i}", [128, 128], F32) for i in range(2)]
                ta = T(s5, "ta", [128, 8, 16], F32)
                tb = T(s5, "tb", [128, 8, 16], F32)
                tcv = T(s5, "tcv", [128, 8, 16], F32)
                ia = T(s5, "ia", [128, 8, 16], U32)
                ib = T(s5, "ib", [128, 8, 16], U32)
                pos = T(s5, "pos", [128, 8, 16], U32)
                k1 = T(s5, "k1", [128, 8, 16], U32)
                k2 = T(s5, "k2", [128, 8, 16], U32)
                k1f = T(s5, "k1f", [128, 8, 16], F32)
                k2f = T(s5, "k2f", [128, 8, 16], F32)
                iaf = T(s5, "iaf", [128, 8, 16], F32)
                ibf = T(s5, "ibf", [128, 8, 16], F32)
                isel = T(s5, "isel", [128, 8, 16], F32)
                jsel = T(s5, "jsel", [128, 8, 16], F32)
                idxf = T(s5, "idxf", [128, 128], F32)
                idxu = [T(s5, f"idxu{i}", [128, 128], U32) for i in range(2)]
                cand = T(s5, "cand", [128, 16, 16], F32)
                cand2 = T(s5, "cand2", [128, 256], F32)
                ee = T(s5, "ee", [128, 8, 16], F32)
                zz = T(s5, "zz", [128, 8], F32)
                gw = [T(s5, f"gw{i}", [128, 128], F32) for i in range(2)]
                actv = T(s5, "actv", [128, 128], F32)
                gact = T(s5, "gact", [128, 128], F32)
                xo = T(s5, "xo", [128, D], F32)
                junk = T(s5, "junk", [128, D], F32)
                gw2 = T(s5, "gw2", [128, 128], F32)
                dgt = [T(s5, f"dgt{i}", [128, 128], BF16) for i in range(4)]
                rem = int(nc.sbuf_bytes_remaining)
                NS = min(24, (rem - 3072) // 4096)
                assert NS >= 16, f"PEER gather pipeline needs >= 16 slots, got {NS} (sbuf remaining {rem})"
                if L == 0:
                    print("PEER gather slots:", NS)
                gbuf = [T(s5, f"gbuf{i}", [128, 2 * D], BF16) for i in range(NS)]
                uvkeys = [("UVb", L, c8, uv) for c8 in range(8) for uv in range(2)]
                for q4 in range(4):
                    DG(wpq[:, :, q4 * 512:(q4 + 1) * 512],
                       peer_w_q[L].rearrange("(kc p) n -> p kc n", p=128)[:, :, q4 * 512:(q4 + 1) * 512], w=["wpq"])
                DG(keysT[:], keysT_in[L], w=["keysT"])
                tiles = list(range(2, 34)) if last else list(range(34))
                qbanks = [(PC0, "pC0"), (PC1, "pC1"), (PD0, "pD0"), (PD1, "pD1")]
                state = {"dcnt": 0, "gcnt": 0, "mod_r": None}

                def front(tix):
                    ti = tiles[tix]
                    b = tix % 2
                    r = 1 if ti < 2 else 0
                    if state["mod_r"] != r:
                        load_bcast(gs2b, 3, r, "gs2b")
                        load_bcast(sh2b, 4, r, "sh2b")
                        state["mod_r"] = r
                    xk = ("xt", b)
                    hk32 = ("h32", b)
                    h32b = h32[b]
                    act(junk[:], xt[b][:], AF.Square, [xk], ["junk", "ssq"], accum=ssq[:])
                    act(ssq[:], ssq[:], AF.Sqrt, ["ssq"], ["ssq"], bias=EPS, scale=1.0 / D)
                    yield
                    recip(ssq[:], ssq[:], ["ssq"], ["ssq"])
                    stt(h32b[:], xt[b][:], ssq[:, 0:1], gs2b[:], ALU.mult, ALU.mult, [xk, "ssq", "gs2b"], [hk32])
                    tt(h32b[:], h32b[:], sh2b[:], ALU.add, [hk32, "sh2b"], [hk32])
                    yield
                    for kc in range(8):
                        tr(pA[:, kc * 128:(kc + 1) * 128], h32b[:, kc * 128:(kc + 1) * 128], identf[:],
                           [hk32, "identf"], ["pA0", "pA1"])
                    yield
                    act(hT2[:], pA[:, 0:1024].rearrange("p (k t) -> p k t", t=128), AF.Copy, ["pA0", "pA1"], ["hT2"])
                    yield
                    for nt, (pb_, pk) in enumerate(qbanks):
                        for kc in range(8):
                            mm(pb_, hT2[:, kc, :], wpq[:, kc, nt * 512:(nt + 1) * 512], kc == 0, kc == 7,
                               ["hT2", "wpq"], [pk])
                    yield
                    for nt, (pb_, pk) in enumerate(qbanks):
                        act(s_sb[:, nt * 512:(nt + 1) * 512], pb_, AF.Square, [pk], ["s_sb"])
                    yield
                    vreduce(ss16[:], s_sb[:].rearrange("p (g d) -> p g d", d=128), ["s_sb"], ["ss16"])
                    yield
                    act(ss16[:], ss16[:], AF.Sqrt, ["ss16"], ["ss16"], bias=EPS, scale=1.0 / 128)
                    yield
                    recip(ss16[:], ss16[:], ["ss16"], ["ss16"])
                    yield
                    for hp in range(16):
                        pb_, pk = qbanks[hp // 4]
                        act(qn[:, hp * 128:(hp + 1) * 128], pb_[:, (hp % 4) * 128:(hp % 4 + 1) * 128], AF.Copy,
                            [pk, "ss16"], ["qn"], scale=ss16[:, hp:hp + 1])
                    yield
                    pav = pA[:, 0:1024].bitcast(BF16)
                    for hp in range(16):
                        tr(pav[:, hp * 128:(hp + 1) * 128], qn[:, hp * 128:(hp + 1) * 128], identb[:],
                           ["qn", "identb"], ["pA0", "pA1"])
                    yield
                    act(qnT[:].rearrange("p h t -> p (h t)"), pav[:, 0:2048], AF.Copy, ["pA0", "pA1"], ["qnT"])
                    yield
                    for hp in range(16):
                        pb_, pk = qbanks[hp // 4]
                        mm(pb_[:, (hp % 4) * 128:(hp % 4 + 1) * 128], qnT[:, hp, :], keysT[:, hp, :], True, True,
                           ["qnT", "keysT"], [pk])
                    yield
                    for nt, (pb_, pk) in enumerate(qbanks):
                        act(s_sb[:, nt * 512:(nt + 1) * 512], pb_, AF.Copy, [pk], ["s_sb"])
                    yield
                    for h in range(8):
                        sides = []
                        for side, (tv, iv) in enumerate([(ta, ia), (tb, ib)]):
                            sv = s_sb[:, (2 * h + side) * 128:(2 * h + side + 1) * 128]
                            tk, ik = ("ta", "ia") if side == 0 else ("tb", "ib")
                            sides.append((tv, iv, sv, tk, ik, s2x[side], ("s2", side)))
                        for tv, iv, sv, tk, ik, s2_, s2k in sides:
                            vmax(tv[:, h, 0:8], sv, ["s_sb"], [tk])
                        for tv, iv, sv, tk, ik, s2_, s2k in sides:
                            vmatchrep(s2_[:], tv[:, h, 0:8], sv, ["s_sb", tk], [s2k])
                        for tv, iv, sv, tk, ik, s2_, s2k in sides:
                            vmaxidx(iv[:, h, 0:8], tv[:, h, 0:8], sv, ["s_sb", tk], [ik])
                        for tv, iv, sv, tk, ik, s2_, s2k in sides:
                            vmax(tv[:, h, 8:16], s2_[:], [s2k], [tk])
                        for tv, iv, sv, tk, ik, s2_, s2k in sides:
                            vmaxidx(iv[:, h, 8:16], tv[:, h, 8:16], s2_[:], [s2k, tk], [ik])
                        tt(cand[:], ta[:, h, :].unsqueeze(2).to_broadcast([128, 16, 16]),
                           tb[:, h, :].unsqueeze(1).to_broadcast([128, 16, 16]), ALU.add, ["ta", "tb"], ["cand"])
                        cf = cand[:].rearrange("p a b -> p (a b)")
                        vmax(tcv[:, h, 0:8], cf, ["cand"], ["tcv"])
                        vmaxidx(pos[:, h, 0:8], tcv[:, h, 0:8], cf, ["cand", "tcv"], ["pos"])
                        vmatchrep(cand2[:], tcv[:, h, 0:8], cf, ["cand", "tcv"], ["cand2"])
                        vmax(tcv[:, h, 8:16], cand2[:], ["cand2"], ["tcv"])
                        vmaxidx(pos[:, h, 8:16], tcv[:, h, 8:16], cand2[:], ["cand2", "tcv"], ["pos"])
                        yield
                    vsingle(k1[:], pos[:], 4, ALU.arith_shift_right, ["pos"], ["k1"])
                    vsingle(k2[:], pos[:], 15, ALU.bitwise_and, ["pos"], ["k2"])
                    vcopy(k1f[:], k1[:], ["k1"], ["k1f"])
                    vcopy(k2f[:], k2[:], ["k2"], ["k2f"])
                    vcopy(iaf[:], ia[:], ["ia"], ["iaf"])
                    vcopy(ibf[:], ib[:], ["ib"], ["ibf"])
                    iob = io16[:].unsqueeze(1).unsqueeze(1).to_broadcast([128, 8, 16, 16])
                    eq4v = s_sb[:].rearrange("p (h a b) -> p h a b", h=8, a=16)
                    for kf, kfk, ixf, ixk, osel, osk in [(k1f, "k1f", iaf, "iaf", isel, "isel"),
                                                         (k2f, "k2f", ibf, "ibf", jsel, "jsel")]:
                        tt(eq4v, kf[:].unsqueeze(3).to_broadcast([128, 8, 16, 16]), iob, ALU.is_equal,
                           [kfk, "io16"], ["s_sb"])
                        tt(eq4v, eq4v, ixf[:].unsqueeze(2).to_broadcast([128, 8, 16, 16]), ALU.mult,
                           ["s_sb", ixk], ["s_sb"])
                        vreduce(osel[:], eq4v, ["s_sb"], [osk])
                    yield
                    stt(idxf[:], isel[:].rearrange("p h k -> p (h k)"), 128.0, jsel[:].rearrange("p h k -> p (h k)"),
                        ALU.mult, ALU.add, ["isel", "jsel"], ["idxf"])
                    if L > 0:
                        ts(idxf[:], idxf[:], float(L * NEXP), None, ALU.add, None, ["idxf"], ["idxf"])
                    vcopy(idxu[b][:], idxf[:], ["idxf"], [("idxu", b)])
                    tt(ee[:], tcv[:], tcv[:, :, 0:1].to_broadcast([128, 8, 16]), ALU.subtract, ["tcv"], ["ee"])
                    yield
                    act(ee[:], ee[:], AF.Exp, ["ee"], ["ee"])
                    yield
                    vreduce(zz[:], ee[:], ["ee"], ["zz"])
                    recip(zz[:], zz[:], ["zz"], ["zz"])
                    tt(gw[b][:].rearrange("p (h k) -> p h k", k=16), ee[:], zz[:].unsqueeze(2).to_broadcast([128, 8, 16]),
                       ALU.mult, ["ee", "zz"], [("gw", b)])

                def back(tix, nxt):
                    ti = tiles[tix]
                    b = tix % 2
                    r = 1 if ti < 2 else 0

                    def stage1(bi, mid=None):
                        for q8 in range(8):
                            hk = bi * 8 + q8
                            sl = state["gcnt"] % NS
                            state["gcnt"] += 1
                            slots[hk] = sl
                            gather(gbuf[sl][:], UVb, idxu[b][:, hk:hk + 1], [("idxu", b)] + uvkeys, [("gb", sl)])
                        for q8 in range(8):
                            hk = bi * 8 + q8
                            sl = slots[hk]
                            stt(junk[:], gbuf[sl][:, 0:D], 1.0, h32[b][:], ALU.mult, ALU.mult, [("gb", sl), ("h32", b)],
                                ["junk", ("actv", bi)], accum=actv[:, hk:hk + 1])
                            if q8 == 1 and mid is not None:
                                mid()
                        act(gact[:, bi * 8:(bi + 1) * 8], actv[:, bi * 8:(bi + 1) * 8], AF.Gelu, [("actv", bi)], [("gact", bi)])

                    def stage2(bi):
                        tt(gw2[:, bi * 8:(bi + 1) * 8], gw[b][:, bi * 8:(bi + 1) * 8], gact[:, bi * 8:(bi + 1) * 8], ALU.mult,
                           [("gw", b), ("gact", bi)], [("gw2", bi)])
                        for q8 in range(8):
                            hk = bi * 8 + q8
                            sl = slots[hk]
                            dd = state["dcnt"] % 4
                            state["dcnt"] += 1
                            act(dgt[dd][:], identb[:], AF.Copy, ["identb", ("gw2", bi)], [("dg", dd)], scale=gw2[:, hk:hk + 1])
                            mm(PB0, dgt[dd][:], gbuf[sl][:, D:D + 512], hk == 0, hk == 127, [("dg", dd), ("gb", sl)], ["pB0"])
                            mm(PB1, dgt[dd][:], gbuf[sl][:, D + 512:2 * D], hk == 0, hk == 127, [("dg", dd), ("gb", sl)], ["pB1"])

                    slots = {}
                    stage1(0)
                    for bi in range(1, 16):
                        stage1(bi, mid=lambda bi=bi: stage2(bi - 1))
                        if nxt is not None:
                            next(nxt, None)
                            next(nxt, None)
                    stage2(15)
                    if nxt is not None:
                        for _ in nxt:
                            pass
                    tt(xo[:, 0:512], PB0, g2b[r][:, 0:512], ALU.mult, ["pB0", ("g2b", r)], ["xo"])
                    tt(xo[:, 512:D], PB1, g2b[r][:, 512:D], ALU.mult, ["pB1", ("g2b", r)], ["xo"])
                    tt(xo[:], xo[:], xt[b][:], ALU.add, ["xo", ("xt", b)], ["xo"])
                    if last:
                        DS(out[(ti - 2) * 128:(ti - 1) * 128, :], xo[:], r=["xo"], w=[("out", ti)])
                    else:
                        DS(xs[ti * 128:(ti + 1) * 128, :], xo[:], r=["xo"], w=[XS[ti]])
                    if tix + 2 < len(tiles):
                        tn = tiles[tix + 2]
                        DS(xt[b][:], xs[tn * 128:(tn + 1) * 128, :], r=[XS[tn]], w=[("xt", b)])

                DS(xt[0][:], xs[tiles[0] * 128:(tiles[0] + 1) * 128, :], r=[XS[tiles[0]]], w=[("xt", 0)])
                if len(tiles) > 1:
                    DS(xt[1][:], xs[tiles[1] * 128:(tiles[1] + 1) * 128, :], r=[XS[tiles[1]]], w=[("xt", 1)])
                for _ in front(0):
                    pass
                for tix in range(len(tiles)):
                    nxt = front(tix + 1) if tix + 1 < len(tiles) else None
                    back(tix, nxt)
            em.barrier_all()

        em.finish("sync")
        semkeys = list(ENGS) + [("dma", i) for i in range(em.n_dma)]
        sems = {k: top.enter_context(nc.semaphore(f"sem{j}")) for j, k in enumerate(semkeys)}
        with nc.Block() as block:
            @block.sync
            def _(e):
                em.replay(sems, "sync", e)

            @block.scalar
            def _(e):
                em.replay(sems, "scalar", e)

            @block.vector
            def _(e):
                em.replay(sems, "vector", e)

            @block.gpsimd
            def _(e):
                em.replay(sems, "gpsimd", e)

            @block.tensor
            def _(e):
                em.replay(sems, "tensor", e)
    return nc, em


_CONST = {}


def _constants():
    if _CONST:
        return _CONST
    bf = ml_dtypes.bfloat16
    t = np.arange(SEQ, dtype=np.int64)
    m = (t[:, None] * t[None, :]) % SEQ
    ang = (2.0 * np.pi / SEQ) * m.astype(np.float64)
    dftL = np.empty((2, SEQ, SEQ), dtype=bf)
    dftL[0] = (np.cos(ang) / 64.0).astype(np.float32).astype(bf)
    dftL[1] = (-np.sin(ang) / 64.0).astype(np.float32).astype(bf)
    del ang, m
    t2 = np.arange(CTX, dtype=np.int64)
    a2 = (2.0 * np.pi / CTX) * ((t2[:, None] * t2[None, :]) % CTX).astype(np.float64)
    dft256 = np.stack([np.cos(a2) / 16.0, -np.sin(a2) / 16.0]).astype(np.float32).astype(bf)
    c = np.arange(128, dtype=np.int64)
    a3 = (2.0 * np.pi / 128) * ((c[:, None] * c[None, :]) % 128).astype(np.float64)
    s128 = 1.0 / math.sqrt(128.0)
    dftC = np.stack([np.cos(a3) * s128, np.sin(a3) * s128]).astype(np.float32).astype(bf)
    freqs = (10000.0 ** (-np.arange(0, 32, 2, dtype=np.float32) / 32.0)).astype(np.float32)
    rr = (t // 64).astype(np.float32)
    cc = (t % 64).astype(np.float32)
    ang_r = rr[:, None] * freqs[None, :]
    ang_c = cc[:, None] * freqs[None, :]
    rope = np.concatenate([np.cos(ang_r), np.cos(ang_c), np.sin(ang_r), np.sin(ang_c)], axis=1).astype(np.float32)
    _CONST.update(dftL=dftL, dft256=dft256, dftC=dftC, rope=rope, identf=np.eye(128, dtype=np.float32))
    return _CONST


def make_in_maps(inputs, depth=DEPTH, cores=NCORES):
    f = lambda a: np.ascontiguousarray(np.asarray(a, dtype=np.float32))
    cst = _constants()
    x = f(inputs["x"]); c = f(inputs["c"]); ctx = f(inputs["ctx"]); c_ctx = f(inputs["c_ctx"])
    sl = slice(0, depth)
    shared = {
        "w_ada": f(inputs["w_ada"])[sl], "b_ada": f(inputs["b_ada"])[sl],
        "norm1_g": f(inputs["norm1_g"])[sl], "norm2_g": f(inputs["norm2_g"])[sl],
        "w_in": f(inputs["w_in"])[sl],
        "bgate_c": np.ascontiguousarray(f(inputs["b_gate"])[sl].reshape(depth, 24, 128).transpose(0, 2, 1)),
        "q_norm_g": f(inputs["q_norm_g"])[sl], "k_norm_g": f(inputs["k_norm_g"])[sl],
        "lam_params": f(inputs["lam_params"])[sl].reshape(depth, 256),
        "subln_c": f(inputs["subln_g"])[sl].reshape(depth, 128, 1),
        "cm_ln_g": f(inputs["cm_ln_g"])[sl],
        "cmws_T": np.ascontiguousarray(f(inputs["cm_w_s"])[sl].transpose(0, 3, 1, 2)),
        "cmbs_c": np.ascontiguousarray(f(inputs["cm_b_s"])[sl].transpose(0, 2, 1)),
        "w_branch": f(inputs["w_branch"])[sl], "w_out": f(inputs["w_out"])[sl],
        "peer_w_q": f(inputs["peer_w_q"])[sl],
        "keysT": np.ascontiguousarray(f(inputs["peer_sub_keys"])[sl].reshape(depth, 16, 128, 128).transpose(0, 3, 1, 2)),
        "peer_u": f(inputs["peer_u"])[sl], "peer_v": f(inputs["peer_v"])[sl],
        "identf": cst["identf"], "rope": cst["rope"], "dftL": cst["dftL"], "dft256": cst["dft256"], "dftC": cst["dftC"],
    }
    maps = []
    for b in range(cores):
        cv = np.stack([c[b], c_ctx], axis=0)
        cT = np.ascontiguousarray(cv.reshape(2, 8, 128).transpose(2, 1, 0))
        mp = dict(shared)
        mp.update({"x": x[b], "ctx": ctx[b], "cT": cT})
        maps.append(mp)
    return maps


_NC = {}


def kernel(**inputs):
    if "nc" not in _NC:
        _NC["nc"] = build()[0]
    nc = _NC["nc"]
    maps = make_in_maps(inputs)
    res = run_bass_kernel_spmd(nc, maps, core_ids=list(range(NCORES)))
    outs = [np.asarray(r["out"], dtype=np.float32) for r in res.results]
    return np.stack(outs, axis=0)
```

```python
import math
from contextlib import ExitStack

import numpy as np
import ml_dtypes

import concourse.bass as bass
import concourse.mybir as mybir
from concourse.bass_utils import run_bass_kernel_spmd

F32 = mybir.dt.float32
BF16 = mybir.dt.bfloat16
U32 = mybir.dt.uint32
AF = mybir.ActivationFunctionType
ALU = mybir.AluOpType
AX = mybir.AxisListType

D = 1024
SEQ = 4096
CTX = 256
NTOK = SEQ + CTX
DEPTH = 4
NCORES = 8
EPS = 1e-6
IN_COLS = 6144
NEXP = 16384

ENGS = ["tensor", "vector", "scalar", "gpsimd", "sync"]


class Emitter:
    def __init__(self, nc, n_dma_sems=28):
        self.nc = nc
        self.lists = {e: [] for e in ENGS}
        self.cnt = {e: 0 for e in ENGS}
        self.known = {e: {} for e in ENGS}
        self.last_w = {}
        self.readers = {}
        self.n_dma = n_dma_sems
        self.dma_val = [0] * n_dma_sems
        self.dma_rr = 0
        self.n_inst = 0

    def _deps(self, reads, writes, eng=None):
        deps = {}

        def add(d, same_ok):
            if d is None:
                return
            s, v = d
            if not same_ok and s == eng:
                return
            if deps.get(s, 0) < v:
                deps[s] = v

        for k in reads:
            add(self.last_w.get(k), True)
        for k in writes:
            add(self.last_w.get(k), False)
            for r in self.readers.get(k, ()):
                add(r, False)
        return deps

    def _emit_waits(self, eng, deps):
        kn = self.known[eng]
        for s, v in deps.items():
            if eng == "tensor" and s == "tensor":
                continue
            if kn.get(s, 0) >= v:
                continue
            kn[s] = v
            self.lists[eng].append(("wait", s, v))

    def _commit(self, token, reads, writes):
        for k in reads:
            lst = self.readers.setdefault(k, [])
            lst.append(token)
            if len(lst) > 64:
                mx = {}
                for s, v in lst:
                    if mx.get(s, 0) < v:
                        mx[s] = v
                self.readers[k] = list(mx.items())
        for k in writes:
            self.last_w[k] = token
            self.readers[k] = []

    def op(self, eng, fn, reads=(), writes=()):
        deps = self._deps(reads, writes, eng)
        self._emit_waits(eng, deps)
        self.cnt[eng] += 1
        token = (eng, self.cnt[eng])
        self.lists[eng].append(("op", fn, eng, 1))
        self._commit(token, reads, writes)
        self.n_inst += 1
        return token

    def dma(self, eng, fn, reads=(), writes=()):
        deps = self._deps(reads, writes)
        i = self.dma_rr
        self.dma_rr = (self.dma_rr + 1) % self.n_dma
        s = ("dma", i)
        if self.dma_val[i] > 0:
            deps[s] = max(deps.get(s, 0), self.dma_val[i])
        self._emit_waits(eng, deps)
        self.dma_val[i] += 16
        token = (s, self.dma_val[i])
        self.lists[eng].append(("op", fn, s, 16))
        self._commit(token, reads, writes)
        self.n_inst += 1
        return token

    def _all(self):
        deps = {e: self.cnt[e] for e in ENGS if self.cnt[e] > 0}
        for i in range(self.n_dma):
            if self.dma_val[i] > 0:
                deps[("dma", i)] = self.dma_val[i]
        return deps

    def barrier_all(self):
        deps = self._all()
        for e in ENGS:
            self._emit_waits(e, dict(deps))

    def finish(self, eng="sync"):
        self._emit_waits(eng, self._all())

    def replay(self, sems, engname, engobj):
        for item in self.lists[engname]:
            if item[0] == "wait":
                engobj.wait_ge(sems[item[1]], item[2])
            else:
                _, fn, s, inc = item
                fn(engobj).then_inc(sems[s], inc)


def build(depth=DEPTH, total_depth=DEPTH, dbg=False, stop_after=None):
    nc = bass.Bass("TRN2", target_bir_lowering=False)
    em = Emitter(nc)

    def din(name, shape, dt=F32):
        return nc.dram_tensor(name, list(shape), dt, kind="ExternalInput").ap()

    x_in = din("x", [SEQ, D])
    ctx_in = din("ctx", [CTX, D])
    cT_in = din("cT", [128, 8, 2])
    w_ada = din("w_ada", [depth, D, 6 * D])
    b_ada = din("b_ada", [depth, 6 * D])
    norm1_g = din("norm1_g", [depth, D])
    norm2_g = din("norm2_g", [depth, D])
    w_in = din("w_in", [depth, D, IN_COLS])
    bgate_c = din("bgate_c", [depth, 128, 24])
    q_norm_g = din("q_norm_g", [depth, 64])
    k_norm_g = din("k_norm_g", [depth, 64])
    lam_params = din("lam_params", [depth, 256])
    subln_c = din("subln_c", [depth, 128, 1])
    cm_ln_g = din("cm_ln_g", [depth, 512])
    cmws_T = din("cmws_T", [depth, 128, 4, 128])
    cmbs_c = din("cmbs_c", [depth, 128, 4])
    w_branch = din("w_branch", [depth, 3, 512, D])
    w_out = din("w_out", [depth, D, D])
    peer_w_q = din("peer_w_q", [depth, D, 2048])
    keysT_in = din("keysT", [depth, 128, 16, 128])
    peer_u = din("peer_u", [depth, NEXP, D])
    peer_v = din("peer_v", [depth, NEXP, D])
    peer_u_flat = peer_u.rearrange("l e d -> (l e) d")
    peer_v_flat = peer_v.rearrange("l e d -> (l e) d")
    identf_in = din("identf", [128, 128])
    rope_in = din("rope", [SEQ, 64])
    dftL = din("dftL", [2, SEQ, SEQ], BF16)
    dft256 = din("dft256", [2, CTX, CTX], BF16)
    dftC = din("dftC", [2, 128, 128], BF16)

    out = nc.dram_tensor("out", [SEQ, D], F32, kind="ExternalOutput").ap()
    skind = "ExternalOutput" if dbg else "Internal"
    xs = nc.dram_tensor("xs", [NTOK, D], F32, kind=skind).ap()
    der = nc.dram_tensor("der", [2, 6, D], F32, kind=skind).ap()
    aT_s = nc.dram_tensor("aT_s", [4, 128, NTOK], BF16, kind=skind).ap()
    fT_s = nc.dram_tensor("fT_s", [4, 128, NTOK], BF16, kind=skind).ap()
    UVb = nc.dram_tensor("UVb", [depth * NEXP, 2 * D], BF16, kind="Internal").ap()

    def V(fn, r=(), w=()):
        return em.op("vector", fn, r, w)

    def A(fn, r=(), w=()):
        return em.op("scalar", fn, r, w)

    def PE(fn, r=(), w=()):
        return em.op("tensor", fn, r, w)

    def G(fn, r=(), w=()):
        return em.op("gpsimd", fn, r, w)

    def DS(out_, in_, r=(), w=()):
        return em.dma("sync", lambda e: e.dma_start(out=out_, in_=in_), r, w)

    def DG(out_, in_, r=(), w=()):
        return em.dma("gpsimd", lambda e: e.dma_start(out=out_, in_=in_), r, w)

    def tt(out_, a, b, op, r, w, eng="vector"):
        return em.op(eng, lambda e: e.tensor_tensor(out=out_, in0=a, in1=b, op=op), r, w)

    def stt(out_, a, s, b, op0, op1, r, w, accum=None):
        return V(lambda e: e.scalar_tensor_tensor(out=out_, in0=a, scalar=s, in1=b, op0=op0, op1=op1,
                                                  accum_out=accum), r, w)

    def ts(out_, a, s1, s2, op0, op1, r, w):
        if s2 is None:
            return V(lambda e: e.tensor_scalar(out=out_, in0=a, scalar1=s1, scalar2=None, op0=op0), r, w)
        return V(lambda e: e.tensor_scalar(out=out_, in0=a, scalar1=s1, scalar2=s2, op0=op0, op1=op1), r, w)

    def act(out_, in_, func, r, w, bias=None, scale=None, accum=None):
        kw = {}
        if bias is not None:
            kw["bias"] = bias
        if scale is not None:
            kw["scale"] = scale
        if accum is not None:
            kw["accum_out"] = accum
        return A(lambda e: e.activation(out=out_, in_=in_, func=func, **kw), r, w)

    def vcopy(out_, in_, r, w):
        return V(lambda e: e.tensor_copy(out=out_, in_=in_), r, w)

    def mm(out_, lhsT, rhs, start, stop, r, w):
        return PE(lambda e: e.matmul(out_, lhsT=lhsT, rhs=rhs, start=start, stop=stop), r, w)

    def tr(out_, in_, ident, r, w):
        return PE(lambda e: e.transpose(out_, in_, ident), r, w)

    def recip(out_, in_, r, w):
        return V(lambda e: e.reciprocal(out=out_, in_=in_), r, w)

    def vmax(out_, in_, r, w):
        return V(lambda e: e.max(out=out_, in_=in_), r, w)

    def vmaxidx(out_, inmax, invals, r, w):
        return V(lambda e: e.max_index(out=out_, in_max=inmax, in_values=invals), r, w)

    def vmatchrep(out_, rep, vals, r, w):
        return V(lambda e: e.match_replace(out=out_, in_to_replace=rep, in_values=vals, imm_value=-1e30), r, w)

    def vreduce(out_, in_, r, w):
        return V(lambda e: e.tensor_reduce(out=out_, in_=in_, axis=AX.X, op=ALU.add), r, w)

    def vsingle(out_, in_, scalar, op, r, w):
        return V(lambda e: e.tensor_single_scalar(out=out_, in_=in_, scalar=scalar, op=op), r, w)

    def rstd_op(t_ap, key, scale, mode):
        if mode == "ln":
            act(t_ap, t_ap, AF.Ln, [key], [key], bias=EPS, scale=scale)
            act(t_ap, t_ap, AF.Exp, [key], [key], scale=-0.5)
        else:
            act(t_ap, t_ap, AF.Sqrt, [key], [key], bias=EPS, scale=scale)
            recip(t_ap, t_ap, [key], [key])

    def gather(out_, table, idx_ap, r, w):
        return em.dma("gpsimd", lambda e: e.indirect_dma_start(
            out=out_, out_offset=None, in_=table,
            in_offset=bass.IndirectOffsetOnAxis(ap=idx_ap, axis=0)), r, w)

    with ExitStack() as top:
        _tcnt = [0]

        def T(es, name, shape, dt):
            _tcnt[0] += 1
            return es.enter_context(nc.sbuf_tensor(f"t{_tcnt[0]}_{name}", list(shape), dt))

        pA = top.enter_context(nc.psum_tensor("pA", [128, 1024], F32))
        pB = top.enter_context(nc.psum_tensor("pB", [128, 1024], F32))
        pC = top.enter_context(nc.psum_tensor("pC", [128, 1024], F32))
        pD = top.enter_context(nc.psum_tensor("pD", [128, 1024], F32))
        PA0, PA1 = pA[:, 0:512], pA[:, 512:1024]
        PB0, PB1 = pB[:, 0:512], pB[:, 512:1024]
        PC0, PC1 = pC[:, 0:512], pC[:, 512:1024]
        PD0, PD1 = pD[:, 0:512], pD[:, 512:1024]

        identf = T(top, "identf", [128, 128], F32)
        identb = T(top, "identb", [128, 128], BF16)
        onesf = T(top, "onesf", [128, 128], F32)
        onesb = T(top, "onesb", [128, 128], BF16)
        ropet = T(top, "ropet", [128, 32, 64], F32)
        io16 = T(top, "io16", [128, 16], F32)
        cact = T(top, "cact", [128, 8, 2], F32)
        CCt = T(top, "CCt", [128, 128], BF16)
        SCt = T(top, "SCt", [128, 128], BF16)
        neglam = T(top, "neglam", [128, 1], F32)
        sgcol = T(top, "sgcol", [128, 1], F32)
        lamt = T(top, "lamt", [128, 2], F32)
        lpb = T(top, "lpb", [128, 256], F32)
        lj = T(top, "lj", [128, 64], F32)

        DS(identf[:], identf_in, w=["identf"])
        vcopy(identb[:], identf[:], ["identf"], ["identb"])
        V(lambda e: e.memset(onesf[:], 1.0), w=["onesf"])
        V(lambda e: e.memset(onesb[:], 1.0), w=["onesb"])
        DS(ropet[:], rope_in.rearrange("(t p) c -> p t c", p=128), w=["ropet"])
        G(lambda e: e.iota(io16[:], pattern=[[1, 16]], base=0, channel_multiplier=0,
                           allow_small_or_imprecise_dtypes=True), w=["io16"])
        DS(cact[:], cT_in, w=["cact"])
        act(cact[:], cact[:], AF.Silu, ["cact"], ["cact"])
        DS(CCt[:], dftC[0], w=["CCt"])
        DS(SCt[:], dftC[1], w=["SCt"])
        XS = [("xs", i) for i in range(34)]
        DS(xs[0:CTX, :], ctx_in, w=XS[0:2])
        for q4 in range(4):
            DS(xs[CTX + q4 * 1024:CTX + (q4 + 1) * 1024, :], x_in[q4 * 1024:(q4 + 1) * 1024, :],
               w=XS[2 + q4 * 8:2 + (q4 + 1) * 8])

        def norm_mod(xtile, xkey, gsb, shb, mkeys, hT_out, hT_key, tp, tpkeys, sqj, ssq, h32,
                     h32key="h32", sqkey="sqj", rs="sqrt"):
            act(sqj[:], xtile, AF.Square, [xkey], [sqkey, "ssq"], accum=ssq[:])
            rstd_op(ssq[:], "ssq", 1.0 / D, rs)
            stt(h32[:], xtile, ssq[:, 0:1], gsb, ALU.mult, ALU.mult, [xkey, "ssq", mkeys[0]], [h32key])
            tt(h32[:], h32[:], shb, ALU.add, [h32key, mkeys[1]], [h32key])
            if isinstance(tp, list):
                for hf in range(2):
                    for k4 in range(4):
                        kc = hf * 4 + k4
                        tr(tp[hf][:, k4 * 128:(k4 + 1) * 128], h32[:, kc * 128:(kc + 1) * 128], identf[:],
                           [h32key, "identf"], [tpkeys[hf]])
                    vcopy(hT_out[:, hf * 4:(hf + 1) * 4, :], tp[hf].rearrange("p (k t) -> p k t", t=128),
                          [tpkeys[hf]], [hT_key])
                return
            for kc in range(8):
                tr(tp[:, kc * 128:(kc + 1) * 128], h32[:, kc * 128:(kc + 1) * 128], identf[:],
                   [h32key, "identf"], tpkeys)
            act(hT_out, tp[:, 0:1024].rearrange("p (k t) -> p k t", t=128), AF.Copy, tpkeys, [hT_key])

        def group_norm_rope(ps, pskey, gb, gbkey, nsq, ss8, kn, ra, rb_, outb, outkey, rope_tile, rs="sqrt"):
            act(nsq[:], ps, AF.Square, [pskey], ["nsq"])
            vreduce(ss8[:], nsq[:].rearrange("p (g d) -> p g d", d=64), ["nsq"], ["ss8"])
            rstd_op(ss8[:], "ss8", 1.0 / 64, rs)
            kn3 = kn[:].rearrange("p (g d) -> p g d", d=64)
            tt(kn3, ps.rearrange("p (g d) -> p g d", d=64), ss8[:].unsqueeze(2).to_broadcast([128, 8, 64]),
               ALU.mult, [pskey, "ss8"], ["kn"])
            if rope_tile is None:
                tt(outb[:].rearrange("p (g d) -> p g d", d=64), kn3, gb[:].unsqueeze(1).to_broadcast([128, 8, 64]),
                   ALU.mult, ["kn", gbkey], [outkey])
                return
            tt(kn3, kn3, gb[:].unsqueeze(1).to_broadcast([128, 8, 64]), ALU.mult, ["kn", gbkey], ["kn"])
            kn5 = kn[:].rearrange("p (g a h f) -> p g a h f", g=8, a=2, h=2, f=16)
            ob5 = outb[:].rearrange("p (g a h f) -> p g a h f", g=8, a=2, h=2, f=16)
            x1, x2 = kn5[:, :, :, 0, :], kn5[:, :, :, 1, :]
            cosb = ropet[:, rope_tile, 0:32].rearrange("p (a f) -> p a f", a=2).unsqueeze(1).to_broadcast([128, 8, 2, 16])
            sinb = ropet[:, rope_tile, 32:64].rearrange("p (a f) -> p a f", a=2).unsqueeze(1).to_broadcast([128, 8, 2, 16])
            ra4 = ra[:].rearrange("p (g a f) -> p g a f", g=8, a=2)
            rb4 = rb_[:].rearrange("p (g a f) -> p g a f", g=8, a=2)
            tt(ra4, x1, cosb, ALU.mult, ["kn", "ropet"], ["ra"])
            tt(rb4, x2, sinb, ALU.mult, ["kn", "ropet"], ["rb"])
            tt(ob5[:, :, :, 0, :], ra4, rb4, ALU.subtract, ["ra", "rb"], [outkey])
            tt(ra4, x2, cosb, ALU.mult, ["kn", "ropet"], ["ra"])
            tt(rb4, x1, sinb, ALU.mult, ["kn", "ropet"], ["rb"])
            tt(ob5[:, :, :, 1, :], ra4, rb4, ALU.add, ["ra", "rb"], [outkey])

        def load_bcast(tile, j, r, key):
            DS(tile[:], der[r, j:j + 1, :].partition_broadcast(128), r=["der"], w=[key])

        for L in range(depth):
            last = (L == total_depth - 1)
            lam_init = 0.8 - 0.6 * math.exp(-0.3 * L)
            groups = []
            if not last:
                groups.append((0, 256, True))
            for g8 in range(8):
                groups.append((CTX + g8 * 512, 512, False))

            em.barrier_all()
            with ExitStack() as s0:
                wada = [T(s0, f"wada{i}", [128, 8, 512], F32) for i in range(2)]
                badat = [T(s0, f"badat{i}", [2, 512], F32) for i in range(2)]
                gch = [T(s0, f"gch{i}", [2, 512], F32) for i in range(2)]
                modrow = [T(s0, f"modrow{i}", [2, 512], F32) for i in range(2)]
                jmap = {0: 1, 1: 0, 2: 2, 3: 4, 4: 3, 5: 5}
                for nt in range(12):
                    b = nt % 2
                    part, half = nt // 2, nt % 2
                    DS(wada[b][:], w_ada[L][:, nt * 512:(nt + 1) * 512].rearrange("(kc p) n -> p kc n", p=128),
                       w=[("wada", b)])
                    DS(badat[b][:], b_ada[L:L + 1, nt * 512:(nt + 1) * 512].partition_broadcast(2), w=[("badat", b)])
                    for kc in range(8):
                        mm(PA0[0:2, :], cact[:, kc, :], wada[b][:, kc, :], kc == 0, kc == 7,
                           [("wada", b), "cact"], ["pA0"])
                    tt(modrow[b][:], PA0[0:2, :], badat[b][:], ALU.add, ["pA0", ("badat", b)], [("modrow", b)])
                    if part in (1, 4):
                        ng = norm1_g if part == 1 else norm2_g
                        DS(gch[b][:], ng[L:L + 1, half * 512:(half + 1) * 512].partition_broadcast(2), w=[("gch", b)])
                        stt(modrow[b][:], modrow[b][:], 1.0, gch[b][:], ALU.add, ALU.mult,
                            [("modrow", b), ("gch", b)], [("modrow", b)])
                    DG(der[:, jmap[part], half * 512:(half + 1) * 512], modrow[b][:], r=[("modrow", b)], w=["der"])
                DS(lpb[:], lam_params[L:L + 1, :].partition_broadcast(128), w=["lpb"])
                stt(lj[:], lpb[:, 0:64], 1.0, lpb[:, 64:128], ALU.mult, ALU.mult, ["lpb"], ["lj", "lamt"], accum=lamt[:, 0:1])
                stt(lj[:], lpb[:, 128:192], 1.0, lpb[:, 192:256], ALU.mult, ALU.mult, ["lpb"], ["lj", "lamt"], accum=lamt[:, 1:2])
                act(lamt[:], lamt[:], AF.Exp, ["lamt"], ["lamt"])
                tt(neglam[:], lamt[:, 1:2], lamt[:, 0:1], ALU.subtract, ["lamt"], ["neglam"])
                ts(neglam[:], neglam[:], -lam_init, None, ALU.add, None, ["neglam"], ["neglam"])
                DS(sgcol[:], subln_c[L], w=["sgcol"])
                ts(sgcol[:], sgcol[:], 1.0 - lam_init, None, ALU.mult, None, ["sgcol"], ["sgcol"])
            em.barrier_all()
            if stop_after == "P0":
                break

            with ExitStack() as sZ:
                Zs = T(sZ, "Zs", [128, 34, 512], BF16)
                gs1b = [T(sZ, f"gs1b{r}", [128, D], F32) for r in range(2)]
                sh1b = [T(sZ, f"sh1b{r}", [128, D], F32) for r in range(2)]
                for r in range(2):
                    load_bcast(gs1b[r], 0, r, ("gs1b", r))
                    load_bcast(sh1b[r], 1, r, ("sh1b", r))
                with ExitStack() as sKV:
                    KT = T(sKV, "KT", [128, 4, NTOK], BF16)
                    Vs = T(sKV, "Vs", [128, 34, 512], BF16)
                    xt = [T(sKV, f"xt{i}", [128, D], F32) for i in range(2)]
                    sqj = T(sKV, "sqj", [128, D], F32)
                    ssq = T(sKV, "ssq", [128, 1], F32)
                    h32 = T(sKV, "h32", [128, D], F32)
                    nsq = T(sKV, "nsq", [128, 512], F32)
                    ss8 = T(sKV, "ss8", [128, 8], F32)
                    kn = T(sKV, "kn", [128, 512], F32)
                    ra = T(sKV, "ra", [128, 256], F32)
                    rb_ = T(sKV, "rb", [128, 256], F32)
                    kb = T(sKV, "kb", [128, 512], BF16)
                    kgb = T(sKV, "kgb", [128, 64], F32)
                    qgb = T(sKV, "qgb", [128, 64], F32)
                    DS(kgb[:], k_norm_g[L:L + 1, :].partition_broadcast(128), w=["kgb"])
                    DS(qgb[:], q_norm_g[L:L + 1, :].partition_broadcast(128), w=["qgb"])
                    ts(qgb[:], qgb[:], 0.125, None, ALU.mult, None, ["qgb"], ["qgb"])
                    with ExitStack() as s1:
                        w1 = T(s1, "w1", [128, 8, 1536], BF16)
                        hT1 = [T(s1, f"hT1_{i}", [128, 8, 128], BF16) for i in range(2)]
                        wsrc = w_in[L].rearrange("(kc p) n -> p kc n", p=128)
                        DG(w1[:, :, 0:512], wsrc[:, :, 512:1024], w=["w1"])
                        DG(w1[:, :, 512:1024], wsrc[:, :, 1024:1536], w=["w1"])
                        DG(w1[:, :, 1024:1536], wsrc[:, :, 2560:3072], w=["w1"])
                        DS(xt[0][:], xs[0:128, :], r=[XS[0]], w=[("xt", 0)])
                        for i in range(34):
                            b = i % 2
                            r = 1 if i < 2 else 0
                            if i + 1 < 34:
                                DS(xt[1 - b][:], xs[(i + 1) * 128:(i + 2) * 128, :], r=[XS[i + 1]], w=[("xt", 1 - b)])
                            tp, tpk = (pA, ["pA0", "pA1"]) if b == 0 else (pD, ["pD0", "pD1"])
                            norm_mod(xt[b][:], ("xt", b), gs1b[r][:], sh1b[r][:], [("gs1b", r), ("sh1b", r)],
                                     hT1[b][:], ("hT1", b), tp, tpk, sqj, ssq, h32)
                            for nt, (pb_, pk) in enumerate([(PB0, "pB0"), (PB1, "pB1"), (PC0, "pC0")]):
                                for kc in range(8):
                                    mm(pb_, hT1[b][:, kc, :], w1[:, kc, nt * 512:(nt + 1) * 512], kc == 0, kc == 7,
                                       [("hT1", b), "w1"], [pk])
                            group_norm_rope(PB0, "pB0", kgb, "kgb", nsq, ss8, kn, ra, rb_, kb, "kb",
                                            None if i < 2 else i - 2)
                            pcv = PC1.bitcast(BF16)
                            for h in range(4):
                                tr(pcv[:, h * 128:(h + 1) * 128], kb[:, h * 128:(h + 1) * 128], identb[:],
                                   ["kb", "identb"], ["pC1"])
                            act(KT[:, :, i * 128:(i + 1) * 128], pcv[:, 0:512].rearrange("p (h t) -> p h t", t=128),
                                AF.Copy, ["pC1"], [("KT", i)])
                            act(Vs[:, i, :], PB1, AF.Copy, ["pB1"], [("Vs", i)])
                            vcopy(Zs[:, i, :], PC0, ["pC0"], [("Zs", i)])
                    em.barrier_all()
                    if stop_after == "P1":
                        break
                    with ExitStack() as s2:
                        wq = T(s2, "wq", [128, 8, 512], BF16)
                        hT = T(s2, "hT", [128, 8, 128], BF16)
                        QT = [T(s2, f"QT{i}", [128, 4, 512], BF16) for i in range(2)]
                        PTp = [T(s2, f"PTp{i}", [128, 1024], BF16) for i in range(3)]
                        zacc = [T(s2, f"zacc{m}", [128, 512], BF16) for m in range(2)]
                        rz = T(s2, "rz", [128, 512], F32)
                        O0 = T(s2, "O0", [128, 512], F32)
                        O1 = T(s2, "O1", [128, 512], F32)
                        att = T(s2, "att", [128, 512], F32)
                        asq = T(s2, "asq", [128, 512], F32)
                        rst = T(s2, "rst", [128, 512], F32)
                        aTb = [T(s2, f"aTb{i}", [128, 512], BF16) for i in range(2)]
                        DG(wq[:], w_in[L].rearrange("(kc p) n -> p kc n", p=128)[:, :, 0:512], w=["wq"])
                        cv_list = []
                        for c8 in range(8):
                            e0 = c8 * 2048
                            cv_list.append((UVb[L * NEXP + e0:L * NEXP + e0 + 2048, 0:D], peer_u[L][e0:e0 + 2048, :], ("UVb", L, c8, 0)))
                            cv_list.append((UVb[L * NEXP + e0:L * NEXP + e0 + 2048, D:2 * D], peer_v[L][e0:e0 + 2048, :], ("UVb", L, c8, 1)))
                        cv_state = {"i": 0}

                        def prep(gi):
                            row0, N, is_ctx = groups[gi]
                            r = 1 if is_ctx else 0
                            qt = QT[gi % 2]
                            qk = ("QT", gi % 2)
                            for j in range(N // 128):
                                ti = row0 // 128 + j
                                DS(xt[0][:], xs[ti * 128:(ti + 1) * 128, :], r=[XS[ti]], w=[("xt", 0)])
                                norm_mod(xt[0][:], ("xt", 0), gs1b[r][:], sh1b[r][:], [("gs1b", r), ("sh1b", r)],
                                         hT[:], "hT", [PB1, PC1], ["pB1", "pC1"], sqj, ssq, h32, rs="ln")
                                yield
                                for kc in range(8):
                                    mm(PB1, hT[:, kc, :], wq[:, kc, :], kc == 0, kc == 7, ["hT", "wq"], ["pB1"])
                                yield
                                group_norm_rope(PB1, "pB1", qgb, "qgb", nsq, ss8, kn, ra, rb_, kb, "kb",
                                                None if is_ctx else ti - 2, rs="ln")
                                yield
                                pcv = PC1.bitcast(BF16)
                                for h in range(4):
                                    tr(pcv[:, h * 128:(h + 1) * 128], kb[:, h * 128:(h + 1) * 128], identb[:],
                                       ["kb", "identb"], ["pC1"])
                                vcopy(qt[:, :, j * 128:(j + 1) * 128], pcv[:, 0:512].rearrange("p (h t) -> p h t", t=128),
                                      ["pC1"], [qk])
                                yield

                        def attend(gi, nxt):
                            row0, N, is_ctx = groups[gi]
                            qt = QT[gi % 2]
                            qk = ("QT", gi % 2)
                            if not is_ctx:
                                for _ in range(2):
                                    o_, i_, k_ = cv_list[cv_state["i"]]
                                    DG(o_, i_, w=[k_, ("cv", cv_state["i"] % 4)])
                                    cv_state["i"] += 1
                            kchunks = [0, 1] if is_ctx else list(range(34))
                            nch = len(kchunks)
                            sb = [(pD, ["pD0", "pD1"]), (pA, ["pA0", "pA1"])]
                            obank = [(PB0, "pB0"), (PC0, "pC0")]
                            step = 0
                            for h in range(4):
                                def s_mm(ci):
                                    kc = kchunks[ci]
                                    pS, pSk = sb[ci % 2]
                                    for m in range(2):
                                        lo, hi = m * 64, (m + 1) * 64
                                        mm(pS[:, m * 512:m * 512 + N], KT[lo:hi, h, kc * 128:(kc + 1) * 128], qt[lo:hi, h, 0:N],
                                           True, True, [("KT", kc), qk], [pSk[m]])

                                s_mm(0)
                                for ci, kc in enumerate(kchunks):
                                    first, lastc = ci == 0, ci == nch - 1
                                    if ci + 1 < nch:
                                        s_mm(ci + 1)
                                    pS, pSk = sb[ci % 2]
                                    pt = PTp[ci % 3]
                                    ptk = ("PT", ci % 3)
                                    act(pt[:].rearrange("p (m n) -> p m n", m=2)[:, :, 0:N],
                                        pS[:, 0:1024].rearrange("p (m n) -> p m n", m=2)[:, :, 0:N], AF.Exp, pSk, [ptk])
                                    for m in range(2):
                                        pO, pOk = obank[m]
                                        mm(pO[:, 0:N], Vs[:, kc, h * 128:(h + 1) * 128], pt[:, m * 512:m * 512 + N], first, lastc,
                                           [("Vs", kc), ptk], [pOk])
                                    for m in range(2):
                                        eng = "vector"
                                        if first:
                                            em.op(eng, lambda e, o_=zacc[m][:, 0:N], i_=pt[:, m * 512:m * 512 + N]: e.tensor_copy(out=o_, in_=i_),
                                                  [ptk], [("zacc", m)])
                                        else:
                                            tt(zacc[m][:, 0:N], zacc[m][:, 0:N], pt[:, m * 512:m * 512 + N], ALU.add,
                                               [("zacc", m), ptk], [("zacc", m)], eng=eng)
                                    step += 1
                                    if nxt is not None and step % 4 == 0:
                                        next(nxt, None)
                                mm(PA0[:, 0:N], onesb[:], zacc[0][:, 0:N], True, True, ["onesb", ("zacc", 0)], ["pA0"])
                                mm(PA1[:, 0:N], onesb[:], zacc[1][:, 0:N], True, True, ["onesb", ("zacc", 1)], ["pA1"])
                                recip(rz[:, 0:N], PA0[:, 0:N], ["pA0"], ["rz"])
                                tt(O0[:, 0:N], PB0[:, 0:N], rz[:, 0:N], ALU.mult, ["pB0", "rz"], ["O0"])
                                recip(rz[:, 0:N], PA1[:, 0:N], ["pA1"], ["rz"])
                                tt(O1[:, 0:N], PC0[:, 0:N], rz[:, 0:N], ALU.mult, ["pC0", "rz"], ["O1"])
                                stt(att[:, 0:N], O1[:, 0:N], neglam[:, 0:1], O0[:, 0:N], ALU.mult, ALU.add,
                                    ["O0", "O1", "neglam"], ["att"])
                                tt(asq[:, 0:N], att[:, 0:N], att[:, 0:N], ALU.mult, ["att"], ["asq"])
                                mm(PA0[:, 0:N], onesf[:], asq[:, 0:N], True, True, ["onesf", "asq"], ["pA0"])
                                act(rst[:, 0:N], PA0[:, 0:N], AF.Ln, ["pA0"], ["rst"], bias=EPS, scale=1.0 / 128)
                                act(rst[:, 0:N], rst[:, 0:N], AF.Exp, ["rst"], ["rst"], scale=-0.5)
                                ab = aTb[h % 2]
                                stt(ab[:, 0:N], att[:, 0:N], sgcol[:, 0:1], rst[:, 0:N], ALU.mult, ALU.mult,
                                    ["att", "sgcol", "rst"], [("aTb", h % 2)])
                                DG(aT_s[h, :, row0:row0 + N], ab[:, 0:N], r=[("aTb", h % 2)], w=[("aT", gi)])
                            if nxt is not None:
                                for _ in nxt:
                                    pass

                        for _ in prep(0):
                            pass
                        for gi in range(len(groups)):
                            attend(gi, prep(gi + 1) if gi + 1 < len(groups) else None)
                    em.barrier_all()
                if stop_after == "P2a":
                    break
                with ExitStack() as s3:
                    tabC = [T(s3, f"tabC{i}", [128, 32, 512], BF16) for i in range(2)]
                    tabS = [T(s3, f"tabS{i}", [128, 32, 512], BF16) for i in range(2)]
                    Wc = T(s3, "Wc", [128, 512], BF16)
                    Ws = T(s3, "Ws", [128, 512], BF16)
                    fTb = [T(s3, f"fTb{i}", [128, 512], BF16) for i in range(2)]

                    def load_tabs(gi):
                        row0, N, is_ctx = groups[gi]
                        tb_ = gi % 2
                        if is_ctx:
                            DS(tabC[tb_][:, 0:2, 0:256], dft256[0].rearrange("(tc p) n -> p tc n", p=128), w=[("tabC", tb_)])
                            DS(tabS[tb_][:, 0:2, 0:256], dft256[1].rearrange("(tc p) n -> p tc n", p=128), w=[("tabS", tb_)])
                        else:
                            t0 = row0 - CTX
                            for q4 in range(4):
                                DS(tabC[tb_][:, q4 * 8:(q4 + 1) * 8, :],
                                   dftL[0][q4 * 1024:(q4 + 1) * 1024, t0:t0 + 512].rearrange("(tc p) n -> p tc n", p=128),
                                   w=[("tabC", tb_)])
                                DS(tabS[tb_][:, q4 * 8:(q4 + 1) * 8, :],
                                   dftL[1][q4 * 1024:(q4 + 1) * 1024, t0:t0 + 512].rearrange("(tc p) n -> p tc n", p=128),
                                   w=[("tabS", tb_)])

                    load_tabs(0)
                    for gi, (row0, N, is_ctx) in enumerate(groups):
                        tb_ = gi % 2
                        if gi + 1 < len(groups):
                            load_tabs(gi + 1)
                        ntc, z0 = (2, 0) if is_ctx else (32, 2)
                        for g in range(4):
                            for tcx in range(ntc):
                                mm(PA0[:, 0:N], Zs[:, z0 + tcx, g * 128:(g + 1) * 128], tabC[tb_][:, tcx, 0:N],
                                   tcx == 0, tcx == ntc - 1, [("Zs", z0 + tcx), ("tabC", tb_)], ["pA0"])
                            for tcx in range(ntc):
                                mm(PA1[:, 0:N], Zs[:, z0 + tcx, g * 128:(g + 1) * 128], tabS[tb_][:, tcx, 0:N],
                                   tcx == 0, tcx == ntc - 1, [("Zs", z0 + tcx), ("tabS", tb_)], ["pA1"])
                            act(Wc[:, 0:N], PA0[:, 0:N], AF.Copy, ["pA0"], ["Wc"])
                            vcopy(Ws[:, 0:N], PA1[:, 0:N], ["pA1"], ["Ws"])
                            pF, pFk = (PB0, "pB0") if g % 2 == 0 else (PB1, "pB1")
                            mm(pF[:, 0:N], CCt[:], Wc[:, 0:N], True, False, ["CCt", "Wc"], [pFk])
                            mm(pF[:, 0:N], SCt[:], Ws[:, 0:N], False, True, ["SCt", "Ws"], [pFk])
                            fb = fTb[g % 2]
                            act(fb[:, 0:N], pF[:, 0:N], AF.Copy, [pFk], [("fTb", g % 2)])
                            DS(fT_s[g, :, row0:row0 + N], fb[:, 0:N], r=[("fTb", g % 2)], w=[("fT", gi)])
                em.barrier_all()
            if stop_after == "P2b":
                break
            with ExitStack() as s4:
                gs1b = [T(s4, f"c_gs1b{r}", [128, D], F32) for r in range(2)]
                sh1b = [T(s4, f"c_sh1b{r}", [128, D], F32) for r in range(2)]
                g1b = [T(s4, f"c_g1b{r}", [128, D], F32) for r in range(2)]
                for r in range(2):
                    load_bcast(gs1b[r], 0, r, ("gs1b", r))
                    load_bcast(sh1b[r], 1, r, ("sh1b", r))
                    load_bcast(g1b[r], 2, r, ("g1b", r))
                xg = T(s4, "xg", [128, 4, D], F32)
                sqj = T(s4, "c_sqj", [128, D], F32)
                ssq = T(s4, "c_ssq", [128, 1], F32)
                h32 = T(s4, "c_h32", [128, D], F32)
                hT = T(s4, "c_hT", [128, 8, 512], BF16)
                wzz = T(s4, "wzz", [128, 8, 1024], BF16)
                wbr = T(s4, "wbr", [128, 12, D], BF16)
                wot = T(s4, "wot", [128, 8, D], BF16)
                wgl = [T(s4, f"wgl{i}", [128, 8, 3, 128], BF16) for i in range(2)]
                wsT = T(s4, "wsT", [128, 4, 128], BF16)
                bsc = T(s4, "bsc", [128, 4], F32)
                bgc = T(s4, "bgc", [128, 24], F32)
                lngb = T(s4, "lngb", [128, 512], F32)
                u_t = T(s4, "u_t", [128, 512], F32)
                gv = T(s4, "gv", [128, 512], F32)
                bst = T(s4, "bst", [128, 6], F32)
                bag = T(s4, "bag", [128, 2], F32)
                vb = T(s4, "vb", [128, 512], BF16)
                mb = T(s4, "mb", [128, 512], BF16)
                mT = T(s4, "mT", [128, 4, 512], BF16)
                aT = T(s4, "aT", [128, 4, 512], BF16)
                fT = T(s4, "fT", [128, 4, 512], BF16)
                zT = T(s4, "zT", [128, 8, 512], BF16)
                gsig = [T(s4, f"gsig{i}", [128, 512], F32) for i in range(2)]
                zacc = T(s4, "zacc", [128, 512], F32)
                ztmp = T(s4, "ztmp", [128, 512], F32)
                otmp = T(s4, "otmp", [128, 512], F32)
                xnew = [T(s4, f"xnew{i}", [128, D], F32) for i in range(2)]
                wsrc = w_in[L].rearrange("(kc p) n -> p kc n", p=128)
                DG(wzz[:, :, 0:512], wsrc[:, :, 1536:2048], w=["wzz"])
                DG(wzz[:, :, 512:1024], wsrc[:, :, 2048:2560], w=["wzz"])
                for n3 in range(3):
                    for hf in range(2):
                        DG(wbr[:, n3 * 4:(n3 + 1) * 4, hf * 512:(hf + 1) * 512],
                           w_branch[L, n3].rearrange("(wc p) d -> p wc d", p=128)[:, :, hf * 512:(hf + 1) * 512], w=["wbr"])
                for hf in range(2):
                    DG(wot[:, :, hf * 512:(hf + 1) * 512],
                       w_out[L].rearrange("(kc p) n -> p kc n", p=128)[:, :, hf * 512:(hf + 1) * 512], w=["wot"])
                DG(wsT[:], cmws_T[L], w=["wsT"])
                DS(bsc[:], cmbs_c[L], w=["bsc"])
                DS(bgc[:], bgate_c[L], w=["bgc"])
                DS(lngb[:], cm_ln_g[L:L + 1, :].partition_broadcast(128), w=["lngb"])
                wgl_i = 0
                for gi, (row0, N, is_ctx) in enumerate(groups):
                    r = 1 if is_ctx else 0
                    nj = N // 128
                    DS(aT[:, :, 0:N], aT_s[:, :, row0:row0 + N].rearrange("h p t -> p h t"), r=[("aT", gi)], w=["aTt"])
                    DS(fT[:, :, 0:N], fT_s[:, :, row0:row0 + N].rearrange("h p t -> p h t"), r=[("fT", gi)], w=["fTt"])
                    for j in range(nj):
                        ti = row0 // 128 + j
                        DS(xg[:, j, :], xs[ti * 128:(ti + 1) * 128, :], r=[XS[ti]], w=[("xg", j)])
                        norm_mod(xg[:, j, :], ("xg", j), gs1b[r][:], sh1b[r][:], [("gs1b", r), ("sh1b", r)],
                                 hT[:, :, j * 128:(j + 1) * 128], "hT", pA, ["pA0", "pA1"], sqj, ssq, h32)
                        for nt, (pb_, pk) in enumerate([(PB0, "pB0"), (PB1, "pB1")]):
                            for kc in range(8):
                                mm(pb_, hT[:, kc, j * 128:(j + 1) * 128], wzz[:, kc, nt * 512:(nt + 1) * 512],
                                   kc == 0, kc == 7, ["hT", "wzz"], [pk])
                        act(u_t[:], PB0, AF.Gelu, ["pB0"], ["u_t"])
                        act(gv[:], PB1, AF.Gelu, ["pB1"], ["gv"])
                        V(lambda e, bst=bst, gv=gv: e.bn_stats(out=bst[:], in_=gv[:]), ["gv"], ["bst"])
                        V(lambda e, bst=bst, bag=bag: e.bn_aggr(out=bag[:], in_=bst[:]), ["bst"], ["bag"])
                        act(bag[:, 1:2], bag[:, 1:2], AF.Sqrt, ["bag"], ["bag"], bias=EPS, scale=1.0)
                        recip(bag[:, 1:2], bag[:, 1:2], ["bag"], ["bag"])
                        ts(gv[:], gv[:], bag[:, 0:1], bag[:, 1:2], ALU.subtract, ALU.mult, ["gv", "bag"], ["gv"])
                        tt(vb[:], gv[:], lngb[:], ALU.mult, ["gv", "lngb"], ["vb"])
                        for g in range(4):
                            mm(PC0[:, g * 128:(g + 1) * 128], wsT[:, g, :], vb[:, g * 128:(g + 1) * 128], True, True,
                               ["wsT", "vb"], ["pC0"])
                        for g in range(4):
                            stt(mb[:, g * 128:(g + 1) * 128], PC0[:, g * 128:(g + 1) * 128], bsc[:, g:g + 1],
                                u_t[:, g * 128:(g + 1) * 128], ALU.add, ALU.mult, ["pC0", "bsc", "u_t"], ["mb"])
                        pcv = PC1.bitcast(BF16)
                        for g in range(4):
                            tr(pcv[:, g * 128:(g + 1) * 128], mb[:, g * 128:(g + 1) * 128], identb[:],
                               ["mb", "identb"], ["pC1"])
                        act(mT[:, :, j * 128:(j + 1) * 128], pcv[:, 0:512].rearrange("p (h t) -> p h t", t=128),
                            AF.Copy, ["pC1"], ["mT"])
                    brs = [(aT, "aTt"), (mT, "mT"), (fT, "fTt")]
                    for dc in range(8):
                        wb_ = wgl_i % 2
                        wgl_i += 1
                        for n3 in range(3):
                            c0 = 3072 + n3 * 1024 + dc * 128
                            DG(wgl[wb_][:, :, n3, :], wsrc[:, :, c0:c0 + 128], w=[("wgl", wb_)])
                        for n3 in range(3):
                            brT, brk = brs[n3]
                            par = (dc * 3 + n3) % 2
                            pY, pYk = (PD0, "pD0") if par == 0 else (PC0, "pC0")
                            pG, pGk = (PD1, "pD1") if par == 0 else (PC1, "pC1")
                            for wc in range(4):
                                mm(pY[:, 0:N], wbr[:, n3 * 4 + wc, dc * 128:(dc + 1) * 128], brT[:, wc, 0:N],
                                   wc == 0, wc == 3, ["wbr", brk], [pYk])
                            for kc in range(8):
                                mm(pG[:, 0:N], wgl[wb_][:, kc, n3, :], hT[:, kc, 0:N], kc == 0, kc == 7,
                                   [("wgl", wb_), "hT"], [pGk])
                            gs_ = gsig[par]
                            act(gs_[:, 0:N], pG[:, 0:N], AF.Sigmoid, [pGk, "bgc"], [("gsig", par)],
                                bias=bgc[:, n3 * 8 + dc:n3 * 8 + dc + 1])
                            if n3 == 0:
                                tt(zacc[:, 0:N], pY[:, 0:N], gs_[:, 0:N], ALU.mult, [pYk, ("gsig", par)], ["zacc"])
                            elif n3 == 1:
                                tt(ztmp[:, 0:N], pY[:, 0:N], gs_[:, 0:N], ALU.mult, [pYk, ("gsig", par)], ["ztmp"])
                                tt(zacc[:, 0:N], zacc[:, 0:N], ztmp[:, 0:N], ALU.add, ["zacc", "ztmp"], ["zacc"])
                            else:
                                tt(ztmp[:, 0:N], pY[:, 0:N], gs_[:, 0:N], ALU.mult, [pYk, ("gsig", par)], ["ztmp"])
                                tt(zT[:, dc, 0:N], zacc[:, 0:N], ztmp[:, 0:N], ALU.add, ["zacc", "ztmp"], ["zT"])
                    for j in range(nj):
                        ti = row0 // 128 + j
                        xn_ = xnew[j % 2]
                        for hf, (pb_, pk) in enumerate([(PB0, "pB0"), (PB1, "pB1")]):
                            for dc in range(8):
                                mm(pb_, zT[:, dc, j * 128:(j + 1) * 128], wot[:, dc, hf * 512:(hf + 1) * 512],
                                   dc == 0, dc == 7, ["zT", "wot"], [pk])
                            tt(otmp[:], pb_, g1b[r][:, hf * 512:(hf + 1) * 512], ALU.mult, [pk, ("g1b", r)], ["otmp"])
                            tt(xn_[:, hf * 512:(hf + 1) * 512], otmp[:], xg[:, j, hf * 512:(hf + 1) * 512], ALU.add,
                               ["otmp", ("xg", j)], [("xnew", j % 2)])
                        DG(xs[ti * 128:(ti + 1) * 128, :], xn_[:], r=[("xnew", j % 2)], w=[XS[ti]])
            em.barrier_all()
            if stop_after == "P2c":
                break
            with ExitStack() as s5:
                gs2b = T(s5, "gs2b", [128, D], F32)
                sh2b = T(s5, "sh2b", [128, D], F32)
                g2b = [T(s5, f"g2b{r}", [128, D], F32) for r in range(2)]
                nr = 1 if last else 2
                for r in range(nr):
                    load_bcast(g2b[r], 5, r, ("g2b", r))
                xt = [T(s5, f"p_xt{i}", [128, D], F32) for i in range(2)]
                ssq = T(s5, "p_ssq", [128, 1], F32)
                h32 = [T(s5, f"p_h32_{i}", [128, D], F32) for i in range(2)]
                hT2 = T(s5, "hT2", [128, 8, 128], BF16)
                wpq = T(s5, "wpq", [128, 8, 2048], BF16)
                keysT = T(s5, "keysT", [128, 16, 128], BF16)
                ss16 = T(s5, "ss16", [128, 16], F32)
                qn = T(s5, "qn", [128, 2048], BF16)
                qnT = T(s5, "qnT", [128, 16, 128], BF16)
                s_sb = T(s5, "s_sb", [128, 2048], F32)
                s2x = [T(s5, f"s2_{i}", [128, 128], F32) for i in range(2)]
                ta = T(s5, "ta", [128, 8, 16], F32)
                tb = T(s5, "tb", [128, 8, 16], F32)
                tcv = T(s5, "tcv", [128, 8, 16], F32)
                ia = T(s5, "ia", [128, 8, 16], U32)
                ib = T(s5, "ib", [128, 8, 16], U32)
                pos = T(s5, "pos", [128, 8, 16], U32)
                k1 = T(s5, "k1", [128, 8, 16], U32)
                k2 = T(s5, "k2", [128, 8, 16], U32)
                k1f = T(s5, "k1f", [128, 8, 16], F32)
                k2f = T(s5, "k2f", [128, 8, 16], F32)
                iaf = T(s5, "iaf", [128, 8, 16], F32)
                ibf = T(s5, "ibf", [128, 8, 16], F32)
                isel = T(s5, "isel", [128, 8, 16], F32)
                jsel = T(s5, "jsel", [128, 8, 16], F32)
                idxf = T(s5, "idxf", [128, 128], F32)
                idxu = [T(s5, f"idxu{i}", [128, 128], U32) for i in range(2)]
                cand = T(s5, "cand", [128, 16, 16], F32)
                cand2 = T(s5, "cand2", [128, 256], F32)
                ee = T(s5, "ee", [128, 8, 16], F32)
                zz = T(s5, "zz", [128, 8], F32)
                gw = [T(s5, f"gw{i}", [128, 128], F32) for i in range(2)]
                actv = T(s5, "actv", [128, 128], F32)
                gact = T(s5, "gact", [128, 128], F32)
                xo = T(s5, "xo", [128, D], F32)
                junk = T(s5, "junk", [128, D], F32)
                gw2 = T(s5, "gw2", [128, 128], F32)
                dgt = [T(s5, f"dgt{i}", [128, 128], BF16) for i in range(4)]
                rem = int(nc.sbuf_bytes_remaining)
                NS = min(24, (rem - 3072) // 4096)
                assert NS >= 16, f"PEER gather pipeline needs >= 16 slots, got {NS} (sbuf remaining {rem})"
                if L == 0:
                    print("PEER gather slots:", NS)
                gbuf = [T(s5, f"gbuf{i}", [128, 2 * D], BF16) for i in range(NS)]
                uvkeys = [("UVb", L, c8, uv) for c8 in range(8) for uv in range(2)]
                for q4 in range(4):
                    DG(wpq[:, :, q4 * 512:(q4 + 1) * 512],
                       peer_w_q[L].rearrange("(kc p) n -> p kc n", p=128)[:, :, q4 * 512:(q4 + 1) * 512], w=["wpq"])
                DG(keysT[:], keysT_in[L], w=["keysT"])
                tiles = list(range(2, 34)) if last else list(range(34))
                qbanks = [(PC0, "pC0"), (PC1, "pC1"), (PD0, "pD0"), (PD1, "pD1")]
                state = {"dcnt": 0, "gcnt": 0, "mod_r": None}

                def front(tix):
                    ti = tiles[tix]
                    b = tix % 2
                    r = 1 if ti < 2 else 0
                    if state["mod_r"] != r:
                        load_bcast(gs2b, 3, r, "gs2b")
                        load_bcast(sh2b, 4, r, "sh2b")
                        state["mod_r"] = r
                    xk = ("xt", b)
                    hk32 = ("h32", b)
                    h32b = h32[b]
                    act(junk[:], xt[b][:], AF.Square, [xk], ["junk", "ssq"], accum=ssq[:])
                    act(ssq[:], ssq[:], AF.Sqrt, ["ssq"], ["ssq"], bias=EPS, scale=1.0 / D)
                    yield
                    recip(ssq[:], ssq[:], ["ssq"], ["ssq"])
                    stt(h32b[:], xt[b][:], ssq[:, 0:1], gs2b[:], ALU.mult, ALU.mult, [xk, "ssq", "gs2b"], [hk32])
                    tt(h32b[:], h32b[:], sh2b[:], ALU.add, [hk32, "sh2b"], [hk32])
                    yield
                    for kc in range(8):
                        tr(pA[:, kc * 128:(kc + 1) * 128], h32b[:, kc * 128:(kc + 1) * 128], identf[:],
                           [hk32, "identf"], ["pA0", "pA1"])
                    yield
                    act(hT2[:], pA[:, 0:1024].rearrange("p (k t) -> p k t", t=128), AF.Copy, ["pA0", "pA1"], ["hT2"])
                    yield
                    for nt, (pb_, pk) in enumerate(qbanks):
                        for kc in range(8):
                            mm(pb_, hT2[:, kc, :], wpq[:, kc, nt * 512:(nt + 1) * 512], kc == 0, kc == 7,
                               ["hT2", "wpq"], [pk])
                    yield
                    for nt, (pb_, pk) in enumerate(qbanks):
                        act(s_sb[:, nt * 512:(nt + 1) * 512], pb_, AF.Square, [pk], ["s_sb"])
                    yield
                    vreduce(ss16[:], s_sb[:].rearrange("p (g d) -> p g d", d=128), ["s_sb"], ["ss16"])
                    yield
                    act(ss16[:], ss16[:], AF.Sqrt, ["ss16"], ["ss16"], bias=EPS, scale=1.0 / 128)
                    yield
                    recip(ss16[:], ss16[:], ["ss16"], ["ss16"])
                    yield
                    for hp in range(16):
                        pb_, pk = qbanks[hp // 4]
                        act(qn[:, hp * 128:(hp + 1) * 128], pb_[:, (hp % 4) * 128:(hp % 4 + 1) * 128], AF.Copy,
                            [pk, "ss16"], ["qn"], scale=ss16[:, hp:hp + 1])
                    yield
                    pav = pA[:, 0:1024].bitcast(BF16)
                    for hp in range(16):
                        tr(pav[:, hp * 128:(hp + 1) * 128], qn[:, hp * 128:(hp + 1) * 128], identb[:],
                           ["qn", "identb"], ["pA0", "pA1"])
                    yield
                    act(qnT[:].rearrange("p h t -> p (h t)"), pav[:, 0:2048], AF.Copy, ["pA0", "pA1"], ["qnT"])
                    yield
                    for hp in range(16):
                        pb_, pk = qbanks[hp // 4]
                        mm(pb_[:, (hp % 4) * 128:(hp % 4 + 1) * 128], qnT[:, hp, :], keysT[:, hp, :], True, True,
                           ["qnT", "keysT"], [pk])
                    yield
                    for nt, (pb_, pk) in enumerate(qbanks):
                        act(s_sb[:, nt * 512:(nt + 1) * 512], pb_, AF.Copy, [pk], ["s_sb"])
                    yield
                    for h in range(8):
                        sides = []
                        for side, (tv, iv) in enumerate([(ta, ia), (tb, ib)]):
                            sv = s_sb[:, (2 * h + side) * 128:(2 * h + side + 1) * 128]
                            tk, ik = ("ta", "ia") if side == 0 else ("tb", "ib")
                            sides.append((tv, iv, sv, tk, ik, s2x[side], ("s2", side)))
                        for tv, iv, sv, tk, ik, s2_, s2k in sides:
                            vmax(tv[:, h, 0:8], sv, ["s_sb"], [tk])
                        for tv, iv, sv, tk, ik, s2_, s2k in sides:
                            vmatchrep(s2_[:], tv[:, h, 0:8], sv, ["s_sb", tk], [s2k])
                        for tv, iv, sv, tk, ik, s2_, s2k in sides:
                            vmaxidx(iv[:, h, 0:8], tv[:, h, 0:8], sv, ["s_sb", tk], [ik])
                        for tv, iv, sv, tk, ik, s2_, s2k in sides:
                            vmax(tv[:, h, 8:16], s2_[:], [s2k], [tk])
                        for tv, iv, sv, tk, ik, s2_, s2k in sides:
                            vmaxidx(iv[:, h, 8:16], tv[:, h, 8:16], s2_[:], [s2k, tk], [ik])
                        tt(cand[:], ta[:, h, :].unsqueeze(2).to_broadcast([128, 16, 16]),
                           tb[:, h, :].unsqueeze(1).to_broadcast([128, 16, 16]), ALU.add, ["ta", "tb"], ["cand"])
                        cf = cand[:].rearrange("p a b -> p (a b)")
                        vmax(tcv[:, h, 0:8], cf, ["cand"], ["tcv"])
                        vmaxidx(pos[:, h, 0:8], tcv[:, h, 0:8], cf, ["cand", "tcv"], ["pos"])
                        vmatchrep(cand2[:], tcv[:, h, 0:8], cf, ["cand", "tcv"], ["cand2"])
                        vmax(tcv[:, h, 8:16], cand2[:], ["cand2"], ["tcv"])
                        vmaxidx(pos[:, h, 8:16], tcv[:, h, 8:16], cand2[:], ["cand2", "tcv"], ["pos"])
                        yield
                    vsingle(k1[:], pos[:], 4, ALU.arith_shift_right, ["pos"], ["k1"])
                    vsingle(k2[:], pos[:], 15, ALU.bitwise_and, ["pos"], ["k2"])
                    vcopy(k1f[:], k1[:], ["k1"], ["k1f"])
                    vcopy(k2f[:], k2[:], ["k2"], ["k2f"])
                    vcopy(iaf[:], ia[:], ["ia"], ["iaf"])
                    vcopy(ibf[:], ib[:], ["ib"], ["ibf"])
                    iob = io16[:].unsqueeze(1).unsqueeze(1).to_broadcast([128, 8, 16, 16])
                    eq4v = s_sb[:].rearrange("p (h a b) -> p h a b", h=8, a=16)
                    for kf, kfk, ixf, ixk, osel, osk in [(k1f, "k1f", iaf, "iaf", isel, "isel"),
                                                         (k2f, "k2f", ibf, "ibf", jsel, "jsel")]:
                        tt(eq4v, kf[:].unsqueeze(3).to_broadcast([128, 8, 16, 16]), iob, ALU.is_equal,
                           [kfk, "io16"], ["s_sb"])
                        tt(eq4v, eq4v, ixf[:].unsqueeze(2).to_broadcast([128, 8, 16, 16]), ALU.mult,
                           ["s_sb", ixk], ["s_sb"])
                        vreduce(osel[:], eq4v, ["s_sb"], [osk])
                    yield
                    stt(idxf[:], isel[:].rearrange("p h k -> p (h k)"), 128.0, jsel[:].rearrange("p h k -> p (h k)"),
                        ALU.mult, ALU.add, ["isel", "jsel"], ["idxf"])
                    if L > 0:
                        ts(idxf[:], idxf[:], float(L * NEXP), None, ALU.add, None, ["idxf"], ["idxf"])
                    vcopy(idxu[b][:], idxf[:], ["idxf"], [("idxu", b)])
                    tt(ee[:], tcv[:], tcv[:, :, 0:1].to_broadcast([128, 8, 16]), ALU.subtract, ["tcv"], ["ee"])
                    yield
                    act(ee[:], ee[:], AF.Exp, ["ee"], ["ee"])
                    yield
                    vreduce(zz[:], ee[:], ["ee"], ["zz"])
                    recip(zz[:], zz[:], ["zz"], ["zz"])
                    tt(gw[b][:].rearrange("p (h k) -> p h k", k=16), ee[:], zz[:].unsqueeze(2).to_broadcast([128, 8, 16]),
                       ALU.mult, ["ee", "zz"], [("gw", b)])

                def back(tix, nxt):
                    ti = tiles[tix]
                    b = tix % 2
                    r = 1 if ti < 2 else 0

                    def stage1(bi, mid=None):
                        for q8 in range(8):
                            hk = bi * 8 + q8
                            sl = state["gcnt"] % NS
                            state["gcnt"] += 1
                            slots[hk] = sl
                            gather(gbuf[sl][:], UVb, idxu[b][:, hk:hk + 1], [("idxu", b)] + uvkeys, [("gb", sl)])
                        for q8 in range(8):
                            hk = bi * 8 + q8
                            sl = slots[hk]
                            stt(junk[:], gbuf[sl][:, 0:D], 1.0, h32[b][:], ALU.mult, ALU.mult, [("gb", sl), ("h32", b)],
                                ["junk", ("actv", bi)], accum=actv[:, hk:hk + 1])
                            if q8 == 1 and mid is not None:
                                mid()
                        act(gact[:, bi * 8:(bi + 1) * 8], actv[:, bi * 8:(bi + 1) * 8], AF.Gelu, [("actv", bi)], [("gact", bi)])

                    def stage2(bi):
                        tt(gw2[:, bi * 8:(bi + 1) * 8], gw[b][:, bi * 8:(bi + 1) * 8], gact[:, bi * 8:(bi + 1) * 8], ALU.mult,
                           [("gw", b), ("gact", bi)], [("gw2", bi)])
                        for q8 in range(8):
                            hk = bi * 8 + q8
                            sl = slots[hk]
                            dd = state["dcnt"] % 4
                            state["dcnt"] += 1
                            act(dgt[dd][:], identb[:], AF.Copy, ["identb", ("gw2", bi)], [("dg", dd)], scale=gw2[:, hk:hk + 1])
                            mm(PB0, dgt[dd][:], gbuf[sl][:, D:D + 512], hk == 0, hk == 127, [("dg", dd), ("gb", sl)], ["pB0"])
                            mm(PB1, dgt[dd][:], gbuf[sl][:, D + 512:2 * D], hk == 0, hk == 127, [("dg", dd), ("gb", sl)], ["pB1"])

                    slots = {}
                    stage1(0)
                    for bi in range(1, 16):
                        stage1(bi, mid=lambda bi=bi: stage2(bi - 1))
                        if nxt is not None:
                            next(nxt, None)
                            next(nxt, None)
                    stage2(15)
                    if nxt is not None:
                        for _ in nxt:
                            pass
                    tt(xo[:, 0:512], PB0, g2b[r][:, 0:512], ALU.mult, ["pB0", ("g2b", r)], ["xo"])
                    tt(xo[:, 512:D], PB1, g2b[r][:, 512:D], ALU.mult, ["pB1", ("g2b", r)], ["xo"])
                    tt(xo[:], xo[:], xt[b][:], ALU.add, ["xo", ("xt", b)], ["xo"])
                    if last:
                        DS(out[(ti - 2) * 128:(ti - 1) * 128, :], xo[:], r=["xo"], w=[("out", ti)])
                    else:
                        DS(xs[ti * 128:(ti + 1) * 128, :], xo[:], r=["xo"], w=[XS[ti]])
                    if tix + 2 < len(tiles):
                        tn = tiles[tix + 2]
                        DS(xt[b][:], xs[tn * 128:(tn + 1) * 128, :], r=[XS[tn]], w=[("xt", b)])

                DS(xt[0][:], xs[tiles[0] * 128:(tiles[0] + 1) * 128, :], r=[XS[tiles[0]]], w=[("xt", 0)])
                if len(tiles) > 1:
                    DS(xt[1][:], xs[tiles[1] * 128:(tiles[1] + 1) * 128, :], r=[XS[tiles[1]]], w=[("xt", 1)])
                for _ in front(0):
                    pass
                for tix in range(len(tiles)):
                    nxt = front(tix + 1) if tix + 1 < len(tiles) else None
                    back(tix, nxt)
            em.barrier_all()

        em.finish("sync")
        semkeys = list(ENGS) + [("dma", i) for i in range(em.n_dma)]
        sems = {k: top.enter_context(nc.semaphore(f"sem{j}")) for j, k in enumerate(semkeys)}
        with nc.Block() as block:
            @block.sync
            def _(e):
                em.replay(sems, "sync", e)

            @block.scalar
            def _(e):
                em.replay(sems, "scalar", e)

            @block.vector
            def _(e):
                em.replay(sems, "vector", e)

            @block.gpsimd
            def _(e):
                em.replay(sems, "gpsimd", e)

            @block.tensor
            def _(e):
                em.replay(sems, "tensor", e)
    return nc, em


_CONST = {}


def _constants():
    if _CONST:
        return _CONST
    bf = ml_dtypes.bfloat16
    t = np.arange(SEQ, dtype=np.int64)
    m = (t[:, None] * t[None, :]) % SEQ
    ang = (2.0 * np.pi / SEQ) * m.astype(np.float64)
    dftL = np.empty((2, SEQ, SEQ), dtype=bf)
    dftL[0] = (np.cos(ang) / 64.0).astype(np.float32).astype(bf)
    dftL[1] = (-np.sin(ang) / 64.0).astype(np.float32).astype(bf)
    del ang, m
    t2 = np.arange(CTX, dtype=np.int64)
    a2 = (2.0 * np.pi / CTX) * ((t2[:, None] * t2[None, :]) % CTX).astype(np.float64)
    dft256 = np.stack([np.cos(a2) / 16.0, -np.sin(a2) / 16.0]).astype(np.float32).astype(bf)
    c = np.arange(128, dtype=np.int64)
    a3 = (2.0 * np.pi / 128) * ((c[:, None] * c[None, :]) % 128).astype(np.float64)
    s128 = 1.0 / math.sqrt(128.0)
    dftC = np.stack([np.cos(a3) * s128, np.sin(a3) * s128]).astype(np.float32).astype(bf)
    freqs = (10000.0 ** (-np.arange(0, 32, 2, dtype=np.float32) / 32.0)).astype(np.float32)
    rr = (t // 64).astype(np.float32)
    cc = (t % 64).astype(np.float32)
    ang_r = rr[:, None] * freqs[None, :]
    ang_c = cc[:, None] * freqs[None, :]
    rope = np.concatenate([np.cos(ang_r), np.cos(ang_c), np.sin(ang_r), np.sin(ang_c)], axis=1).astype(np.float32)
    _CONST.update(dftL=dftL, dft256=dft256, dftC=dftC, rope=rope, identf=np.eye(128, dtype=np.float32))
    return _CONST


def make_in_maps(inputs, depth=DEPTH, cores=NCORES):
    f = lambda a: np.ascontiguousarray(np.asarray(a, dtype=np.float32))
    cst = _constants()
    x = f(inputs["x"]); c = f(inputs["c"]); ctx = f(inputs["ctx"]); c_ctx = f(inputs["c_ctx"])
    sl = slice(0, depth)
    shared = {
        "w_ada": f(inputs["w_ada"])[sl], "b_ada": f(inputs["b_ada"])[sl],
        "norm1_g": f(inputs["norm1_g"])[sl], "norm2_g": f(inputs["norm2_g"])[sl],
        "w_in": f(inputs["w_in"])[sl],
        "bgate_c": np.ascontiguousarray(f(inputs["b_gate"])[sl].reshape(depth, 24, 128).transpose(0, 2, 1)),
        "q_norm_g": f(inputs["q_norm_g"])[sl], "k_norm_g": f(inputs["k_norm_g"])[sl],
        "lam_params": f(inputs["lam_params"])[sl].reshape(depth, 256),
        "subln_c": f(inputs["subln_g"])[sl].reshape(depth, 128, 1),
        "cm_ln_g": f(inputs["cm_ln_g"])[sl],
        "cmws_T": np.ascontiguousarray(f(inputs["cm_w_s"])[sl].transpose(0, 3, 1, 2)),
        "cmbs_c": np.ascontiguousarray(f(inputs["cm_b_s"])[sl].transpose(0, 2, 1)),
        "w_branch": f(inputs["w_branch"])[sl], "w_out": f(inputs["w_out"])[sl],
        "peer_w_q": f(inputs["peer_w_q"])[sl],
        "keysT": np.ascontiguousarray(f(inputs["peer_sub_keys"])[sl].reshape(depth, 16, 128, 128).transpose(0, 3, 1, 2)),
        "peer_u": f(inputs["peer_u"])[sl], "peer_v": f(inputs["peer_v"])[sl],
        "identf": cst["identf"], "rope": cst["rope"], "dftL": cst["dftL"], "dft256": cst["dft256"], "dftC": cst["dftC"],
    }
    maps = []
    for b in range(cores):
        cv = np.stack([c[b], c_ctx], axis=0)
        cT = np.ascontiguousarray(cv.reshape(2, 8, 128).transpose(2, 1, 0))
        mp = dict(shared)
        mp.update({"x": x[b], "ctx": ctx[b], "cT": cT})
        maps.append(mp)
    return maps


_NC = {}


def kernel(**inputs):
    if "nc" not in _NC:
        _NC["nc"] = build()[0]
    nc = _NC["nc"]
    maps = make_in_maps(inputs)
    res = run_bass_kernel_spmd(nc, maps, core_ids=list(range(NCORES)))
    outs = [np.asarray(r["out"], dtype=np.float32) for r in res.results]
    return np.stack(outs, axis=0)
```

```python
import math
from contextlib import ExitStack

import numpy as np
import ml_dtypes

import concourse.bass as bass
import concourse.mybir as mybir
from concourse.bass_utils import run_bass_kernel_spmd

F32 = mybir.dt.float32
BF16 = mybir.dt.bfloat16
U32 = mybir.dt.uint32
AF = mybir.ActivationFunctionType
ALU = mybir.AluOpType
AX = mybir.AxisListType

D = 1024
SEQ = 4096
CTX = 256
NTOK = SEQ + CTX
DEPTH = 4
NCORES = 8
EPS = 1e-6
IN_COLS = 6144
NEXP = 16384

ENGS = ["tensor", "vector", "scalar", "gpsimd", "sync"]


class Emitter:
    def __init__(self, nc, n_dma_sems=28):
        self.nc = nc
        self.lists = {e: [] for e in ENGS}
        self.cnt = {e: 0 for e in ENGS}
        self.known = {e: {} for e in ENGS}
        self.last_w = {}
        self.readers = {}
        self.n_dma = n_dma_sems
        self.dma_val = [0] * n_dma_sems
        self.dma_rr = 0
        self.n_inst = 0

    def _deps(self, reads, writes, eng=None):
        deps = {}

        def add(d, same_ok):
            if d is None:
                return
            s, v = d
            if not same_ok and s == eng:
                return
            if deps.get(s, 0) < v:
                deps[s] = v

        for k in reads:
            add(self.last_w.get(k), True)
        for k in writes:
            add(self.last_w.get(k), False)
            for r in self.readers.get(k, ()):
                add(r, False)
        return deps

    def _emit_waits(self, eng, deps):
        kn = self.known[eng]
        for s, v in deps.items():
            if eng == "tensor" and s == "tensor":
                continue
            if kn.get(s, 0) >= v:
                continue
            kn[s] = v
            self.lists[eng].append(("wait", s, v))

    def _commit(self, token, reads, writes):
        for k in reads:
            lst = self.readers.setdefault(k, [])
            lst.append(token)
            if len(lst) > 64:
                mx = {}
                for s, v in lst:
                    if mx.get(s, 0) < v:
                        mx[s] = v
                self.readers[k] = list(mx.items())
        for k in writes:
            self.last_w[k] = token
            self.readers[k] = []

    def op(self, eng, fn, reads=(), writes=()):
        deps = self._deps(reads, writes, eng)
        self._emit_waits(eng, deps)
        self.cnt[eng] += 1
        token = (eng, self.cnt[eng])
        self.lists[eng].append(("op", fn, eng, 1))
        self._commit(token, reads, writes)
        self.n_inst += 1
        return token

    def dma(self, eng, fn, reads=(), writes=()):
        deps = self._deps(reads, writes)
        i = self.dma_rr
        self.dma_rr = (self.dma_rr + 1) % self.n_dma
        s = ("dma", i)
        if self.dma_val[i] > 0:
            deps[s] = max(deps.get(s, 0), self.dma_val[i])
        self._emit_waits(eng, deps)
        self.dma_val[i] += 16
        token = (s, self.dma_val[i])
        self.lists[eng].append(("op", fn, s, 16))
        self._commit(token, reads, writes)
        self.n_inst += 1
        return token

    def _all(self):
        deps = {e: self.cnt[e] for e in ENGS if self.cnt[e] > 0}
        for i in range(self.n_dma):
            if self.dma_val[i] > 0:
                deps[("dma", i)] = self.dma_val[i]
        return deps

    def barrier_all(self):
        deps = self._all()
        for e in ENGS:
            self._emit_waits(e, dict(deps))

    def finish(self, eng="sync"):
        self._emit_waits(eng, self._all())

    def replay(self, sems, engname, engobj):
        for item in self.lists[engname]:
            if item[0] == "wait":
                engobj.wait_ge(sems[item[1]], item[2])
            else:
                _, fn, s, inc = item
                fn(engobj).then_inc(sems[s], inc)


def build(depth=DEPTH, total_depth=DEPTH, dbg=False, stop_after=None):
    nc = bass.Bass("TRN2", target_bir_lowering=False)
    em = Emitter(nc)

    def din(name, shape, dt=F32):
        return nc.dram_tensor(name, list(shape), dt, kind="ExternalInput").ap()

    x_in = din("x", [SEQ, D])
    ctx_in = din("ctx", [CTX, D])
    cT_in = din("cT", [128, 8, 2])
    w_ada = din("w_ada", [depth, D, 6 * D])
    b_ada = din("b_ada", [depth, 6 * D])
    norm1_g = din("norm1_g", [depth, D])
    norm2_g = din("norm2_g", [depth, D])
    w_in = din("w_in", [depth, D, IN_COLS])
    bgate_c = din("bgate_c", [depth, 128, 24])
    q_norm_g = din("q_norm_g", [depth, 64])
    k_norm_g = din("k_norm_g", [depth, 64])
    lam_params = din("lam_params", [depth, 256])
    subln_c = din("subln_c", [depth, 128, 1])
    cm_ln_g = din("cm_ln_g", [depth, 512])
    cmws_T = din("cmws_T", [depth, 128, 4, 128])
    cmbs_c = din("cmbs_c", [depth, 128, 4])
    w_branch = din("w_branch", [depth, 3, 512, D])
    w_out = din("w_out", [depth, D, D])
    peer_w_q = din("peer_w_q", [depth, D, 2048])
    keysT_in = din("keysT", [depth, 128, 16, 128])
    peer_u = din("peer_u", [depth, NEXP, D])
    peer_v = din("peer_v", [depth, NEXP, D])
    peer_u_flat = peer_u.rearrange("l e d -> (l e) d")
    peer_v_flat = peer_v.rearrange("l e d -> (l e) d")
    identf_in = din("identf", [128, 128])
    rope_in = din("rope", [SEQ, 64])
    dftL = din("dftL", [2, SEQ, SEQ], BF16)
    dft256 = din("dft256", [2, CTX, CTX], BF16)
    dftC = din("dftC", [2, 128, 128], BF16)

    out = nc.dram_tensor("out", [SEQ, D], F32, kind="ExternalOutput").ap()
    skind = "ExternalOutput" if dbg else "Internal"
    xs = nc.dram_tensor("xs", [NTOK, D], F32, kind=skind).ap()
    der = nc.dram_tensor("der", [2, 6, D], F32, kind=skind).ap()
    aT_s = nc.dram_tensor("aT_s", [4, 128, NTOK], BF16, kind=skind).ap()
    fT_s = nc.dram_tensor("fT_s", [4, 128, NTOK], BF16, kind=skind).ap()
    UVb = nc.dram_tensor("UVb", [depth * NEXP, 2 * D], BF16, kind="Internal").ap()

    def V(fn, r=(), w=()):
        return em.op("vector", fn, r, w)

    def A(fn, r=(), w=()):
        return em.op("scalar", fn, r, w)

    def PE(fn, r=(), w=()):
        return em.op("tensor", fn, r, w)

    def G(fn, r=(), w=()):
        return em.op("gpsimd", fn, r, w)

    def DS(out_, in_, r=(), w=()):
        return em.dma("sync", lambda e: e.dma_start(out=out_, in_=in_), r, w)

    def DG(out_, in_, r=(), w=()):
        return em.dma("gpsimd", lambda e: e.dma_start(out=out_, in_=in_), r, w)

    def tt(out_, a, b, op, r, w, eng="vector"):
        return em.op(eng, lambda e: e.tensor_tensor(out=out_, in0=a, in1=b, op=op), r, w)

    def stt(out_, a, s, b, op0, op1, r, w, accum=None):
        return V(lambda e: e.scalar_tensor_tensor(out=out_, in0=a, scalar=s, in1=b, op0=op0, op1=op1,
                                                  accum_out=accum), r, w)

    def ts(out_, a, s1, s2, op0, op1, r, w):
        if s2 is None:
            return V(lambda e: e.tensor_scalar(out=out_, in0=a, scalar1=s1, scalar2=None, op0=op0), r, w)
        return V(lambda e: e.tensor_scalar(out=out_, in0=a, scalar1=s1, scalar2=s2, op0=op0, op1=op1), r, w)

    def act(out_, in_, func, r, w, bias=None, scale=None, accum=None):
        kw = {}
        if bias is not None:
            kw["bias"] = bias
        if scale is not None:
            kw["scale"] = scale
        if accum is not None:
            kw["accum_out"] = accum
        return A(lambda e: e.activation(out=out_, in_=in_, func=func, **kw), r, w)

    def vcopy(out_, in_, r, w):
        return V(lambda e: e.tensor_copy(out=out_, in_=in_), r, w)

    def mm(out_, lhsT, rhs, start, stop, r, w):
        return PE(lambda e: e.matmul(out_, lhsT=lhsT, rhs=rhs, start=start, stop=stop), r, w)

    def tr(out_, in_, ident, r, w):
        return PE(lambda e: e.transpose(out_, in_, ident), r, w)

    def recip(out_, in_, r, w):
        return V(lambda e: e.reciprocal(out=out_, in_=in_), r, w)

    def vmax(out_, in_, r, w):
        return V(lambda e: e.max(out=out_, in_=in_), r, w)

    def vmaxidx(out_, inmax, invals, r, w):
        return V(lambda e: e.max_index(out=out_, in_max=inmax, in_values=invals), r, w)

    def vmatchrep(out_, rep, vals, r, w):
        return V(lambda e: e.match_replace(out=out_, in_to_replace=rep, in_values=vals, imm_value=-1e30), r, w)

    def vreduce(out_, in_, r, w):
        return V(lambda e: e.tensor_reduce(out=out_, in_=in_, axis=AX.X, op=ALU.add), r, w)

    def vsingle(out_, in_, scalar, op, r, w):
        return V(lambda e: e.tensor_single_scalar(out=out_, in_=in_, scalar=scalar, op=op), r, w)

    def rstd_op(t_ap, key, scale, mode):
        if mode == "ln":
            act(t_ap, t_ap, AF.Ln, [key], [key], bias=EPS, scale=scale)
            act(t_ap, t_ap, AF.Exp, [key], [key], scale=-0.5)
        else:
            act(t_ap, t_ap, AF.Sqrt, [key], [key], bias=EPS, scale=scale)
            recip(t_ap, t_ap, [key], [key])

    def gather(out_, table, idx_ap, r, w):
        return em.dma("gpsimd", lambda e: e.indirect_dma_start(
            out=out_, out_offset=None, in_=table,
            in_offset=bass.IndirectOffsetOnAxis(ap=idx_ap, axis=0)), r, w)

    with ExitStack() as top:
        _tcnt = [0]

        def T(es, name, shape, dt):
            _tcnt[0] += 1
            return es.enter_context(nc.sbuf_tensor(f"t{_tcnt[0]}_{name}", list(shape), dt))

        pA = top.enter_context(nc.psum_tensor("pA", [128, 1024], F32))
        pB = top.enter_context(nc.psum_tensor("pB", [128, 1024], F32))
        pC = top.enter_context(nc.psum_tensor("pC", [128, 1024], F32))
        pD = top.enter_context(nc.psum_tensor("pD", [128, 1024], F32))
        PA0, PA1 = pA[:, 0:512], pA[:, 512:1024]
        PB0, PB1 = pB[:, 0:512], pB[:, 512:1024]
        PC0, PC1 = pC[:, 0:512], pC[:, 512:1024]
        PD0, PD1 = pD[:, 0:512], pD[:, 512:1024]

        identf = T(top, "identf", [128, 128], F32)
        identb = T(top, "identb", [128, 128], BF16)
        onesf = T(top, "onesf", [128, 128], F32)
        onesb = T(top, "onesb", [128, 128], BF16)
        ropet = T(top, "ropet", [128, 32, 64], F32)
        io16 = T(top, "io16", [128, 16], F32)
        cact = T(top, "cact", [128, 8, 2], F32)
        CCt = T(top, "CCt", [128, 128], BF16)
        SCt = T(top, "SCt", [128, 128], BF16)
        neglam = T(top, "neglam", [128, 1], F32)
        sgcol = T(top, "sgcol", [128, 1], F32)
        lamt = T(top, "lamt", [128, 2], F32)
        lpb = T(top, "lpb", [128, 256], F32)
        lj = T(top, "lj", [128, 64], F32)

        DS(identf[:], identf_in, w=["identf"])
        vcopy(identb[:], identf[:], ["identf"], ["identb"])
        V(lambda e: e.memset(onesf[:], 1.0), w=["onesf"])
        V(lambda e: e.memset(onesb[:], 1.0), w=["onesb"])
        DS(ropet[:], rope_in.rearrange("(t p) c -> p t c", p=128), w=["ropet"])
        G(lambda e: e.iota(io16[:], pattern=[[1, 16]], base=0, channel_multiplier=0,
                           allow_small_or_imprecise_dtypes=True), w=["io16"])
        DS(cact[:], cT_in, w=["cact"])
        act(cact[:], cact[:], AF.Silu, ["cact"], ["cact"])
        DS(CCt[:], dftC[0], w=["CCt"])
        DS(SCt[:], dftC[1], w=["SCt"])
        XS = [("xs", i) for i in range(34)]
        DS(xs[0:CTX, :], ctx_in, w=XS[0:2])
        for q4 in range(4):
            DS(xs[CTX + q4 * 1024:CTX + (q4 + 1) * 1024, :], x_in[q4 * 1024:(q4 + 1) * 1024, :],
               w=XS[2 + q4 * 8:2 + (q4 + 1) * 8])

        def norm_mod(xtile, xkey, gsb, shb, mkeys, hT_out, hT_key, tp, tpkeys, sqj, ssq, h32,
                     h32key="h32", sqkey="sqj", rs="sqrt"):
            if isinstance(tp, list):
                stt(sqj[:], xtile, 1.0, xtile, ALU.mult, ALU.mult, [xkey], [sqkey, "ssq"], accum=ssq[:])
            else:
                act(sqj[:], xtile, AF.Square, [xkey], [sqkey, "ssq"], accum=ssq[:])
            rstd_op(ssq[:], "ssq", 1.0 / D, rs)
            stt(h32[:], xtile, ssq[:, 0:1], gsb, ALU.mult, ALU.mult, [xkey, "ssq", mkeys[0]], [h32key])
            tt(h32[:], h32[:], shb, ALU.add, [h32key, mkeys[1]], [h32key])
            if isinstance(tp, list):
                for hf in range(2):
                    for k4 in range(4):
                        kc = hf * 4 + k4
                        tr(tp[hf][:, k4 * 128:(k4 + 1) * 128], h32[:, kc * 128:(kc + 1) * 128], identf[:],
                           [h32key, "identf"], [tpkeys[hf]])
                    vcopy(hT_out[:, hf * 4:(hf + 1) * 4, :], tp[hf].rearrange("p (k t) -> p k t", t=128),
                          [tpkeys[hf]], [hT_key])
                return
            for kc in range(8):
                tr(tp[:, kc * 128:(kc + 1) * 128], h32[:, kc * 128:(kc + 1) * 128], identf[:],
                   [h32key, "identf"], tpkeys)
            act(hT_out, tp[:, 0:1024].rearrange("p (k t) -> p k t", t=128), AF.Copy, tpkeys, [hT_key])

        def group_norm_rope(ps, pskey, gb, gbkey, nsq, ss8, kn, ra, rb_, outb, outkey, rope_tile, rs="sqrt"):
            act(nsq[:], ps, AF.Square, [pskey], ["nsq"])
            vreduce(ss8[:], nsq[:].rearrange("p (g d) -> p g d", d=64), ["nsq"], ["ss8"])
            rstd_op(ss8[:], "ss8", 1.0 / 64, rs)
            kn3 = kn[:].rearrange("p (g d) -> p g d", d=64)
            tt(kn3, ps.rearrange("p (g d) -> p g d", d=64), ss8[:].unsqueeze(2).to_broadcast([128, 8, 64]),
               ALU.mult, [pskey, "ss8"], ["kn"])
            if rope_tile is None:
                tt(outb[:].rearrange("p (g d) -> p g d", d=64), kn3, gb[:].unsqueeze(1).to_broadcast([128, 8, 64]),
                   ALU.mult, ["kn", gbkey], [outkey])
                return
            tt(kn3, kn3, gb[:].unsqueeze(1).to_broadcast([128, 8, 64]), ALU.mult, ["kn", gbkey], ["kn"])
            kn5 = kn[:].rearrange("p (g a h f) -> p g a h f", g=8, a=2, h=2, f=16)
            ob5 = outb[:].rearrange("p (g a h f) -> p g a h f", g=8, a=2, h=2, f=16)
            x1, x2 = kn5[:, :, :, 0, :], kn5[:, :, :, 1, :]
            cosb = ropet[:, rope_tile, 0:32].rearrange("p (a f) -> p a f", a=2).unsqueeze(1).to_broadcast([128, 8, 2, 16])
            sinb = ropet[:, rope_tile, 32:64].rearrange("p (a f) -> p a f", a=2).unsqueeze(1).to_broadcast([128, 8, 2, 16])
            ra4 = ra[:].rearrange("p (g a f) -> p g a f", g=8, a=2)
            rb4 = rb_[:].rearrange("p (g a f) -> p g a f", g=8, a=2)
            tt(ra4, x1, cosb, ALU.mult, ["kn", "ropet"], ["ra"])
            tt(rb4, x2, sinb, ALU.mult, ["kn", "ropet"], ["rb"])
            tt(ob5[:, :, :, 0, :], ra4, rb4, ALU.subtract, ["ra", "rb"], [outkey])
            tt(ra4, x2, cosb, ALU.mult, ["kn", "ropet"], ["ra"])
            tt(rb4, x1, sinb, ALU.mult, ["kn", "ropet"], ["rb"])
            tt(ob5[:, :, :, 1, :], ra4, rb4, ALU.add, ["ra", "rb"], [outkey])

        def load_bcast(tile, j, r, key):
            DS(tile[:], der[r, j:j + 1, :].partition_broadcast(128), r=["der"], w=[key])

        for L in range(depth):
            last = (L == total_depth - 1)
            lam_init = 0.8 - 0.6 * math.exp(-0.3 * L)
            groups = []
            if not last:
                groups.append((0, 256, True))
            for g8 in range(8):
                groups.append((CTX + g8 * 512, 512, False))

            em.barrier_all()
            with ExitStack() as s0:
                wada = [T(s0, f"wada{i}", [128, 8, 512], F32) for i in range(2)]
                badat = [T(s0, f"badat{i}", [2, 512], F32) for i in range(2)]
                gch = [T(s0, f"gch{i}", [2, 512], F32) for i in range(2)]
                modrow = [T(s0, f"modrow{i}", [2, 512], F32) for i in range(2)]
                jmap = {0: 1, 1: 0, 2: 2, 3: 4, 4: 3, 5: 5}
                for nt in range(12):
                    b = nt % 2
                    part, half = nt // 2, nt % 2
                    DS(wada[b][:], w_ada[L][:, nt * 512:(nt + 1) * 512].rearrange("(kc p) n -> p kc n", p=128),
                       w=[("wada", b)])
                    DS(badat[b][:], b_ada[L:L + 1, nt * 512:(nt + 1) * 512].partition_broadcast(2), w=[("badat", b)])
                    for kc in range(8):
                        mm(PA0[0:2, :], cact[:, kc, :], wada[b][:, kc, :], kc == 0, kc == 7,
                           [("wada", b), "cact"], ["pA0"])
                    tt(modrow[b][:], PA0[0:2, :], badat[b][:], ALU.add, ["pA0", ("badat", b)], [("modrow", b)])
                    if part in (1, 4):
                        ng = norm1_g if part == 1 else norm2_g
                        DS(gch[b][:], ng[L:L + 1, half * 512:(half + 1) * 512].partition_broadcast(2), w=[("gch", b)])
                        stt(modrow[b][:], modrow[b][:], 1.0, gch[b][:], ALU.add, ALU.mult,
                            [("modrow", b), ("gch", b)], [("modrow", b)])
                    DG(der[:, jmap[part], half * 512:(half + 1) * 512], modrow[b][:], r=[("modrow", b)], w=["der"])
                DS(lpb[:], lam_params[L:L + 1, :].partition_broadcast(128), w=["lpb"])
                stt(lj[:], lpb[:, 0:64], 1.0, lpb[:, 64:128], ALU.mult, ALU.mult, ["lpb"], ["lj", "lamt"], accum=lamt[:, 0:1])
                stt(lj[:], lpb[:, 128:192], 1.0, lpb[:, 192:256], ALU.mult, ALU.mult, ["lpb"], ["lj", "lamt"], accum=lamt[:, 1:2])
                act(lamt[:], lamt[:], AF.Exp, ["lamt"], ["lamt"])
                tt(neglam[:], lamt[:, 1:2], lamt[:, 0:1], ALU.subtract, ["lamt"], ["neglam"])
                ts(neglam[:], neglam[:], -lam_init, None, ALU.add, None, ["neglam"], ["neglam"])
                DS(sgcol[:], subln_c[L], w=["sgcol"])
                ts(sgcol[:], sgcol[:], 1.0 - lam_init, None, ALU.mult, None, ["sgcol"], ["sgcol"])
            em.barrier_all()
            if stop_after == "P0":
                break

            with ExitStack() as sZ:
                Zs = T(sZ, "Zs", [128, 34, 512], BF16)
                gs1b = [T(sZ, f"gs1b{r}", [128, D], F32) for r in range(2)]
                sh1b = [T(sZ, f"sh1b{r}", [128, D], F32) for r in range(2)]
                for r in range(2):
                    load_bcast(gs1b[r], 0, r, ("gs1b", r))
                    load_bcast(sh1b[r], 1, r, ("sh1b", r))
                with ExitStack() as sKV:
                    KT = T(sKV, "KT", [128, 4, NTOK], BF16)
                    Vs = T(sKV, "Vs", [128, 34, 512], BF16)
                    xt = [T(sKV, f"xt{i}", [128, D], F32) for i in range(2)]
                    sqj = T(sKV, "sqj", [128, D], F32)
                    ssq = T(sKV, "ssq", [128, 1], F32)
                    h32 = T(sKV, "h32", [128, D], F32)
                    nsq = T(sKV, "nsq", [128, 512], F32)
                    ss8 = T(sKV, "ss8", [128, 8], F32)
                    kn = T(sKV, "kn", [128, 512], F32)
                    ra = T(sKV, "ra", [128, 256], F32)
                    rb_ = T(sKV, "rb", [128, 256], F32)
                    kb = T(sKV, "kb", [128, 512], BF16)
                    kgb = T(sKV, "kgb", [128, 64], F32)
                    qgb = T(sKV, "qgb", [128, 64], F32)
                    DS(kgb[:], k_norm_g[L:L + 1, :].partition_broadcast(128), w=["kgb"])
                    DS(qgb[:], q_norm_g[L:L + 1, :].partition_broadcast(128), w=["qgb"])
                    ts(qgb[:], qgb[:], 0.125, None, ALU.mult, None, ["qgb"], ["qgb"])
                    with ExitStack() as s1:
                        w1 = T(s1, "w1", [128, 8, 1536], BF16)
                        hT1 = [T(s1, f"hT1_{i}", [128, 8, 128], BF16) for i in range(2)]
                        wsrc = w_in[L].rearrange("(kc p) n -> p kc n", p=128)
                        DG(w1[:, :, 0:512], wsrc[:, :, 512:1024], w=["w1"])
                        DG(w1[:, :, 512:1024], wsrc[:, :, 1024:1536], w=["w1"])
                        DG(w1[:, :, 1024:1536], wsrc[:, :, 2560:3072], w=["w1"])
                        DS(xt[0][:], xs[0:128, :], r=[XS[0]], w=[("xt", 0)])
                        for i in range(34):
                            b = i % 2
                            r = 1 if i < 2 else 0
                            if i + 1 < 34:
                                DS(xt[1 - b][:], xs[(i + 1) * 128:(i + 2) * 128, :], r=[XS[i + 1]], w=[("xt", 1 - b)])
                            tp, tpk = (pA, ["pA0", "pA1"]) if b == 0 else (pD, ["pD0", "pD1"])
                            norm_mod(xt[b][:], ("xt", b), gs1b[r][:], sh1b[r][:], [("gs1b", r), ("sh1b", r)],
                                     hT1[b][:], ("hT1", b), tp, tpk, sqj, ssq, h32)
                            for nt, (pb_, pk) in enumerate([(PB0, "pB0"), (PB1, "pB1"), (PC0, "pC0")]):
                                for kc in range(8):
                                    mm(pb_, hT1[b][:, kc, :], w1[:, kc, nt * 512:(nt + 1) * 512], kc == 0, kc == 7,
                                       [("hT1", b), "w1"], [pk])
                            group_norm_rope(PB0, "pB0", kgb, "kgb", nsq, ss8, kn, ra, rb_, kb, "kb",
                                            None if i < 2 else i - 2)
                            pcv = PC1.bitcast(BF16)
                            for h in range(4):
                                tr(pcv[:, h * 128:(h + 1) * 128], kb[:, h * 128:(h + 1) * 128], identb[:],
                                   ["kb", "identb"], ["pC1"])
                            act(KT[:, :, i * 128:(i + 1) * 128], pcv[:, 0:512].rearrange("p (h t) -> p h t", t=128),
                                AF.Copy, ["pC1"], [("KT", i)])
                            act(Vs[:, i, :], PB1, AF.Copy, ["pB1"], [("Vs", i)])
                            vcopy(Zs[:, i, :], PC0, ["pC0"], [("Zs", i)])
                    em.barrier_all()
                    if stop_after == "P1":
                        break
                    with ExitStack() as s2:
                        wq = T(s2, "wq", [128, 8, 512], BF16)
                        hT = T(s2, "hT", [128, 8, 128], BF16)
                        QT = [T(s2, f"QT{i}", [128, 4, 512], BF16) for i in range(2)]
                        PTp = [T(s2, f"PTp{i}", [128, 1024], BF16) for i in range(3)]
                        zacc = [T(s2, f"zacc{m}", [128, 512], BF16) for m in range(2)]
                        rz = T(s2, "rz", [128, 512], F32)
                        O0 = T(s2, "O0", [128, 512], F32)
                        O1 = T(s2, "O1", [128, 512], F32)
                        att = T(s2, "att", [128, 512], F32)
                        asq = T(s2, "asq", [128, 512], F32)
                        rst = T(s2, "rst", [128, 512], F32)
                        aTb = [T(s2, f"aTb{i}", [128, 512], BF16) for i in range(2)]
                        DG(wq[:], w_in[L].rearrange("(kc p) n -> p kc n", p=128)[:, :, 0:512], w=["wq"])
                        cv_list = []
                        for c8 in range(8):
                            e0 = c8 * 2048
                            cv_list.append((UVb[L * NEXP + e0:L * NEXP + e0 + 2048, 0:D], peer_u[L][e0:e0 + 2048, :], ("UVb", L, c8, 0)))
                            cv_list.append((UVb[L * NEXP + e0:L * NEXP + e0 + 2048, D:2 * D], peer_v[L][e0:e0 + 2048, :], ("UVb", L, c8, 1)))
                        cv_state = {"i": 0}

                        def prep(gi):
                            row0, N, is_ctx = groups[gi]
                            r = 1 if is_ctx else 0
                            qt = QT[gi % 2]
                            qk = ("QT", gi % 2)
                            for j in range(N // 128):
                                ti = row0 // 128 + j
                                DS(xt[0][:], xs[ti * 128:(ti + 1) * 128, :], r=[XS[ti]], w=[("xt", 0)])
                                norm_mod(xt[0][:], ("xt", 0), gs1b[r][:], sh1b[r][:], [("gs1b", r), ("sh1b", r)],
                                         hT[:], "hT", [PB1, PC1], ["pB1", "pC1"], sqj, ssq, h32, rs="ln")
                                yield
                                for kc in range(8):
                                    mm(PB1, hT[:, kc, :], wq[:, kc, :], kc == 0, kc == 7, ["hT", "wq"], ["pB1"])
                                yield
                                group_norm_rope(PB1, "pB1", qgb, "qgb", nsq, ss8, kn, ra, rb_, kb, "kb",
                                                None if is_ctx else ti - 2, rs="ln")
                                yield
                                pcv = PC1.bitcast(BF16)
                                for h in range(4):
                                    tr(pcv[:, h * 128:(h + 1) * 128], kb[:, h * 128:(h + 1) * 128], identb[:],
                                       ["kb", "identb"], ["pC1"])
                                vcopy(qt[:, :, j * 128:(j + 1) * 128], pcv[:, 0:512].rearrange("p (h t) -> p h t", t=128),
                                      ["pC1"], [qk])
                                yield

                        def attend(gi, nxt):
                            row0, N, is_ctx = groups[gi]
                            qt = QT[gi % 2]
                            qk = ("QT", gi % 2)
                            if not is_ctx:
                                for _ in range(2):
                                    o_, i_, k_ = cv_list[cv_state["i"]]
                                    DG(o_, i_, w=[k_, ("cv", cv_state["i"] % 4)])
                                    cv_state["i"] += 1
                            kchunks = [0, 1] if is_ctx else list(range(34))
                            nch = len(kchunks)
                            sb = [(pD, ["pD0", "pD1"]), (pA, ["pA0", "pA1"])]
                            obank = [(PB0, "pB0"), (PC0, "pC0")]
                            step = 0
                            for h in range(4):
                                def s_mm(ci):
                                    kc = kchunks[ci]
                                    pS, pSk = sb[ci % 2]
                                    for m in range(2):
                                        lo, hi = m * 64, (m + 1) * 64
                                        mm(pS[:, m * 512:m * 512 + N], KT[lo:hi, h, kc * 128:(kc + 1) * 128], qt[lo:hi, h, 0:N],
                                           True, True, [("KT", kc), qk], [pSk[m]])

                                s_mm(0)
                                for ci, kc in enumerate(kchunks):
                                    first, lastc = ci == 0, ci == nch - 1
                                    if ci + 1 < nch:
                                        s_mm(ci + 1)
                                    pS, pSk = sb[ci % 2]
                                    pt = PTp[ci % 3]
                                    ptk = ("PT", ci % 3)
                                    act(pt[:].rearrange("p (m n) -> p m n", m=2)[:, :, 0:N],
                                        pS[:, 0:1024].rearrange("p (m n) -> p m n", m=2)[:, :, 0:N], AF.Exp, pSk, [ptk])
                                    for m in range(2):
                                        pO, pOk = obank[m]
                                        mm(pO[:, 0:N], Vs[:, kc, h * 128:(h + 1) * 128], pt[:, m * 512:m * 512 + N], first, lastc,
                                           [("Vs", kc), ptk], [pOk])
                                    for m in range(2):
                                        eng = "vector"
                                        if first:
                                            em.op(eng, lambda e, o_=zacc[m][:, 0:N], i_=pt[:, m * 512:m * 512 + N]: e.tensor_copy(out=o_, in_=i_),
                                                  [ptk], [("zacc", m)])
                                        else:
                                            tt(zacc[m][:, 0:N], zacc[m][:, 0:N], pt[:, m * 512:m * 512 + N], ALU.add,
                                               [("zacc", m), ptk], [("zacc", m)], eng=eng)
                                    step += 1
                                    if nxt is not None and step % 4 == 0:
                                        next(nxt, None)
                                mm(PA0[:, 0:N], onesb[:], zacc[0][:, 0:N], True, True, ["onesb", ("zacc", 0)], ["pA0"])
                                mm(PA1[:, 0:N], onesb[:], zacc[1][:, 0:N], True, True, ["onesb", ("zacc", 1)], ["pA1"])
                                recip(rz[:, 0:N], PA0[:, 0:N], ["pA0"], ["rz"])
                                tt(O0[:, 0:N], PB0[:, 0:N], rz[:, 0:N], ALU.mult, ["pB0", "rz"], ["O0"])
                                recip(rz[:, 0:N], PA1[:, 0:N], ["pA1"], ["rz"])
                                tt(O1[:, 0:N], PC0[:, 0:N], rz[:, 0:N], ALU.mult, ["pC0", "rz"], ["O1"])
                                stt(att[:, 0:N], O1[:, 0:N], neglam[:, 0:1], O0[:, 0:N], ALU.mult, ALU.add,
                                    ["O0", "O1", "neglam"], ["att"])
                                tt(asq[:, 0:N], att[:, 0:N], att[:, 0:N], ALU.mult, ["att"], ["asq"])
                                mm(PA0[:, 0:N], onesf[:], asq[:, 0:N], True, True, ["onesf", "asq"], ["pA0"])
                                act(rst[:, 0:N], PA0[:, 0:N], AF.Ln, ["pA0"], ["rst"], bias=EPS, scale=1.0 / 128)
                                act(rst[:, 0:N], rst[:, 0:N], AF.Exp, ["rst"], ["rst"], scale=-0.5)
                                ab = aTb[h % 2]
                                stt(ab[:, 0:N], att[:, 0:N], sgcol[:, 0:1], rst[:, 0:N], ALU.mult, ALU.mult,
                                    ["att", "sgcol", "rst"], [("aTb", h % 2)])
                                DG(aT_s[h, :, row0:row0 + N], ab[:, 0:N], r=[("aTb", h % 2)], w=[("aT", gi)])
                            if nxt is not None:
                                for _ in nxt:
                                    pass

                        for _ in prep(0):
                            pass
                        for gi in range(len(groups)):
                            attend(gi, prep(gi + 1) if gi + 1 < len(groups) else None)
                    em.barrier_all()
                if stop_after == "P2a":
                    break
                with ExitStack() as s3:
                    tabC = [T(s3, f"tabC{i}", [128, 32, 512], BF16) for i in range(2)]
                    tabS = [T(s3, f"tabS{i}", [128, 32, 512], BF16) for i in range(2)]
                    Wc = T(s3, "Wc", [128, 512], BF16)
                    Ws = T(s3, "Ws", [128, 512], BF16)
                    fTb = [T(s3, f"fTb{i}", [128, 512], BF16) for i in range(2)]

                    def load_tabs(gi):
                        row0, N, is_ctx = groups[gi]
                        tb_ = gi % 2
                        if is_ctx:
                            DS(tabC[tb_][:, 0:2, 0:256], dft256[0].rearrange("(tc p) n -> p tc n", p=128), w=[("tabC", tb_)])
                            DS(tabS[tb_][:, 0:2, 0:256], dft256[1].rearrange("(tc p) n -> p tc n", p=128), w=[("tabS", tb_)])
                        else:
                            t0 = row0 - CTX
                            for q4 in range(4):
                                DS(tabC[tb_][:, q4 * 8:(q4 + 1) * 8, :],
                                   dftL[0][q4 * 1024:(q4 + 1) * 1024, t0:t0 + 512].rearrange("(tc p) n -> p tc n", p=128),
                                   w=[("tabC", tb_)])
                                DS(tabS[tb_][:, q4 * 8:(q4 + 1) * 8, :],
                                   dftL[1][q4 * 1024:(q4 + 1) * 1024, t0:t0 + 512].rearrange("(tc p) n -> p tc n", p=128),
                                   w=[("tabS", tb_)])

                    load_tabs(0)
                    for gi, (row0, N, is_ctx) in enumerate(groups):
                        tb_ = gi % 2
                        if gi + 1 < len(groups):
                            load_tabs(gi + 1)
                        ntc, z0 = (2, 0) if is_ctx else (32, 2)
                        for g in range(4):
                            for tcx in range(ntc):
                                mm(PA0[:, 0:N], Zs[:, z0 + tcx, g * 128:(g + 1) * 128], tabC[tb_][:, tcx, 0:N],
                                   tcx == 0, tcx == ntc - 1, [("Zs", z0 + tcx), ("tabC", tb_)], ["pA0"])
                            for tcx in range(ntc):
                                mm(PA1[:, 0:N], Zs[:, z0 + tcx, g * 128:(g + 1) * 128], tabS[tb_][:, tcx, 0:N],
                                   tcx == 0, tcx == ntc - 1, [("Zs", z0 + tcx), ("tabS", tb_)], ["pA1"])
                            act(Wc[:, 0:N], PA0[:, 0:N], AF.Copy, ["pA0"], ["Wc"])
                            vcopy(Ws[:, 0:N], PA1[:, 0:N], ["pA1"], ["Ws"])
                            pF, pFk = (PB0, "pB0") if g % 2 == 0 else (PB1, "pB1")
                            mm(pF[:, 0:N], CCt[:], Wc[:, 0:N], True, False, ["CCt", "Wc"], [pFk])
                            mm(pF[:, 0:N], SCt[:], Ws[:, 0:N], False, True, ["SCt", "Ws"], [pFk])
                            fb = fTb[g % 2]
                            act(fb[:, 0:N], pF[:, 0:N], AF.Copy, [pFk], [("fTb", g % 2)])
                            DS(fT_s[g, :, row0:row0 + N], fb[:, 0:N], r=[("fTb", g % 2)], w=[("fT", gi)])
                em.barrier_all()
            if stop_after == "P2b":
                break
            with ExitStack() as s4:
                gs1b = [T(s4, f"c_gs1b{r}", [128, D], F32) for r in range(2)]
                sh1b = [T(s4, f"c_sh1b{r}", [128, D], F32) for r in range(2)]
                g1b = [T(s4, f"c_g1b{r}", [128, D], F32) for r in range(2)]
                for r in range(2):
                    load_bcast(gs1b[r], 0, r, ("gs1b", r))
                    load_bcast(sh1b[r], 1, r, ("sh1b", r))
                    load_bcast(g1b[r], 2, r, ("g1b", r))
                xg = T(s4, "xg", [128, 4, D], F32)
                sqj = T(s4, "c_sqj", [128, D], F32)
                ssq = T(s4, "c_ssq", [128, 1], F32)
                h32 = T(s4, "c_h32", [128, D], F32)
                hT = T(s4, "c_hT", [128, 8, 512], BF16)
                wzz = T(s4, "wzz", [128, 8, 1024], BF16)
                wbr = T(s4, "wbr", [128, 12, D], BF16)
                wot = T(s4, "wot", [128, 8, D], BF16)
                wgl = [T(s4, f"wgl{i}", [128, 8, 3, 128], BF16) for i in range(3)]
                wsT = T(s4, "wsT", [128, 4, 128], BF16)
                bsc = T(s4, "bsc", [128, 4], F32)
                bgc = T(s4, "bgc", [128, 24], F32)
                lngb = T(s4, "lngb", [128, 512], F32)
                u_t = T(s4, "u_t", [128, 512], F32)
                gv = T(s4, "gv", [128, 512], F32)
                bst = T(s4, "bst", [128, 6], F32)
                bag = T(s4, "bag", [128, 2], F32)
                vb = T(s4, "vb", [128, 512], BF16)
                mb = T(s4, "mb", [128, 512], BF16)
                mT = T(s4, "mT", [128, 4, 512], BF16)
                aT = T(s4, "aT", [128, 4, 512], BF16)
                fT = T(s4, "fT", [128, 4, 512], BF16)
                zT = T(s4, "zT", [128, 8, 512], BF16)
                gsig = [T(s4, f"gsig{i}", [128, 512], F32) for i in range(2)]
                zacc = T(s4, "zacc", [128, 512], F32)
                ztmp = T(s4, "ztmp", [128, 512], F32)
                otmp = T(s4, "otmp", [128, 512], F32)
                xnew = [T(s4, f"xnew{i}", [128, D], F32) for i in range(2)]
                wsrc = w_in[L].rearrange("(kc p) n -> p kc n", p=128)
                DG(wzz[:, :, 0:512], wsrc[:, :, 1536:2048], w=["wzz"])
                DG(wzz[:, :, 512:1024], wsrc[:, :, 2048:2560], w=["wzz"])
                for n3 in range(3):
                    for hf in range(2):
                        DG(wbr[:, n3 * 4:(n3 + 1) * 4, hf * 512:(hf + 1) * 512],
                           w_branch[L, n3].rearrange("(wc p) d -> p wc d", p=128)[:, :, hf * 512:(hf + 1) * 512], w=["wbr"])
                for hf in range(2):
                    DG(wot[:, :, hf * 512:(hf + 1) * 512],
                       w_out[L].rearrange("(kc p) n -> p kc n", p=128)[:, :, hf * 512:(hf + 1) * 512], w=["wot"])
                DG(wsT[:], cmws_T[L], w=["wsT"])
                DS(bsc[:], cmbs_c[L], w=["bsc"])
                DS(bgc[:], bgate_c[L], w=["bgc"])
                DS(lngb[:], cm_ln_g[L:L + 1, :].partition_broadcast(128), w=["lngb"])
                wgl_i = 0
                for gi, (row0, N, is_ctx) in enumerate(groups):
                    r = 1 if is_ctx else 0
                    nj = N // 128
                    DS(aT[:, :, 0:N], aT_s[:, :, row0:row0 + N].rearrange("h p t -> p h t"), r=[("aT", gi)], w=["aTt"])
                    DS(fT[:, :, 0:N], fT_s[:, :, row0:row0 + N].rearrange("h p t -> p h t"), r=[("fT", gi)], w=["fTt"])
                    for j in range(nj):
                        ti = row0 // 128 + j
                        DS(xg[:, j, :], xs[ti * 128:(ti + 1) * 128, :], r=[XS[ti]], w=[("xg", j)])
                        norm_mod(xg[:, j, :], ("xg", j), gs1b[r][:], sh1b[r][:], [("gs1b", r), ("sh1b", r)],
                                 hT[:, :, j * 128:(j + 1) * 128], "hT", pA, ["pA0", "pA1"], sqj, ssq, h32)
                        for nt, (pb_, pk) in enumerate([(PB0, "pB0"), (PB1, "pB1")]):
                            for kc in range(8):
                                mm(pb_, hT[:, kc, j * 128:(j + 1) * 128], wzz[:, kc, nt * 512:(nt + 1) * 512],
                                   kc == 0, kc == 7, ["hT", "wzz"], [pk])
                        act(u_t[:], PB0, AF.Gelu, ["pB0"], ["u_t"])
                        act(gv[:], PB1, AF.Gelu, ["pB1"], ["gv"])
                        V(lambda e, bst=bst, gv=gv: e.bn_stats(out=bst[:], in_=gv[:]), ["gv"], ["bst"])
                        V(lambda e, bst=bst, bag=bag: e.bn_aggr(out=bag[:], in_=bst[:]), ["bst"], ["bag"])
                        act(bag[:, 1:2], bag[:, 1:2], AF.Sqrt, ["bag"], ["bag"], bias=EPS, scale=1.0)
                        recip(bag[:, 1:2], bag[:, 1:2], ["bag"], ["bag"])
                        ts(gv[:], gv[:], bag[:, 0:1], bag[:, 1:2], ALU.subtract, ALU.mult, ["gv", "bag"], ["gv"])
                        tt(vb[:], gv[:], lngb[:], ALU.mult, ["gv", "lngb"], ["vb"])
                        for g in range(4):
                            mm(PC0[:, g * 128:(g + 1) * 128], wsT[:, g, :], vb[:, g * 128:(g + 1) * 128], True, True,
                               ["wsT", "vb"], ["pC0"])
                        for g in range(4):
                            stt(mb[:, g * 128:(g + 1) * 128], PC0[:, g * 128:(g + 1) * 128], bsc[:, g:g + 1],
                                u_t[:, g * 128:(g + 1) * 128], ALU.add, ALU.mult, ["pC0", "bsc", "u_t"], ["mb"])
                        pcv = PC1.bitcast(BF16)
                        for g in range(4):
                            tr(pcv[:, g * 128:(g + 1) * 128], mb[:, g * 128:(g + 1) * 128], identb[:],
                               ["mb", "identb"], ["pC1"])
                        act(mT[:, :, j * 128:(j + 1) * 128], pcv[:, 0:512].rearrange("p (h t) -> p h t", t=128),
                            AF.Copy, ["pC1"], ["mT"])
                    brs = [(aT, "aTt"), (mT, "mT"), (fT, "fTt")]
                    for dc in range(8):
                        wb_ = wgl_i % 3
                        wgl_i += 1
                        for n3 in range(3):
                            c0 = 3072 + n3 * 1024 + dc * 128
                            DG(wgl[wb_][:, :, n3, :], wsrc[:, :, c0:c0 + 128], w=[("wgl", wb_)])
                        for n3 in range(3):
                            brT, brk = brs[n3]
                            par = (dc * 3 + n3) % 2
                            pY, pYk = (PD0, "pD0") if par == 0 else (PC0, "pC0")
                            pG, pGk = (PD1, "pD1") if par == 0 else (PC1, "pC1")
                            for wc in range(4):
                                mm(pY[:, 0:N], wbr[:, n3 * 4 + wc, dc * 128:(dc + 1) * 128], brT[:, wc, 0:N],
                                   wc == 0, wc == 3, ["wbr", brk], [pYk])
                            for kc in range(8):
                                mm(pG[:, 0:N], wgl[wb_][:, kc, n3, :], hT[:, kc, 0:N], kc == 0, kc == 7,
                                   [("wgl", wb_), "hT"], [pGk])
                            gs_ = gsig[par]
                            act(gs_[:, 0:N], pG[:, 0:N], AF.Sigmoid, [pGk, "bgc"], [("gsig", par)],
                                bias=bgc[:, n3 * 8 + dc:n3 * 8 + dc + 1])
                            if n3 == 0:
                                tt(zacc[:, 0:N], pY[:, 0:N], gs_[:, 0:N], ALU.mult, [pYk, ("gsig", par)], ["zacc"])
                            elif n3 == 1:
                                tt(ztmp[:, 0:N], pY[:, 0:N], gs_[:, 0:N], ALU.mult, [pYk, ("gsig", par)], ["ztmp"])
                                tt(zacc[:, 0:N], zacc[:, 0:N], ztmp[:, 0:N], ALU.add, ["zacc", "ztmp"], ["zacc"])
                            else:
                                tt(ztmp[:, 0:N], pY[:, 0:N], gs_[:, 0:N], ALU.mult, [pYk, ("gsig", par)], ["ztmp"])
                                tt(zT[:, dc, 0:N], zacc[:, 0:N], ztmp[:, 0:N], ALU.add, ["zacc", "ztmp"], ["zT"])
                    for j in range(nj):
                        ti = row0 // 128 + j
                        xn_ = xnew[j % 2]
                        obanks = [(PB0, "pB0"), (PB1, "pB1")] if j % 2 == 0 else [(PA0, "pA0"), (PA1, "pA1")]
                        for hf, (pb_, pk) in enumerate(obanks):
                            for dc in range(8):
                                mm(pb_, zT[:, dc, j * 128:(j + 1) * 128], wot[:, dc, hf * 512:(hf + 1) * 512],
                                   dc == 0, dc == 7, ["zT", "wot"], [pk])
                            tt(otmp[:], pb_, g1b[r][:, hf * 512:(hf + 1) * 512], ALU.mult, [pk, ("g1b", r)], ["otmp"])
                            tt(xn_[:, hf * 512:(hf + 1) * 512], otmp[:], xg[:, j, hf * 512:(hf + 1) * 512], ALU.add,
                               ["otmp", ("xg", j)], [("xnew", j % 2)])
                        DG(xs[ti * 128:(ti + 1) * 128, :], xn_[:], r=[("xnew", j % 2)], w=[XS[ti]])
            em.barrier_all()
            if stop_after == "P2c":
                break
            with ExitStack() as s5:
                gs2b = T(s5, "gs2b", [128, D], F32)
                sh2b = T(s5, "sh2b", [128, D], F32)
                g2b = [T(s5, f"g2b{r}", [128, D], F32) for r in range(2)]
                nr = 1 if last else 2
                for r in range(nr):
                    load_bcast(g2b[r], 5, r, ("g2b", r))
                xt = [T(s5, f"p_xt{i}", [128, D], F32) for i in range(2)]
                ssq = T(s5, "p_ssq", [128, 1], F32)
                h32 = [T(s5, f"p_h32_{i}", [128, D], F32) for i in range(2)]
                hT2 = T(s5, "hT2", [128, 8, 128], BF16)
                wpq = T(s5, "wpq", [128, 8, 2048], BF16)
                keysT = T(s5, "keysT", [128, 16, 128], BF16)
                ss16 = T(s5, "ss16", [128, 16], F32)
                qn = T(s5, "qn", [128, 2048], BF16)
                qnT = T(s5, "qnT", [128, 16, 128], BF16)
                s_sb = T(s5, "s_sb", [128, 2048], F32)
                s2x = [T(s5, f"s2_{i}", [128, 128], F32) for i in range(2)]
                ta = T(s5, "ta", [128, 8, 16], F32)
                tb = T(s5, "tb", [128, 8, 16], F32)
                tcv = T(s5, "tcv", [128, 8, 16], F32)
                ia = T(s5, "ia", [128, 8, 16], U32)
                ib = T(s5, "ib", [128, 8, 16], U32)
                pos = T(s5, "pos", [128, 8, 16], U32)
                k1 = T(s5, "k1", [128, 8, 16], U32)
                k2 = T(s5, "k2", [128, 8, 16], U32)
                k1f = T(s5, "k1f", [128, 8, 16], F32)
                k2f = T(s5, "k2f", [128, 8, 16], F32)
                iaf = T(s5, "iaf", [128, 8, 16], F32)
                ibf = T(s5, "ibf", [128, 8, 16], F32)
                isel = T(s5, "isel", [128, 8, 16], F32)
                jsel = T(s5, "jsel", [128, 8, 16], F32)
                idxf = T(s5, "idxf", [128, 128], F32)
                idxu = [T(s5, f"idxu{i}", [128, 128], U32) for i in range(2)]
                cand = T(s5, "cand", [128, 16, 16], F32)
                cand2 = T(s5, "cand2", [128, 256], F32)
                ee = T(s5, "ee", [128, 8, 16], F32)
                zz = T(s5, "zz", [128, 8], F32)
                gw = [T(s5, f"gw{i}", [128, 128], F32) for i in range(2)]
                actv = T(s5, "actv", [128, 128], F32)
                gact = T(s5, "gact", [128, 128], F32)
                xo = T(s5, "xo", [128, D], F32)
                junk = T(s5, "junk", [128, D], F32)
                gw2 = T(s5, "gw2", [128, 128], F32)
                dgt = [T(s5, f"dgt{i}", [128, 128], BF16) for i in range(4)]
                rem = int(nc.sbuf_bytes_remaining)
                NS = min(24, (rem - 3072) // 4096)
                assert NS >= 16, f"PEER gather pipeline needs >= 16 slots, got {NS} (sbuf remaining {rem})"
                if L == 0:
                    print("PEER gather slots:", NS)
                gbuf = [T(s5, f"gbuf{i}", [128, 2 * D], BF16) for i in range(NS)]
                uvkeys = [("UVb", L, c8, uv) for c8 in range(8) for uv in range(2)]
                for q4 in range(4):
                    DG(wpq[:, :, q4 * 512:(q4 + 1) * 512],
                       peer_w_q[L].rearrange("(kc p) n -> p kc n", p=128)[:, :, q4 * 512:(q4 + 1) * 512], w=["wpq"])
                DG(keysT[:], keysT_in[L], w=["keysT"])
                tiles = list(range(2, 34)) if last else list(range(34))
                qbanks = [(PC0, "pC0"), (PC1, "pC1"), (PD0, "pD0"), (PD1, "pD1")]
                state = {"dcnt": 0, "gcnt": 0, "mod_r": None}

                def front(tix):
                    ti = tiles[tix]
                    b = tix % 2
                    r = 1 if ti < 2 else 0
                    if state["mod_r"] != r:
                        load_bcast(gs2b, 3, r, "gs2b")
                        load_bcast(sh2b, 4, r, "sh2b")
                        state["mod_r"] = r
                    xk = ("xt", b)
                    hk32 = ("h32", b)
                    h32b = h32[b]
                    act(junk[:], xt[b][:], AF.Square, [xk], ["junk", "ssq"], accum=ssq[:])
                    act(ssq[:], ssq[:], AF.Sqrt, ["ssq"], ["ssq"], bias=EPS, scale=1.0 / D)
                    yield
                    recip(ssq[:], ssq[:], ["ssq"], ["ssq"])
                    stt(h32b[:], xt[b][:], ssq[:, 0:1], gs2b[:], ALU.mult, ALU.mult, [xk, "ssq", "gs2b"], [hk32])
                    tt(h32b[:], h32b[:], sh2b[:], ALU.add, [hk32, "sh2b"], [hk32])
                    yield
                    for kc in range(8):
                        tr(pA[:, kc * 128:(kc + 1) * 128], h32b[:, kc * 128:(kc + 1) * 128], identf[:],
                           [hk32, "identf"], ["pA0", "pA1"])
                    yield
                    act(hT2[:], pA[:, 0:1024].rearrange("p (k t) -> p k t", t=128), AF.Copy, ["pA0", "pA1"], ["hT2"])
                    yield
                    for nt, (pb_, pk) in enumerate(qbanks):
                        for kc in range(8):
                            mm(pb_, hT2[:, kc, :], wpq[:, kc, nt * 512:(nt + 1) * 512], kc == 0, kc == 7,
                               ["hT2", "wpq"], [pk])
                    yield
                    for nt, (pb_, pk) in enumerate(qbanks):
                        act(s_sb[:, nt * 512:(nt + 1) * 512], pb_, AF.Square, [pk], ["s_sb"])
                    yield
                    vreduce(ss16[:], s_sb[:].rearrange("p (g d) -> p g d", d=128), ["s_sb"], ["ss16"])
                    yield
                    act(ss16[:], ss16[:], AF.Sqrt, ["ss16"], ["ss16"], bias=EPS, scale=1.0 / 128)
                    yield
                    recip(ss16[:], ss16[:], ["ss16"], ["ss16"])
                    yield
                    for hp in range(16):
                        pb_, pk = qbanks[hp // 4]
                        act(qn[:, hp * 128:(hp + 1) * 128], pb_[:, (hp % 4) * 128:(hp % 4 + 1) * 128], AF.Copy,
                            [pk, "ss16"], ["qn"], scale=ss16[:, hp:hp + 1])
                    yield
                    pav = pA[:, 0:1024].bitcast(BF16)
                    for hp in range(16):
                        tr(pav[:, hp * 128:(hp + 1) * 128], qn[:, hp * 128:(hp + 1) * 128], identb[:],
                           ["qn", "identb"], ["pA0", "pA1"])
                    yield
                    act(qnT[:].rearrange("p h t -> p (h t)"), pav[:, 0:2048], AF.Copy, ["pA0", "pA1"], ["qnT"])
                    yield
                    for hp in range(16):
                        pb_, pk = qbanks[hp // 4]
                        mm(pb_[:, (hp % 4) * 128:(hp % 4 + 1) * 128], qnT[:, hp, :], keysT[:, hp, :], True, True,
                           ["qnT", "keysT"], [pk])
                    yield
                    for nt, (pb_, pk) in enumerate(qbanks):
                        act(s_sb[:, nt * 512:(nt + 1) * 512], pb_, AF.Copy, [pk], ["s_sb"])
                    yield
                    for h in range(8):
                        sides = []
                        for side, (tv, iv) in enumerate([(ta, ia), (tb, ib)]):
                            sv = s_sb[:, (2 * h + side) * 128:(2 * h + side + 1) * 128]
                            tk, ik = ("ta", "ia") if side == 0 else ("tb", "ib")
                            sides.append((tv, iv, sv, tk, ik, s2x[side], ("s2", side)))
                        for tv, iv, sv, tk, ik, s2_, s2k in sides:
                            vmax(tv[:, h, 0:8], sv, ["s_sb"], [tk])
                        for tv, iv, sv, tk, ik, s2_, s2k in sides:
                            vmatchrep(s2_[:], tv[:, h, 0:8], sv, ["s_sb", tk], [s2k])
                        for tv, iv, sv, tk, ik, s2_, s2k in sides:
                            vmaxidx(iv[:, h, 0:8], tv[:, h, 0:8], sv, ["s_sb", tk], [ik])
                        for tv, iv, sv, tk, ik, s2_, s2k in sides:
                            vmax(tv[:, h, 8:16], s2_[:], [s2k], [tk])
                        for tv, iv, sv, tk, ik, s2_, s2k in sides:
                            vmaxidx(iv[:, h, 8:16], tv[:, h, 8:16], s2_[:], [s2k, tk], [ik])
                        tt(cand[:], ta[:, h, :].unsqueeze(2).to_broadcast([128, 16, 16]),
                           tb[:, h, :].unsqueeze(1).to_broadcast([128, 16, 16]), ALU.add, ["ta", "tb"], ["cand"])
                        cf = cand[:].rearrange("p a b -> p (a b)")
                        vmax(tcv[:, h, 0:8], cf, ["cand"], ["tcv"])
                        vmaxidx(pos[:, h, 0:8], tcv[:, h, 0:8], cf, ["cand", "tcv"], ["pos"])
                        vmatchrep(cand2[:], tcv[:, h, 0:8], cf, ["cand", "tcv"], ["cand2"])
                        vmax(tcv[:, h, 8:16], cand2[:], ["cand2"], ["tcv"])
                        vmaxidx(pos[:, h, 8:16], tcv[:, h, 8:16], cand2[:], ["cand2", "tcv"], ["pos"])
                        yield
                    vsingle(k1[:], pos[:], 4, ALU.arith_shift_right, ["pos"], ["k1"])
                    vsingle(k2[:], pos[:], 15, ALU.bitwise_and, ["pos"], ["k2"])
                    vcopy(k1f[:], k1[:], ["k1"], ["k1f"])
                    vcopy(k2f[:], k2[:], ["k2"], ["k2f"])
                    vcopy(iaf[:], ia[:], ["ia"], ["iaf"])
                    vcopy(ibf[:], ib[:], ["ib"], ["ibf"])
                    iob = io16[:].unsqueeze(1).unsqueeze(1).to_broadcast([128, 8, 16, 16])
                    eq4v = s_sb[:].rearrange("p (h a b) -> p h a b", h=8, a=16)
                    for kf, kfk, ixf, ixk, osel, osk in [(k1f, "k1f", iaf, "iaf", isel, "isel"),
                                                         (k2f, "k2f", ibf, "ibf", jsel, "jsel")]:
                        tt(eq4v, kf[:].unsqueeze(3).to_broadcast([128, 8, 16, 16]), iob, ALU.is_equal,
                           [kfk, "io16"], ["s_sb"])
                        tt(eq4v, eq4v, ixf[:].unsqueeze(2).to_broadcast([128, 8, 16, 16]), ALU.mult,
                           ["s_sb", ixk], ["s_sb"])
                        vreduce(osel[:], eq4v, ["s_sb"], [osk])
                    yield
                    stt(idxf[:], isel[:].rearrange("p h k -> p (h k)"), 128.0, jsel[:].rearrange("p h k -> p (h k)"),
                        ALU.mult, ALU.add, ["isel", "jsel"], ["idxf"])
                    if L > 0:
                        ts(idxf[:], idxf[:], float(L * NEXP), None, ALU.add, None, ["idxf"], ["idxf"])
                    vcopy(idxu[b][:], idxf[:], ["idxf"], [("idxu", b)])
                    tt(ee[:], tcv[:], tcv[:, :, 0:1].to_broadcast([128, 8, 16]), ALU.subtract, ["tcv"], ["ee"])
                    yield
                    act(ee[:], ee[:], AF.Exp, ["ee"], ["ee"])
                    yield
                    vreduce(zz[:], ee[:], ["ee"], ["zz"])
                    recip(zz[:], zz[:], ["zz"], ["zz"])
                    tt(gw[b][:].rearrange("p (h k) -> p h k", k=16), ee[:], zz[:].unsqueeze(2).to_broadcast([128, 8, 16]),
                       ALU.mult, ["ee", "zz"], [("gw", b)])

                def back(tix, nxt):
                    ti = tiles[tix]
                    b = tix % 2
                    r = 1 if ti < 2 else 0

                    def stage1(bi, mid=None):
                        for q8 in range(8):
                            hk = bi * 8 + q8
                            sl = state["gcnt"] % NS
                            state["gcnt"] += 1
                            slots[hk] = sl
                            gather(gbuf[sl][:], UVb, idxu[b][:, hk:hk + 1], [("idxu", b)] + uvkeys, [("gb", sl)])
                        for q8 in range(8):
                            hk = bi * 8 + q8
                            sl = slots[hk]
                            stt(junk[:], gbuf[sl][:, 0:D], 1.0, h32[b][:], ALU.mult, ALU.mult, [("gb", sl), ("h32", b)],
                                ["junk", ("actv", bi)], accum=actv[:, hk:hk + 1])
                            if q8 == 1 and mid is not None:
                                mid()
                        act(gact[:, bi * 8:(bi + 1) * 8], actv[:, bi * 8:(bi + 1) * 8], AF.Gelu, [("actv", bi)], [("gact", bi)])

                    def stage2(bi):
                        tt(gw2[:, bi * 8:(bi + 1) * 8], gw[b][:, bi * 8:(bi + 1) * 8], gact[:, bi * 8:(bi + 1) * 8], ALU.mult,
                           [("gw", b), ("gact", bi)], [("gw2", bi)])
                        for q8 in range(8):
                            hk = bi * 8 + q8
                            sl = slots[hk]
                            dd = state["dcnt"] % 4
                            state["dcnt"] += 1
                            act(dgt[dd][:], identb[:], AF.Copy, ["identb", ("gw2", bi)], [("dg", dd)], scale=gw2[:, hk:hk + 1])
                            mm(PB0, dgt[dd][:], gbuf[sl][:, D:D + 512], hk == 0, hk == 127, [("dg", dd), ("gb", sl)], ["pB0"])
                            mm(PB1, dgt[dd][:], gbuf[sl][:, D + 512:2 * D], hk == 0, hk == 127, [("dg", dd), ("gb", sl)], ["pB1"])

                    slots = {}
                    stage1(0)
                    for bi in range(1, 16):
                        stage1(bi, mid=lambda bi=bi: stage2(bi - 1))
                        if nxt is not None:
                            next(nxt, None)
                            next(nxt, None)
                    stage2(15)
                    if nxt is not None:
                        for _ in nxt:
                            pass
                    tt(xo[:, 0:512], PB0, g2b[r][:, 0:512], ALU.mult, ["pB0", ("g2b", r)], ["xo"])
                    tt(xo[:, 512:D], PB1, g2b[r][:, 512:D], ALU.mult, ["pB1", ("g2b", r)], ["xo"])
                    tt(xo[:], xo[:], xt[b][:], ALU.add, ["xo", ("xt", b)], ["xo"])
                    if last:
                        DS(out[(ti - 2) * 128:(ti - 1) * 128, :], xo[:], r=["xo"], w=[("out", ti)])
                    else:
                        DS(xs[ti * 128:(ti + 1) * 128, :], xo[:], r=["xo"], w=[XS[ti]])
                    if tix + 2 < len(tiles):
                        tn = tiles[tix + 2]
                        DS(xt[b][:], xs[tn * 128:(tn + 1) * 128, :], r=[XS[tn]], w=[("xt", b)])

                DS(xt[0][:], xs[tiles[0] * 128:(tiles[0] + 1) * 128, :], r=[XS[tiles[0]]], w=[("xt", 0)])
                if len(tiles) > 1:
                    DS(xt[1][:], xs[tiles[1] * 128:(tiles[1] + 1) * 128, :], r=[XS[tiles[1]]], w=[("xt", 1)])
                for _ in front(0):
                    pass
                for tix in range(len(tiles)):
                    nxt = front(tix + 1) if tix + 1 < len(tiles) else None
                    back(tix, nxt)
            em.barrier_all()

        em.finish("sync")
        semkeys = list(ENGS) + [("dma", i) for i in range(em.n_dma)]
        sems = {k: top.enter_context(nc.semaphore(f"sem{j}")) for j, k in enumerate(semkeys)}
        with nc.Block() as block:
            @block.sync
            def _(e):
                em.replay(sems, "sync", e)

            @block.scalar
            def _(e):
                em.replay(sems, "scalar", e)

            @block.vector
            def _(e):
                em.replay(sems, "vector", e)

            @block.gpsimd
            def _(e):
                em.replay(sems, "gpsimd", e)

            @block.tensor
            def _(e):
                em.replay(sems, "tensor", e)
    return nc, em


_CONST = {}


def _constants():
    if _CONST:
        return _CONST
    bf = ml_dtypes.bfloat16
    t = np.arange(SEQ, dtype=np.int64)
    m = (t[:, None] * t[None, :]) % SEQ
    ang = (2.0 * np.pi / SEQ) * m.astype(np.float64)
    dftL = np.empty((2, SEQ, SEQ), dtype=bf)
    dftL[0] = (np.cos(ang) / 64.0).astype(np.float32).astype(bf)
    dftL[1] = (-np.sin(ang) / 64.0).astype(np.float32).astype(bf)
    del ang, m
    t2 = np.arange(CTX, dtype=np.int64)
    a2 = (2.0 * np.pi / CTX) * ((t2[:, None] * t2[None, :]) % CTX).astype(np.float64)
    dft256 = np.stack([np.cos(a2) / 16.0, -np.sin(a2) / 16.0]).astype(np.float32).astype(bf)
    c = np.arange(128, dtype=np.int64)
    a3 = (2.0 * np.pi / 128) * ((c[:, None] * c[None, :]) % 128).astype(np.float64)
    s128 = 1.0 / math.sqrt(128.0)
    dftC = np.stack([np.cos(a3) * s128, np.sin(a3) * s128]).astype(np.float32).astype(bf)
    freqs = (10000.0 ** (-np.arange(0, 32, 2, dtype=np.float32) / 32.0)).astype(np.float32)
    rr = (t // 64).astype(np.float32)
    cc = (t % 64).astype(np.float32)
    ang_r = rr[:, None] * freqs[None, :]
    ang_c = cc[:, None] * freqs[None, :]
    rope = np.concatenate([np.cos(ang_r), np.cos(ang_c), np.sin(ang_r), np.sin(ang_c)], axis=1).astype(np.float32)
    _CONST.update(dftL=dftL, dft256=dft256, dftC=dftC, rope=rope, identf=np.eye(128, dtype=np.float32))
    return _CONST


def make_in_maps(inputs, depth=DEPTH, cores=NCORES):
    f = lambda a: np.ascontiguousarray(np.asarray(a, dtype=np.float32))
    cst = _constants()
    x = f(inputs["x"]); c = f(inputs["c"]); ctx = f(inputs["ctx"]); c_ctx = f(inputs["c_ctx"])
    sl = slice(0, depth)
    shared = {
        "w_ada": f(inputs["w_ada"])[sl], "b_ada": f(inputs["b_ada"])[sl],
        "norm1_g": f(inputs["norm1_g"])[sl], "norm2_g": f(inputs["norm2_g"])[sl],
        "w_in": f(inputs["w_in"])[sl],
        "bgate_c": np.ascontiguousarray(f(inputs["b_gate"])[sl].reshape(depth, 24, 128).transpose(0, 2, 1)),
        "q_norm_g": f(inputs["q_norm_g"])[sl], "k_norm_g": f(inputs["k_norm_g"])[sl],
        "lam_params": f(inputs["lam_params"])[sl].reshape(depth, 256),
        "subln_c": f(inputs["subln_g"])[sl].reshape(depth, 128, 1),
        "cm_ln_g": f(inputs["cm_ln_g"])[sl],
        "cmws_T": np.ascontiguousarray(f(inputs["cm_w_s"])[sl].transpose(0, 3, 1, 2)),
        "cmbs_c": np.ascontiguousarray(f(inputs["cm_b_s"])[sl].transpose(0, 2, 1)),
        "w_branch": f(inputs["w_branch"])[sl], "w_out": f(inputs["w_out"])[sl],
        "peer_w_q": f(inputs["peer_w_q"])[sl],
        "keysT": np.ascontiguousarray(f(inputs["peer_sub_keys"])[sl].reshape(depth, 16, 128, 128).transpose(0, 3, 1, 2)),
        "peer_u": f(inputs["peer_u"])[sl], "peer_v": f(inputs["peer_v"])[sl],
        "identf": cst["identf"], "rope": cst["rope"], "dftL": cst["dftL"], "dft256": cst["dft256"], "dftC": cst["dftC"],
    }
    maps = []
    for b in range(cores):
        cv = np.stack([c[b], c_ctx], axis=0)
        cT = np.ascontiguousarray(cv.reshape(2, 8, 128).transpose(2, 1, 0))
        mp = dict(shared)
        mp.update({"x": x[b], "ctx": ctx[b], "cT": cT})
        maps.append(mp)
    return maps


_NC = {}


def kernel(**inputs):
    if "nc" not in _NC:
        _NC["nc"] = build()[0]
    nc = _NC["nc"]
    maps = make_in_maps(inputs)
    res = run_bass_kernel_spmd(nc, maps, core_ids=list(range(NCORES)))
    outs = [np.asarray(r["out"], dtype=np.float32) for r in res.results]
    return np.stack(outs, axis=0)
```
